# Optimizing a Trainium2 kernel written in Bass

```python
import math
import jax, jax.numpy as jnp
from jax import lax
import numpy as np

D_MODEL = 1024
BATCH = 8
SEQ = 4096
DEPTH = 4
DEC_BATCH = 8
DEC_SEQ = 2048
PAST_LEN = 128

N_HEADS_ATT = 8
HEAD_DIM = 64
V_HEAD_DIM = 2 * HEAD_DIM
D_QK = N_HEADS_ATT * 2 * HEAD_DIM
D_ATT = N_HEADS_ATT * V_HEAD_DIM
ROT_DIM = HEAD_DIM // 4
ROPE_THETA = 500000.0
Q_BLOCK = 128
D_RNN = D_MODEL
N_RNN_BLOCKS = 16
RNN_BLOCK = D_RNN // N_RNN_BLOCKS
CONV_W = 4
CONV_LEFT = 2
RG_C = 8.0
N_BRANCH = 2
D_FF = 4 * D_MODEL
EPS = 1e-6
D_IN = 2 * D_QK + D_ATT + 2 * D_RNN + N_BRANCH * D_MODEL
SPLITS = (D_QK, 2 * D_QK, 2 * D_QK + D_ATT, 2 * D_QK + D_ATT + D_RNN, 2 * D_QK + D_ATT + 2 * D_RNN)

kernel_name = "hybrid_diffattn_rglru_encoder"


def rmsnorm(x, g):
    xf = x.astype(jnp.float32)
    y = xf * lax.rsqrt(jnp.mean(xf * xf, axis=-1, keepdims=True) + EPS) * g.astype(jnp.float32)
    return y.astype(x.dtype)


def rope_partial(x, cos, sin):
    half = ROT_DIM // 2
    x1 = x[..., :half].astype(jnp.float32)
    x2 = x[..., half:ROT_DIM].astype(jnp.float32)
    rot = jnp.concatenate([x1 * cos - x2 * sin, x2 * cos + x1 * sin], axis=-1).astype(x.dtype)
    return jnp.concatenate([rot, x[..., ROT_DIM:]], axis=-1)


def diff_attention(q1, q2, k1, k2, v, lam):
    b, s, h, dh = q1.shape
    nb = s // Q_BLOCK
    scale = dh ** -0.5
    qb = jnp.stack([q1, q2], 0).reshape(2, b, nb, Q_BLOCK, h, dh).transpose(2, 0, 1, 3, 4, 5)
    k = jnp.stack([k1, k2], 0)

    def block(qblk):
        sc = jnp.einsum('cbqhd,cbkhd->cbhqk', qblk, k).astype(jnp.float32) * scale
        p = jax.nn.softmax(sc, axis=-1)
        w = p[0] - lam * p[1]
        return jnp.einsum('bhqk,bkhe->bqhe', w.astype(v.dtype), v)

    out = lax.map(block, qb)
    return out.transpose(1, 0, 2, 3, 4).reshape(b, s, h, V_HEAD_DIM)


def centred_depthwise_conv(x, w, bias):
    s = x.shape[1]
    xp = jnp.pad(x, ((0, 0), (CONV_LEFT, CONV_W - 1 - CONV_LEFT), (0, 0)))
    out = bias
    for j in range(CONV_W):
        out = out + xp[:, j:j + s, :] * w[j]
    return out


def rg_lru(x, w_gate, b_gate, lam_param, reverse):
    b, s, _ = x.shape
    xb = x.reshape(b, s, N_RNN_BLOCKS, RNN_BLOCK)
    gates = jnp.einsum('bsnc,gncd->gbsnd', xb, w_gate).reshape(2, b, s, D_RNN)
    gates = gates.astype(jnp.float32) + b_gate.astype(jnp.float32)[:, None, None, :]
    r = jax.nn.sigmoid(gates[0])
    i = jax.nn.sigmoid(gates[1])
    log_a = -RG_C * r * jax.nn.softplus(-lam_param.astype(jnp.float32))
    a = jnp.exp(log_a)
    u = jnp.sqrt(-jnp.expm1(2.0 * log_a)) * i * x.astype(jnp.float32)

    def combine(left, right):
        a_l, h_l = left
        a_r, h_r = right
        return a_l * a_r, a_r * h_l + h_r

    _, h = lax.associative_scan(combine, (a, u), axis=1, reverse=reverse)
    return h.astype(x.dtype)


def encoder_layer(x, l, norm1_g, w_in, b_gate, q_norm_g, k_norm_g, lam_vecs, subln_g,
                  conv_w, conv_b, rg_w, rg_b, rg_L, w_branch_att, w_branch_rec, w_out,
                  norm2_g, w_ff1, w_ff2):
    b, s, d = x.shape
    lam_init = 0.8 - 0.6 * math.exp(-0.3 * l)
    n = rmsnorm(x, norm1_g)
    proj = n @ w_in
    q, k, v, xr, yr, g = jnp.split(proj, SPLITS, axis=-1)

    q = rmsnorm(q.reshape(b, s, N_HEADS_ATT, 2, HEAD_DIM), q_norm_g)
    k = rmsnorm(k.reshape(b, s, N_HEADS_ATT, 2, HEAD_DIM), k_norm_g)
    pos = jnp.arange(s, dtype=jnp.float32)
    inv_freq = ROPE_THETA ** (-jnp.arange(0, ROT_DIM, 2, dtype=jnp.float32) / ROT_DIM)
    ang = pos[:, None] * inv_freq[None, :]
    cos = jnp.cos(ang)[:, None, None, :]
    sin = jnp.sin(ang)[:, None, None, :]
    q = rope_partial(q, cos, sin)
    k = rope_partial(k, cos, sin)
    v = v.reshape(b, s, N_HEADS_ATT, V_HEAD_DIM)
    lv = lam_vecs.astype(jnp.float32)
    lam = jnp.exp(jnp.sum(lv[0] * lv[1])) - jnp.exp(jnp.sum(lv[2] * lv[3])) + lam_init
    att = diff_attention(q[..., 0, :], q[..., 1, :], k[..., 0, :], k[..., 1, :], v, lam)
    att = (rmsnorm(att, subln_g) * (1.0 - lam_init)).astype(x.dtype).reshape(b, s, D_ATT)

    xc = centred_depthwise_conv(xr, conv_w, conv_b)
    hr = rg_lru(xc, rg_w[0], rg_b[0], rg_L[0], False) + rg_lru(xc, rg_w[1], rg_b[1], rg_L[1], True)
    rec = hr * jax.nn.gelu(yr)

    gates = jax.nn.sigmoid(g.astype(jnp.float32) + b_gate.astype(jnp.float32)).reshape(b, s, N_BRANCH, d)
    merged = (gates[..., 0, :] * (att @ w_branch_att) + gates[..., 1, :] * (rec @ w_branch_rec)).astype(x.dtype)
    x = x + merged @ w_out

    h2 = rmsnorm(x, norm2_g)
    return x + jnp.square(jax.nn.relu(h2 @ w_ff1)) @ w_ff2


def trunk(x, norm1_g, w_in, b_gate, q_norm_g, k_norm_g, lam_vecs, subln_g, conv_w, conv_b,
          rg_w, rg_b, rg_L, w_branch_att, w_branch_rec, w_out, norm2_g, w_ff1, w_ff2):
    for l in range(DEPTH):
        x = encoder_layer(x, l, norm1_g[l], w_in[l], b_gate[l], q_norm_g[l], k_norm_g[l], lam_vecs[l],
                          subln_g[l], conv_w[l], conv_b[l], rg_w[l], rg_b[l], rg_L[l], w_branch_att[l],
                          w_branch_rec[l], w_out[l], norm2_g[l], w_ff1[l], w_ff2[l])
    return x


def setup_inputs(seed: int = 0) -> dict:
    key = jax.random.key(seed)
    ks = jax.random.split(key, 22)
    f32 = jnp.float32
    nrm = lambda k, shape, scale: jax.random.normal(k, shape, f32) * scale
    u = jax.random.uniform(ks[13], (DEPTH, 2, D_RNN), f32, 0.9, 0.999)
    a_base = u ** (1.0 / RG_C)
    rg_L = jnp.log(a_base) - jnp.log1p(-a_base)
    return {
        "x_prompt": nrm(ks[0], (BATCH, SEQ, D_MODEL), 1.0),
        "x_sample": nrm(ks[1], (DEC_BATCH, DEC_SEQ, D_MODEL), 1.0),
        "norm1_g": 1.0 + nrm(ks[2], (DEPTH, D_MODEL), 0.02),
        "w_in": nrm(ks[3], (DEPTH, D_MODEL, D_IN), D_MODEL ** -0.5),
        "b_gate": nrm(ks[4], (DEPTH, N_BRANCH * D_MODEL), 0.02),
        "q_norm_g": 1.0 + nrm(ks[5], (DEPTH, HEAD_DIM), 0.02),
        "k_norm_g": 1.0 + nrm(ks[6], (DEPTH, HEAD_DIM), 0.02),
        "lam_vecs": nrm(ks[7], (DEPTH, 4, HEAD_DIM), 0.1),
        "subln_g": 1.0 + nrm(ks[8], (DEPTH, V_HEAD_DIM), 0.02),
        "conv_w": nrm(ks[9], (DEPTH, CONV_W, D_RNN), CONV_W ** -0.5),
        "conv_b": nrm(ks[10], (DEPTH, D_RNN), 0.02),
        "rg_w": nrm(ks[11], (DEPTH, 2, 2, N_RNN_BLOCKS, RNN_BLOCK, RNN_BLOCK), RNN_BLOCK ** -0.5),
        "rg_b": nrm(ks[12], (DEPTH, 2, 2, D_RNN), 0.02),
        "rg_L": rg_L,
        "w_branch_att": nrm(ks[14], (DEPTH, D_ATT, D_MODEL), D_ATT ** -0.5),
        "w_branch_rec": nrm(ks[15], (DEPTH, D_RNN, D_MODEL), D_RNN ** -0.5),
        "w_out": nrm(ks[16], (DEPTH, D_MODEL, D_MODEL), D_MODEL ** -0.5),
        "norm2_g": 1.0 + nrm(ks[17], (DEPTH, D_MODEL), 0.02),
        "w_ff1": nrm(ks[18], (DEPTH, D_MODEL, D_FF), D_MODEL ** -0.5),
        "w_ff2": nrm(ks[19], (DEPTH, D_FF, D_MODEL), D_FF ** -0.5),
    }


def reference(x_prompt, x_sample, norm1_g, w_in, b_gate, q_norm_g, k_norm_g, lam_vecs, subln_g,
              conv_w, conv_b, rg_w, rg_b, rg_L, w_branch_att, w_branch_rec, w_out, norm2_g,
              w_ff1, w_ff2):
    y_prompt = trunk(x_prompt, norm1_g, w_in, b_gate, q_norm_g, k_norm_g, lam_vecs, subln_g, conv_w,
                     conv_b, rg_w, rg_b, rg_L, w_branch_att, w_branch_rec, w_out, norm2_g, w_ff1, w_ff2)
    y_sample = trunk(x_sample, norm1_g, w_in, b_gate, q_norm_g, k_norm_g, lam_vecs, subln_g, conv_w,
                     conv_b, rg_w, rg_b, rg_L, w_branch_att, w_branch_rec, w_out, norm2_g, w_ff1, w_ff2)
    return (y_prompt, y_sample)
```

```python
import math
import numpy as np
import concourse.bass as bass
import concourse.mybir as mybir
from concourse.bass_utils import run_bass_kernel_spmd
from concourse.ap import AP

F32 = mybir.dt.float32
BF16 = mybir.dt.bfloat16
ALU = mybir.AluOpType
AF = mybir.ActivationFunctionType
AX = mybir.AxisListType

D = 1024
DC = 8
D_IN = 7168
D_FF = 4096
EPS = 1e-6
NT_W = 36
TT = 512
SEM_LIMIT = 30000
SAME_ENG_SYNC = True
Q_IO = "pool"
Q_W = "sp"
NW = 4
DEBUG = False


class Slot:
    __slots__ = ("cnt", "sem")

    def __init__(self):
        self.cnt = 0
        self.sem = None


class Buf:
    __slots__ = ("name", "w", "r", "slot", "glob")

    def __init__(self, name, glob=False):
        self.name = name
        self.w = None
        self.r = {}
        self.slot = None
        self.glob = glob


class Op:
    __slots__ = ("eng", "fn", "deps", "need_inc", "idx", "dma_sem", "ev")


class Sched:
    ENGS = ("pe", "act", "dve", "pool", "sp")

    def __init__(self):
        self.ops = {e: [] for e in self.ENGS}
        self.dirty = {}
        self.slots = []
        self.pools = {}
        self.cur_pool = None
        self.cur_idx = 0

    def set_pool(self, name):
        self.cur_pool = name
        self.cur_idx = 0

    def get_slot(self, buf):
        if buf.slot is None:
            if buf.glob or self.cur_pool is None:
                sl = Slot()
                self.slots.append(sl)
                buf.slot = sl
            else:
                pool = self.pools.setdefault(self.cur_pool, [])
                if self.cur_idx >= len(pool):
                    sl = Slot()
                    pool.append(sl)
                    self.slots.append(sl)
                buf.slot = pool[self.cur_idx]
                self.cur_idx += 1
        return buf.slot

    def add(self, eng, fn, reads=(), writes=(), dma=None, extra_deps=()):
        op = Op()
        op.eng = eng
        op.fn = fn
        op.need_inc = False
        op.idx = None
        op.dma_sem = None
        deps = list(extra_deps)
        war = []
        for b in reads:
            if b.w is not None:
                deps.append(b.w)
        for b in writes:
            if b.w is not None:
                deps.append(b.w)
            for rv in b.r.values():
                if rv[0] == "eng" and rv[1].eng == eng:
                    continue
                deps.append(rv)
        if dma is not None:
            sl = self.get_slot(dma)
            sl.cnt += 16
            ev = ("dma", sl, sl.cnt)
            op.dma_sem = sl
            self.dirty[id(sl)] = ev
            rkey = ("dma", id(sl))
        else:
            ev = ("eng", op)
            rkey = ("eng", eng)
        op.ev = ev
        for d in deps:
            if d[0] == "eng":
                d[1].need_inc = True
        op.deps = deps
        for b in reads:
            b.r[rkey] = ev
        for b in writes:
            b.w = ev
            b.r = {}
        self.ops[eng].append(op)
        return op

    def fence(self):
        deps = []
        for e in self.ENGS:
            if e == "sp":
                continue
            for op in reversed(self.ops[e]):
                if op.dma_sem is None:
                    deps.append(op.ev)
                    break
        deps.extend(self.dirty.values())
        self.dirty = {}
        f = self.add("sp", lambda e: e.nop(), extra_deps=deps)
        for e in self.ENGS:
            if e != "sp":
                self.add(e, lambda en: en.nop(), extra_deps=[f.ev])
        return f


def build_program(SP, SS, L, n_layers_lam_off=0):
    nc = bass.Bass("TRN2", target_bir_lowering=False)
    S_MAX = max(SP, SS)
    sd = Sched()

    def din(name, shape, dt=F32):
        return nc.dram_tensor(name, list(shape), dt, kind="ExternalInput").ap()

    def dscr(name, shape, dt):
        if DEBUG and name != "wsc":
            return nc.dram_tensor(name, list(shape), dt, kind="ExternalOutput").ap()
        return nc.dram_tensor(name, list(shape), dt, kind="Internal").ap()

    x_in = {"p": din("xp", [SP, D]), "s": din("xs", [SS, D])}
    y_out = {"p": nc.dram_tensor("yp", [SP, D], F32, kind="ExternalOutput").ap(),
             "s": nc.dram_tensor("ys", [SS, D], F32, kind="ExternalOutput").ap()}
    w_in = din("w_in", [L, D, D_IN])
    wba = din("wba", [L, D, D])
    wbr = din("wbr", [L, D, D])
    wout = din("wout", [L, D, D])
    w1 = din("w1", [L, D, D_FF])
    w2 = din("w2", [L, D_FF, D])
    g1_d = din("g1", [L, 128, 8])
    g2_d = din("g2", [L, 128, 8])
    bgate_d = din("bgate", [L, 128, 16])
    qkg_d = din("qkg", [L, 128, 2])
    lamv_d = din("lamv", [L, 128, 256])
    subg_d = din("subg", [L, 128, 128])
    convw_d = din("convw", [L, 128, 32])
    convb_d = din("convb", [L, 128, 8])
    rgb_d = din("rgb", [L, 128, 32])
    rgL_d = din("rgL", [L, 128, 16])
    rgw_d = din("rgw", [L, 32, 128, 128])
    ident_d = din("ident", [128, 128])
    onesbd_d = din("onesbd", [128, 128])
    onesfull_d = din("onesfull", [128, 128])
    perm_d = din("perm", [128, 128])
    cos_d = din("cosT", [128, S_MAX])
    sin_d = din("sinT", [128, S_MAX])

    wsc = dscr("wsc", [L, NT_W, 128, 4096], BF16)
    xT = dscr("xT", [D, S_MAX], F32)
    qT = dscr("qT", [D, S_MAX], BF16)
    kT = dscr("kT", [D, S_MAX], BF16)
    Vs = dscr("Vs", [S_MAX, D], BF16)
    xrT = dscr("xrT", [D, S_MAX], F32)
    gyT = dscr("gyT", [D, S_MAX], BF16)
    thT = dscr("thT", [2 * D, S_MAX], BF16)
    attT = dscr("attT", [D, S_MAX], BF16)
    recT = dscr("recT", [D, S_MAX], BF16)

    ARENA_BYTES = 140 * 1024
    arena = nc.alloc_sbuf_tensor("arena", [128, ARENA_BYTES // 4], F32)
    arena_ap = arena[:] if not isinstance(arena, AP) else arena
    psum = nc.alloc_psum_tensor("psum", [128, 8, 512], F32)
    psum_ap = psum[:] if not isinstance(psum, AP) else psum

    class Arena:
        def __init__(self):
            self.off = 0

        def reset(self):
            self.off = 0

        def alloc(self, shape, dt):
            n = 1
            for s in shape:
                n *= s
            nbytes = n * (4 if dt == F32 else 2)
            nbytes = (nbytes + 31) // 32 * 32
            assert self.off + nbytes <= ARENA_BYTES, ("arena overflow", self.off, nbytes)
            a = arena_ap[:, self.off // 4:(self.off + nbytes) // 4]
            self.off += nbytes
            if dt != F32:
                a = a.bitcast(dt)
            a = a[:, 0:n]
            if len(shape) == 2:
                a = a.rearrange("p (a b) -> p a b", b=shape[1])
            elif len(shape) == 3:
                a = a.rearrange("p (a b c) -> p a b c", b=shape[1], c=shape[2])
            return a

    ar = Arena()

    def sb(name, shape, dt):
        t = nc.alloc_sbuf_tensor(name, [128] + list(shape), dt)
        return t[:] if not isinstance(t, AP) else t

    ident_f = sb("ident_f", [128], F32)
    identb = sb("identb", [128], BF16)
    onesbd = sb("onesbd_s", [128], BF16)
    onesfull = sb("onesfull_s", [128], BF16)
    perm = sb("perm_s", [128], BF16)
    const_b = Buf("consts", glob=True)
    g1 = sb("g1_s", [8], F32)
    g2 = sb("g2_s", [8], F32)
    bgh = sb("bgh_s", [16], F32)
    qkg = sb("qkg_s", [2], F32)
    lamv = sb("lamv_s", [256], F32)
    lamw = sb("lamw_s", [128], F32)
    lams = sb("lams_s", [8], F32)
    subg = sb("subg_s", [128], F32)
    convw = sb("convw_s", [32], F32)
    convb = sb("convb_s", [8], F32)
    rgbh = sb("rgbh_s", [32], F32)
    rgL = sb("rgL_s", [16], F32)
    cLh = sb("cLh_s", [16], F32)
    rgw = sb("rgw_s", [32, 128], BF16)
    par_b = Buf("params", glob=True)
    wring = [sb("wring%d" % i, [4096], BF16) for i in range(NW)]
    wring_b = [Buf("wring%d" % i, glob=True) for i in range(NW)]
    ps_b = [Buf("ps%d" % i) for i in range(8)]

    def ps(i):
        return psum_ap[:, i, :]

    def mm(out, lhsT, rhs, start, stop, reads, writes, skip=False):
        if skip:
            return sd.add("pe", lambda e: e.matmul(out, lhsT, rhs, start=start, stop=stop, skip_group_check=True), reads, writes)
        return sd.add("pe", lambda e: e.matmul(out, lhsT, rhs, start=start, stop=stop), reads, writes)

    def tr(out, in_, ident, reads, writes):
        return sd.add("pe", lambda e: e.transpose(out, in_, ident), reads, writes)

    def act(out, in_, func, reads, writes, scale=1.0, bias=0.0, accum=None):
        def f(e):
            if accum is not None:
                return e.activation(out=out, in_=in_, func=func, bias=bias, scale=scale, accum_out=accum)
            return e.activation(out=out, in_=in_, func=func, bias=bias, scale=scale)
        return sd.add("act", f, reads, writes)

    def stt(eng, out, in0, scalar, in1, op0, op1, reads, writes):
        return sd.add(eng, lambda e: e.scalar_tensor_tensor(out, in0, scalar, in1, op0, op1), reads, writes)

    def ts(eng, out, in0, s1, s2, op0, op1, reads, writes):
        if s2 is None:
            return sd.add(eng, lambda e: e.tensor_scalar(out, in0, s1, None, op0), reads, writes)
        return sd.add(eng, lambda e: e.tensor_scalar(out, in0, s1, s2, op0, op1), reads, writes)

    def tt(eng, out, in0, in1, op, reads, writes):
        return sd.add(eng, lambda e: e.tensor_tensor(out, in0, in1, op), reads, writes)

    def cp(eng, out, in_, reads, writes):
        if eng == "act":
            return sd.add("act", lambda e: e.copy(out, in_), reads, writes)
        return sd.add(eng, lambda e: e.tensor_copy(out, in_), reads, writes)

    def dma(q, out, in_, owner, reads=(), writes=()):
        return sd.add(q, lambda e: e.dma_start(out=out, in_=in_), reads, writes, dma=owner)

    def memset(eng, ap, val, writes):
        return sd.add(eng, lambda e: e.memset(ap, val), (), writes)

    dbg_owner = Buf("dbg", glob=True)
    cur = {"key": None}

    def dump(name, ap, reads):
        if not DEBUG or cur["key"] != "s":
            return
        shape = list(ap.shape)
        dt_ = nc.dram_tensor("dbg_" + name, shape, ap.dtype, kind="ExternalOutput").ap()
        dma(Q_IO, dt_, ap, dbg_owner, reads, ())

    wstate = {"n": 0}

    def wload(l, t):
        i = wstate["n"] % NW
        wstate["n"] += 1
        dma(Q_W, wring[i], wsc[l, t], wring_b[i], (), (wring_b[i],))
        return wring[i], wring_b[i]

    class WStream:
        def __init__(self, tiles):
            self.tiles = tiles
            self.q = []
            self.pos = 0
            for _ in range(min(NW - 1, len(tiles))):
                self._issue()

        def _issue(self):
            if self.pos < len(self.tiles):
                self.q.append(wload(*self.tiles[self.pos]))
                self.pos += 1

        def next(self):
            r = self.q.pop(0)
            return r

        def done_one(self):
            self._issue()

    dma("pool", ident_f, ident_d, const_b, (), (const_b,))
    dma("pool", identb, ident_d, const_b, (), (const_b,))
    dma("pool", onesbd, onesbd_d, const_b, (), (const_b,))
    dma("pool", onesfull, onesfull_d, const_b, (), (const_b,))
    dma("pool", perm, perm_d, const_b, (), (const_b,))
    cast_b = Buf("cast", glob=True)

    def cast_layer(l):
        def wview(t):
            return wsc[l, t].rearrange("p (c n) -> p c n", n=512)
        srcs = []
        win_v = w_in[l].rearrange("(c p) n -> p c n", p=128)
        for g in range(14):
            srcs.append(win_v[:, :, g * 512:(g + 1) * 512])
        a_v = wba[l].rearrange("(c p) n -> p c n", p=128)
        r_v = wbr[l].rearrange("(c p) n -> p c n", p=128)
        o_v = wout[l].rearrange("(c p) n -> p c n", p=128)
        srcs += [a_v[:, :, 0:512], r_v[:, :, 0:512], a_v[:, :, 512:1024], r_v[:, :, 512:1024],
                 o_v[:, :, 0:512], o_v[:, :, 512:1024]]
        w1_v = w1[l].rearrange("(c p) n -> p c n", p=128)
        for g in range(8):
            srcs.append(w1_v[:, :, g * 512:(g + 1) * 512])
        for t, s in enumerate(srcs):
            dma("pool", wview(t), s, cast_b)
        w2_v = w2[l].rearrange("(k p) n -> p k n", p=128)
        for oc in range(8):
            dma("pool", wsc[l, 28 + oc].rearrange("p (k n) -> p k n", n=128),
                w2_v[:, :, oc * 128:(oc + 1) * 128], cast_b)

    for l in range(L):
        cast_layer(l)
    sd.fence()

    def layer_params(l):
        lam_init = 0.8 - 0.6 * math.exp(-0.3 * l)
        pb = par_b
        for dst, src in ((g1, g1_d), (g2, g2_d), (bgh, bgate_d), (qkg, qkg_d), (lamv, lamv_d), (subg, subg_d),
                         (convw, convw_d), (convb, convb_d), (rgbh, rgb_d), (rgL, rgL_d)):
            dma("sp", dst, src[l], pb, (), (pb,))
        dma("pool", rgw, rgw_d[l].rearrange("t p n -> p t n"), pb, (), (pb,))
        ts("dve", bgh, bgh, 0.5, None, ALU.mult, None, (pb,), (pb,))
        ts("dve", rgbh, rgbh, 0.5, None, ALU.mult, None, (pb,), (pb,))
        ts("dve", subg, subg, 1.0 - lam_init, None, ALU.mult, None, (pb,), (pb,))
        tt("dve", lamw[:, 0:64], lamv[:, 0:64], lamv[:, 64:128], ALU.mult, (pb,), (pb,))
        tt("dve", lamw[:, 64:128], lamv[:, 128:192], lamv[:, 192:256], ALU.mult, (pb,), (pb,))
        sd.add("dve", lambda e: e.reduce_sum(lams[:, 0:1], lamw[:, 0:64], AX.X), (pb,), (pb,))
        sd.add("dve", lambda e: e.reduce_sum(lams[:, 1:2], lamw[:, 64:128], AX.X), (pb,), (pb,))
        act(lams[:, 2:4], lams[:, 0:2], AF.Exp, (pb,), (pb,))
        tt("dve", lams[:, 4:5], lams[:, 2:3], lams[:, 3:4], ALU.subtract, (pb,), (pb,))
        ts("dve", lams[:, 5:6], lams[:, 4:5], lam_init, -1.0, ALU.add, ALU.mult, (pb,), (pb,))
        act(cLh, rgL, AF.Exp, (pb,), (pb,), scale=-1.0)
        act(cLh, cLh, AF.Ln, (pb,), (pb,), bias=1.0)
        ts("dve", cLh, cLh, -4.0, None, ALU.mult, None, (pb,), (pb,))
        sd.fence()

    def phase0(key, S):
        ar.reset()
        sd.set_pool("P0")
        xin = [ar.alloc([4, D], F32) for _ in range(2)]
        xin_b = [Buf("xin0"), Buf("xin1")]
        xt = [ar.alloc([8, TT], F32) for _ in range(2)]
        xt_b = [Buf("xt0"), Buf("xt1")]
        xsrc = x_in[key]
        xTv = xT.rearrange("(c p) s -> p c s", p=128)
        k = 0
        for ti in range(S // TT):
            t0 = ti * TT
            i = ti % 2
            dma(Q_IO, xin[i], xsrc[t0:t0 + TT, :].rearrange("(tb p) d -> p tb d", p=128), xin_b[i], (), (xin_b[i],))
            for c in range(8):
                b = k % 8
                k += 1
                for tb in range(4):
                    tr(ps(b)[:, tb * 128:(tb + 1) * 128], xin[i][:, tb, c * 128:(c + 1) * 128], ident_f,
                       (xin_b[i], const_b), (ps_b[b],))
                cp("act" if c % 2 else "dve", xt[i][:, c, :], ps(b), (ps_b[b],), (xt_b[i],))
            dma(Q_IO, xTv[:, :, t0:t0 + TT], xt[i], xt_b[i], (xt_b[i],), ())
        sd.fence()

    def phaseA(l, S):
        ar.reset()
        sd.set_pool("A")
        xt = ar.alloc([8, TT], F32); xt_b = Buf("A.xt")
        cs = ar.alloc([2, TT], F32); cs_b = Buf("A.cs")
        sq = ar.alloc([8, TT], BF16); sq_b = Buf("A.sq")
        nT = ar.alloc([8, TT], BF16); nT_b = Buf("A.nT")
        rstd = ar.alloc([TT], F32); rstd_b = Buf("A.rstd")
        NB = 2
        sq2 = [ar.alloc([TT], BF16) for _ in range(NB)]; sq2_b = [Buf("A.sq2") for _ in range(NB)]
        r2 = [ar.alloc([TT], F32) for _ in range(NB)]; r2_b = [Buf("A.r2") for _ in range(NB)]
        qn = [ar.alloc([TT], F32) for _ in range(NB)]; qn_b = [Buf("A.qn") for _ in range(NB)]
        qnb = [ar.alloc([TT], BF16) for _ in range(NB)]; qnb_b = [Buf("A.qnb") for _ in range(NB)]
        t1 = [ar.alloc([TT], F32) for _ in range(NB)]; t1_b = [Buf("A.t1") for _ in range(NB)]
        tmp = [ar.alloc([TT], F32) for _ in range(NB)]; tmp_b = [Buf("A.tmp") for _ in range(NB)]
        qks = [ar.alloc([8, TT], BF16) for _ in range(2)]; qks_b = [Buf("A.qs"), Buf("A.ks")]
        vst = ar.alloc([4, D], BF16); vst_b = Buf("A.vs")
        xrs = ar.alloc([8, TT], F32); xrs_b = Buf("A.xrs")
        gys = ar.alloc([8, TT], BF16); gys_b = Buf("A.gys")
        ths = ar.alloc([16, TT], BF16); ths_b = Buf("A.ths")
        xTv = xT.rearrange("(c p) s -> p c s", p=128)
        qTv = [qT.rearrange("(c p) s -> p c s", p=128), kT.rearrange("(c p) s -> p c s", p=128)]
        xrTv = xrT.rearrange("(c p) s -> p c s", p=128)
        gyTv = gyT.rearrange("(c p) s -> p c s", p=128)
        thTv = thT.rearrange("(c p) s -> p c s", p=128)
        tiles = []
        for ti in range(S // TT):
            tiles += [(l, g) for g in range(14)]
        ws = WStream(tiles)
        kb = [0]

        def nb():
            b = kb[0] % 8
            kb[0] += 1
            return b
        kk = 0
        for ti in range(S // TT):
            t0 = ti * TT
            dma(Q_IO, xt, xTv[:, :, t0:t0 + TT], xt_b, (), (xt_b,))
            dma(Q_IO, cs[:, 0, :], cos_d[:, t0:t0 + TT], cs_b, (), (cs_b,))
            dma(Q_IO, cs[:, 1, :], sin_d[:, t0:t0 + TT], cs_b, (), (cs_b,))
            act(sq, xt, AF.Square, (xt_b,), (sq_b,))
            b = nb()
            for c in range(8):
                mm(ps(b), onesfull, sq[:, c, :], c == 0, c == 7, (sq_b, const_b), (ps_b[b],))
            act(rstd, ps(b), AF.Ln, (ps_b[b],), (rstd_b,), bias=EPS)
            act(rstd, rstd, AF.Exp, (rstd_b,), (rstd_b,), scale=-0.5)
            for c in range(8):
                stt("dve", nT[:, c, :], xt[:, c, :], g1[:, c:c + 1], rstd, ALU.mult, ALU.mult,
                    (xt_b, rstd_b, par_b), (nT_b,))
            for g in range(14):
                wt, wt_b = ws.next()
                wv = wt.rearrange("p (c n) -> p c n", n=512)
                if g < 4:
                    which = g // 2
                    for j in range(4):
                        h = (g % 2) * 4 + j
                        b = nb()
                        for c in range(8):
                            mm(ps(b), wv[:, c, j * 128:(j + 1) * 128], nT[:, c, :], c == 0, c == 7,
                               (nT_b, wt_b), (ps_b[b],))
                        i = kk % NB
                        kk += 1
                        act(sq2[i], ps(b), AF.Square, (ps_b[b],), (sq2_b[i],))
                        b2 = nb()
                        mm(ps(b2), onesbd, sq2[i], True, True, (sq2_b[i], const_b), (ps_b[b2],))
                        act(r2[i], ps(b2), AF.Ln, (ps_b[b2],), (r2_b[i],), bias=EPS)
                        act(r2[i], r2[i], AF.Exp, (r2_b[i],), (r2_b[i],), scale=-0.5)
                        stt("dve", qn[i], ps(b), qkg[:, which:which + 1], r2[i], ALU.mult, ALU.mult,
                            (ps_b[b], r2_b[i], par_b), (qn_b[i],))
                        cp("pool", qnb[i], qn[i], (qn_b[i],), (qnb_b[i],))
                        b3 = nb()
                        mm(ps(b3), perm, qnb[i], True, True, (qnb_b[i], const_b), (ps_b[b3],))
                        tt("pool", t1[i], qn[i], cs[:, 0, :], ALU.mult, (qn_b[i], cs_b), (t1_b[i],))
                        tt("dve", tmp[i], ps(b3), cs[:, 1, :], ALU.mult, (ps_b[b3], cs_b), (tmp_b[i],))
                        tt("dve", qks[which][:, h, :], t1[i], tmp[i], ALU.add, (t1_b[i], tmp_b[i]), (qks_b[which],))
                elif g < 6:
                    half = g - 4
                    for tb in range(4):
                        b = nb()
                        for c in range(8):
                            mm(ps(b), nT[:, c, tb * 128:(tb + 1) * 128], wv[:, c, :], c == 0, c == 7,
                               (nT_b, wt_b), (ps_b[b],))
                        cp("act", vst[:, tb, half * 512:(half + 1) * 512], ps(b), (ps_b[b],), (vst_b,))
                elif g < 8:
                    for j in range(4):
                        cc = (g - 6) * 4 + j
                        b = nb()
                        for c in range(8):
                            mm(ps(b), wv[:, c, j * 128:(j + 1) * 128], nT[:, c, :], c == 0, c == 7,
                               (nT_b, wt_b), (ps_b[b],))
                        cp("act" if j % 2 else "dve", xrs[:, cc, :], ps(b), (ps_b[b],), (xrs_b,))
                elif g < 10:
                    for j in range(4):
                        cc = (g - 8) * 4 + j
                        b = nb()
                        for c in range(8):
                            mm(ps(b), wv[:, c, j * 128:(j + 1) * 128], nT[:, c, :], c == 0, c == 7,
                               (nT_b, wt_b), (ps_b[b],))
                        i = kk % NB
                        kk += 1
                        act(r2[i], ps(b), AF.Square, (ps_b[b],), (r2_b[i],))
                        ts("dve", r2[i], r2[i], 0.044715, 1.0, ALU.mult, ALU.add, (r2_b[i],), (r2_b[i],))
                        tt("dve", qn[i], r2[i], ps(b), ALU.mult, (r2_b[i], ps_b[b]), (qn_b[i],))
                        act(t1[i], qn[i], AF.Tanh, (qn_b[i],), (t1_b[i],), scale=0.7978845608028654)
                        stt("dve", gys[:, cc, :], t1[i], 1.0, ps(b), ALU.add, ALU.mult, (t1_b[i], ps_b[b]), (gys_b,))
                else:
                    for j in range(4):
                        gi = (g - 10) * 4 + j
                        b = nb()
                        for c in range(8):
                            mm(ps(b), wv[:, c, j * 128:(j + 1) * 128], nT[:, c, :], c == 0, c == 7,
                               (nT_b, wt_b), (ps_b[b],))
                        act(ths[:, gi, :], ps(b), AF.Tanh, (ps_b[b], par_b), (ths_b,), scale=0.5, bias=bgh[:, gi:gi + 1])
                ws.done_one()
                if g == 1:
                    dma(Q_IO, qTv[0][:, :, t0:t0 + TT], qks[0], qks_b[0], (qks_b[0],), ())
                elif g == 3:
                    dma(Q_IO, qTv[1][:, :, t0:t0 + TT], qks[1], qks_b[1], (qks_b[1],), ())
                elif g == 5:
                    dma(Q_IO, Vs[t0:t0 + TT, :].rearrange("(tb p) d -> p tb d", p=128), vst, vst_b, (vst_b,), ())
                elif g == 7:
                    dma(Q_IO, xrTv[:, :, t0:t0 + TT], xrs, xrs_b, (xrs_b,), ())
                elif g == 9:
                    dma(Q_IO, gyTv[:, :, t0:t0 + TT], gys, gys_b, (gys_b,), ())
                elif g == 13:
                    dma(Q_IO, thTv[:, 0:8, t0:t0 + TT], ths[:, 0:8, :], ths_b, (ths_b,), ())
                    dma(Q_IO, thTv[:, 8:16, t0:t0 + TT], ths[:, 8:16, :], ths_b, (ths_b,), ())
        sd.fence()

    def phaseB1(l, S):
        ar.reset()
        sd.set_pool("B1")
        NKB = S // 128
        NQB = S // TT
        qh = [ar.alloc([S], BF16) for _ in range(2)]
        kh = [ar.alloc([S], BF16) for _ in range(2)]
        Vh = [ar.alloc([NKB, 130], BF16) for _ in range(2)]
        qkv_b = [Buf("B1.qkv0"), Buf("B1.qkv1")]
        NE = 3
        Eb = [[ar.alloc([TT], BF16) for _ in range(2)] for _ in range(NE)]
        Eb_b = [[Buf("B1.E") for _ in range(2)] for _ in range(NE)]
        rden = ar.alloc([16], F32); rden_b = Buf("B1.rden")
        t0b = [ar.alloc([128], F32) for _ in range(2)]; t0b_b = [Buf("B1.t0") for _ in range(2)]
        attf = [ar.alloc([128], F32) for _ in range(4)]; attf_b = [Buf("B1.attf") for _ in range(4)]
        junk = ar.alloc([128], F32); junk_b = Buf("B1.junk")
        ssq = ar.alloc([8], F32); ssq_b = Buf("B1.ssq")
        attb = [ar.alloc([128], BF16) for _ in range(4)]; attb_b = [Buf("B1.attb") for _ in range(4)]
        attTs = [ar.alloc([TT], BF16) for _ in range(2)]; attTs_b = [Buf("B1.attTs0"), Buf("B1.attTs1")]
        psT = psum_ap[:, 7, :].bitcast(BF16)
        dbgO = ar.alloc([3, 512], F32); dbgO_b = Buf("dbgO")
        for i in range(2):
            memset("dve", Vh[i][:, :, 128:129], 1.0, (qkv_b[i],))
        sbank = 0
        ei = 0
        nst = 0
        for h in range(8):
            i = h % 2
            dma(Q_IO, qh[i], qT[h * 128:(h + 1) * 128, 0:S], qkv_b[i], (), (qkv_b[i],))
            dma(Q_IO, kh[i], kT[h * 128:(h + 1) * 128, 0:S], qkv_b[i], (), (qkv_b[i],))
            for k4 in range(0, NKB, 8):
                ke = min(NKB, k4 + 8)
                dma(Q_IO, Vh[i][:, k4:ke, 0:128],
                    Vs[k4 * 128:ke * 128, h * 128:(h + 1) * 128].rearrange("(kb p) e -> p kb e", p=128),
                    qkv_b[i], (), (qkv_b[i],))
            for qb in range(NQB):
                q0 = qb * TT
                pend = None
                for kc in range(NKB + 1):
                    cur = None
                    if kc < NKB:
                        b0 = sbank * 2
                        sbank = (sbank + 1) % 2
                        e = ei % NE
                        ei += 1
                        mm(ps(b0), kh[i][0:64, kc * 128:(kc + 1) * 128], qh[i][0:64, q0:q0 + TT], True, True,
                           (qkv_b[i],), (ps_b[b0],))
                        mm(ps(b0 + 1), kh[i][64:128, kc * 128:(kc + 1) * 128], qh[i][64:128, q0:q0 + TT], True, True,
                           (qkv_b[i],), (ps_b[b0 + 1],))
                        act(Eb[e][0], ps(b0), AF.Exp, (ps_b[b0],), (Eb_b[e][0],), scale=0.125)
                        act(Eb[e][1], ps(b0 + 1), AF.Exp, (ps_b[b0 + 1],), (Eb_b[e][1],), scale=0.125)
                        cur = (e, kc)
                    if pend is not None:
                        e_, kc_ = pend
                        for c in range(2):
                            for j in range(4):
                                s = c * 4 + j
                                ob = 4 + s // 3
                                oc = (s % 3) * 129
                                mm(ps(ob)[:, oc:oc + 129], Eb[e_][c][:, j * 128:(j + 1) * 128], Vh[i][:, kc_, 0:129],
                                   kc_ == 0 and s % 3 == 0, kc_ == NKB - 1, (Eb_b[e_][c], qkv_b[i]), (ps_b[ob],), skip=True)
                    pend = cur
                if h == 0 and qb == 0:
                    dump("E0", Eb[(ei - 1) % NE][0], (Eb_b[(ei - 1) % NE][0],))
                    for ob_ in (4, 5, 6):
                        cp("dve", dbgO[:, ob_ - 4, :], ps(ob_), (ps_b[ob_],), (dbgO_b,))
                    dump("O", dbgO, (dbgO_b,))
                for s in range(8):
                    ob = 4 + s // 3
                    oc = (s % 3) * 129
                    sd.add("dve", (lambda ob=ob, oc=oc, s=s: (lambda e: e.reciprocal(rden[:, s:s + 1], ps(ob)[:, oc + 128:oc + 129])))(),
                           (ps_b[ob],), (rden_b,))
                ts("dve", rden[:, 8:12], rden[:, 4:8], lams[:, 5:6], None, ALU.mult, None, (rden_b, par_b), (rden_b,))
                memset("dve", ssq[:, 0:4], 0.0, (ssq_b,))
                for j in range(4):
                    s0 = j
                    s1 = 4 + j
                    ti_ = nst % 2
                    nst += 1
                    ts("dve", t0b[ti_], ps(4 + s0 // 3)[:, (s0 % 3) * 129:(s0 % 3) * 129 + 128], rden[:, j:j + 1], None,
                       ALU.mult, None, (ps_b[4 + s0 // 3], rden_b), (t0b_b[ti_],))
                    stt("dve", attf[j], ps(4 + s1 // 3)[:, (s1 % 3) * 129:(s1 % 3) * 129 + 128], rden[:, 8 + j:9 + j], t0b[ti_],
                        ALU.mult, ALU.add, (ps_b[4 + s1 // 3], rden_b, t0b_b[ti_]), (attf_b[j],))
                    act(junk, attf[j], AF.Square, (attf_b[j],), (junk_b, ssq_b), accum=ssq[:, j:j + 1])
                act(ssq[:, 4:8], ssq[:, 0:4], AF.Ln, (ssq_b,), (ssq_b,), scale=1.0 / 128.0, bias=EPS)
                act(ssq[:, 4:8], ssq[:, 4:8], AF.Exp, (ssq_b,), (ssq_b,), scale=-0.5)
                ai = (h * NQB + qb) % 2
                if h == 0 and qb == 0:
                    dump("rden", rden, (rden_b,))
                    dump("attf0", attf[0], (attf_b[0],))
                    dump("ssq", ssq, (ssq_b,))
                for j in range(4):
                    stt("dve", attb[j], attf[j], ssq[:, 4 + j:5 + j], subg, ALU.mult, ALU.mult,
                        (attf_b[j], ssq_b, par_b), (attb_b[j],))
                    tr(psT[:, j * 128:(j + 1) * 128], attb[j], identb, (attb_b[j], const_b), (ps_b[7],))
                if h == 0 and qb == 0:
                    dump("attb0", attb[0], (attb_b[0],))
                cp("act", attTs[ai], psT[:, 0:TT], (ps_b[7],), (attTs_b[ai],))
                dma(Q_IO, attT[h * 128:(h + 1) * 128, q0:q0 + TT], attTs[ai], attTs_b[ai], (attTs_b[ai],), ())
        sd.fence()

    def phaseB2(l, S):
        ar.reset()
        sd.set_pool("B2")
        xpad = ar.alloc([S + 8], F32); xpad_b = Buf("B2.xpad")
        gy = ar.alloc([S], BF16); gy_b = Buf("B2.gy")
        xc = ar.alloc([S], F32); xc_b = Buf("B2.xc")
        xcb = ar.alloc([S], BF16); xcb_b = Buf("B2.xcb")
        A_ = ar.alloc([S], F32); A_b = Buf("B2.A")
        B_ = ar.alloc([S], F32); B_b = Buf("B2.B")
        I_ = ar.alloc([S], F32); I_b = Buf("B2.I")
        T_ = ar.alloc([S], F32); T_b = Buf("B2.T")
        hf = ar.alloc([S], F32); hf_b = Buf("B2.hf")
        rec = ar.alloc([S], BF16); rec_b = Buf("B2.rec")
        memset("dve", xpad[:, 0:2], 0.0, (xpad_b,))
        memset("dve", xpad[:, S + 2:S + 8], 0.0, (xpad_b,))
        kb = 0

        def rev(a):
            base = a
            apl = [list(x) for x in base.ap]
            n = apl[-1][1]
            apl[-1] = [-1, n]
            return AP(base.tensor, base.offset + (n - 1), apl)
        for c in range(8):
            dma(Q_IO, xpad[:, 2:2 + S], xrT[c * 128:(c + 1) * 128, 0:S], xpad_b, (), (xpad_b,))
            dma(Q_IO, gy, gyT[c * 128:(c + 1) * 128, 0:S], gy_b, (), (gy_b,))
            ts("dve", xc, xpad[:, 0:S], convw[:, c * 4:c * 4 + 1], convb[:, c:c + 1], ALU.mult, ALU.add,
               (xpad_b, par_b), (xc_b,))
            for j in range(1, 4):
                stt("dve", xc, xpad[:, j:j + S], convw[:, c * 4 + j:c * 4 + j + 1], xc, ALU.mult, ALU.add,
                    (xpad_b, xc_b, par_b), (xc_b,))
            cp("act", xcb, xc, (xc_b,), (xcb_b,))
            if c == 0:
                dump("xc", xc, (xc_b,))
            for d in range(2):
                for blk in range(S // TT):
                    for gt in range(2):
                        b = kb % 8
                        kb += 1
                        idx = (d * 2 + gt) * 8 + c
                        mm(ps(b), rgw[:, idx, :], xcb[:, blk * TT:(blk + 1) * TT], True, True, (xcb_b, par_b), (ps_b[b],))
                        dst, dst_b = (A_, A_b) if gt == 0 else (I_, I_b)
                        act(dst[:, blk * TT:(blk + 1) * TT], ps(b), AF.Tanh, (ps_b[b], par_b), (dst_b,),
                            scale=0.5, bias=rgbh[:, idx:idx + 1])
                ci = d * 8 + c
                if c == 0 and d == 0:
                    dump("thr", A_, (A_b,))
                    dump("thi", I_, (I_b,))
                ts("dve", A_, A_, cLh[:, ci:ci + 1], cLh[:, ci:ci + 1], ALU.mult, ALU.add, (A_b, par_b), (A_b,))
                if c == 0 and d == 0:
                    dump("la", A_, (A_b,))
                act(B_, A_, AF.Exp, (A_b,), (B_b,), scale=2.0)
                act(T_, A_, AF.Tanh, (A_b,), (T_b,))
                act(A_, A_, AF.Exp, (A_b,), (A_b,))
                stt("dve", B_, B_, 1.0, T_, ALU.add, ALU.mult, (B_b, T_b), (B_b,))
                act(B_, B_, AF.Sqrt, (B_b,), (B_b,), scale=-1.0)
                if c == 0 and d == 0:
                    dump("a", A_, (A_b,))
                    dump("w", B_, (B_b,))
                stt("dve", B_, I_, 1.0, B_, ALU.add, ALU.mult, (I_b, B_b), (B_b,))
                stt("dve", B_, B_, 0.5, xc, ALU.mult, ALU.mult, (B_b, xc_b), (B_b,))
                if c == 0 and d == 0:
                    dump("u", B_, (B_b,))
                if d == 0:
                    sd.add("dve", lambda e: e.tensor_tensor_scan(hf, A_, B_, 0.0, ALU.mult, ALU.add), (A_b, B_b), (hf_b,))
                    if c == 0:
                        dump("hf", hf, (hf_b,))
                else:
                    sd.add("dve", lambda e: e.tensor_tensor_scan(rev(I_), rev(A_), rev(B_), 0.0, ALU.mult, ALU.add),
                           (A_b, B_b), (I_b,))
            if c == 0:
                dump("hrev", I_, (I_b,))
            tt("dve", hf, hf, I_, ALU.add, (hf_b, I_b), (hf_b,))
            stt("dve", rec, hf, 0.5, gy, ALU.mult, ALU.mult, (hf_b, gy_b), (rec_b,))
            dma(Q_IO, recT[c * 128:(c + 1) * 128, 0:S], rec, rec_b, (rec_b,), ())
        sd.fence()

    def phaseC(l, S, key, last):
        ar.reset()
        sd.set_pool("C")
        x = ar.alloc([8, TT], F32); x_b = Buf("C.x")
        att = ar.alloc([8, TT], BF16); att_b = Buf("C.att")
        rec = ar.alloc([8, TT], BF16); rec_b = Buf("C.rec")
        th = [ar.alloc([8, TT], BF16) for _ in range(2)]; th_b = [Buf("C.th0"), Buf("C.th1")]
        m0 = [ar.alloc([TT], F32) for _ in range(2)]; m0_b = [Buf("C.m0") for _ in range(2)]
        m1 = [ar.alloc([TT], F32) for _ in range(2)]; m1_b = [Buf("C.m1") for _ in range(2)]
        mg = ar.alloc([8, TT], BF16); mg_b = Buf("C.mg")
        sq = ar.alloc([8, TT], BF16); sq_b = Buf("C.sq")
        rstd = ar.alloc([TT], F32); rstd_b = Buf("C.rstd")
        n2 = ar.alloc([8, TT], BF16); n2_b = Buf("C.n2")
        rl = [ar.alloc([TT], F32) for _ in range(3)]; rl_b = [Buf("C.rl") for _ in range(3)]
        hh = ar.alloc([32, TT], BF16); hh_b = Buf("C.h")
        yt = [ar.alloc([D], F32) for _ in range(2)]; yt_b = [Buf("C.yt0"), Buf("C.yt1")]
        xTv = xT.rearrange("(c p) s -> p c s", p=128)
        attTv = attT.rearrange("(c p) s -> p c s", p=128)
        recTv = recT.rearrange("(c p) s -> p c s", p=128)
        thTv = thT.rearrange("(c p) s -> p c s", p=128)
        tiles = []
        for ti in range(S // TT):
            tiles += [(l, t) for t in range(14, 36)]
        ws = WStream(tiles)
        kb = [0]

        def nb():
            b = kb[0] % 8
            kb[0] += 1
            return b
        km = 0
        kr = 0
        ky = 0
        for ti in range(S // TT):
            t0 = ti * TT
            dma(Q_IO, att, attTv[:, :, t0:t0 + TT], att_b, (), (att_b,))
            dma(Q_IO, rec, recTv[:, :, t0:t0 + TT], rec_b, (), (rec_b,))
            dma(Q_IO, th[0], thTv[:, 0:8, t0:t0 + TT], th_b[0], (), (th_b[0],))
            dma(Q_IO, th[1], thTv[:, 8:16, t0:t0 + TT], th_b[1], (), (th_b[1],))
            dma(Q_IO, x, xTv[:, :, t0:t0 + TT], x_b, (), (x_b,))
            for half in range(2):
                wa, wa_b = ws.next()
                wr, wr_b = ws.next()
                wav = wa.rearrange("p (c n) -> p c n", n=512)
                wrv = wr.rearrange("p (c n) -> p c n", n=512)
                for j in range(4):
                    oc = half * 4 + j
                    bA = nb()
                    for c in range(8):
                        mm(ps(bA), wav[:, c, j * 128:(j + 1) * 128], att[:, c, :], c == 0, c == 7, (att_b, wa_b), (ps_b[bA],))
                    bR = nb()
                    for c in range(8):
                        mm(ps(bR), wrv[:, c, j * 128:(j + 1) * 128], rec[:, c, :], c == 0, c == 7, (rec_b, wr_b), (ps_b[bR],))
                    i = km % 2
                    km += 1
                    stt("dve", m0[i], th[0][:, oc, :], 1.0, ps(bA), ALU.add, ALU.mult, (th_b[0], ps_b[bA]), (m0_b[i],))
                    stt("dve", m1[i], th[1][:, oc, :], 1.0, ps(bR), ALU.add, ALU.mult, (th_b[1], ps_b[bR]), (m1_b[i],))
                    tt("pool", mg[:, oc, :], m0[i], m1[i], ALU.add, (m0_b[i], m1_b[i]), (mg_b,))
                ws.done_one()
                ws.done_one()
            for half in range(2):
                wo, wo_b = ws.next()
                wov = wo.rearrange("p (c n) -> p c n", n=512)
                for j in range(4):
                    oc = half * 4 + j
                    b = nb()
                    for c in range(8):
                        mm(ps(b), wov[:, c, j * 128:(j + 1) * 128], mg[:, c, :], c == 0, c == 7, (mg_b, wo_b), (ps_b[b],))
                    stt("dve", x[:, oc, :], ps(b), 0.5, x[:, oc, :], ALU.mult, ALU.add, (ps_b[b], x_b), (x_b,))
                ws.done_one()
            act(sq, x, AF.Square, (x_b,), (sq_b,))
            b = nb()
            for c in range(8):
                mm(ps(b), onesfull, sq[:, c, :], c == 0, c == 7, (sq_b, const_b), (ps_b[b],))
            act(rstd, ps(b), AF.Ln, (ps_b[b],), (rstd_b,), bias=EPS)
            act(rstd, rstd, AF.Exp, (rstd_b,), (rstd_b,), scale=-0.5)
            for c in range(8):
                stt("dve", n2[:, c, :], x[:, c, :], g2[:, c:c + 1], rstd, ALU.mult, ALU.mult, (x_b, rstd_b, par_b), (n2_b,))
            for g in range(8):
                w1t, w1_b = ws.next()
                w1v = w1t.rearrange("p (c n) -> p c n", n=512)
                for j in range(4):
                    f = g * 4 + j
                    b = nb()
                    for c in range(8):
                        mm(ps(b), w1v[:, c, j * 128:(j + 1) * 128], n2[:, c, :], c == 0, c == 7, (n2_b, w1_b), (ps_b[b],))
                    i = kr % 3
                    kr += 1
                    act(rl[i], ps(b), AF.Relu, (ps_b[b],), (rl_b[i],))
                    tt("pool", hh[:, f, :], rl[i], rl[i], ALU.mult, (rl_b[i],), (hh_b,))
                ws.done_one()
            for oc in range(8):
                w2t, w2_b = ws.next()
                w2v = w2t.rearrange("p (k n) -> p k n", n=128)
                b = nb()
                for kf in range(32):
                    mm(ps(b), w2v[:, kf, :], hh[:, kf, :], kf == 0, kf == 31, (hh_b, w2_b), (ps_b[b],))
                tt("dve", x[:, oc, :], ps(b), x[:, oc, :], ALU.add, (ps_b[b], x_b), (x_b,))
                ws.done_one()
            if not last:
                dma(Q_IO, xTv[:, :, t0:t0 + TT], x, x_b, (x_b,), ())
            else:
                for tb in range(4):
                    i = ky % 2
                    ky += 1
                    for hv in range(2):
                        b = nb()
                        for c4 in range(4):
                            c = hv * 4 + c4
                            tr(ps(b)[:, c4 * 128:(c4 + 1) * 128], x[:, c, tb * 128:(tb + 1) * 128], ident_f,
                               (x_b, const_b), (ps_b[b],))
                        cp("act" if hv else "dve", yt[i][:, hv * 512:(hv + 1) * 512], ps(b), (ps_b[b],), (yt_b[i],))
                    dma(Q_IO, y_out[key][t0 + tb * 128:t0 + (tb + 1) * 128, :], yt[i], yt_b[i], (yt_b[i],), ())
        sd.fence()

    for key, S in (("p", SP), ("s", SS)):
        cur["key"] = key
        phase0(key, S)
        for l in range(L):
            layer_params(l)
            phaseA(l, S)
            phaseB1(l, S)
            phaseB2(l, S)
            phaseC(l, S, key, l == L - 1)

    n_sems = {e: max(1, (len(sd.ops[e]) + SEM_LIMIT - 1) // SEM_LIMIT) for e in Sched.ENGS}
    for e in Sched.ENGS:
        k = 0
        for op in sd.ops[e]:
            if op.need_inc and op.dma_sem is None:
                k += 1
                op.idx = k
    import contextlib
    with contextlib.ExitStack() as st:
        eng_sems = {}
        for e in Sched.ENGS:
            cnt = sum(1 for op in sd.ops[e] if op.idx is not None)
            ns = max(1, (cnt + SEM_LIMIT - 1) // SEM_LIMIT)
            eng_sems[e] = [st.enter_context(nc.semaphore("se_%s_%d" % (e, i))) for i in range(ns)]
        for i, b in enumerate(sd.slots):
            b.sem = st.enter_context(nc.semaphore("sd_%d" % i))
        block = st.enter_context(nc.Block())

        def resolve(d):
            if d[0] == "dma":
                return d[1].sem, d[2], None
            op = d[1]
            ep = (op.idx - 1) // SEM_LIMIT
            return eng_sems[op.eng][ep], (op.idx - 1) % SEM_LIMIT + 1, op.eng

        def emit(e, h):
            waited = {}
            for op in sd.ops[e]:
                for d in op.deps:
                    if d[0] == "eng":
                        if d[1].idx is None:
                            continue
                        if d[1].eng == e and (e == "pe" or not SAME_ENG_SYNC):
                            continue
                    sem, val, _ = resolve(d)
                    key_ = id(sem)
                    if waited.get(key_, 0) >= val:
                        continue
                    waited[key_] = val
                    h.wait_ge(sem, val)
                ins = op.fn(h)
                if op.dma_sem is not None:
                    ins.then_inc(op.dma_sem.sem, 16)
                elif op.idx is not None:
                    ep = (op.idx - 1) // SEM_LIMIT
                    ins.then_inc(eng_sems[e][ep], 1)

        @block.tensor
        def _(h):
            emit("pe", h)

        @block.scalar
        def _(h):
            emit("act", h)

        @block.vector
        def _(h):
            emit("dve", h)

        @block.gpsimd
        def _(h):
            emit("pool", h)

        @block.sync
        def _(h):
            emit("sp", h)
    stats = {e: len(sd.ops[e]) for e in Sched.ENGS}
    return nc, stats


def _host_layout(inp, L, S_MAX):
    f = np.float32

    def fm(v, nch):
        return np.ascontiguousarray(np.asarray(v, f).reshape(nch, 128).T)
    out = {}
    out["g1"] = np.stack([fm(inp["norm1_g"][l], 8) for l in range(L)])
    out["g2"] = np.stack([fm(inp["norm2_g"][l], 8) for l in range(L)])
    out["bgate"] = np.stack([fm(inp["b_gate"][l], 16) for l in range(L)])
    qkg = np.zeros((L, 128, 2), f)
    for l in range(L):
        qkg[l, :, 0] = np.tile(np.asarray(inp["q_norm_g"][l], f), 2)
        qkg[l, :, 1] = np.tile(np.asarray(inp["k_norm_g"][l], f), 2)
    out["qkg"] = qkg
    out["lamv"] = np.ascontiguousarray(np.broadcast_to(np.asarray(inp["lam_vecs"], f)[:L].reshape(L, 1, 256), (L, 128, 256)))
    out["subg"] = np.ascontiguousarray(np.broadcast_to(np.asarray(inp["subln_g"], f)[:L].reshape(L, 1, 128), (L, 128, 128)))
    cw = np.asarray(inp["conv_w"], f)[:L]
    convw = np.zeros((L, 128, 8, 4), f)
    for l in range(L):
        for j in range(4):
            convw[l, :, :, j] = fm(cw[l, j], 8)
    out["convw"] = convw.reshape(L, 128, 32)
    out["convb"] = np.stack([fm(inp["conv_b"][l], 8) for l in range(L)])
    rb = np.asarray(inp["rg_b"], f)[:L]
    rgb = np.zeros((L, 128, 2, 2, 8), f)
    for l in range(L):
        for d in range(2):
            for g in range(2):
                rgb[l, :, d, g, :] = fm(rb[l, d, g], 8)
    out["rgb"] = rgb.reshape(L, 128, 32)
    rL = np.asarray(inp["rg_L"], f)[:L]
    rgL = np.zeros((L, 128, 2, 8), f)
    for l in range(L):
        for d in range(2):
            rgL[l, :, d, :] = fm(rL[l, d], 8)
    out["rgL"] = rgL.reshape(L, 128, 16)
    rw = np.asarray(inp["rg_w"], f)[:L]
    rgw = np.zeros((L, 2, 2, 8, 128, 128), f)
    for c in range(8):
        rgw[:, :, :, c, 0:64, 0:64] = rw[:, :, :, 2 * c]
        rgw[:, :, :, c, 64:128, 64:128] = rw[:, :, :, 2 * c + 1]
    out["rgw"] = rgw.reshape(L, 32, 128, 128)
    out["ident"] = np.eye(128, dtype=f)
    obd = np.zeros((128, 128), f)
    obd[0:64, 0:64] = 1.0 / 64.0
    obd[64:128, 64:128] = 1.0 / 64.0
    out["onesbd"] = obd
    out["onesfull"] = np.full((128, 128), 1.0 / 1024.0, f)
    pm = np.zeros((128, 128), f)
    cosT = np.ones((128, S_MAX), f)
    sinT = np.zeros((128, S_MAX), f)
    pos = np.arange(S_MAX, dtype=f)
    inv_freq = (np.float32(500000.0) ** (-np.arange(0, 16, 2, dtype=f) / np.float32(16))).astype(f)
    ang = (pos[:, None] * inv_freq[None, :]).astype(f)
    cs = np.cos(ang).astype(f).T
    sn = np.sin(ang).astype(f).T
    for gb in (0, 64):
        for m in range(8):
            pm[gb + m + 8, gb + m] = 1.0
            pm[gb + m, gb + m + 8] = 1.0
            cosT[gb + m] = cs[m]
            cosT[gb + m + 8] = cs[m]
            sinT[gb + m] = -sn[m]
            sinT[gb + m + 8] = sn[m]
    out["perm"] = pm
    out["cosT"] = cosT
    out["sinT"] = sinT
    return out


_CACHE = {}


def run(inputs, L=4, n_cores=8):
    xp = np.asarray(inputs["x_prompt"], np.float32)
    xs = np.asarray(inputs["x_sample"], np.float32)
    SP, SS = xp.shape[1], xs.shape[1]
    keyc = (SP, SS, L)
    if keyc not in _CACHE:
        _CACHE[keyc] = build_program(SP, SS, L)
    nc, stats = _CACHE[keyc]
    lay = _host_layout(inputs, L, max(SP, SS))
    shared = {
        "w_in": np.ascontiguousarray(np.asarray(inputs["w_in"], np.float32)[:L]),
        "wba": np.ascontiguousarray(np.asarray(inputs["w_branch_att"], np.float32)[:L]),
        "wbr": np.ascontiguousarray(np.asarray(inputs["w_branch_rec"], np.float32)[:L]),
        "wout": np.ascontiguousarray(np.asarray(inputs["w_out"], np.float32)[:L]),
        "w1": np.ascontiguousarray(np.asarray(inputs["w_ff1"], np.float32)[:L]),
        "w2": np.ascontiguousarray(np.asarray(inputs["w_ff2"], np.float32)[:L]),
    }
    shared.update(lay)
    in_maps = []
    for i in range(n_cores):
        m = dict(shared)
        m["xp"] = np.ascontiguousarray(xp[i])
        m["xs"] = np.ascontiguousarray(xs[i])
        in_maps.append(m)
    res = run_bass_kernel_spmd(nc, in_maps, core_ids=list(range(n_cores)))
    if DEBUG:
        _CACHE["dbg"] = res.results
    yp = np.stack([np.asarray(r["yp"], np.float32) for r in res.results])
    ys = np.stack([np.asarray(r["ys"], np.float32) for r in res.results])
    return yp, ys


def kernel(**inputs):
    return run(inputs, L=4, n_cores=8)
```

```python
import math
import numpy as np
import concourse.bass as bass
import concourse.mybir as mybir
from concourse.bass_utils import run_bass_kernel_spmd
from concourse.ap import AP

F32 = mybir.dt.float32
BF16 = mybir.dt.bfloat16
ALU = mybir.AluOpType
AF = mybir.ActivationFunctionType
AX = mybir.AxisListType

D = 1024
DC = 8
D_IN = 7168
D_FF = 4096
EPS = 1e-6
NT_W = 36
TT = 512
SEM_LIMIT = 30000
SAME_ENG_SYNC = True
Q_IO = "pool"
Q_W = "sp"
NW = 4
DEBUG = False


class Slot:
    __slots__ = ("cnt", "sem")

    def __init__(self):
        self.cnt = 0
        self.sem = None


class Buf:
    __slots__ = ("name", "w", "r", "slot", "glob", "nofence")

    def __init__(self, name, glob=False, nofence=False):
        self.name = name
        self.w = None
        self.r = {}
        self.slot = None
        self.glob = glob
        self.nofence = nofence


class Op:
    __slots__ = ("eng", "fn", "deps", "need_inc", "idx", "dma_sem", "ev")


class Sched:
    ENGS = ("pe", "act", "dve", "pool", "sp")

    def __init__(self):
        self.ops = {e: [] for e in self.ENGS}
        self.dirty = {}
        self.slots = []
        self.pools = {}
        self.cur_pool = None
        self.cur_idx = 0

    def set_pool(self, name):
        self.cur_pool = name
        self.cur_idx = 0

    def get_slot(self, buf):
        if buf.slot is None:
            if buf.glob or self.cur_pool is None:
                sl = Slot()
                self.slots.append(sl)
                buf.slot = sl
            else:
                pool = self.pools.setdefault(self.cur_pool, [])
                if self.cur_idx >= len(pool):
                    sl = Slot()
                    pool.append(sl)
                    self.slots.append(sl)
                buf.slot = pool[self.cur_idx]
                self.cur_idx += 1
        return buf.slot

    def add(self, eng, fn, reads=(), writes=(), dma=None, extra_deps=()):
        op = Op()
        op.eng = eng
        op.fn = fn
        op.need_inc = False
        op.idx = None
        op.dma_sem = None
        deps = list(extra_deps)
        war = []
        for b in reads:
            if b.w is not None:
                deps.append(b.w)
        for b in writes:
            if b.w is not None:
                deps.append(b.w)
            for rv in b.r.values():
                if rv[0] == "eng" and rv[1].eng == eng:
                    continue
                deps.append(rv)
        if dma is not None:
            sl = self.get_slot(dma)
            sl.cnt += 16
            ev = ("dma", sl, sl.cnt)
            op.dma_sem = sl
            if not dma.nofence:
                self.dirty[id(sl)] = ev
            rkey = ("dma", id(sl))
        else:
            ev = ("eng", op)
            rkey = ("eng", eng)
        op.ev = ev
        for d in deps:
            if d[0] == "eng":
                d[1].need_inc = True
        op.deps = deps
        for b in reads:
            b.r[rkey] = ev
        for b in writes:
            b.w = ev
            b.r = {}
        self.ops[eng].append(op)
        return op

    def fence(self):
        deps = []
        for e in self.ENGS:
            if e == "sp":
                continue
            for op in reversed(self.ops[e]):
                if op.dma_sem is None:
                    deps.append(op.ev)
                    break
        deps.extend(self.dirty.values())
        self.dirty = {}
        f = self.add("sp", lambda e: e.nop(), extra_deps=deps)
        for e in self.ENGS:
            if e != "sp":
                self.add(e, lambda en: en.nop(), extra_deps=[f.ev])
        return f


def build_program(SP, SS, L, n_layers_lam_off=0):
    nc = bass.Bass("TRN2", target_bir_lowering=False)
    S_MAX = max(SP, SS)
    sd = Sched()

    def din(name, shape, dt=F32):
        return nc.dram_tensor(name, list(shape), dt, kind="ExternalInput").ap()

    def dscr(name, shape, dt):
        if DEBUG and name != "wsc":
            return nc.dram_tensor(name, list(shape), dt, kind="ExternalOutput").ap()
        return nc.dram_tensor(name, list(shape), dt, kind="Internal").ap()

    x_in = {"p": din("xp", [SP, D]), "s": din("xs", [SS, D])}
    y_out = {"p": nc.dram_tensor("yp", [SP, D], F32, kind="ExternalOutput").ap(),
             "s": nc.dram_tensor("ys", [SS, D], F32, kind="ExternalOutput").ap()}
    w_in = din("w_in", [L, D, D_IN])
    wba = din("wba", [L, D, D])
    wbr = din("wbr", [L, D, D])
    wout = din("wout", [L, D, D])
    w1 = din("w1", [L, D, D_FF])
    w2 = din("w2", [L, D_FF, D])
    g1_d = din("g1", [L, 128, 8])
    g2_d = din("g2", [L, 128, 8])
    bgate_d = din("bgate", [L, 128, 16])
    qkg_d = din("qkg", [L, 128, 2])
    lamv_d = din("lamv", [L, 128, 256])
    subg_d = din("subg", [L, 128, 128])
    convw_d = din("convw", [L, 128, 32])
    convb_d = din("convb", [L, 128, 8])
    rgb_d = din("rgb", [L, 128, 32])
    rgL_d = din("rgL", [L, 128, 16])
    rgw_d = din("rgw", [L, 32, 128, 128])
    ident_d = din("ident", [128, 128])
    onesbd_d = din("onesbd", [128, 128])
    onesfull_d = din("onesfull", [128, 128])
    perm_d = din("perm", [128, 128])
    cos_d = din("cosT", [128, S_MAX])
    sin_d = din("sinT", [128, S_MAX])

    wsc = dscr("wsc", [L, NT_W, 128, 4096], BF16)
    xT = dscr("xT", [D, S_MAX], F32)
    qT = dscr("qT", [D, S_MAX], BF16)
    kT = dscr("kT", [D, S_MAX], BF16)
    Vs = dscr("Vs", [S_MAX, D], BF16)
    xrT = dscr("xrT", [D, S_MAX], F32)
    gyT = dscr("gyT", [D, S_MAX], BF16)
    thT = dscr("thT", [2 * D, S_MAX], BF16)
    attT = dscr("attT", [D, S_MAX], BF16)
    recT = dscr("recT", [D, S_MAX], BF16)

    ARENA_BYTES = 158 * 1024
    arena = nc.alloc_sbuf_tensor("arena", [128, ARENA_BYTES // 4], F32)
    arena_ap = arena[:] if not isinstance(arena, AP) else arena
    psum = nc.alloc_psum_tensor("psum", [128, 8, 512], F32)
    psum_ap = psum[:] if not isinstance(psum, AP) else psum

    class Arena:
        def __init__(self):
            self.off = 0

        def reset(self):
            self.off = 0

        def alloc(self, shape, dt):
            n = 1
            for s in shape:
                n *= s
            nbytes = n * (4 if dt == F32 else 2)
            nbytes = (nbytes + 31) // 32 * 32
            assert self.off + nbytes <= ARENA_BYTES, ("arena overflow", self.off, nbytes)
            a = arena_ap[:, self.off // 4:(self.off + nbytes) // 4]
            self.off += nbytes
            if dt != F32:
                a = a.bitcast(dt)
            a = a[:, 0:n]
            if len(shape) == 2:
                a = a.rearrange("p (a b) -> p a b", b=shape[1])
            elif len(shape) == 3:
                a = a.rearrange("p (a b c) -> p a b c", b=shape[1], c=shape[2])
            return a

    ar = Arena()

    def sb(name, shape, dt):
        t = nc.alloc_sbuf_tensor(name, [128] + list(shape), dt)
        return t[:] if not isinstance(t, AP) else t

    ident_f = sb("ident_f", [128], F32)
    identb = sb("identb", [128], BF16)
    onesbd = sb("onesbd_s", [128], BF16)
    onesfull = sb("onesfull_s", [128], BF16)
    perm = sb("perm_s", [128], BF16)
    const_b = Buf("consts", glob=True)
    g1 = sb("g1_s", [8], F32)
    g2 = sb("g2_s", [8], F32)
    bgh = sb("bgh_s", [16], F32)
    qkg = sb("qkg_s", [2], F32)
    lamv = sb("lamv_s", [256], F32)
    lamw = sb("lamw_s", [128], F32)
    lams = sb("lams_s", [8], F32)
    subg = sb("subg_s", [128], F32)
    convw = sb("convw_s", [32], F32)
    convb = sb("convb_s", [8], F32)
    rgbh = sb("rgbh_s", [32], F32)
    rgL = sb("rgL_s", [16], F32)
    cLh = sb("cLh_s", [16], F32)
    rgw = sb("rgw_s", [32, 128], BF16)
    par_b = Buf("params", glob=True)
    wring = [sb("wring%d" % i, [4096], BF16) for i in range(NW)]
    wring_b = [Buf("wring%d" % i, glob=True) for i in range(NW)]
    ps_b = [Buf("ps%d" % i) for i in range(8)]

    def ps(i):
        return psum_ap[:, i, :]

    def mm(out, lhsT, rhs, start, stop, reads, writes, skip=False):
        if skip:
            return sd.add("pe", lambda e: e.matmul(out, lhsT, rhs, start=start, stop=stop, skip_group_check=True), reads, writes)
        return sd.add("pe", lambda e: e.matmul(out, lhsT, rhs, start=start, stop=stop), reads, writes)

    def tr(out, in_, ident, reads, writes):
        return sd.add("pe", lambda e: e.transpose(out, in_, ident), reads, writes)

    def act(out, in_, func, reads, writes, scale=1.0, bias=0.0, accum=None):
        def f(e):
            if accum is not None:
                return e.activation(out=out, in_=in_, func=func, bias=bias, scale=scale, accum_out=accum)
            return e.activation(out=out, in_=in_, func=func, bias=bias, scale=scale)
        return sd.add("act", f, reads, writes)

    def stt(eng, out, in0, scalar, in1, op0, op1, reads, writes):
        return sd.add(eng, lambda e: e.scalar_tensor_tensor(out, in0, scalar, in1, op0, op1), reads, writes)

    def ts(eng, out, in0, s1, s2, op0, op1, reads, writes):
        if s2 is None:
            return sd.add(eng, lambda e: e.tensor_scalar(out, in0, s1, None, op0), reads, writes)
        return sd.add(eng, lambda e: e.tensor_scalar(out, in0, s1, s2, op0, op1), reads, writes)

    def tt(eng, out, in0, in1, op, reads, writes):
        return sd.add(eng, lambda e: e.tensor_tensor(out, in0, in1, op), reads, writes)

    def cp(eng, out, in_, reads, writes):
        if eng == "act":
            return sd.add("act", lambda e: e.copy(out, in_), reads, writes)
        return sd.add(eng, lambda e: e.tensor_copy(out, in_), reads, writes)

    def dma(q, out, in_, owner, reads=(), writes=()):
        return sd.add(q, lambda e: e.dma_start(out=out, in_=in_), reads, writes, dma=owner)

    def memset(eng, ap, val, writes):
        return sd.add(eng, lambda e: e.memset(ap, val), (), writes)

    dbg_owner = Buf("dbg", glob=True)
    cur = {"key": None}

    def dump(name, ap, reads):
        if not DEBUG or cur["key"] != "s":
            return
        shape = list(ap.shape)
        dt_ = nc.dram_tensor("dbg_" + name, shape, ap.dtype, kind="ExternalOutput").ap()
        dma(Q_IO, dt_, ap, dbg_owner, reads, ())

    wstate = {"n": 0}

    def wload(l, t):
        i = wstate["n"] % NW
        wstate["n"] += 1
        dma(Q_W, wring[i], wsc[l, t], wring_b[i], (cast_bs[l],), (wring_b[i],))
        return wring[i], wring_b[i]

    class WStream:
        def __init__(self, tiles):
            self.tiles = tiles
            self.q = []
            self.pos = 0
            for _ in range(min(NW - 1, len(tiles))):
                self._issue()

        def _issue(self):
            if self.pos < len(self.tiles):
                self.q.append(wload(*self.tiles[self.pos]))
                self.pos += 1

        def next(self):
            r = self.q.pop(0)
            return r

        def done_one(self):
            self._issue()

    dma("pool", ident_f, ident_d, const_b, (), (const_b,))
    dma("pool", identb, ident_d, const_b, (), (const_b,))
    dma("pool", onesbd, onesbd_d, const_b, (), (const_b,))
    dma("pool", onesfull, onesfull_d, const_b, (), (const_b,))
    dma("pool", perm, perm_d, const_b, (), (const_b,))
    cast_bs = [Buf("cast%d" % l, glob=True, nofence=True) for l in range(L)]

    def cast_layer(l):
        cast_b = cast_bs[l]

        def wview(t):
            return wsc[l, t].rearrange("p (c n) -> p c n", n=512)
        srcs = []
        win_v = w_in[l].rearrange("(c p) n -> p c n", p=128)
        for g in range(14):
            srcs.append(win_v[:, :, g * 512:(g + 1) * 512])
        a_v = wba[l].rearrange("(c p) n -> p c n", p=128)
        r_v = wbr[l].rearrange("(c p) n -> p c n", p=128)
        o_v = wout[l].rearrange("(c p) n -> p c n", p=128)
        srcs += [a_v[:, :, 0:512], r_v[:, :, 0:512], a_v[:, :, 512:1024], r_v[:, :, 512:1024],
                 o_v[:, :, 0:512], o_v[:, :, 512:1024]]
        w1_v = w1[l].rearrange("(c p) n -> p c n", p=128)
        for g in range(8):
            srcs.append(w1_v[:, :, g * 512:(g + 1) * 512])
        for t, s in enumerate(srcs):
            dma("pool", wview(t), s, cast_b, (), (cast_b,))
        w2_v = w2[l].rearrange("(k p) n -> p k n", p=128)
        for oc in range(8):
            dma("pool", wsc[l, 28 + oc].rearrange("p (k n) -> p k n", n=128),
                w2_v[:, :, oc * 128:(oc + 1) * 128], cast_b, (), (cast_b,))

    cast_layer(0)
    sd.fence()

    def layer_params(l):
        lam_init = 0.8 - 0.6 * math.exp(-0.3 * l)
        pb = par_b
        for dst, src in ((g1, g1_d), (g2, g2_d), (bgh, bgate_d), (qkg, qkg_d), (lamv, lamv_d), (subg, subg_d),
                         (convw, convw_d), (convb, convb_d), (rgbh, rgb_d), (rgL, rgL_d)):
            dma("sp", dst, src[l], pb, (), (pb,))
        dma("pool", rgw, rgw_d[l].rearrange("t p n -> p t n"), pb, (), (pb,))
        ts("dve", bgh, bgh, 0.5, None, ALU.mult, None, (pb,), (pb,))
        ts("dve", rgbh, rgbh, 0.5, None, ALU.mult, None, (pb,), (pb,))
        ts("dve", subg, subg, 1.0 - lam_init, None, ALU.mult, None, (pb,), (pb,))
        tt("dve", lamw[:, 0:64], lamv[:, 0:64], lamv[:, 64:128], ALU.mult, (pb,), (pb,))
        tt("dve", lamw[:, 64:128], lamv[:, 128:192], lamv[:, 192:256], ALU.mult, (pb,), (pb,))
        sd.add("dve", lambda e: e.reduce_sum(lams[:, 0:1], lamw[:, 0:64], AX.X), (pb,), (pb,))
        sd.add("dve", lambda e: e.reduce_sum(lams[:, 1:2], lamw[:, 64:128], AX.X), (pb,), (pb,))
        act(lams[:, 2:4], lams[:, 0:2], AF.Exp, (pb,), (pb,))
        tt("dve", lams[:, 4:5], lams[:, 2:3], lams[:, 3:4], ALU.subtract, (pb,), (pb,))
        ts("dve", lams[:, 5:6], lams[:, 4:5], lam_init, -1.0, ALU.add, ALU.mult, (pb,), (pb,))
        act(cLh, rgL, AF.Exp, (pb,), (pb,), scale=-1.0)
        act(cLh, cLh, AF.Ln, (pb,), (pb,), bias=1.0)
        ts("dve", cLh, cLh, -4.0, None, ALU.mult, None, (pb,), (pb,))
        sd.fence()

    def phase0(key, S):
        ar.reset()
        sd.set_pool("P0")
        xin = [ar.alloc([4, D], F32) for _ in range(2)]
        xin_b = [Buf("xin0"), Buf("xin1")]
        xt = [ar.alloc([8, TT], F32) for _ in range(2)]
        xt_b = [Buf("xt0"), Buf("xt1")]
        xsrc = x_in[key]
        xTv = xT.rearrange("(c p) s -> p c s", p=128)
        k = 0
        for ti in range(S // TT):
            t0 = ti * TT
            i = ti % 2
            dma(Q_IO, xin[i], xsrc[t0:t0 + TT, :].rearrange("(tb p) d -> p tb d", p=128), xin_b[i], (), (xin_b[i],))
            for c in range(8):
                b = k % 8
                k += 1
                for tb in range(4):
                    tr(ps(b)[:, tb * 128:(tb + 1) * 128], xin[i][:, tb, c * 128:(c + 1) * 128], ident_f,
                       (xin_b[i], const_b), (ps_b[b],))
                cp("act" if c % 2 else "dve", xt[i][:, c, :], ps(b), (ps_b[b],), (xt_b[i],))
            dma(Q_IO, xTv[:, :, t0:t0 + TT], xt[i], xt_b[i], (xt_b[i],), ())
        sd.fence()

    def phaseA(l, S):
        ar.reset()
        sd.set_pool("A")
        xt = ar.alloc([8, TT], F32); xt_b = Buf("A.xt")
        cs = ar.alloc([2, TT], F32); cs_b = Buf("A.cs")
        sq = ar.alloc([8, TT], BF16); sq_b = Buf("A.sq")
        nT = ar.alloc([8, TT], BF16); nT_b = Buf("A.nT")
        rstd = ar.alloc([TT], F32); rstd_b = Buf("A.rstd")
        NB = 4
        sq2 = [ar.alloc([TT], BF16) for _ in range(NB)]; sq2_b = [Buf("A.sq2") for _ in range(NB)]
        r2 = [ar.alloc([TT], F32) for _ in range(NB)]; r2_b = [Buf("A.r2") for _ in range(NB)]
        qn = [ar.alloc([TT], F32) for _ in range(NB)]; qn_b = [Buf("A.qn") for _ in range(NB)]
        qnb = [ar.alloc([TT], BF16) for _ in range(NB)]; qnb_b = [Buf("A.qnb") for _ in range(NB)]
        t1 = [ar.alloc([TT], F32) for _ in range(NB)]; t1_b = [Buf("A.t1") for _ in range(NB)]
        tmp = [ar.alloc([TT], F32) for _ in range(NB)]; tmp_b = [Buf("A.tmp") for _ in range(NB)]
        qks = [ar.alloc([8, TT], BF16) for _ in range(2)]; qks_b = [Buf("A.qs"), Buf("A.ks")]
        vst = ar.alloc([4, D], BF16); vst_b = Buf("A.vs")
        xrs = ar.alloc([8, TT], F32); xrs_b = Buf("A.xrs")
        gys = ar.alloc([8, TT], BF16); gys_b = Buf("A.gys")
        ths = ar.alloc([16, TT], BF16); ths_b = Buf("A.ths")
        xTv = xT.rearrange("(c p) s -> p c s", p=128)
        qTv = [qT.rearrange("(c p) s -> p c s", p=128), kT.rearrange("(c p) s -> p c s", p=128)]
        xrTv = xrT.rearrange("(c p) s -> p c s", p=128)
        gyTv = gyT.rearrange("(c p) s -> p c s", p=128)
        thTv = thT.rearrange("(c p) s -> p c s", p=128)
        tiles = []
        for ti in range(S // TT):
            tiles += [(l, g) for g in range(14)]
        ws = WStream(tiles)
        kb = [0]

        def nb():
            b = kb[0] % 8
            kb[0] += 1
            return b
        kk = 0
        qk_pipe = []

        def qk_stage2(cx):
            b, i, which = cx["b"], cx["i"], cx["which"]
            b2 = nb()
            mm(ps(b2), onesbd, sq2[i], True, True, (sq2_b[i], const_b), (ps_b[b2],))
            act(r2[i], ps(b2), AF.Ln, (ps_b[b2],), (r2_b[i],), bias=EPS)
            act(r2[i], r2[i], AF.Exp, (r2_b[i],), (r2_b[i],), scale=-0.5)
            stt("dve", qn[i], ps(b), qkg[:, which:which + 1], r2[i], ALU.mult, ALU.mult,
                (ps_b[b], r2_b[i], par_b), (qn_b[i],))
            cp("pool", qnb[i], qn[i], (qn_b[i],), (qnb_b[i],))
            cx["st"] = 2

        def qk_stage3(cx):
            i, which, h = cx["i"], cx["which"], cx["h"]
            b3 = nb()
            mm(ps(b3), perm, qnb[i], True, True, (qnb_b[i], const_b), (ps_b[b3],))
            tt("pool", t1[i], qn[i], cs[:, 0, :], ALU.mult, (qn_b[i], cs_b), (t1_b[i],))
            tt("dve", tmp[i], ps(b3), cs[:, 1, :], ALU.mult, (ps_b[b3], cs_b), (tmp_b[i],))
            tt("dve", qks[which][:, h, :], t1[i], tmp[i], ALU.add, (t1_b[i], tmp_b[i]), (qks_b[which],))
            cx["st"] = 3

        def qk_advance(flush):
            while True:
                n = len(qk_pipe)
                if n >= 3 or (flush and n >= 1 and qk_pipe[0]["st"] == 2):
                    qk_stage3(qk_pipe.pop(0))
                    continue
                break
            for cx in qk_pipe:
                if cx["st"] == 1 and (flush or cx is not qk_pipe[-1]):
                    qk_stage2(cx)
            if flush:
                while qk_pipe:
                    cx = qk_pipe.pop(0)
                    if cx["st"] == 1:
                        qk_stage2(cx)
                    qk_stage3(cx)

        for ti in range(S // TT):
            t0 = ti * TT
            dma(Q_IO, xt, xTv[:, :, t0:t0 + TT], xt_b, (), (xt_b,))
            dma(Q_IO, cs[:, 0, :], cos_d[:, t0:t0 + TT], cs_b, (), (cs_b,))
            dma(Q_IO, cs[:, 1, :], sin_d[:, t0:t0 + TT], cs_b, (), (cs_b,))
            act(sq, xt, AF.Square, (xt_b,), (sq_b,))
            b = nb()
            for c in range(8):
                mm(ps(b), onesfull, sq[:, c, :], c == 0, c == 7, (sq_b, const_b), (ps_b[b],))
            act(rstd, ps(b), AF.Ln, (ps_b[b],), (rstd_b,), bias=EPS)
            act(rstd, rstd, AF.Exp, (rstd_b,), (rstd_b,), scale=-0.5)
            for c in range(8):
                stt("dve", nT[:, c, :], xt[:, c, :], g1[:, c:c + 1], rstd, ALU.mult, ALU.mult,
                    (xt_b, rstd_b, par_b), (nT_b,))
            for g in range(14):
                wt, wt_b = ws.next()
                wv = wt.rearrange("p (c n) -> p c n", n=512)
                if g < 4:
                    which = g // 2
                    for j in range(4):
                        h = (g % 2) * 4 + j
                        b = nb()
                        for c in range(8):
                            mm(ps(b), wv[:, c, j * 128:(j + 1) * 128], nT[:, c, :], c == 0, c == 7,
                               (nT_b, wt_b), (ps_b[b],))
                        i = kk % NB
                        kk += 1
                        act(sq2[i], ps(b), AF.Square, (ps_b[b],), (sq2_b[i],))
                        qk_pipe.append({"b": b, "i": i, "which": which, "h": h, "st": 1})
                        qk_advance(False)
                    if g == 3:
                        qk_advance(True)
                elif g < 6:
                    half = g - 4
                    for tb in range(4):
                        b = nb()
                        for c in range(8):
                            mm(ps(b), nT[:, c, tb * 128:(tb + 1) * 128], wv[:, c, :], c == 0, c == 7,
                               (nT_b, wt_b), (ps_b[b],))
                        cp("act", vst[:, tb, half * 512:(half + 1) * 512], ps(b), (ps_b[b],), (vst_b,))
                elif g < 8:
                    for j in range(4):
                        cc = (g - 6) * 4 + j
                        b = nb()
                        for c in range(8):
                            mm(ps(b), wv[:, c, j * 128:(j + 1) * 128], nT[:, c, :], c == 0, c == 7,
                               (nT_b, wt_b), (ps_b[b],))
                        cp("act" if j % 2 else "dve", xrs[:, cc, :], ps(b), (ps_b[b],), (xrs_b,))
                elif g < 10:
                    for j in range(4):
                        cc = (g - 8) * 4 + j
                        b = nb()
                        for c in range(8):
                            mm(ps(b), wv[:, c, j * 128:(j + 1) * 128], nT[:, c, :], c == 0, c == 7,
                               (nT_b, wt_b), (ps_b[b],))
                        i = kk % NB
                        kk += 1
                        act(r2[i], ps(b), AF.Square, (ps_b[b],), (r2_b[i],))
                        ts("dve", r2[i], r2[i], 0.044715, 1.0, ALU.mult, ALU.add, (r2_b[i],), (r2_b[i],))
                        tt("dve", qn[i], r2[i], ps(b), ALU.mult, (r2_b[i], ps_b[b]), (qn_b[i],))
                        act(t1[i], qn[i], AF.Tanh, (qn_b[i],), (t1_b[i],), scale=0.7978845608028654)
                        stt("dve", gys[:, cc, :], t1[i], 1.0, ps(b), ALU.add, ALU.mult, (t1_b[i], ps_b[b]), (gys_b,))
                else:
                    for j in range(4):
                        gi = (g - 10) * 4 + j
                        b = nb()
                        for c in range(8):
                            mm(ps(b), wv[:, c, j * 128:(j + 1) * 128], nT[:, c, :], c == 0, c == 7,
                               (nT_b, wt_b), (ps_b[b],))
                        act(ths[:, gi, :], ps(b), AF.Tanh, (ps_b[b], par_b), (ths_b,), scale=0.5, bias=bgh[:, gi:gi + 1])
                ws.done_one()
                if g == 3:
                    dma(Q_IO, qTv[0][:, :, t0:t0 + TT], qks[0], qks_b[0], (qks_b[0],), ())
                    dma(Q_IO, qTv[1][:, :, t0:t0 + TT], qks[1], qks_b[1], (qks_b[1],), ())
                elif g == 5:
                    dma(Q_IO, Vs[t0:t0 + TT, :].rearrange("(tb p) d -> p tb d", p=128), vst, vst_b, (vst_b,), ())
                elif g == 7:
                    dma(Q_IO, xrTv[:, :, t0:t0 + TT], xrs, xrs_b, (xrs_b,), ())
                elif g == 9:
                    dma(Q_IO, gyTv[:, :, t0:t0 + TT], gys, gys_b, (gys_b,), ())
                elif g == 13:
                    dma(Q_IO, thTv[:, 0:8, t0:t0 + TT], ths[:, 0:8, :], ths_b, (ths_b,), ())
                    dma(Q_IO, thTv[:, 8:16, t0:t0 + TT], ths[:, 8:16, :], ths_b, (ths_b,), ())
        sd.fence()

    def phaseB1(l, S):
        ar.reset()
        sd.set_pool("B1")
        NKB = S // 128
        NQB = S // TT
        qh = [ar.alloc([S], BF16) for _ in range(2)]
        kh = [ar.alloc([S], BF16) for _ in range(2)]
        Vh = [ar.alloc([NKB, 130], BF16) for _ in range(2)]
        qkv_b = [Buf("B1.qkv0"), Buf("B1.qkv1")]
        NE = 3
        Eb = [[ar.alloc([TT], BF16) for _ in range(2)] for _ in range(NE)]
        Eb_b = [[Buf("B1.E") for _ in range(2)] for _ in range(NE)]
        rden = ar.alloc([16], F32); rden_b = Buf("B1.rden")
        t0b = [ar.alloc([128], F32) for _ in range(2)]; t0b_b = [Buf("B1.t0") for _ in range(2)]
        attf = [ar.alloc([128], F32) for _ in range(4)]; attf_b = [Buf("B1.attf") for _ in range(4)]
        junk = ar.alloc([128], F32); junk_b = Buf("B1.junk")
        ssq = ar.alloc([8], F32); ssq_b = Buf("B1.ssq")
        attb = [ar.alloc([128], BF16) for _ in range(4)]; attb_b = [Buf("B1.attb") for _ in range(4)]
        attTs = [ar.alloc([TT], BF16) for _ in range(2)]; attTs_b = [Buf("B1.attTs0"), Buf("B1.attTs1")]
        psT = psum_ap[:, 7, :].bitcast(BF16)
        dbgO = ar.alloc([3, 512], F32); dbgO_b = Buf("dbgO")
        Oc = ar.alloc([3, 512], F32); Oc_b = [Buf("B1.Oc%d" % i_) for i_ in range(3)]
        for i in range(2):
            memset("dve", Vh[i][:, :, 128:129], 1.0, (qkv_b[i],))
        sbank = 0
        ei = 0
        nst = 0
        for h in range(8):
            i = h % 2
            dma(Q_IO, qh[i], qT[h * 128:(h + 1) * 128, 0:S], qkv_b[i], (), (qkv_b[i],))
            dma(Q_IO, kh[i], kT[h * 128:(h + 1) * 128, 0:S], qkv_b[i], (), (qkv_b[i],))
            for k4 in range(0, NKB, 8):
                ke = min(NKB, k4 + 8)
                dma(Q_IO, Vh[i][:, k4:ke, 0:128],
                    Vs[k4 * 128:ke * 128, h * 128:(h + 1) * 128].rearrange("(kb p) e -> p kb e", p=128),
                    qkv_b[i], (), (qkv_b[i],))
            for qb in range(NQB):
                q0 = qb * TT
                pend = None
                for kc in range(NKB + 1):
                    cur = None
                    if kc < NKB:
                        b0 = sbank * 2
                        sbank = (sbank + 1) % 2
                        e = ei % NE
                        ei += 1
                        mm(ps(b0), kh[i][0:64, kc * 128:(kc + 1) * 128], qh[i][0:64, q0:q0 + TT], True, True,
                           (qkv_b[i],), (ps_b[b0],))
                        mm(ps(b0 + 1), kh[i][64:128, kc * 128:(kc + 1) * 128], qh[i][64:128, q0:q0 + TT], True, True,
                           (qkv_b[i],), (ps_b[b0 + 1],))
                        act(Eb[e][0], ps(b0), AF.Exp, (ps_b[b0],), (Eb_b[e][0],), scale=0.125)
                        act(Eb[e][1], ps(b0 + 1), AF.Exp, (ps_b[b0 + 1],), (Eb_b[e][1],), scale=0.125)
                        cur = (e, kc)
                    if pend is not None:
                        e_, kc_ = pend
                        for c in range(2):
                            for j in range(4):
                                s = c * 4 + j
                                ob = 4 + s // 3
                                oc = (s % 3) * 129
                                mm(ps(ob)[:, oc:oc + 129], Eb[e_][c][:, j * 128:(j + 1) * 128], Vh[i][:, kc_, 0:129],
                                   kc_ == 0 and s % 3 == 0, kc_ == NKB - 1, (Eb_b[e_][c], qkv_b[i]), (ps_b[ob],), skip=True)
                    pend = cur
                if DEBUG and h == 0 and qb == 0:
                    dump("E0", Eb[(ei - 1) % NE][0], (Eb_b[(ei - 1) % NE][0],))
                    for ob_ in (4, 5, 6):
                        cp("dve", dbgO[:, ob_ - 4, :], ps(ob_), (ps_b[ob_],), (dbgO_b,))
                    dump("O", dbgO, (dbgO_b,))
                for ob_ in range(3):
                    ncol = 387 if ob_ < 2 else 258
                    cp("dve", Oc[:, ob_, 0:ncol], ps(4 + ob_)[:, 0:ncol], (ps_b[4 + ob_],), (Oc_b[ob_],))
                for s in range(8):
                    ob = s // 3
                    oc = (s % 3) * 129
                    sd.add("dve", (lambda ob=ob, oc=oc, s=s: (lambda e: e.reciprocal(rden[:, s:s + 1], Oc[:, ob, oc + 128:oc + 129])))(),
                           (Oc_b[ob],), (rden_b,))
                ts("dve", rden[:, 8:12], rden[:, 4:8], lams[:, 5:6], None, ALU.mult, None, (rden_b, par_b), (rden_b,))
                memset("dve", ssq[:, 0:4], 0.0, (ssq_b,))
                for j in range(4):
                    s0 = j
                    s1 = 4 + j
                    ti_ = nst % 2
                    nst += 1
                    ts("dve", t0b[ti_], Oc[:, s0 // 3, (s0 % 3) * 129:(s0 % 3) * 129 + 128], rden[:, j:j + 1], None,
                       ALU.mult, None, (Oc_b[s0 // 3], rden_b), (t0b_b[ti_],))
                    stt("dve", attf[j], Oc[:, s1 // 3, (s1 % 3) * 129:(s1 % 3) * 129 + 128], rden[:, 8 + j:9 + j], t0b[ti_],
                        ALU.mult, ALU.add, (Oc_b[s1 // 3], rden_b, t0b_b[ti_]), (attf_b[j],))
                    act(junk, attf[j], AF.Square, (attf_b[j],), (junk_b, ssq_b), accum=ssq[:, j:j + 1])
                act(ssq[:, 4:8], ssq[:, 0:4], AF.Ln, (ssq_b,), (ssq_b,), scale=1.0 / 128.0, bias=EPS)
                act(ssq[:, 4:8], ssq[:, 4:8], AF.Exp, (ssq_b,), (ssq_b,), scale=-0.5)
                ai = (h * NQB + qb) % 2
                if h == 0 and qb == 0:
                    dump("rden", rden, (rden_b,))
                    dump("attf0", attf[0], (attf_b[0],))
                    dump("ssq", ssq, (ssq_b,))
                for j in range(4):
                    stt("dve", attb[j], attf[j], ssq[:, 4 + j:5 + j], subg, ALU.mult, ALU.mult,
                        (attf_b[j], ssq_b, par_b), (attb_b[j],))
                    tr(psT[:, j * 128:(j + 1) * 128], attb[j], identb, (attb_b[j], const_b), (ps_b[7],))
                if h == 0 and qb == 0:
                    dump("attb0", attb[0], (attb_b[0],))
                cp("dve", attTs[ai], psT[:, 0:TT], (ps_b[7],), (attTs_b[ai],))
                dma(Q_IO, attT[h * 128:(h + 1) * 128, q0:q0 + TT], attTs[ai], attTs_b[ai], (attTs_b[ai],), ())
        sd.fence()

    def phaseB2(l, S):
        ar.reset()
        sd.set_pool("B2")
        NH = 2 if S >= 1024 else 1
        HS = S // NH
        xpad = ar.alloc([S + 8], F32); xpad_b = Buf("B2.xpad")
        gy = ar.alloc([S], BF16); gy_b = Buf("B2.gy")
        xc = ar.alloc([S], F32); xc_b = Buf("B2.xc")
        xcb = ar.alloc([S], BF16); xcb_b = Buf("B2.xcb")
        A_ = ar.alloc([S], F32); A_b = [Buf("B2.A%d" % i) for i in range(NH)]
        B_ = ar.alloc([S], F32); B_b = [Buf("B2.B%d" % i) for i in range(NH)]
        I_ = ar.alloc([S], F32); I_b = [Buf("B2.I%d" % i) for i in range(NH)]
        T_ = ar.alloc([S], F32); T_b = [Buf("B2.T%d" % i) for i in range(NH)]
        hf = ar.alloc([S], F32); hf_b = [Buf("B2.hf%d" % i) for i in range(NH)]
        rec = ar.alloc([S], BF16); rec_b = Buf("B2.rec")
        memset("dve", xpad[:, 0:2], 0.0, (xpad_b,))
        memset("dve", xpad[:, S + 2:S + 8], 0.0, (xpad_b,))
        kb = 0

        def rev(a):
            base = a
            apl = [list(x) for x in base.ap]
            n = apl[-1][1]
            apl[-1] = [-1, n]
            return AP(base.tensor, base.offset + (n - 1), apl)

        def scan_op(out, a0, a1, init):
            return lambda e: e.tensor_tensor_scan(out, a0, a1, init, ALU.mult, ALU.add)
        for c in range(8):
            dma(Q_IO, xpad[:, 2:2 + S], xrT[c * 128:(c + 1) * 128, 0:S], xpad_b, (), (xpad_b,))
            dma(Q_IO, gy, gyT[c * 128:(c + 1) * 128, 0:S], gy_b, (), (gy_b,))
            ts("dve", xc, xpad[:, 0:S], convw[:, c * 4:c * 4 + 1], convb[:, c:c + 1], ALU.mult, ALU.add,
               (xpad_b, par_b), (xc_b,))
            for j in range(1, 4):
                stt("dve", xc, xpad[:, j:j + S], convw[:, c * 4 + j:c * 4 + j + 1], xc, ALU.mult, ALU.add,
                    (xpad_b, xc_b, par_b), (xc_b,))
            cp("act", xcb, xc, (xc_b,), (xcb_b,))
            if c == 0:
                dump("xc", xc, (xc_b,))
            for d in range(2):
                ci = d * 8 + c
                order = list(range(NH)) if d == 0 else list(range(NH - 1, -1, -1))
                for blk in range(S // TT):
                    hh_ = (blk * TT) // HS
                    for gt in range(2):
                        b = kb % 8
                        kb += 1
                        idx = (d * 2 + gt) * 8 + c
                        mm(ps(b), rgw[:, idx, :], xcb[:, blk * TT:(blk + 1) * TT], True, True, (xcb_b, par_b), (ps_b[b],))
                        dst, dst_b = (A_, A_b[hh_]) if gt == 0 else (I_, I_b[hh_])
                        act(dst[:, blk * TT:(blk + 1) * TT], ps(b), AF.Tanh, (ps_b[b], par_b), (dst_b,),
                            scale=0.5, bias=rgbh[:, idx:idx + 1])
                sl = [slice(hx * HS, (hx + 1) * HS) for hx in range(NH)]
                for hx in order:
                    ts("dve", A_[:, sl[hx]], A_[:, sl[hx]], cLh[:, ci:ci + 1], cLh[:, ci:ci + 1], ALU.mult, ALU.add,
                       (A_b[hx], par_b), (A_b[hx],))
                for hx in order:
                    act(B_[:, sl[hx]], A_[:, sl[hx]], AF.Exp, (A_b[hx],), (B_b[hx],), scale=2.0)
                    act(T_[:, sl[hx]], A_[:, sl[hx]], AF.Tanh, (A_b[hx],), (T_b[hx],))
                    act(A_[:, sl[hx]], A_[:, sl[hx]], AF.Exp, (A_b[hx],), (A_b[hx],))
                for hx in order:
                    stt("dve", B_[:, sl[hx]], B_[:, sl[hx]], 1.0, T_[:, sl[hx]], ALU.add, ALU.mult,
                        (B_b[hx], T_b[hx]), (B_b[hx],))
                for hx in order:
                    act(B_[:, sl[hx]], B_[:, sl[hx]], AF.Sqrt, (B_b[hx],), (B_b[hx],), scale=-1.0)
                for hx in order:
                    stt("dve", B_[:, sl[hx]], I_[:, sl[hx]], 1.0, B_[:, sl[hx]], ALU.add, ALU.mult,
                        (I_b[hx], B_b[hx]), (B_b[hx],))
                    stt("dve", B_[:, sl[hx]], B_[:, sl[hx]], 0.5, xc[:, sl[hx]], ALU.mult, ALU.mult,
                        (B_b[hx], xc_b), (B_b[hx],))
                prev = None
                for hx in order:
                    if d == 0:
                        init = 0.0 if prev is None else hf[:, prev * HS + HS - 1:prev * HS + HS]
                        rd = (A_b[hx], B_b[hx]) + (() if prev is None else (hf_b[prev],))
                        sd.add("dve", scan_op(hf[:, sl[hx]], A_[:, sl[hx]], B_[:, sl[hx]], init), rd, (hf_b[hx],))
                    else:
                        init = 0.0 if prev is None else I_[:, prev * HS:prev * HS + 1]
                        rd = (A_b[hx], B_b[hx]) + (() if prev is None else (I_b[prev],))
                        sd.add("dve", scan_op(rev(I_[:, sl[hx]]), rev(A_[:, sl[hx]]), rev(B_[:, sl[hx]]), init),
                               rd, (I_b[hx],))
                    prev = hx
            for hx in range(NH):
                s_ = slice(hx * HS, (hx + 1) * HS)
                tt("dve", hf[:, s_], hf[:, s_], I_[:, s_], ALU.add, (hf_b[hx], I_b[hx]), (hf_b[hx],))
                stt("dve", rec[:, s_], hf[:, s_], 0.5, gy[:, s_], ALU.mult, ALU.mult, (hf_b[hx], gy_b), (rec_b,))
            dma(Q_IO, recT[c * 128:(c + 1) * 128, 0:S], rec, rec_b, (rec_b,), ())
        sd.fence()

    def phaseC(l, S, key, last):
        ar.reset()
        sd.set_pool("C")
        x = ar.alloc([8, TT], F32); x_b = Buf("C.x")
        att = ar.alloc([8, TT], BF16); att_b = Buf("C.att")
        rec = ar.alloc([8, TT], BF16); rec_b = Buf("C.rec")
        th = [ar.alloc([8, TT], BF16) for _ in range(2)]; th_b = [Buf("C.th0"), Buf("C.th1")]
        m0 = [ar.alloc([TT], F32) for _ in range(2)]; m0_b = [Buf("C.m0") for _ in range(2)]
        m1 = [ar.alloc([TT], F32) for _ in range(2)]; m1_b = [Buf("C.m1") for _ in range(2)]
        mg = ar.alloc([8, TT], BF16); mg_b = Buf("C.mg")
        sq = ar.alloc([8, TT], BF16); sq_b = Buf("C.sq")
        rstd = ar.alloc([TT], F32); rstd_b = Buf("C.rstd")
        n2 = ar.alloc([8, TT], BF16); n2_b = Buf("C.n2")
        rl = [ar.alloc([TT], F32) for _ in range(3)]; rl_b = [Buf("C.rl") for _ in range(3)]
        hh = ar.alloc([32, TT], BF16); hh_b = Buf("C.h")
        yt = [ar.alloc([D], F32) for _ in range(2)]; yt_b = [Buf("C.yt0"), Buf("C.yt1")]
        xTv = xT.rearrange("(c p) s -> p c s", p=128)
        attTv = attT.rearrange("(c p) s -> p c s", p=128)
        recTv = recT.rearrange("(c p) s -> p c s", p=128)
        thTv = thT.rearrange("(c p) s -> p c s", p=128)
        tiles = []
        for ti in range(S // TT):
            tiles += [(l, t) for t in range(14, 36)]
        ws = WStream(tiles)
        kb = [0]

        def nb():
            b = kb[0] % 8
            kb[0] += 1
            return b
        km = 0
        kr = 0
        ky = 0
        for ti in range(S // TT):
            t0 = ti * TT
            dma(Q_IO, att, attTv[:, :, t0:t0 + TT], att_b, (), (att_b,))
            dma(Q_IO, rec, recTv[:, :, t0:t0 + TT], rec_b, (), (rec_b,))
            dma(Q_IO, th[0], thTv[:, 0:8, t0:t0 + TT], th_b[0], (), (th_b[0],))
            dma(Q_IO, th[1], thTv[:, 8:16, t0:t0 + TT], th_b[1], (), (th_b[1],))
            dma(Q_IO, x, xTv[:, :, t0:t0 + TT], x_b, (), (x_b,))
            for half in range(2):
                wa, wa_b = ws.next()
                wr, wr_b = ws.next()
                wav = wa.rearrange("p (c n) -> p c n", n=512)
                wrv = wr.rearrange("p (c n) -> p c n", n=512)
                for j in range(4):
                    oc = half * 4 + j
                    bA = nb()
                    for c in range(8):
                        mm(ps(bA), wav[:, c, j * 128:(j + 1) * 128], att[:, c, :], c == 0, c == 7, (att_b, wa_b), (ps_b[bA],))
                    bR = nb()
                    for c in range(8):
                        mm(ps(bR), wrv[:, c, j * 128:(j + 1) * 128], rec[:, c, :], c == 0, c == 7, (rec_b, wr_b), (ps_b[bR],))
                    i = km % 2
                    km += 1
                    stt("dve", m0[i], th[0][:, oc, :], 1.0, ps(bA), ALU.add, ALU.mult, (th_b[0], ps_b[bA]), (m0_b[i],))
                    stt("dve", m1[i], th[1][:, oc, :], 1.0, ps(bR), ALU.add, ALU.mult, (th_b[1], ps_b[bR]), (m1_b[i],))
                    tt("pool", mg[:, oc, :], m0[i], m1[i], ALU.add, (m0_b[i], m1_b[i]), (mg_b,))
                ws.done_one()
                ws.done_one()
            for half in range(2):
                wo, wo_b = ws.next()
                wov = wo.rearrange("p (c n) -> p c n", n=512)
                for j in range(4):
                    oc = half * 4 + j
                    b = nb()
                    for c in range(8):
                        mm(ps(b), wov[:, c, j * 128:(j + 1) * 128], mg[:, c, :], c == 0, c == 7, (mg_b, wo_b), (ps_b[b],))
                    stt("dve", x[:, oc, :], ps(b), 0.5, x[:, oc, :], ALU.mult, ALU.add, (ps_b[b], x_b), (x_b,))
                ws.done_one()
            act(sq, x, AF.Square, (x_b,), (sq_b,))
            b = nb()
            for c in range(8):
                mm(ps(b), onesfull, sq[:, c, :], c == 0, c == 7, (sq_b, const_b), (ps_b[b],))
            act(rstd, ps(b), AF.Ln, (ps_b[b],), (rstd_b,), bias=EPS)
            act(rstd, rstd, AF.Exp, (rstd_b,), (rstd_b,), scale=-0.5)
            for c in range(8):
                stt("dve", n2[:, c, :], x[:, c, :], g2[:, c:c + 1], rstd, ALU.mult, ALU.mult, (x_b, rstd_b, par_b), (n2_b,))
            for g in range(8):
                w1t, w1_b = ws.next()
                w1v = w1t.rearrange("p (c n) -> p c n", n=512)
                for j in range(4):
                    f = g * 4 + j
                    b = nb()
                    for c in range(8):
                        mm(ps(b), w1v[:, c, j * 128:(j + 1) * 128], n2[:, c, :], c == 0, c == 7, (n2_b, w1_b), (ps_b[b],))
                    i = kr % 3
                    kr += 1
                    act(rl[i], ps(b), AF.Relu, (ps_b[b],), (rl_b[i],))
                    tt("pool", hh[:, f, :], rl[i], rl[i], ALU.mult, (rl_b[i],), (hh_b,))
                ws.done_one()
            for oc in range(8):
                w2t, w2_b = ws.next()
                w2v = w2t.rearrange("p (k n) -> p k n", n=128)
                b = nb()
                for kf in range(32):
                    mm(ps(b), w2v[:, kf, :], hh[:, kf, :], kf == 0, kf == 31, (hh_b, w2_b), (ps_b[b],))
                tt("dve", x[:, oc, :], ps(b), x[:, oc, :], ALU.add, (ps_b[b], x_b), (x_b,))
                ws.done_one()
            if not last:
                dma(Q_IO, xTv[:, :, t0:t0 + TT], x, x_b, (x_b,), ())
            else:
                for tb in range(4):
                    i = ky % 2
                    ky += 1
                    for hv in range(2):
                        b = nb()
                        for c4 in range(4):
                            c = hv * 4 + c4
                            tr(ps(b)[:, c4 * 128:(c4 + 1) * 128], x[:, c, tb * 128:(tb + 1) * 128], ident_f,
                               (x_b, const_b), (ps_b[b],))
                        cp("act" if hv else "dve", yt[i][:, hv * 512:(hv + 1) * 512], ps(b), (ps_b[b],), (yt_b[i],))
                    dma(Q_IO, y_out[key][t0 + tb * 128:t0 + (tb + 1) * 128, :], yt[i], yt_b[i], (yt_b[i],), ())
        sd.fence()

    for key, S in (("p", SP), ("s", SS)):
        cur["key"] = key
        phase0(key, S)
        for l in range(L):
            layer_params(l)
            phaseA(l, S)
            if key == "p" and l + 1 < L:
                cast_layer(l + 1)
            phaseB1(l, S)
            phaseB2(l, S)
            phaseC(l, S, key, l == L - 1)

    n_sems = {e: max(1, (len(sd.ops[e]) + SEM_LIMIT - 1) // SEM_LIMIT) for e in Sched.ENGS}
    for e in Sched.ENGS:
        k = 0
        for op in sd.ops[e]:
            if op.need_inc and op.dma_sem is None:
                k += 1
                op.idx = k
    import contextlib
    with contextlib.ExitStack() as st:
        eng_sems = {}
        for e in Sched.ENGS:
            cnt = sum(1 for op in sd.ops[e] if op.idx is not None)
            ns = max(1, (cnt + SEM_LIMIT - 1) // SEM_LIMIT)
            eng_sems[e] = [st.enter_context(nc.semaphore("se_%s_%d" % (e, i))) for i in range(ns)]
        for i, b in enumerate(sd.slots):
            b.sem = st.enter_context(nc.semaphore("sd_%d" % i))
        block = st.enter_context(nc.Block())

        def resolve(d):
            if d[0] == "dma":
                return d[1].sem, d[2], None
            op = d[1]
            ep = (op.idx - 1) // SEM_LIMIT
            return eng_sems[op.eng][ep], (op.idx - 1) % SEM_LIMIT + 1, op.eng

        def emit(e, h):
            waited = {}
            for op in sd.ops[e]:
                for d in op.deps:
                    if d[0] == "eng":
                        if d[1].idx is None:
                            continue
                        if d[1].eng == e and (e == "pe" or not SAME_ENG_SYNC):
                            continue
                    sem, val, _ = resolve(d)
                    key_ = id(sem)
                    if waited.get(key_, 0) >= val:
                        continue
                    waited[key_] = val
                    h.wait_ge(sem, val)
                ins = op.fn(h)
                if op.dma_sem is not None:
                    ins.then_inc(op.dma_sem.sem, 16)
                elif op.idx is not None:
                    ep = (op.idx - 1) // SEM_LIMIT
                    ins.then_inc(eng_sems[e][ep], 1)

        @block.tensor
        def _(h):
            emit("pe", h)

        @block.scalar
        def _(h):
            emit("act", h)

        @block.vector
        def _(h):
            emit("dve", h)

        @block.gpsimd
        def _(h):
            emit("pool", h)

        @block.sync
        def _(h):
            emit("sp", h)
    stats = {e: len(sd.ops[e]) for e in Sched.ENGS}
    return nc, stats


def _host_layout(inp, L, S_MAX):
    f = np.float32

    def fm(v, nch):
        return np.ascontiguousarray(np.asarray(v, f).reshape(nch, 128).T)
    out = {}
    out["g1"] = np.stack([fm(inp["norm1_g"][l], 8) for l in range(L)])
    out["g2"] = np.stack([fm(inp["norm2_g"][l], 8) for l in range(L)])
    out["bgate"] = np.stack([fm(inp["b_gate"][l], 16) for l in range(L)])
    qkg = np.zeros((L, 128, 2), f)
    for l in range(L):
        qkg[l, :, 0] = np.tile(np.asarray(inp["q_norm_g"][l], f), 2)
        qkg[l, :, 1] = np.tile(np.asarray(inp["k_norm_g"][l], f), 2)
    out["qkg"] = qkg
    out["lamv"] = np.ascontiguousarray(np.broadcast_to(np.asarray(inp["lam_vecs"], f)[:L].reshape(L, 1, 256), (L, 128, 256)))
    out["subg"] = np.ascontiguousarray(np.broadcast_to(np.asarray(inp["subln_g"], f)[:L].reshape(L, 1, 128), (L, 128, 128)))
    cw = np.asarray(inp["conv_w"], f)[:L]
    convw = np.zeros((L, 128, 8, 4), f)
    for l in range(L):
        for j in range(4):
            convw[l, :, :, j] = fm(cw[l, j], 8)
    out["convw"] = convw.reshape(L, 128, 32)
    out["convb"] = np.stack([fm(inp["conv_b"][l], 8) for l in range(L)])
    rb = np.asarray(inp["rg_b"], f)[:L]
    rgb = np.zeros((L, 128, 2, 2, 8), f)
    for l in range(L):
        for d in range(2):
            for g in range(2):
                rgb[l, :, d, g, :] = fm(rb[l, d, g], 8)
    out["rgb"] = rgb.reshape(L, 128, 32)
    rL = np.asarray(inp["rg_L"], f)[:L]
    rgL = np.zeros((L, 128, 2, 8), f)
    for l in range(L):
        for d in range(2):
            rgL[l, :, d, :] = fm(rL[l, d], 8)
    out["rgL"] = rgL.reshape(L, 128, 16)
    rw = np.asarray(inp["rg_w"], f)[:L]
    rgw = np.zeros((L, 2, 2, 8, 128, 128), f)
    for c in range(8):
        rgw[:, :, :, c, 0:64, 0:64] = rw[:, :, :, 2 * c]
        rgw[:, :, :, c, 64:128, 64:128] = rw[:, :, :, 2 * c + 1]
    out["rgw"] = rgw.reshape(L, 32, 128, 128)
    out["ident"] = np.eye(128, dtype=f)
    obd = np.zeros((128, 128), f)
    obd[0:64, 0:64] = 1.0 / 64.0
    obd[64:128, 64:128] = 1.0 / 64.0
    out["onesbd"] = obd
    out["onesfull"] = np.full((128, 128), 1.0 / 1024.0, f)
    pm = np.zeros((128, 128), f)
    cosT = np.ones((128, S_MAX), f)
    sinT = np.zeros((128, S_MAX), f)
    pos = np.arange(S_MAX, dtype=f)
    inv_freq = (np.float32(500000.0) ** (-np.arange(0, 16, 2, dtype=f) / np.float32(16))).astype(f)
    ang = (pos[:, None] * inv_freq[None, :]).astype(f)
    cs = np.cos(ang).astype(f).T
    sn = np.sin(ang).astype(f).T
    for gb in (0, 64):
        for m in range(8):
            pm[gb + m + 8, gb + m] = 1.0
            pm[gb + m, gb + m + 8] = 1.0
            cosT[gb + m] = cs[m]
            cosT[gb + m + 8] = cs[m]
            sinT[gb + m] = -sn[m]
            sinT[gb + m + 8] = sn[m]
    out["perm"] = pm
    out["cosT"] = cosT
    out["sinT"] = sinT
    return out


_CACHE = {}


def run(inputs, L=4, n_cores=8):
    xp = np.asarray(inputs["x_prompt"], np.float32)
    xs = np.asarray(inputs["x_sample"], np.float32)
    SP, SS = xp.shape[1], xs.shape[1]
    keyc = (SP, SS, L)
    if keyc not in _CACHE:
        _CACHE[keyc] = build_program(SP, SS, L)
    nc, stats = _CACHE[keyc]
    lay = _host_layout(inputs, L, max(SP, SS))
    shared = {
        "w_in": np.ascontiguousarray(np.asarray(inputs["w_in"], np.float32)[:L]),
        "wba": np.ascontiguousarray(np.asarray(inputs["w_branch_att"], np.float32)[:L]),
        "wbr": np.ascontiguousarray(np.asarray(inputs["w_branch_rec"], np.float32)[:L]),
        "wout": np.ascontiguousarray(np.asarray(inputs["w_out"], np.float32)[:L]),
        "w1": np.ascontiguousarray(np.asarray(inputs["w_ff1"], np.float32)[:L]),
        "w2": np.ascontiguousarray(np.asarray(inputs["w_ff2"], np.float32)[:L]),
    }
    shared.update(lay)
    in_maps = []
    for i in range(n_cores):
        m = dict(shared)
        m["xp"] = np.ascontiguousarray(xp[i])
        m["xs"] = np.ascontiguousarray(xs[i])
        in_maps.append(m)
    res = run_bass_kernel_spmd(nc, in_maps, core_ids=list(range(n_cores)))
    if DEBUG:
        _CACHE["dbg"] = res.results
    yp = np.stack([np.asarray(r["yp"], np.float32) for r in res.results])
    ys = np.stack([np.asarray(r["ys"], np.float32) for r in res.results])
    return yp, ys


def kernel(**inputs):
    return run(inputs, L=4, n_cores=8)
```

```python
import math
import numpy as np
import concourse.bass as bass
import concourse.mybir as mybir
from concourse.bass_utils import run_bass_kernel_spmd
from concourse.ap import AP

F32 = mybir.dt.float32
BF16 = mybir.dt.bfloat16
ALU = mybir.AluOpType
AF = mybir.ActivationFunctionType
AX = mybir.AxisListType

D = 1024
DC = 8
D_IN = 7168
D_FF = 4096
EPS = 1e-6
NT_W = 36
TT = 512
SEM_LIMIT = 30000
SAME_ENG_SYNC = True
Q_IO = "pool"
Q_W = "sp"
NW = 4
DEBUG = False


class Slot:
    __slots__ = ("cnt", "sem")

    def __init__(self):
        self.cnt = 0
        self.sem = None


class Buf:
    __slots__ = ("name", "w", "r", "slot", "glob", "nofence")

    def __init__(self, name, glob=False, nofence=False):
        self.name = name
        self.w = None
        self.r = {}
        self.slot = None
        self.glob = glob
        self.nofence = nofence


class Op:
    __slots__ = ("eng", "fn", "deps", "need_inc", "idx", "dma_sem", "ev")


class Sched:
    ENGS = ("pe", "act", "dve", "pool", "sp")

    def __init__(self):
        self.ops = {e: [] for e in self.ENGS}
        self.dirty = {}
        self.slots = []
        self.pools = {}
        self.cur_pool = None
        self.cur_idx = 0

    def set_pool(self, name):
        self.cur_pool = name
        self.cur_idx = 0

    def get_slot(self, buf):
        if buf.slot is None:
            if buf.glob or self.cur_pool is None:
                sl = Slot()
                self.slots.append(sl)
                buf.slot = sl
            else:
                pool = self.pools.setdefault(self.cur_pool, [])
                if self.cur_idx >= len(pool):
                    sl = Slot()
                    pool.append(sl)
                    self.slots.append(sl)
                buf.slot = pool[self.cur_idx]
                self.cur_idx += 1
        return buf.slot

    def add(self, eng, fn, reads=(), writes=(), dma=None, extra_deps=()):
        op = Op()
        op.eng = eng
        op.fn = fn
        op.need_inc = False
        op.idx = None
        op.dma_sem = None
        deps = list(extra_deps)
        war = []
        for b in reads:
            if b.w is not None:
                deps.append(b.w)
        for b in writes:
            if b.w is not None:
                deps.append(b.w)
            for rv in b.r.values():
                if rv[0] == "eng" and rv[1].eng == eng:
                    continue
                deps.append(rv)
        if dma is not None:
            sl = self.get_slot(dma)
            sl.cnt += 16
            ev = ("dma", sl, sl.cnt)
            op.dma_sem = sl
            if not dma.nofence:
                self.dirty[id(sl)] = ev
            rkey = ("dma", id(sl))
        else:
            ev = ("eng", op)
            rkey = ("eng", eng)
        op.ev = ev
        for d in deps:
            if d[0] == "eng":
                d[1].need_inc = True
        op.deps = deps
        for b in reads:
            b.r[rkey] = ev
        for b in writes:
            b.w = ev
            b.r = {}
        self.ops[eng].append(op)
        return op

    def fence(self):
        deps = []
        for e in self.ENGS:
            if e == "sp":
                continue
            for op in reversed(self.ops[e]):
                if op.dma_sem is None:
                    deps.append(op.ev)
                    break
        deps.extend(self.dirty.values())
        self.dirty = {}
        f = self.add("sp", lambda e: e.nop(), extra_deps=deps)
        for e in self.ENGS:
            if e != "sp":
                self.add(e, lambda en: en.nop(), extra_deps=[f.ev])
        return f


def build_program(SP, SS, L, n_layers_lam_off=0):
    nc = bass.Bass("TRN2", target_bir_lowering=False)
    S_MAX = max(SP, SS)
    sd = Sched()

    def din(name, shape, dt=F32):
        return nc.dram_tensor(name, list(shape), dt, kind="ExternalInput").ap()

    def dscr(name, shape, dt):
        if DEBUG and name != "wsc":
            return nc.dram_tensor(name, list(shape), dt, kind="ExternalOutput").ap()
        return nc.dram_tensor(name, list(shape), dt, kind="Internal").ap()

    x_in = {"p": din("xp", [SP, D]), "s": din("xs", [SS, D])}
    y_out = {"p": nc.dram_tensor("yp", [SP, D], F32, kind="ExternalOutput").ap(),
             "s": nc.dram_tensor("ys", [SS, D], F32, kind="ExternalOutput").ap()}
    w_in = din("w_in", [L, D, D_IN])
    wba = din("wba", [L, D, D])
    wbr = din("wbr", [L, D, D])
    wout = din("wout", [L, D, D])
    w1 = din("w1", [L, D, D_FF])
    w2 = din("w2", [L, D_FF, D])
    g1_d = din("g1", [L, 128, 8])
    g2_d = din("g2", [L, 128, 8])
    bgate_d = din("bgate", [L, 128, 16])
    qkg_d = din("qkg", [L, 128, 2])
    lamv_d = din("lamv", [L, 128, 256])
    subg_d = din("subg", [L, 128, 128])
    convw_d = din("convw", [L, 128, 32])
    convb_d = din("convb", [L, 128, 8])
    rgb_d = din("rgb", [L, 128, 32])
    rgL_d = din("rgL", [L, 128, 16])
    rgw_d = din("rgw", [L, 32, 128, 128])
    ident_d = din("ident", [128, 128])
    onesbd_d = din("onesbd", [128, 128])
    onesfull_d = din("onesfull", [128, 128])
    perm_d = din("perm", [128, 128])
    cos_d = din("cosT", [128, S_MAX])
    sin_d = din("sinT", [128, S_MAX])

    wsc = dscr("wsc", [L, NT_W, 128, 4096], BF16)
    xT = dscr("xT", [D, S_MAX], F32)
    qT = dscr("qT", [D, S_MAX], BF16)
    kT = dscr("kT", [D, S_MAX], BF16)
    Vs = dscr("Vs", [S_MAX, D], BF16)
    xrT = dscr("xrT", [D, S_MAX], F32)
    gyT = dscr("gyT", [D, S_MAX], BF16)
    thT = dscr("thT", [2 * D, S_MAX], BF16)
    attT = dscr("attT", [D, S_MAX], BF16)
    recT = dscr("recT", [D, S_MAX], BF16)

    ARENA_BYTES = 158 * 1024
    arena = nc.alloc_sbuf_tensor("arena", [128, ARENA_BYTES // 4], F32)
    arena_ap = arena[:] if not isinstance(arena, AP) else arena
    psum = nc.alloc_psum_tensor("psum", [128, 8, 512], F32)
    psum_ap = psum[:] if not isinstance(psum, AP) else psum

    class Arena:
        def __init__(self):
            self.off = 0

        def reset(self):
            self.off = 0

        def alloc(self, shape, dt):
            n = 1
            for s in shape:
                n *= s
            nbytes = n * (4 if dt == F32 else 2)
            nbytes = (nbytes + 31) // 32 * 32
            assert self.off + nbytes <= ARENA_BYTES, ("arena overflow", self.off, nbytes)
            a = arena_ap[:, self.off // 4:(self.off + nbytes) // 4]
            self.off += nbytes
            if dt != F32:
                a = a.bitcast(dt)
            a = a[:, 0:n]
            if len(shape) == 2:
                a = a.rearrange("p (a b) -> p a b", b=shape[1])
            elif len(shape) == 3:
                a = a.rearrange("p (a b c) -> p a b c", b=shape[1], c=shape[2])
            return a

    ar = Arena()

    def sb(name, shape, dt):
        t = nc.alloc_sbuf_tensor(name, [128] + list(shape), dt)
        return t[:] if not isinstance(t, AP) else t

    ident_f = sb("ident_f", [128], F32)
    identb = sb("identb", [128], BF16)
    onesbd = sb("onesbd_s", [128], BF16)
    onesfull = sb("onesfull_s", [128], BF16)
    perm = sb("perm_s", [128], BF16)
    const_b = Buf("consts", glob=True)
    g1 = sb("g1_s", [8], F32)
    g2 = sb("g2_s", [8], F32)
    bgh = sb("bgh_s", [16], F32)
    qkg = sb("qkg_s", [2], F32)
    lamv = sb("lamv_s", [256], F32)
    lamw = sb("lamw_s", [128], F32)
    lams = sb("lams_s", [8], F32)
    subg = sb("subg_s", [128], F32)
    convw = sb("convw_s", [32], F32)
    convb = sb("convb_s", [8], F32)
    rgbh = sb("rgbh_s", [32], F32)
    rgL = sb("rgL_s", [16], F32)
    cLh = sb("cLh_s", [16], F32)
    rgw = sb("rgw_s", [32, 128], BF16)
    par_b = Buf("params", glob=True)
    wring = [sb("wring%d" % i, [4096], BF16) for i in range(NW)]
    wring_b = [Buf("wring%d" % i, glob=True) for i in range(NW)]
    ps_b = [Buf("ps%d" % i) for i in range(8)]

    def ps(i):
        return psum_ap[:, i, :]

    def mm(out, lhsT, rhs, start, stop, reads, writes, skip=False):
        if skip:
            return sd.add("pe", lambda e: e.matmul(out, lhsT, rhs, start=start, stop=stop, skip_group_check=True), reads, writes)
        return sd.add("pe", lambda e: e.matmul(out, lhsT, rhs, start=start, stop=stop), reads, writes)

    def tr(out, in_, ident, reads, writes):
        return sd.add("pe", lambda e: e.transpose(out, in_, ident), reads, writes)

    def act(out, in_, func, reads, writes, scale=1.0, bias=0.0, accum=None):
        def f(e):
            if accum is not None:
                return e.activation(out=out, in_=in_, func=func, bias=bias, scale=scale, accum_out=accum)
            return e.activation(out=out, in_=in_, func=func, bias=bias, scale=scale)
        return sd.add("act", f, reads, writes)

    def stt(eng, out, in0, scalar, in1, op0, op1, reads, writes):
        return sd.add(eng, lambda e: e.scalar_tensor_tensor(out, in0, scalar, in1, op0, op1), reads, writes)

    def ts(eng, out, in0, s1, s2, op0, op1, reads, writes):
        if s2 is None:
            return sd.add(eng, lambda e: e.tensor_scalar(out, in0, s1, None, op0), reads, writes)
        return sd.add(eng, lambda e: e.tensor_scalar(out, in0, s1, s2, op0, op1), reads, writes)

    def tt(eng, out, in0, in1, op, reads, writes):
        return sd.add(eng, lambda e: e.tensor_tensor(out, in0, in1, op), reads, writes)

    def cp(eng, out, in_, reads, writes):
        if eng == "act":
            return sd.add("act", lambda e: e.copy(out, in_), reads, writes)
        return sd.add(eng, lambda e: e.tensor_copy(out, in_), reads, writes)

    def dma(q, out, in_, owner, reads=(), writes=()):
        return sd.add(q, lambda e: e.dma_start(out=out, in_=in_), reads, writes, dma=owner)

    def memset(eng, ap, val, writes):
        return sd.add(eng, lambda e: e.memset(ap, val), (), writes)

    dbg_owner = Buf("dbg", glob=True)
    cur = {"key": None}

    def dump(name, ap, reads):
        if not DEBUG or cur["key"] != "s":
            return
        shape = list(ap.shape)
        dt_ = nc.dram_tensor("dbg_" + name, shape, ap.dtype, kind="ExternalOutput").ap()
        dma(Q_IO, dt_, ap, dbg_owner, reads, ())

    wstate = {"n": 0}

    def wload(l, t):
        i = wstate["n"] % NW
        wstate["n"] += 1
        dma(Q_W, wring[i], wsc[l, t], wring_b[i], (cast_bs[l],), (wring_b[i],))
        return wring[i], wring_b[i]

    class WStream:
        def __init__(self, tiles):
            self.tiles = tiles
            self.q = []
            self.pos = 0
            for _ in range(min(NW - 1, len(tiles))):
                self._issue()

        def _issue(self):
            if self.pos < len(self.tiles):
                self.q.append(wload(*self.tiles[self.pos]))
                self.pos += 1

        def next(self):
            r = self.q.pop(0)
            return r

        def done_one(self):
            self._issue()

    dma("pool", ident_f, ident_d, const_b, (), (const_b,))
    dma("pool", identb, ident_d, const_b, (), (const_b,))
    dma("pool", onesbd, onesbd_d, const_b, (), (const_b,))
    dma("pool", onesfull, onesfull_d, const_b, (), (const_b,))
    dma("pool", perm, perm_d, const_b, (), (const_b,))
    cast_bs = [Buf("cast%d" % l, glob=True, nofence=True) for l in range(L)]

    def cast_layer(l, lazy=False):
        cast_b = cast_bs[l]

        def wview(t):
            return wsc[l, t].rearrange("p (c n) -> p c n", n=512)
        srcs = []
        win_v = w_in[l].rearrange("(c p) n -> p c n", p=128)
        for g in range(14):
            srcs.append(win_v[:, :, g * 512:(g + 1) * 512])
        a_v = wba[l].rearrange("(c p) n -> p c n", p=128)
        r_v = wbr[l].rearrange("(c p) n -> p c n", p=128)
        o_v = wout[l].rearrange("(c p) n -> p c n", p=128)
        srcs += [a_v[:, :, 0:512], r_v[:, :, 0:512], a_v[:, :, 512:1024], r_v[:, :, 512:1024],
                 o_v[:, :, 0:512], o_v[:, :, 512:1024]]
        w1_v = w1[l].rearrange("(c p) n -> p c n", p=128)
        for g in range(8):
            srcs.append(w1_v[:, :, g * 512:(g + 1) * 512])
        jobs = []
        for t, s in enumerate(srcs):
            jobs.append((lambda t=t, s=s: dma("pool", wview(t), s, cast_b, (), (cast_b,))))
        w2_v = w2[l].rearrange("(k p) n -> p k n", p=128)
        for oc in range(8):
            jobs.append((lambda oc=oc: dma("pool", wsc[l, 28 + oc].rearrange("p (k n) -> p k n", n=128),
                                           w2_v[:, :, oc * 128:(oc + 1) * 128], cast_b, (), (cast_b,))))
        if lazy:
            return jobs
        for j in jobs:
            j()
        return []

    cast_layer(0)
    sd.fence()

    def layer_params(l):
        lam_init = 0.8 - 0.6 * math.exp(-0.3 * l)
        pb = par_b
        for dst, src in ((g1, g1_d), (g2, g2_d), (bgh, bgate_d), (qkg, qkg_d), (lamv, lamv_d), (subg, subg_d),
                         (convw, convw_d), (convb, convb_d), (rgbh, rgb_d), (rgL, rgL_d)):
            dma("sp", dst, src[l], pb, (), (pb,))
        dma("pool", rgw, rgw_d[l].rearrange("t p n -> p t n"), pb, (), (pb,))
        ts("dve", bgh, bgh, 0.5, None, ALU.mult, None, (pb,), (pb,))
        ts("dve", rgbh, rgbh, 0.5, None, ALU.mult, None, (pb,), (pb,))
        ts("dve", subg, subg, 1.0 - lam_init, None, ALU.mult, None, (pb,), (pb,))
        tt("dve", lamw[:, 0:64], lamv[:, 0:64], lamv[:, 64:128], ALU.mult, (pb,), (pb,))
        tt("dve", lamw[:, 64:128], lamv[:, 128:192], lamv[:, 192:256], ALU.mult, (pb,), (pb,))
        sd.add("dve", lambda e: e.reduce_sum(lams[:, 0:1], lamw[:, 0:64], AX.X), (pb,), (pb,))
        sd.add("dve", lambda e: e.reduce_sum(lams[:, 1:2], lamw[:, 64:128], AX.X), (pb,), (pb,))
        act(lams[:, 2:4], lams[:, 0:2], AF.Exp, (pb,), (pb,))
        tt("dve", lams[:, 4:5], lams[:, 2:3], lams[:, 3:4], ALU.subtract, (pb,), (pb,))
        ts("dve", lams[:, 5:6], lams[:, 4:5], lam_init, -1.0, ALU.add, ALU.mult, (pb,), (pb,))
        act(cLh, rgL, AF.Exp, (pb,), (pb,), scale=-1.0)
        act(cLh, cLh, AF.Ln, (pb,), (pb,), bias=1.0)
        ts("dve", cLh, cLh, -4.0, None, ALU.mult, None, (pb,), (pb,))
        sd.fence()

    def phase0(key, S):
        ar.reset()
        sd.set_pool("P0")
        xin = [ar.alloc([4, D], F32) for _ in range(2)]
        xin_b = [Buf("xin0"), Buf("xin1")]
        xt = [ar.alloc([8, TT], F32) for _ in range(2)]
        xt_b = [Buf("xt0"), Buf("xt1")]
        xsrc = x_in[key]
        xTv = xT.rearrange("(c p) s -> p c s", p=128)
        k = 0
        for ti in range(S // TT):
            t0 = ti * TT
            i = ti % 2
            dma(Q_IO, xin[i], xsrc[t0:t0 + TT, :].rearrange("(tb p) d -> p tb d", p=128), xin_b[i], (), (xin_b[i],))
            for c in range(8):
                b = k % 8
                k += 1
                for tb in range(4):
                    tr(ps(b)[:, tb * 128:(tb + 1) * 128], xin[i][:, tb, c * 128:(c + 1) * 128], ident_f,
                       (xin_b[i], const_b), (ps_b[b],))
                cp("act" if c % 2 else "dve", xt[i][:, c, :], ps(b), (ps_b[b],), (xt_b[i],))
            dma(Q_IO, xTv[:, :, t0:t0 + TT], xt[i], xt_b[i], (xt_b[i],), ())
        sd.fence()

    def phaseA(l, S):
        ar.reset()
        sd.set_pool("A")
        xt = ar.alloc([8, TT], F32); xt_b = Buf("A.xt")
        cs = ar.alloc([2, TT], F32); cs_b = Buf("A.cs")
        sq = ar.alloc([8, TT], BF16); sq_b = Buf("A.sq")
        nT = ar.alloc([8, TT], BF16); nT_b = Buf("A.nT")
        rstd = ar.alloc([TT], F32); rstd_b = Buf("A.rstd")
        NB = 4
        sq2 = [ar.alloc([TT], BF16) for _ in range(NB)]; sq2_b = [Buf("A.sq2") for _ in range(NB)]
        r2 = [ar.alloc([TT], F32) for _ in range(NB)]; r2_b = [Buf("A.r2") for _ in range(NB)]
        qn = [ar.alloc([TT], F32) for _ in range(NB)]; qn_b = [Buf("A.qn") for _ in range(NB)]
        qnb = [ar.alloc([TT], BF16) for _ in range(NB)]; qnb_b = [Buf("A.qnb") for _ in range(NB)]
        t1 = [ar.alloc([TT], F32) for _ in range(NB)]; t1_b = [Buf("A.t1") for _ in range(NB)]
        tmp = [ar.alloc([TT], F32) for _ in range(NB)]; tmp_b = [Buf("A.tmp") for _ in range(NB)]
        qks = [ar.alloc([8, TT], BF16) for _ in range(2)]; qks_b = [Buf("A.qs"), Buf("A.ks")]
        vst = ar.alloc([4, D], BF16); vst_b = Buf("A.vs")
        xrs = ar.alloc([8, TT], F32); xrs_b = Buf("A.xrs")
        gys = ar.alloc([8, TT], BF16); gys_b = Buf("A.gys")
        ths = ar.alloc([16, TT], BF16); ths_b = Buf("A.ths")
        xTv = xT.rearrange("(c p) s -> p c s", p=128)
        qTv = [qT.rearrange("(c p) s -> p c s", p=128), kT.rearrange("(c p) s -> p c s", p=128)]
        xrTv = xrT.rearrange("(c p) s -> p c s", p=128)
        gyTv = gyT.rearrange("(c p) s -> p c s", p=128)
        thTv = thT.rearrange("(c p) s -> p c s", p=128)
        tiles = []
        for ti in range(S // TT):
            tiles += [(l, g) for g in range(14)]
        ws = WStream(tiles)
        kb = [0]

        def nb():
            b = kb[0] % 8
            kb[0] += 1
            return b
        kk = 0
        qk_pipe = []

        def qk_stage2(cx):
            b, i, which = cx["b"], cx["i"], cx["which"]
            b2 = nb()
            mm(ps(b2), onesbd, sq2[i], True, True, (sq2_b[i], const_b), (ps_b[b2],))
            act(r2[i], ps(b2), AF.Ln, (ps_b[b2],), (r2_b[i],), bias=EPS)
            act(r2[i], r2[i], AF.Exp, (r2_b[i],), (r2_b[i],), scale=-0.5)
            stt("dve", qn[i], ps(b), qkg[:, which:which + 1], r2[i], ALU.mult, ALU.mult,
                (ps_b[b], r2_b[i], par_b), (qn_b[i],))
            cp("pool", qnb[i], qn[i], (qn_b[i],), (qnb_b[i],))
            cx["st"] = 2

        def qk_stage3(cx):
            i, which, h = cx["i"], cx["which"], cx["h"]
            b3 = nb()
            mm(ps(b3), perm, qnb[i], True, True, (qnb_b[i], const_b), (ps_b[b3],))
            tt("pool", t1[i], qn[i], cs[:, 0, :], ALU.mult, (qn_b[i], cs_b), (t1_b[i],))
            tt("dve", tmp[i], ps(b3), cs[:, 1, :], ALU.mult, (ps_b[b3], cs_b), (tmp_b[i],))
            tt("dve", qks[which][:, h, :], t1[i], tmp[i], ALU.add, (t1_b[i], tmp_b[i]), (qks_b[which],))
            cx["st"] = 3

        def qk_advance(flush):
            while True:
                n = len(qk_pipe)
                if n >= 3 or (flush and n >= 1 and qk_pipe[0]["st"] == 2):
                    qk_stage3(qk_pipe.pop(0))
                    continue
                break
            for cx in qk_pipe:
                if cx["st"] == 1 and (flush or cx is not qk_pipe[-1]):
                    qk_stage2(cx)
            if flush:
                while qk_pipe:
                    cx = qk_pipe.pop(0)
                    if cx["st"] == 1:
                        qk_stage2(cx)
                    qk_stage3(cx)

        for ti in range(S // TT):
            t0 = ti * TT
            dma(Q_IO, xt, xTv[:, :, t0:t0 + TT], xt_b, (), (xt_b,))
            dma(Q_IO, cs[:, 0, :], cos_d[:, t0:t0 + TT], cs_b, (), (cs_b,))
            dma(Q_IO, cs[:, 1, :], sin_d[:, t0:t0 + TT], cs_b, (), (cs_b,))
            act(sq, xt, AF.Square, (xt_b,), (sq_b,))
            b = nb()
            for c in range(8):
                mm(ps(b), onesfull, sq[:, c, :], c == 0, c == 7, (sq_b, const_b), (ps_b[b],))
            act(rstd, ps(b), AF.Ln, (ps_b[b],), (rstd_b,), bias=EPS)
            act(rstd, rstd, AF.Exp, (rstd_b,), (rstd_b,), scale=-0.5)
            for c in range(8):
                stt("dve", nT[:, c, :], xt[:, c, :], g1[:, c:c + 1], rstd, ALU.mult, ALU.mult,
                    (xt_b, rstd_b, par_b), (nT_b,))
            for g in range(14):
                wt, wt_b = ws.next()
                wv = wt.rearrange("p (c n) -> p c n", n=512)
                if g < 4:
                    which = g // 2
                    for j in range(4):
                        h = (g % 2) * 4 + j
                        b = nb()
                        for c in range(8):
                            mm(ps(b), wv[:, c, j * 128:(j + 1) * 128], nT[:, c, :], c == 0, c == 7,
                               (nT_b, wt_b), (ps_b[b],))
                        i = kk % NB
                        kk += 1
                        act(sq2[i], ps(b), AF.Square, (ps_b[b],), (sq2_b[i],))
                        qk_pipe.append({"b": b, "i": i, "which": which, "h": h, "st": 1})
                        qk_advance(False)
                    if g == 3:
                        qk_advance(True)
                elif g < 6:
                    half = g - 4
                    for tb in range(4):
                        b = nb()
                        for c in range(8):
                            mm(ps(b), nT[:, c, tb * 128:(tb + 1) * 128], wv[:, c, :], c == 0, c == 7,
                               (nT_b, wt_b), (ps_b[b],))
                        cp("act", vst[:, tb, half * 512:(half + 1) * 512], ps(b), (ps_b[b],), (vst_b,))
                elif g < 8:
                    for j in range(4):
                        cc = (g - 6) * 4 + j
                        b = nb()
                        for c in range(8):
                            mm(ps(b), wv[:, c, j * 128:(j + 1) * 128], nT[:, c, :], c == 0, c == 7,
                               (nT_b, wt_b), (ps_b[b],))
                        cp("act" if j % 2 else "dve", xrs[:, cc, :], ps(b), (ps_b[b],), (xrs_b,))
                elif g < 10:
                    for j in range(4):
                        cc = (g - 8) * 4 + j
                        b = nb()
                        for c in range(8):
                            mm(ps(b), wv[:, c, j * 128:(j + 1) * 128], nT[:, c, :], c == 0, c == 7,
                               (nT_b, wt_b), (ps_b[b],))
                        i = kk % NB
                        kk += 1
                        act(r2[i], ps(b), AF.Square, (ps_b[b],), (r2_b[i],))
                        ts("dve", r2[i], r2[i], 0.044715, 1.0, ALU.mult, ALU.add, (r2_b[i],), (r2_b[i],))
                        tt("dve", qn[i], r2[i], ps(b), ALU.mult, (r2_b[i], ps_b[b]), (qn_b[i],))
                        act(t1[i], qn[i], AF.Tanh, (qn_b[i],), (t1_b[i],), scale=0.7978845608028654)
                        stt("dve", gys[:, cc, :], t1[i], 1.0, ps(b), ALU.add, ALU.mult, (t1_b[i], ps_b[b]), (gys_b,))
                else:
                    for j in range(4):
                        gi = (g - 10) * 4 + j
                        b = nb()
                        for c in range(8):
                            mm(ps(b), wv[:, c, j * 128:(j + 1) * 128], nT[:, c, :], c == 0, c == 7,
                               (nT_b, wt_b), (ps_b[b],))
                        act(ths[:, gi, :], ps(b), AF.Tanh, (ps_b[b], par_b), (ths_b,), scale=0.5, bias=bgh[:, gi:gi + 1])
                ws.done_one()
                if g == 3:
                    dma(Q_IO, qTv[0][:, :, t0:t0 + TT], qks[0], qks_b[0], (qks_b[0],), ())
                    dma(Q_IO, qTv[1][:, :, t0:t0 + TT], qks[1], qks_b[1], (qks_b[1],), ())
                elif g == 5:
                    dma(Q_IO, Vs[t0:t0 + TT, :].rearrange("(tb p) d -> p tb d", p=128), vst, vst_b, (vst_b,), ())
                elif g == 7:
                    dma(Q_IO, xrTv[:, :, t0:t0 + TT], xrs, xrs_b, (xrs_b,), ())
                elif g == 9:
                    dma(Q_IO, gyTv[:, :, t0:t0 + TT], gys, gys_b, (gys_b,), ())
                elif g == 13:
                    dma(Q_IO, thTv[:, 0:8, t0:t0 + TT], ths[:, 0:8, :], ths_b, (ths_b,), ())
                    dma(Q_IO, thTv[:, 8:16, t0:t0 + TT], ths[:, 8:16, :], ths_b, (ths_b,), ())
        sd.fence()

    def phaseB1(l, S, cast_jobs=()):
        ar.reset()
        sd.set_pool("B1")
        NKB = S // 128
        NQB = S // TT
        qh = [ar.alloc([S], BF16) for _ in range(2)]
        kh = [ar.alloc([S], BF16) for _ in range(2)]
        Vh = [ar.alloc([NKB, 130], BF16) for _ in range(2)]
        qkv_b = [Buf("B1.qkv0"), Buf("B1.qkv1")]
        NE = 4
        Eb = [[ar.alloc([TT], BF16) for _ in range(2)] for _ in range(NE)]
        Eb_b = [[Buf("B1.E") for _ in range(2)] for _ in range(NE)]
        rden = ar.alloc([16], F32); rden_b = Buf("B1.rden")
        t0b = [ar.alloc([128], F32) for _ in range(2)]; t0b_b = [Buf("B1.t0") for _ in range(2)]
        attf = [ar.alloc([128], F32) for _ in range(4)]; attf_b = [Buf("B1.attf") for _ in range(4)]
        junk = ar.alloc([128], F32); junk_b = Buf("B1.junk")
        ssq = ar.alloc([8], F32); ssq_b = Buf("B1.ssq")
        attb = [ar.alloc([128], BF16) for _ in range(4)]; attb_b = [Buf("B1.attb") for _ in range(4)]
        attTs = [ar.alloc([TT], BF16) for _ in range(2)]; attTs_b = [Buf("B1.attTs0"), Buf("B1.attTs1")]
        psT = psum_ap[:, 7, :].bitcast(BF16)
        dbgO = ar.alloc([3, 512], F32); dbgO_b = Buf("dbgO")
        Oc = ar.alloc([3, 512], F32); Oc_b = [Buf("B1.Oc%d" % i_) for i_ in range(3)]
        for i in range(2):
            memset("dve", Vh[i][:, :, 128:129], 1.0, (qkv_b[i],))
        sbank = 0
        ei = 0
        nst = 0
        cast_jobs = list(cast_jobs)
        pendP2 = [None]
        pendP3 = [None]

        def make_post(h, qb, q0):
            def P1():
                for ob_ in range(3):
                    ncol = 387 if ob_ < 2 else 258
                    cp("dve", Oc[:, ob_, 0:ncol], ps(4 + ob_)[:, 0:ncol], (ps_b[4 + ob_],), (Oc_b[ob_],))
                for s_ in range(8):
                    ob = s_ // 3
                    oc = (s_ % 3) * 129
                    sd.add("dve", (lambda ob=ob, oc=oc, s_=s_: (lambda e: e.reciprocal(rden[:, s_:s_ + 1], Oc[:, ob, oc + 128:oc + 129])))(),
                           (Oc_b[ob],), (rden_b,))
                ts("dve", rden[:, 8:12], rden[:, 4:8], lams[:, 5:6], None, ALU.mult, None, (rden_b, par_b), (rden_b,))
                memset("dve", ssq[:, 0:4], 0.0, (ssq_b,))
                for j in range(4):
                    s0 = j
                    s1 = 4 + j
                    ti_ = j % 2
                    ts("dve", t0b[ti_], Oc[:, s0 // 3, (s0 % 3) * 129:(s0 % 3) * 129 + 128], rden[:, j:j + 1], None,
                       ALU.mult, None, (Oc_b[s0 // 3], rden_b), (t0b_b[ti_],))
                    stt("dve", attf[j], Oc[:, s1 // 3, (s1 % 3) * 129:(s1 % 3) * 129 + 128], rden[:, 8 + j:9 + j], t0b[ti_],
                        ALU.mult, ALU.add, (Oc_b[s1 // 3], rden_b, t0b_b[ti_]), (attf_b[j],))

            def P2():
                for j in range(4):
                    act(junk, attf[j], AF.Square, (attf_b[j],), (junk_b, ssq_b), accum=ssq[:, j:j + 1])
                act(ssq[:, 4:8], ssq[:, 0:4], AF.Ln, (ssq_b,), (ssq_b,), scale=1.0 / 128.0, bias=EPS)
                act(ssq[:, 4:8], ssq[:, 4:8], AF.Exp, (ssq_b,), (ssq_b,), scale=-0.5)

            def P3():
                ai = (h * NQB + qb) % 2
                for j in range(4):
                    stt("dve", attb[j], attf[j], ssq[:, 4 + j:5 + j], subg, ALU.mult, ALU.mult,
                        (attf_b[j], ssq_b, par_b), (attb_b[j],))
                    tr(psT[:, j * 128:(j + 1) * 128], attb[j], identb, (attb_b[j], const_b), (ps_b[7],))
                cp("dve", attTs[ai], psT[:, 0:TT], (ps_b[7],), (attTs_b[ai],))
                dma(Q_IO, attT[h * 128:(h + 1) * 128, q0:q0 + TT], attTs[ai], attTs_b[ai], (attTs_b[ai],), ())
            return P1, P2, P3

        def flush_post():
            if pendP2[0] is not None:
                pendP2[0]()
                pendP2[0] = None
            if pendP3[0] is not None:
                pendP3[0]()
                pendP3[0] = None

        for h in range(8):
            i = h % 2
            dma(Q_IO, qh[i], qT[h * 128:(h + 1) * 128, 0:S], qkv_b[i], (), (qkv_b[i],))
            dma(Q_IO, kh[i], kT[h * 128:(h + 1) * 128, 0:S], qkv_b[i], (), (qkv_b[i],))
            for k4 in range(0, NKB, 8):
                ke = min(NKB, k4 + 8)
                dma(Q_IO, Vh[i][:, k4:ke, 0:128],
                    Vs[k4 * 128:ke * 128, h * 128:(h + 1) * 128].rearrange("(kb p) e -> p kb e", p=128),
                    qkv_b[i], (), (qkv_b[i],))
            for _ in range(5):
                if cast_jobs:
                    cast_jobs.pop(0)()
            for qb in range(NQB):
                q0 = qb * TT
                fifo = []
                for kc in range(NKB + 2):
                    if kc < NKB:
                        b0 = sbank * 2
                        sbank = (sbank + 1) % 2
                        e = ei % NE
                        ei += 1
                        mm(ps(b0), kh[i][0:64, kc * 128:(kc + 1) * 128], qh[i][0:64, q0:q0 + TT], True, True,
                           (qkv_b[i],), (ps_b[b0],))
                        mm(ps(b0 + 1), kh[i][64:128, kc * 128:(kc + 1) * 128], qh[i][64:128, q0:q0 + TT], True, True,
                           (qkv_b[i],), (ps_b[b0 + 1],))
                        act(Eb[e][0], ps(b0), AF.Exp, (ps_b[b0],), (Eb_b[e][0],), scale=0.125)
                        act(Eb[e][1], ps(b0 + 1), AF.Exp, (ps_b[b0 + 1],), (Eb_b[e][1],), scale=0.125)
                        fifo.append((e, kc))
                    if kc == 2 and pendP2[0] is not None:
                        pendP2[0]()
                        pendP2[0] = None
                    if kc == min(6, NKB + 1) and pendP3[0] is not None:
                        pendP3[0]()
                        pendP3[0] = None
                    if kc >= 2:
                        e_, kc_ = fifo.pop(0)
                        for c in range(2):
                            for j in range(4):
                                s = c * 4 + j
                                ob = 4 + s // 3
                                oc = (s % 3) * 129
                                mm(ps(ob)[:, oc:oc + 129], Eb[e_][c][:, j * 128:(j + 1) * 128], Vh[i][:, kc_, 0:129],
                                   kc_ == 0 and s % 3 == 0, kc_ == NKB - 1, (Eb_b[e_][c], qkv_b[i]), (ps_b[ob],), skip=True)
                flush_post()
                P1, P2, P3 = make_post(h, qb, q0)
                P1()
                pendP2[0] = P2
                pendP3[0] = P3
        flush_post()
        for j_ in cast_jobs:
            j_()
        sd.fence()

    def phaseB2(l, S):
        ar.reset()
        sd.set_pool("B2")
        NH = 2 if S >= 1024 else 1
        HS = S // NH
        xpad = ar.alloc([S + 8], F32); xpad_b = Buf("B2.xpad")
        gy = ar.alloc([S], BF16); gy_b = Buf("B2.gy")
        xc = ar.alloc([S], F32); xc_b = Buf("B2.xc")
        xcb = ar.alloc([S], BF16); xcb_b = Buf("B2.xcb")
        A_ = ar.alloc([S], F32); A_b = [Buf("B2.A%d" % i) for i in range(NH)]
        B_ = ar.alloc([S], F32); B_b = [Buf("B2.B%d" % i) for i in range(NH)]
        I_ = ar.alloc([S], F32); I_b = [Buf("B2.I%d" % i) for i in range(NH)]
        T_ = ar.alloc([S], F32); T_b = [Buf("B2.T%d" % i) for i in range(NH)]
        hf = ar.alloc([S], F32); hf_b = [Buf("B2.hf%d" % i) for i in range(NH)]
        rec = ar.alloc([S], BF16); rec_b = Buf("B2.rec")
        memset("dve", xpad[:, 0:2], 0.0, (xpad_b,))
        memset("dve", xpad[:, S + 2:S + 8], 0.0, (xpad_b,))
        kb = 0

        def rev(a):
            base = a
            apl = [list(x) for x in base.ap]
            n = apl[-1][1]
            apl[-1] = [-1, n]
            return AP(base.tensor, base.offset + (n - 1), apl)

        def scan_op(out, a0, a1, init):
            return lambda e: e.tensor_tensor_scan(out, a0, a1, init, ALU.mult, ALU.add)
        for c in range(8):
            dma(Q_IO, xpad[:, 2:2 + S], xrT[c * 128:(c + 1) * 128, 0:S], xpad_b, (), (xpad_b,))
            dma(Q_IO, gy, gyT[c * 128:(c + 1) * 128, 0:S], gy_b, (), (gy_b,))
            ts("dve", xc, xpad[:, 0:S], convw[:, c * 4:c * 4 + 1], convb[:, c:c + 1], ALU.mult, ALU.add,
               (xpad_b, par_b), (xc_b,))
            for j in range(1, 4):
                stt("dve", xc, xpad[:, j:j + S], convw[:, c * 4 + j:c * 4 + j + 1], xc, ALU.mult, ALU.add,
                    (xpad_b, xc_b, par_b), (xc_b,))
            cp("act", xcb, xc, (xc_b,), (xcb_b,))
            if c == 0:
                dump("xc", xc, (xc_b,))
            for d in range(2):
                ci = d * 8 + c
                order = list(range(NH)) if d == 0 else list(range(NH - 1, -1, -1))
                for blk in range(S // TT):
                    hh_ = (blk * TT) // HS
                    for gt in range(2):
                        b = kb % 8
                        kb += 1
                        idx = (d * 2 + gt) * 8 + c
                        mm(ps(b), rgw[:, idx, :], xcb[:, blk * TT:(blk + 1) * TT], True, True, (xcb_b, par_b), (ps_b[b],))
                        dst, dst_b = (A_, A_b[hh_]) if gt == 0 else (I_, I_b[hh_])
                        act(dst[:, blk * TT:(blk + 1) * TT], ps(b), AF.Tanh, (ps_b[b], par_b), (dst_b,),
                            scale=0.5, bias=rgbh[:, idx:idx + 1])
                sl = [slice(hx * HS, (hx + 1) * HS) for hx in range(NH)]
                for hx in order:
                    ts("dve", A_[:, sl[hx]], A_[:, sl[hx]], cLh[:, ci:ci + 1], cLh[:, ci:ci + 1], ALU.mult, ALU.add,
                       (A_b[hx], par_b), (A_b[hx],))
                for hx in order:
                    act(B_[:, sl[hx]], A_[:, sl[hx]], AF.Exp, (A_b[hx],), (B_b[hx],), scale=2.0)
                    act(T_[:, sl[hx]], A_[:, sl[hx]], AF.Tanh, (A_b[hx],), (T_b[hx],))
                    act(A_[:, sl[hx]], A_[:, sl[hx]], AF.Exp, (A_b[hx],), (A_b[hx],))
                for hx in order:
                    stt("dve", B_[:, sl[hx]], B_[:, sl[hx]], 1.0, T_[:, sl[hx]], ALU.add, ALU.mult,
                        (B_b[hx], T_b[hx]), (B_b[hx],))
                for hx in order:
                    act(B_[:, sl[hx]], B_[:, sl[hx]], AF.Sqrt, (B_b[hx],), (B_b[hx],), scale=-1.0)
                for hx in order:
                    stt("dve", B_[:, sl[hx]], I_[:, sl[hx]], 1.0, B_[:, sl[hx]], ALU.add, ALU.mult,
                        (I_b[hx], B_b[hx]), (B_b[hx],))
                    stt("dve", B_[:, sl[hx]], B_[:, sl[hx]], 0.5, xc[:, sl[hx]], ALU.mult, ALU.mult,
                        (B_b[hx], xc_b), (B_b[hx],))
                prev = None
                for hx in order:
                    if d == 0:
                        init = 0.0 if prev is None else hf[:, prev * HS + HS - 1:prev * HS + HS]
                        rd = (A_b[hx], B_b[hx]) + (() if prev is None else (hf_b[prev],))
                        sd.add("dve", scan_op(hf[:, sl[hx]], A_[:, sl[hx]], B_[:, sl[hx]], init), rd, (hf_b[hx],))
                    else:
                        init = 0.0 if prev is None else I_[:, prev * HS:prev * HS + 1]
                        rd = (A_b[hx], B_b[hx]) + (() if prev is None else (I_b[prev],))
                        sd.add("dve", scan_op(rev(I_[:, sl[hx]]), rev(A_[:, sl[hx]]), rev(B_[:, sl[hx]]), init),
                               rd, (I_b[hx],))
                    prev = hx
            for hx in range(NH):
                s_ = slice(hx * HS, (hx + 1) * HS)
                tt("dve", hf[:, s_], hf[:, s_], I_[:, s_], ALU.add, (hf_b[hx], I_b[hx]), (hf_b[hx],))
                stt("dve", rec[:, s_], hf[:, s_], 0.5, gy[:, s_], ALU.mult, ALU.mult, (hf_b[hx], gy_b), (rec_b,))
            dma(Q_IO, recT[c * 128:(c + 1) * 128, 0:S], rec, rec_b, (rec_b,), ())
        sd.fence()

    def phaseC(l, S, key, last):
        ar.reset()
        sd.set_pool("C")
        x = ar.alloc([8, TT], F32); x_b = Buf("C.x")
        att = ar.alloc([8, TT], BF16); att_b = Buf("C.att")
        rec = ar.alloc([8, TT], BF16); rec_b = Buf("C.rec")
        th = [ar.alloc([8, TT], BF16) for _ in range(2)]; th_b = [Buf("C.th0"), Buf("C.th1")]
        m0 = [ar.alloc([TT], F32) for _ in range(2)]; m0_b = [Buf("C.m0") for _ in range(2)]
        m1 = [ar.alloc([TT], F32) for _ in range(2)]; m1_b = [Buf("C.m1") for _ in range(2)]
        mg = ar.alloc([8, TT], BF16); mg_b = Buf("C.mg")
        sq = ar.alloc([8, TT], BF16); sq_b = Buf("C.sq")
        rstd = ar.alloc([TT], F32); rstd_b = Buf("C.rstd")
        n2 = ar.alloc([8, TT], BF16); n2_b = Buf("C.n2")
        rl = [ar.alloc([TT], F32) for _ in range(3)]; rl_b = [Buf("C.rl") for _ in range(3)]
        hh = ar.alloc([32, TT], BF16); hh_b = Buf("C.h")
        yt = [ar.alloc([D], F32) for _ in range(2)]; yt_b = [Buf("C.yt0"), Buf("C.yt1")]
        xTv = xT.rearrange("(c p) s -> p c s", p=128)
        attTv = attT.rearrange("(c p) s -> p c s", p=128)
        recTv = recT.rearrange("(c p) s -> p c s", p=128)
        thTv = thT.rearrange("(c p) s -> p c s", p=128)
        tiles = []
        for ti in range(S // TT):
            tiles += [(l, t) for t in range(14, 36)]
        ws = WStream(tiles)
        kb = [0]

        def nb():
            b = kb[0] % 8
            kb[0] += 1
            return b
        km = 0
        kr = 0
        ky = 0
        for ti in range(S // TT):
            t0 = ti * TT
            dma(Q_IO, att, attTv[:, :, t0:t0 + TT], att_b, (), (att_b,))
            dma(Q_IO, rec, recTv[:, :, t0:t0 + TT], rec_b, (), (rec_b,))
            dma(Q_IO, th[0], thTv[:, 0:8, t0:t0 + TT], th_b[0], (), (th_b[0],))
            dma(Q_IO, th[1], thTv[:, 8:16, t0:t0 + TT], th_b[1], (), (th_b[1],))
            dma(Q_IO, x, xTv[:, :, t0:t0 + TT], x_b, (), (x_b,))
            for half in range(2):
                wa, wa_b = ws.next()
                wr, wr_b = ws.next()
                wav = wa.rearrange("p (c n) -> p c n", n=512)
                wrv = wr.rearrange("p (c n) -> p c n", n=512)
                for j in range(4):
                    oc = half * 4 + j
                    bA = nb()
                    for c in range(8):
                        mm(ps(bA), wav[:, c, j * 128:(j + 1) * 128], att[:, c, :], c == 0, c == 7, (att_b, wa_b), (ps_b[bA],))
                    bR = nb()
                    for c in range(8):
                        mm(ps(bR), wrv[:, c, j * 128:(j + 1) * 128], rec[:, c, :], c == 0, c == 7, (rec_b, wr_b), (ps_b[bR],))
                    i = km % 2
                    km += 1
                    stt("dve", m0[i], th[0][:, oc, :], 1.0, ps(bA), ALU.add, ALU.mult, (th_b[0], ps_b[bA]), (m0_b[i],))
                    stt("dve", m1[i], th[1][:, oc, :], 1.0, ps(bR), ALU.add, ALU.mult, (th_b[1], ps_b[bR]), (m1_b[i],))
                    tt("pool", mg[:, oc, :], m0[i], m1[i], ALU.add, (m0_b[i], m1_b[i]), (mg_b,))
                ws.done_one()
                ws.done_one()
            for half in range(2):
                wo, wo_b = ws.next()
                wov = wo.rearrange("p (c n) -> p c n", n=512)
                for j in range(4):
                    oc = half * 4 + j
                    b = nb()
                    for c in range(8):
                        mm(ps(b), wov[:, c, j * 128:(j + 1) * 128], mg[:, c, :], c == 0, c == 7, (mg_b, wo_b), (ps_b[b],))
                    stt("dve", x[:, oc, :], ps(b), 0.5, x[:, oc, :], ALU.mult, ALU.add, (ps_b[b], x_b), (x_b,))
                ws.done_one()
            act(sq, x, AF.Square, (x_b,), (sq_b,))
            b = nb()
            for c in range(8):
                mm(ps(b), onesfull, sq[:, c, :], c == 0, c == 7, (sq_b, const_b), (ps_b[b],))
            act(rstd, ps(b), AF.Ln, (ps_b[b],), (rstd_b,), bias=EPS)
            act(rstd, rstd, AF.Exp, (rstd_b,), (rstd_b,), scale=-0.5)
            for c in range(8):
                stt("dve", n2[:, c, :], x[:, c, :], g2[:, c:c + 1], rstd, ALU.mult, ALU.mult, (x_b, rstd_b, par_b), (n2_b,))
            for g in range(8):
                w1t, w1_b = ws.next()
                w1v = w1t.rearrange("p (c n) -> p c n", n=512)
                for j in range(4):
                    f = g * 4 + j
                    b = nb()
                    for c in range(8):
                        mm(ps(b), w1v[:, c, j * 128:(j + 1) * 128], n2[:, c, :], c == 0, c == 7, (n2_b, w1_b), (ps_b[b],))
                    i = kr % 3
                    kr += 1
                    act(rl[i], ps(b), AF.Relu, (ps_b[b],), (rl_b[i],))
                    tt("pool", hh[:, f, :], rl[i], rl[i], ALU.mult, (rl_b[i],), (hh_b,))
                ws.done_one()
            for oc in range(8):
                w2t, w2_b = ws.next()
                w2v = w2t.rearrange("p (k n) -> p k n", n=128)
                b = nb()
                for kf in range(32):
                    mm(ps(b), w2v[:, kf, :], hh[:, kf, :], kf == 0, kf == 31, (hh_b, w2_b), (ps_b[b],))
                tt("dve", x[:, oc, :], ps(b), x[:, oc, :], ALU.add, (ps_b[b], x_b), (x_b,))
                ws.done_one()
            if not last:
                dma(Q_IO, xTv[:, :, t0:t0 + TT], x, x_b, (x_b,), ())
            else:
                for tb in range(4):
                    i = ky % 2
                    ky += 1
                    for hv in range(2):
                        b = nb()
                        for c4 in range(4):
                            c = hv * 4 + c4
                            tr(ps(b)[:, c4 * 128:(c4 + 1) * 128], x[:, c, tb * 128:(tb + 1) * 128], ident_f,
                               (x_b, const_b), (ps_b[b],))
                        cp("act" if hv else "dve", yt[i][:, hv * 512:(hv + 1) * 512], ps(b), (ps_b[b],), (yt_b[i],))
                    dma(Q_IO, y_out[key][t0 + tb * 128:t0 + (tb + 1) * 128, :], yt[i], yt_b[i], (yt_b[i],), ())
        sd.fence()

    for key, S in (("p", SP), ("s", SS)):
        cur["key"] = key
        phase0(key, S)
        for l in range(L):
            layer_params(l)
            phaseA(l, S)
            cast_jobs = cast_layer(l + 1, lazy=True) if (key == "p" and l + 1 < L) else []
            phaseB1(l, S, cast_jobs)
            phaseB2(l, S)
            phaseC(l, S, key, l == L - 1)

    n_sems = {e: max(1, (len(sd.ops[e]) + SEM_LIMIT - 1) // SEM_LIMIT) for e in Sched.ENGS}
    for e in Sched.ENGS:
        k = 0
        for op in sd.ops[e]:
            if op.need_inc and op.dma_sem is None:
                k += 1
                op.idx = k
    import contextlib
    with contextlib.ExitStack() as st:
        eng_sems = {}
        for e in Sched.ENGS:
            cnt = sum(1 for op in sd.ops[e] if op.idx is not None)
            ns = max(1, (cnt + SEM_LIMIT - 1) // SEM_LIMIT)
            eng_sems[e] = [st.enter_context(nc.semaphore("se_%s_%d" % (e, i))) for i in range(ns)]
        for i, b in enumerate(sd.slots):
            b.sem = st.enter_context(nc.semaphore("sd_%d" % i))
        block = st.enter_context(nc.Block())

        def resolve(d):
            if d[0] == "dma":
                return d[1].sem, d[2], None
            op = d[1]
            ep = (op.idx - 1) // SEM_LIMIT
            return eng_sems[op.eng][ep], (op.idx - 1) % SEM_LIMIT + 1, op.eng

        def emit(e, h):
            waited = {}
            for op in sd.ops[e]:
                for d in op.deps:
                    if d[0] == "eng":
                        if d[1].idx is None:
                            continue
                        if d[1].eng == e and (e == "pe" or not SAME_ENG_SYNC):
                            continue
                    sem, val, _ = resolve(d)
                    key_ = id(sem)
                    if waited.get(key_, 0) >= val:
                        continue
                    waited[key_] = val
                    h.wait_ge(sem, val)
                ins = op.fn(h)
                if op.dma_sem is not None:
                    ins.then_inc(op.dma_sem.sem, 16)
                elif op.idx is not None:
                    ep = (op.idx - 1) // SEM_LIMIT
                    ins.then_inc(eng_sems[e][ep], 1)

        @block.tensor
        def _(h):
            emit("pe", h)

        @block.scalar
        def _(h):
            emit("act", h)

        @block.vector
        def _(h):
            emit("dve", h)

        @block.gpsimd
        def _(h):
            emit("pool", h)

        @block.sync
        def _(h):
            emit("sp", h)
    stats = {e: len(sd.ops[e]) for e in Sched.ENGS}
    return nc, stats


def _host_layout(inp, L, S_MAX):
    f = np.float32

    def fm(v, nch):
        return np.ascontiguousarray(np.asarray(v, f).reshape(nch, 128).T)
    out = {}
    out["g1"] = np.stack([fm(inp["norm1_g"][l], 8) for l in range(L)])
    out["g2"] = np.stack([fm(inp["norm2_g"][l], 8) for l in range(L)])
    out["bgate"] = np.stack([fm(inp["b_gate"][l], 16) for l in range(L)])
    qkg = np.zeros((L, 128, 2), f)
    for l in range(L):
        qkg[l, :, 0] = np.tile(np.asarray(inp["q_norm_g"][l], f), 2)
        qkg[l, :, 1] = np.tile(np.asarray(inp["k_norm_g"][l], f), 2)
    out["qkg"] = qkg
    out["lamv"] = np.ascontiguousarray(np.broadcast_to(np.asarray(inp["lam_vecs"], f)[:L].reshape(L, 1, 256), (L, 128, 256)))
    out["subg"] = np.ascontiguousarray(np.broadcast_to(np.asarray(inp["subln_g"], f)[:L].reshape(L, 1, 128), (L, 128, 128)))
    cw = np.asarray(inp["conv_w"], f)[:L]
    convw = np.zeros((L, 128, 8, 4), f)
    for l in range(L):
        for j in range(4):
            convw[l, :, :, j] = fm(cw[l, j], 8)
    out["convw"] = convw.reshape(L, 128, 32)
    out["convb"] = np.stack([fm(inp["conv_b"][l], 8) for l in range(L)])
    rb = np.asarray(inp["rg_b"], f)[:L]
    rgb = np.zeros((L, 128, 2, 2, 8), f)
    for l in range(L):
        for d in range(2):
            for g in range(2):
                rgb[l, :, d, g, :] = fm(rb[l, d, g], 8)
    out["rgb"] = rgb.reshape(L, 128, 32)
    rL = np.asarray(inp["rg_L"], f)[:L]
    rgL = np.zeros((L, 128, 2, 8), f)
    for l in range(L):
        for d in range(2):
            rgL[l, :, d, :] = fm(rL[l, d], 8)
    out["rgL"] = rgL.reshape(L, 128, 16)
    rw = np.asarray(inp["rg_w"], f)[:L]
    rgw = np.zeros((L, 2, 2, 8, 128, 128), f)
    for c in range(8):
        rgw[:, :, :, c, 0:64, 0:64] = rw[:, :, :, 2 * c]
        rgw[:, :, :, c, 64:128, 64:128] = rw[:, :, :, 2 * c + 1]
    out["rgw"] = rgw.reshape(L, 32, 128, 128)
    out["ident"] = np.eye(128, dtype=f)
    obd = np.zeros((128, 128), f)
    obd[0:64, 0:64] = 1.0 / 64.0
    obd[64:128, 64:128] = 1.0 / 64.0
    out["onesbd"] = obd
    out["onesfull"] = np.full((128, 128), 1.0 / 1024.0, f)
    pm = np.zeros((128, 128), f)
    cosT = np.ones((128, S_MAX), f)
    sinT = np.zeros((128, S_MAX), f)
    pos = np.arange(S_MAX, dtype=f)
    inv_freq = (np.float32(500000.0) ** (-np.arange(0, 16, 2, dtype=f) / np.float32(16))).astype(f)
    ang = (pos[:, None] * inv_freq[None, :]).astype(f)
    cs = np.cos(ang).astype(f).T
    sn = np.sin(ang).astype(f).T
    for gb in (0, 64):
        for m in range(8):
            pm[gb + m + 8, gb + m] = 1.0
            pm[gb + m, gb + m + 8] = 1.0
            cosT[gb + m] = cs[m]
            cosT[gb + m + 8] = cs[m]
            sinT[gb + m] = -sn[m]
            sinT[gb + m + 8] = sn[m]
    out["perm"] = pm
    out["cosT"] = cosT
    out["sinT"] = sinT
    return out


_CACHE = {}


def run(inputs, L=4, n_cores=8):
    xp = np.asarray(inputs["x_prompt"], np.float32)
    xs = np.asarray(inputs["x_sample"], np.float32)
    SP, SS = xp.shape[1], xs.shape[1]
    keyc = (SP, SS, L)
    if keyc not in _CACHE:
        _CACHE[keyc] = build_program(SP, SS, L)
    nc, stats = _CACHE[keyc]
    lay = _host_layout(inputs, L, max(SP, SS))
    shared = {
        "w_in": np.ascontiguousarray(np.asarray(inputs["w_in"], np.float32)[:L]),
        "wba": np.ascontiguousarray(np.asarray(inputs["w_branch_att"], np.float32)[:L]),
        "wbr": np.ascontiguousarray(np.asarray(inputs["w_branch_rec"], np.float32)[:L]),
        "wout": np.ascontiguousarray(np.asarray(inputs["w_out"], np.float32)[:L]),
        "w1": np.ascontiguousarray(np.asarray(inputs["w_ff1"], np.float32)[:L]),
        "w2": np.ascontiguousarray(np.asarray(inputs["w_ff2"], np.float32)[:L]),
    }
    shared.update(lay)
    in_maps = []
    for i in range(n_cores):
        m = dict(shared)
        m["xp"] = np.ascontiguousarray(xp[i])
        m["xs"] = np.ascontiguousarray(xs[i])
        in_maps.append(m)
    res = run_bass_kernel_spmd(nc, in_maps, core_ids=list(range(n_cores)))
    if DEBUG:
        _CACHE["dbg"] = res.results
    yp = np.stack([np.asarray(r["yp"], np.float32) for r in res.results])
    ys = np.stack([np.asarray(r["ys"], np.float32) for r in res.results])
    return yp, ys


def kernel(**inputs):
    return run(inputs, L=4, n_cores=8)
```

```python
import math
import numpy as np
import concourse.bass as bass
import concourse.mybir as mybir
from concourse.bass_utils import run_bass_kernel_spmd
from concourse.ap import AP

F32 = mybir.dt.float32
BF16 = mybir.dt.bfloat16
ALU = mybir.AluOpType
AF = mybir.ActivationFunctionType
AX = mybir.AxisListType

D = 1024
DC = 8
D_IN = 7168
D_FF = 4096
EPS = 1e-6
NT_W = 36
TT = 512
SEM_LIMIT = 30000
SAME_ENG_SYNC = True
Q_IO = "pool"
Q_W = "sp"
NW = 4
DEBUG = False


class Slot:
    __slots__ = ("cnt", "sem")

    def __init__(self):
        self.cnt = 0
        self.sem = None


class Buf:
    __slots__ = ("name", "w", "r", "slot", "glob", "nofence")

    def __init__(self, name, glob=False, nofence=False):
        self.name = name
        self.w = None
        self.r = {}
        self.slot = None
        self.glob = glob
        self.nofence = nofence


class Op:
    __slots__ = ("eng", "fn", "deps", "need_inc", "idx", "dma_sem", "ev")


class Sched:
    ENGS = ("pe", "act", "dve", "pool", "sp")

    def __init__(self):
        self.ops = {e: [] for e in self.ENGS}
        self.dirty = {}
        self.slots = []
        self.pools = {}
        self.cur_pool = None
        self.cur_idx = 0

    def set_pool(self, name):
        self.cur_pool = name
        self.cur_idx = 0

    def get_slot(self, buf):
        if buf.slot is None:
            if buf.glob or self.cur_pool is None:
                sl = Slot()
                self.slots.append(sl)
                buf.slot = sl
            else:
                pool = self.pools.setdefault(self.cur_pool, [])
                if self.cur_idx >= len(pool):
                    sl = Slot()
                    pool.append(sl)
                    self.slots.append(sl)
                buf.slot = pool[self.cur_idx]
                self.cur_idx += 1
        return buf.slot

    def add(self, eng, fn, reads=(), writes=(), dma=None, extra_deps=()):
        op = Op()
        op.eng = eng
        op.fn = fn
        op.need_inc = False
        op.idx = None
        op.dma_sem = None
        deps = list(extra_deps)
        war = []
        for b in reads:
            if b.w is not None:
                deps.append(b.w)
        for b in writes:
            if b.w is not None:
                deps.append(b.w)
            for rv in b.r.values():
                if rv[0] == "eng" and rv[1].eng == eng:
                    continue
                deps.append(rv)
        if dma is not None:
            sl = self.get_slot(dma)
            sl.cnt += 16
            ev = ("dma", sl, sl.cnt)
            op.dma_sem = sl
            if not dma.nofence:
                self.dirty[id(sl)] = ev
            rkey = ("dma", id(sl))
        else:
            ev = ("eng", op)
            rkey = ("eng", eng)
        op.ev = ev
        for d in deps:
            if d[0] == "eng":
                d[1].need_inc = True
        op.deps = deps
        for b in reads:
            b.r[rkey] = ev
        for b in writes:
            b.w = ev
            b.r = {}
        self.ops[eng].append(op)
        return op

    def fence(self):
        deps = []
        for e in self.ENGS:
            if e == "sp":
                continue
            for op in reversed(self.ops[e]):
                if op.dma_sem is None:
                    deps.append(op.ev)
                    break
        deps.extend(self.dirty.values())
        self.dirty = {}
        f = self.add("sp", lambda e: e.nop(), extra_deps=deps)
        for e in self.ENGS:
            if e != "sp":
                self.add(e, lambda en: en.nop(), extra_deps=[f.ev])
        return f


def build_program(SP, SS, L, n_layers_lam_off=0):
    nc = bass.Bass("TRN2", target_bir_lowering=False)
    S_MAX = max(SP, SS)
    sd = Sched()

    def din(name, shape, dt=F32):
        return nc.dram_tensor(name, list(shape), dt, kind="ExternalInput").ap()

    def dscr(name, shape, dt):
        if DEBUG and name != "wsc":
            return nc.dram_tensor(name, list(shape), dt, kind="ExternalOutput").ap()
        return nc.dram_tensor(name, list(shape), dt, kind="Internal").ap()

    x_in = {"p": din("xp", [SP, D]), "s": din("xs", [SS, D])}
    y_out = {"p": nc.dram_tensor("yp", [SP, D], F32, kind="ExternalOutput").ap(),
             "s": nc.dram_tensor("ys", [SS, D], F32, kind="ExternalOutput").ap()}
    w_in = din("w_in", [L, D, D_IN])
    wba = din("wba", [L, D, D])
    wbr = din("wbr", [L, D, D])
    wout = din("wout", [L, D, D])
    w1 = din("w1", [L, D, D_FF])
    w2 = din("w2", [L, D_FF, D])
    g1_d = din("g1", [L, 128, 8])
    g2_d = din("g2", [L, 128, 8])
    bgate_d = din("bgate", [L, 128, 16])
    qkg_d = din("qkg", [L, 128, 2])
    lamv_d = din("lamv", [L, 128, 256])
    subg_d = din("subg", [L, 128, 128])
    convw_d = din("convw", [L, 128, 32])
    convb_d = din("convb", [L, 128, 8])
    rgb_d = din("rgb", [L, 128, 32])
    rgL_d = din("rgL", [L, 128, 16])
    rgw_d = din("rgw", [L, 32, 128, 128])
    ident_d = din("ident", [128, 128])
    onesbd_d = din("onesbd", [128, 128])
    onesfull_d = din("onesfull", [128, 128])
    perm_d = din("perm", [128, 128])
    cos_d = din("cosT", [128, S_MAX])
    sin_d = din("sinT", [128, S_MAX])

    wsc = dscr("wsc", [L, NT_W, 128, 4096], BF16)
    xT = dscr("xT", [D, S_MAX], F32)
    qT = dscr("qT", [D, S_MAX], BF16)
    kT = dscr("kT", [D, S_MAX], BF16)
    Vs = dscr("Vs", [S_MAX, D], BF16)
    xrT = dscr("xrT", [D, S_MAX], F32)
    gyT = dscr("gyT", [D, S_MAX], BF16)
    thT = dscr("thT", [2 * D, S_MAX], BF16)
    attT = dscr("attT", [D, S_MAX], BF16)
    recT = dscr("recT", [D, S_MAX], BF16)

    ARENA_BYTES = 158 * 1024
    arena = nc.alloc_sbuf_tensor("arena", [128, ARENA_BYTES // 4], F32)
    arena_ap = arena[:] if not isinstance(arena, AP) else arena
    psum = nc.alloc_psum_tensor("psum", [128, 8, 512], F32)
    psum_ap = psum[:] if not isinstance(psum, AP) else psum

    class Arena:
        def __init__(self):
            self.off = 0

        def reset(self):
            self.off = 0

        def alloc(self, shape, dt):
            n = 1
            for s in shape:
                n *= s
            nbytes = n * (4 if dt == F32 else 2)
            nbytes = (nbytes + 31) // 32 * 32
            assert self.off + nbytes <= ARENA_BYTES, ("arena overflow", self.off, nbytes)
            a = arena_ap[:, self.off // 4:(self.off + nbytes) // 4]
            self.off += nbytes
            if dt != F32:
                a = a.bitcast(dt)
            a = a[:, 0:n]
            if len(shape) == 2:
                a = a.rearrange("p (a b) -> p a b", b=shape[1])
            elif len(shape) == 3:
                a = a.rearrange("p (a b c) -> p a b c", b=shape[1], c=shape[2])
            return a

    ar = Arena()

    def sb(name, shape, dt):
        t = nc.alloc_sbuf_tensor(name, [128] + list(shape), dt)
        return t[:] if not isinstance(t, AP) else t

    ident_f = sb("ident_f", [128], F32)
    identb = sb("identb", [128], BF16)
    onesbd = sb("onesbd_s", [128], BF16)
    onesfull = sb("onesfull_s", [128], BF16)
    perm = sb("perm_s", [128], BF16)
    const_b = Buf("consts", glob=True)
    g1 = sb("g1_s", [8], F32)
    g2 = sb("g2_s", [8], F32)
    bgh = sb("bgh_s", [16], F32)
    qkg = sb("qkg_s", [2], F32)
    lamv = sb("lamv_s", [256], F32)
    lamw = sb("lamw_s", [128], F32)
    lams = sb("lams_s", [8], F32)
    subg = sb("subg_s", [128], F32)
    convw = sb("convw_s", [32], F32)
    convb = sb("convb_s", [8], F32)
    rgbh = sb("rgbh_s", [32], F32)
    rgL = sb("rgL_s", [16], F32)
    cLh = sb("cLh_s", [16], F32)
    rgw = sb("rgw_s", [32, 128], BF16)
    par_b = Buf("params", glob=True)
    wring = [sb("wring%d" % i, [4096], BF16) for i in range(NW)]
    wring_b = [Buf("wring%d" % i, glob=True) for i in range(NW)]
    ps_b = [Buf("ps%d" % i) for i in range(8)]

    def ps(i):
        return psum_ap[:, i, :]

    def mm(out, lhsT, rhs, start, stop, reads, writes, skip=False):
        if skip:
            return sd.add("pe", lambda e: e.matmul(out, lhsT, rhs, start=start, stop=stop, skip_group_check=True), reads, writes)
        return sd.add("pe", lambda e: e.matmul(out, lhsT, rhs, start=start, stop=stop), reads, writes)

    def tr(out, in_, ident, reads, writes):
        return sd.add("pe", lambda e: e.transpose(out, in_, ident), reads, writes)

    def act(out, in_, func, reads, writes, scale=1.0, bias=0.0, accum=None):
        def f(e):
            if accum is not None:
                return e.activation(out=out, in_=in_, func=func, bias=bias, scale=scale, accum_out=accum)
            return e.activation(out=out, in_=in_, func=func, bias=bias, scale=scale)
        return sd.add("act", f, reads, writes)

    def stt(eng, out, in0, scalar, in1, op0, op1, reads, writes):
        return sd.add(eng, lambda e: e.scalar_tensor_tensor(out, in0, scalar, in1, op0, op1), reads, writes)

    def ts(eng, out, in0, s1, s2, op0, op1, reads, writes):
        if s2 is None:
            return sd.add(eng, lambda e: e.tensor_scalar(out, in0, s1, None, op0), reads, writes)
        return sd.add(eng, lambda e: e.tensor_scalar(out, in0, s1, s2, op0, op1), reads, writes)

    def tt(eng, out, in0, in1, op, reads, writes):
        return sd.add(eng, lambda e: e.tensor_tensor(out, in0, in1, op), reads, writes)

    def cp(eng, out, in_, reads, writes):
        if eng == "act":
            return sd.add("act", lambda e: e.copy(out, in_), reads, writes)
        return sd.add(eng, lambda e: e.tensor_copy(out, in_), reads, writes)

    def dma(q, out, in_, owner, reads=(), writes=()):
        return sd.add(q, lambda e: e.dma_start(out=out, in_=in_), reads, writes, dma=owner)

    def memset(eng, ap, val, writes):
        return sd.add(eng, lambda e: e.memset(ap, val), (), writes)

    dbg_owner = Buf("dbg", glob=True)
    cur = {"key": None}

    def dump(name, ap, reads):
        if not DEBUG or cur["key"] != "s":
            return
        shape = list(ap.shape)
        dt_ = nc.dram_tensor("dbg_" + name, shape, ap.dtype, kind="ExternalOutput").ap()
        dma(Q_IO, dt_, ap, dbg_owner, reads, ())

    wstate = {"n": 0}

    def wload(l, t):
        i = wstate["n"] % NW
        wstate["n"] += 1
        dma(Q_W, wring[i], wsc[l, t], wring_b[i], (cast_bs[l],), (wring_b[i],))
        return wring[i], wring_b[i]

    class WStream:
        def __init__(self, tiles):
            self.tiles = tiles
            self.q = []
            self.pos = 0
            for _ in range(min(NW - 1, len(tiles))):
                self._issue()

        def _issue(self):
            if self.pos < len(self.tiles):
                self.q.append(wload(*self.tiles[self.pos]))
                self.pos += 1

        def next(self):
            r = self.q.pop(0)
            return r

        def done_one(self):
            self._issue()

    dma("pool", ident_f, ident_d, const_b, (), (const_b,))
    dma("pool", identb, ident_d, const_b, (), (const_b,))
    dma("pool", onesbd, onesbd_d, const_b, (), (const_b,))
    dma("pool", onesfull, onesfull_d, const_b, (), (const_b,))
    dma("pool", perm, perm_d, const_b, (), (const_b,))
    cast_bs = [Buf("cast%d" % l, glob=True, nofence=True) for l in range(L)]

    def cast_layer(l, lazy=False):
        cast_b = cast_bs[l]

        def wview(t):
            return wsc[l, t].rearrange("p (c n) -> p c n", n=512)
        srcs = []
        win_v = w_in[l].rearrange("(c p) n -> p c n", p=128)
        for g in range(14):
            srcs.append(win_v[:, :, g * 512:(g + 1) * 512])
        a_v = wba[l].rearrange("(c p) n -> p c n", p=128)
        r_v = wbr[l].rearrange("(c p) n -> p c n", p=128)
        o_v = wout[l].rearrange("(c p) n -> p c n", p=128)
        srcs += [a_v[:, :, 0:512], r_v[:, :, 0:512], a_v[:, :, 512:1024], r_v[:, :, 512:1024],
                 o_v[:, :, 0:512], o_v[:, :, 512:1024]]
        w1_v = w1[l].rearrange("(c p) n -> p c n", p=128)
        for g in range(8):
            srcs.append(w1_v[:, :, g * 512:(g + 1) * 512])
        jobs = []
        for t, s in enumerate(srcs):
            jobs.append((lambda t=t, s=s: dma("pool", wview(t), s, cast_b, (), (cast_b,))))
        w2_v = w2[l].rearrange("(k p) n -> p k n", p=128)
        for oc in range(8):
            jobs.append((lambda oc=oc: dma("pool", wsc[l, 28 + oc].rearrange("p (k n) -> p k n", n=128),
                                           w2_v[:, :, oc * 128:(oc + 1) * 128], cast_b, (), (cast_b,))))
        if lazy:
            return jobs
        for j in jobs:
            j()
        return []

    cast_layer(0)
    sd.fence()

    def layer_params(l):
        lam_init = 0.8 - 0.6 * math.exp(-0.3 * l)
        pb = par_b
        for dst, src in ((g1, g1_d), (g2, g2_d), (bgh, bgate_d), (qkg, qkg_d), (lamv, lamv_d), (subg, subg_d),
                         (convw, convw_d), (convb, convb_d), (rgbh, rgb_d), (rgL, rgL_d)):
            dma("sp", dst, src[l], pb, (), (pb,))
        dma("pool", rgw, rgw_d[l].rearrange("t p n -> p t n"), pb, (), (pb,))
        ts("dve", bgh, bgh, 0.5, None, ALU.mult, None, (pb,), (pb,))
        ts("dve", rgbh, rgbh, 0.5, None, ALU.mult, None, (pb,), (pb,))
        ts("dve", subg, subg, 1.0 - lam_init, None, ALU.mult, None, (pb,), (pb,))
        tt("dve", lamw[:, 0:64], lamv[:, 0:64], lamv[:, 64:128], ALU.mult, (pb,), (pb,))
        tt("dve", lamw[:, 64:128], lamv[:, 128:192], lamv[:, 192:256], ALU.mult, (pb,), (pb,))
        sd.add("dve", lambda e: e.reduce_sum(lams[:, 0:1], lamw[:, 0:64], AX.X), (pb,), (pb,))
        sd.add("dve", lambda e: e.reduce_sum(lams[:, 1:2], lamw[:, 64:128], AX.X), (pb,), (pb,))
        act(lams[:, 2:4], lams[:, 0:2], AF.Exp, (pb,), (pb,))
        tt("dve", lams[:, 4:5], lams[:, 2:3], lams[:, 3:4], ALU.subtract, (pb,), (pb,))
        ts("dve", lams[:, 5:6], lams[:, 4:5], lam_init, -1.0, ALU.add, ALU.mult, (pb,), (pb,))
        act(cLh, rgL, AF.Exp, (pb,), (pb,), scale=-1.0)
        act(cLh, cLh, AF.Ln, (pb,), (pb,), bias=1.0)
        ts("dve", cLh, cLh, -4.0, None, ALU.mult, None, (pb,), (pb,))
        sd.fence()

    def phase0(key, S):
        ar.reset()
        sd.set_pool("P0")
        xin = [ar.alloc([4, D], F32) for _ in range(2)]
        xin_b = [Buf("xin0"), Buf("xin1")]
        xt = [ar.alloc([8, TT], F32) for _ in range(2)]
        xt_b = [Buf("xt0"), Buf("xt1")]
        xsrc = x_in[key]
        xTv = xT.rearrange("(c p) s -> p c s", p=128)
        k = 0
        for ti in range(S // TT):
            t0 = ti * TT
            i = ti % 2
            dma(Q_IO, xin[i], xsrc[t0:t0 + TT, :].rearrange("(tb p) d -> p tb d", p=128), xin_b[i], (), (xin_b[i],))
            for c in range(8):
                b = k % 8
                k += 1
                for tb in range(4):
                    tr(ps(b)[:, tb * 128:(tb + 1) * 128], xin[i][:, tb, c * 128:(c + 1) * 128], ident_f,
                       (xin_b[i], const_b), (ps_b[b],))
                cp("act" if c % 2 else "dve", xt[i][:, c, :], ps(b), (ps_b[b],), (xt_b[i],))
            dma(Q_IO, xTv[:, :, t0:t0 + TT], xt[i], xt_b[i], (xt_b[i],), ())
        sd.fence()

    def phaseA(l, S):
        ar.reset()
        sd.set_pool("A")
        xt = ar.alloc([8, TT], F32); xt_b = Buf("A.xt")
        cs = ar.alloc([2, TT], F32); cs_b = Buf("A.cs")
        sq = ar.alloc([8, TT], BF16); sq_b = Buf("A.sq")
        nT2 = [ar.alloc([8, TT], BF16) for _ in range(2)]; nT2_b = [Buf("A.nT0"), Buf("A.nT1")]
        rstd = ar.alloc([TT], F32); rstd_b = Buf("A.rstd")
        NB = 5
        sq2 = [ar.alloc([TT], BF16) for _ in range(NB)]; sq2_b = [Buf("A.sq2") for _ in range(NB)]
        r2 = [ar.alloc([TT], F32) for _ in range(NB)]; r2_b = [Buf("A.r2") for _ in range(NB)]
        qn = [ar.alloc([TT], F32) for _ in range(2)]; qn_b = [Buf("A.qn") for _ in range(2)]
        qnb = [ar.alloc([TT], BF16) for _ in range(NB)]; qnb_b = [Buf("A.qnb") for _ in range(NB)]
        t1 = [ar.alloc([TT], F32) for _ in range(NB)]; t1_b = [Buf("A.t1") for _ in range(NB)]
        tmp = [ar.alloc([TT], F32) for _ in range(NB)]; tmp_b = [Buf("A.tmp") for _ in range(NB)]
        qks = [ar.alloc([8, TT], BF16) for _ in range(2)]; qks_b = [Buf("A.qs"), Buf("A.ks")]
        vst = ar.alloc([4, D], BF16); vst_b = Buf("A.vs")
        xrs = ar.alloc([8, TT], F32); xrs_b = Buf("A.xrs")
        gys = ar.alloc([8, TT], BF16); gys_b = Buf("A.gys")
        ths = ar.alloc([16, TT], BF16); ths_b = Buf("A.ths")
        xTv = xT.rearrange("(c p) s -> p c s", p=128)
        qTv = [qT.rearrange("(c p) s -> p c s", p=128), kT.rearrange("(c p) s -> p c s", p=128)]
        xrTv = xrT.rearrange("(c p) s -> p c s", p=128)
        gyTv = gyT.rearrange("(c p) s -> p c s", p=128)
        thTv = thT.rearrange("(c p) s -> p c s", p=128)
        tiles = []
        for ti in range(S // TT):
            tiles += [(l, g) for g in range(14)]
        ws = WStream(tiles)
        kb = [0]

        def nb():
            b = kb[0] % 8
            kb[0] += 1
            return b
        kk = 0
        qk_pipe = []

        def qk_stage2(cx):
            b, i, which = cx["b"], cx["i"], cx["which"]
            b2 = nb()
            mm(ps(b2), onesbd, sq2[i], True, True, (sq2_b[i], const_b), (ps_b[b2],))
            act(r2[i], ps(b2), AF.Ln, (ps_b[b2],), (r2_b[i],), bias=EPS)
            act(r2[i], r2[i], AF.Exp, (r2_b[i],), (r2_b[i],), scale=-0.5)
            stt("dve", qnb[i], ps(b), qkg[:, which:which + 1], r2[i], ALU.mult, ALU.mult,
                (ps_b[b], r2_b[i], par_b), (qnb_b[i],))
            tt("pool", t1[i], qnb[i], cs[:, 0, :], ALU.mult, (qnb_b[i], cs_b), (t1_b[i],))
            cx["st"] = 2

        def qk_stage3(cx):
            i, which, h = cx["i"], cx["which"], cx["h"]
            b3 = nb()
            mm(ps(b3), perm, qnb[i], True, True, (qnb_b[i], const_b), (ps_b[b3],))
            tt("dve", tmp[i], ps(b3), cs[:, 1, :], ALU.mult, (ps_b[b3], cs_b), (tmp_b[i],))
            tt("dve", qks[which][:, h, :], t1[i], tmp[i], ALU.add, (t1_b[i], tmp_b[i]), (qks_b[which],))
            cx["st"] = 3

        def qk_advance(flush):
            while True:
                n = len(qk_pipe)
                if n >= 4 or (flush and n >= 1 and qk_pipe[0]["st"] == 2):
                    qk_stage3(qk_pipe.pop(0))
                    continue
                break
            for cx in qk_pipe:
                if cx["st"] == 1 and (flush or cx is not qk_pipe[-1]):
                    qk_stage2(cx)
            if flush:
                while qk_pipe:
                    cx = qk_pipe.pop(0)
                    if cx["st"] == 1:
                        qk_stage2(cx)
                    qk_stage3(cx)

        def prologue(ti):
            t0 = ti * TT
            nT, nT_b = nT2[ti % 2], nT2_b[ti % 2]
            dma(Q_IO, xt, xTv[:, :, t0:t0 + TT], xt_b, (), (xt_b,))
            dma(Q_IO, cs[:, 0, :], cos_d[:, t0:t0 + TT], cs_b, (), (cs_b,))
            dma(Q_IO, cs[:, 1, :], sin_d[:, t0:t0 + TT], cs_b, (), (cs_b,))
            act(sq, xt, AF.Square, (xt_b,), (sq_b,))
            b = nb()
            for c in range(8):
                mm(ps(b), onesfull, sq[:, c, :], c == 0, c == 7, (sq_b, const_b), (ps_b[b],))
            act(rstd, ps(b), AF.Ln, (ps_b[b],), (rstd_b,), bias=EPS)
            act(rstd, rstd, AF.Exp, (rstd_b,), (rstd_b,), scale=-0.5)
            for c in range(8):
                stt("dve", nT[:, c, :], xt[:, c, :], g1[:, c:c + 1], rstd, ALU.mult, ALU.mult,
                    (xt_b, rstd_b, par_b), (nT_b,))

        prologue(0)
        for ti in range(S // TT):
            t0 = ti * TT
            nT, nT_b = nT2[ti % 2], nT2_b[ti % 2]
            for g in range(14):
                wt, wt_b = ws.next()
                wv = wt.rearrange("p (c n) -> p c n", n=512)
                if g < 4:
                    which = g // 2
                    for j in range(4):
                        h = (g % 2) * 4 + j
                        b = nb()
                        for c in range(8):
                            mm(ps(b), wv[:, c, j * 128:(j + 1) * 128], nT[:, c, :], c == 0, c == 7,
                               (nT_b, wt_b), (ps_b[b],))
                        i = kk % NB
                        kk += 1
                        act(sq2[i], ps(b), AF.Square, (ps_b[b],), (sq2_b[i],))
                        qk_pipe.append({"b": b, "i": i, "which": which, "h": h, "st": 1})
                        qk_advance(False)
                    if g == 3:
                        qk_advance(True)
                elif g < 6:
                    half = g - 4
                    for tb in range(4):
                        b = nb()
                        for c in range(8):
                            mm(ps(b), nT[:, c, tb * 128:(tb + 1) * 128], wv[:, c, :], c == 0, c == 7,
                               (nT_b, wt_b), (ps_b[b],))
                        cp("act", vst[:, tb, half * 512:(half + 1) * 512], ps(b), (ps_b[b],), (vst_b,))
                elif g < 8:
                    for j in range(4):
                        cc = (g - 6) * 4 + j
                        b = nb()
                        for c in range(8):
                            mm(ps(b), wv[:, c, j * 128:(j + 1) * 128], nT[:, c, :], c == 0, c == 7,
                               (nT_b, wt_b), (ps_b[b],))
                        cp("act" if j % 2 else "dve", xrs[:, cc, :], ps(b), (ps_b[b],), (xrs_b,))
                elif g < 10:
                    for j in range(4):
                        cc = (g - 8) * 4 + j
                        b = nb()
                        for c in range(8):
                            mm(ps(b), wv[:, c, j * 128:(j + 1) * 128], nT[:, c, :], c == 0, c == 7,
                               (nT_b, wt_b), (ps_b[b],))
                        i = kk % NB
                        kk += 1
                        act(r2[i], ps(b), AF.Square, (ps_b[b],), (r2_b[i],))
                        ts("dve", r2[i], r2[i], 0.044715, 1.0, ALU.mult, ALU.add, (r2_b[i],), (r2_b[i],))
                        tt("dve", qn[i % 2], r2[i], ps(b), ALU.mult, (r2_b[i], ps_b[b]), (qn_b[i % 2],))
                        act(t1[i], qn[i % 2], AF.Tanh, (qn_b[i % 2],), (t1_b[i],), scale=0.7978845608028654)
                        stt("dve", gys[:, cc, :], t1[i], 1.0, ps(b), ALU.add, ALU.mult, (t1_b[i], ps_b[b]), (gys_b,))
                else:
                    for j in range(4):
                        gi = (g - 10) * 4 + j
                        b = nb()
                        for c in range(8):
                            mm(ps(b), wv[:, c, j * 128:(j + 1) * 128], nT[:, c, :], c == 0, c == 7,
                               (nT_b, wt_b), (ps_b[b],))
                        act(ths[:, gi, :], ps(b), AF.Tanh, (ps_b[b], par_b), (ths_b,), scale=0.5, bias=bgh[:, gi:gi + 1])
                ws.done_one()
                if g == 3:
                    dma(Q_IO, qTv[0][:, :, t0:t0 + TT], qks[0], qks_b[0], (qks_b[0],), ())
                    dma(Q_IO, qTv[1][:, :, t0:t0 + TT], qks[1], qks_b[1], (qks_b[1],), ())
                elif g == 5:
                    dma(Q_IO, Vs[t0:t0 + TT, :].rearrange("(tb p) d -> p tb d", p=128), vst, vst_b, (vst_b,), ())
                elif g == 7:
                    dma(Q_IO, xrTv[:, :, t0:t0 + TT], xrs, xrs_b, (xrs_b,), ())
                    if ti + 1 < S // TT:
                        prologue(ti + 1)
                elif g == 9:
                    dma(Q_IO, gyTv[:, :, t0:t0 + TT], gys, gys_b, (gys_b,), ())
                elif g == 13:
                    dma(Q_IO, thTv[:, 0:8, t0:t0 + TT], ths[:, 0:8, :], ths_b, (ths_b,), ())
                    dma(Q_IO, thTv[:, 8:16, t0:t0 + TT], ths[:, 8:16, :], ths_b, (ths_b,), ())
        sd.fence()

    def phaseB1(l, S, cast_jobs=()):
        ar.reset()
        sd.set_pool("B1")
        NKB = S // 128
        NQB = S // TT
        qh = [ar.alloc([S], BF16) for _ in range(2)]
        kh = [ar.alloc([S], BF16) for _ in range(2)]
        Vh = [ar.alloc([NKB, 130], BF16) for _ in range(2)]
        qkv_b = [Buf("B1.qkv0"), Buf("B1.qkv1")]
        NE = 4
        Eb = [[ar.alloc([TT], BF16) for _ in range(2)] for _ in range(NE)]
        Eb_b = [[Buf("B1.E") for _ in range(2)] for _ in range(NE)]
        rden = ar.alloc([16], F32); rden_b = Buf("B1.rden")
        t0b = [ar.alloc([128], F32) for _ in range(2)]; t0b_b = [Buf("B1.t0") for _ in range(2)]
        attf = [ar.alloc([128], F32) for _ in range(4)]; attf_b = [Buf("B1.attf") for _ in range(4)]
        junk = ar.alloc([128], F32); junk_b = Buf("B1.junk")
        ssq = ar.alloc([8], F32); ssq_b = Buf("B1.ssq")
        attb = [ar.alloc([128], BF16) for _ in range(4)]; attb_b = [Buf("B1.attb") for _ in range(4)]
        attTs = [ar.alloc([TT], BF16) for _ in range(2)]; attTs_b = [Buf("B1.attTs0"), Buf("B1.attTs1")]
        psT = psum_ap[:, 7, :].bitcast(BF16)
        dbgO = ar.alloc([3, 512], F32); dbgO_b = Buf("dbgO")
        Oc = ar.alloc([3, 512], F32); Oc_b = [Buf("B1.Oc%d" % i_) for i_ in range(3)]
        for i in range(2):
            memset("dve", Vh[i][:, :, 128:129], 1.0, (qkv_b[i],))
        sbank = 0
        ei = 0
        nst = 0
        cast_jobs = list(cast_jobs)
        pendP2 = [None]
        pendP3 = [None]

        def make_post(h, qb, q0):
            def P1():
                for ob_ in range(3):
                    ncol = 387 if ob_ < 2 else 258
                    cp("dve", Oc[:, ob_, 0:ncol], ps(4 + ob_)[:, 0:ncol], (ps_b[4 + ob_],), (Oc_b[ob_],))
                for s_ in range(8):
                    ob = s_ // 3
                    oc = (s_ % 3) * 129
                    sd.add("dve", (lambda ob=ob, oc=oc, s_=s_: (lambda e: e.reciprocal(rden[:, s_:s_ + 1], Oc[:, ob, oc + 128:oc + 129])))(),
                           (Oc_b[ob],), (rden_b,))
                ts("dve", rden[:, 8:12], rden[:, 4:8], lams[:, 5:6], None, ALU.mult, None, (rden_b, par_b), (rden_b,))
                memset("dve", ssq[:, 0:4], 0.0, (ssq_b,))
                for j in range(4):
                    s0 = j
                    s1 = 4 + j
                    ti_ = j % 2
                    ts("dve", t0b[ti_], Oc[:, s0 // 3, (s0 % 3) * 129:(s0 % 3) * 129 + 128], rden[:, j:j + 1], None,
                       ALU.mult, None, (Oc_b[s0 // 3], rden_b), (t0b_b[ti_],))
                    stt("dve", attf[j], Oc[:, s1 // 3, (s1 % 3) * 129:(s1 % 3) * 129 + 128], rden[:, 8 + j:9 + j], t0b[ti_],
                        ALU.mult, ALU.add, (Oc_b[s1 // 3], rden_b, t0b_b[ti_]), (attf_b[j],))

            def P2():
                for j in range(4):
                    act(junk, attf[j], AF.Square, (attf_b[j],), (junk_b, ssq_b), accum=ssq[:, j:j + 1])
                act(ssq[:, 4:8], ssq[:, 0:4], AF.Ln, (ssq_b,), (ssq_b,), scale=1.0 / 128.0, bias=EPS)
                act(ssq[:, 4:8], ssq[:, 4:8], AF.Exp, (ssq_b,), (ssq_b,), scale=-0.5)

            def P3():
                ai = (h * NQB + qb) % 2
                for j in range(4):
                    stt("dve", attb[j], attf[j], ssq[:, 4 + j:5 + j], subg, ALU.mult, ALU.mult,
                        (attf_b[j], ssq_b, par_b), (attb_b[j],))
                    tr(psT[:, j * 128:(j + 1) * 128], attb[j], identb, (attb_b[j], const_b), (ps_b[7],))
                cp("dve", attTs[ai], psT[:, 0:TT], (ps_b[7],), (attTs_b[ai],))
                dma(Q_IO, attT[h * 128:(h + 1) * 128, q0:q0 + TT], attTs[ai], attTs_b[ai], (attTs_b[ai],), ())
            return P1, P2, P3

        def flush_post():
            if pendP2[0] is not None:
                pendP2[0]()
                pendP2[0] = None
            if pendP3[0] is not None:
                pendP3[0]()
                pendP3[0] = None

        for h in range(8):
            i = h % 2
            dma(Q_IO, qh[i], qT[h * 128:(h + 1) * 128, 0:S], qkv_b[i], (), (qkv_b[i],))
            dma(Q_IO, kh[i], kT[h * 128:(h + 1) * 128, 0:S], qkv_b[i], (), (qkv_b[i],))
            for k4 in range(0, NKB, 8):
                ke = min(NKB, k4 + 8)
                dma(Q_IO, Vh[i][:, k4:ke, 0:128],
                    Vs[k4 * 128:ke * 128, h * 128:(h + 1) * 128].rearrange("(kb p) e -> p kb e", p=128),
                    qkv_b[i], (), (qkv_b[i],))
            for _ in range(5):
                if cast_jobs:
                    cast_jobs.pop(0)()
            for qb in range(NQB):
                q0 = qb * TT
                fifo = []
                for kc in range(NKB + 2):
                    if kc < NKB:
                        b0 = sbank * 2
                        sbank = (sbank + 1) % 2
                        e = ei % NE
                        ei += 1
                        mm(ps(b0), kh[i][0:64, kc * 128:(kc + 1) * 128], qh[i][0:64, q0:q0 + TT], True, True,
                           (qkv_b[i],), (ps_b[b0],))
                        mm(ps(b0 + 1), kh[i][64:128, kc * 128:(kc + 1) * 128], qh[i][64:128, q0:q0 + TT], True, True,
                           (qkv_b[i],), (ps_b[b0 + 1],))
                        act(Eb[e][0], ps(b0), AF.Exp, (ps_b[b0],), (Eb_b[e][0],), scale=0.125)
                        act(Eb[e][1], ps(b0 + 1), AF.Exp, (ps_b[b0 + 1],), (Eb_b[e][1],), scale=0.125)
                        fifo.append((e, kc))
                    if kc == 2 and pendP2[0] is not None:
                        pendP2[0]()
                        pendP2[0] = None
                    if kc == min(6, NKB + 1) and pendP3[0] is not None:
                        pendP3[0]()
                        pendP3[0] = None
                    if kc >= 2:
                        e_, kc_ = fifo.pop(0)
                        for c in range(2):
                            for j in range(4):
                                s = c * 4 + j
                                ob = 4 + s // 3
                                oc = (s % 3) * 129
                                mm(ps(ob)[:, oc:oc + 129], Eb[e_][c][:, j * 128:(j + 1) * 128], Vh[i][:, kc_, 0:129],
                                   kc_ == 0 and s % 3 == 0, kc_ == NKB - 1, (Eb_b[e_][c], qkv_b[i]), (ps_b[ob],), skip=True)
                flush_post()
                P1, P2, P3 = make_post(h, qb, q0)
                P1()
                pendP2[0] = P2
                pendP3[0] = P3
        flush_post()
        for j_ in cast_jobs:
            j_()
        sd.fence()

    def phaseB2(l, S):
        ar.reset()
        sd.set_pool("B2")
        NH = 2 if S >= 1024 else 1
        HS = S // NH
        xpad = ar.alloc([S + 8], F32); xpad_b = Buf("B2.xpad")
        gy = ar.alloc([S], BF16); gy_b = Buf("B2.gy")
        xc = ar.alloc([S], F32); xc_b = Buf("B2.xc")
        xcb = ar.alloc([S], BF16); xcb_b = Buf("B2.xcb")
        A_ = [ar.alloc([S], F32) for _ in range(2)]; A_b = [[Buf("B2.A") for _ in range(NH)] for _ in range(2)]
        B_ = [ar.alloc([S], F32) for _ in range(2)]; B_b = [[Buf("B2.B") for _ in range(NH)] for _ in range(2)]
        T_ = [ar.alloc([S], F32) for _ in range(2)]; T_b = [[Buf("B2.T") for _ in range(NH)] for _ in range(2)]
        rec = ar.alloc([S], BF16); rec_b = Buf("B2.rec")
        memset("dve", xpad[:, 0:2], 0.0, (xpad_b,))
        memset("dve", xpad[:, S + 2:S + 8], 0.0, (xpad_b,))
        kbs = [0]

        def rev(a):
            base = a
            apl = [list(x) for x in base.ap]
            n = apl[-1][1]
            apl[-1] = [-1, n]
            return AP(base.tensor, base.offset + (n - 1), apl)

        def scan_op(out, a0, a1, init):
            return lambda e: e.tensor_tensor_scan(out, a0, a1, init, ALU.mult, ALU.add)
        sl = [slice(hx * HS, (hx + 1) * HS) for hx in range(NH)]

        def dir_steps(c, d):
            ci = d * 8 + c
            order = list(range(NH)) if d == 0 else list(range(NH - 1, -1, -1))
            A, B, T = A_[d], B_[d], T_[d]
            Ab, Bb, Tb = A_b[d], B_b[d], T_b[d]

            def gate(gt, dst, dstb):
                def f():
                    for blk in range(S // TT):
                        hh_ = (blk * TT) // HS
                        b = kbs[0] % 8
                        kbs[0] += 1
                        idx = (d * 2 + gt) * 8 + c
                        mm(ps(b), rgw[:, idx, :], xcb[:, blk * TT:(blk + 1) * TT], True, True, (xcb_b, par_b), (ps_b[b],))
                        act(dst[:, blk * TT:(blk + 1) * TT], ps(b), AF.Tanh, (ps_b[b], par_b), (dstb[hh_],),
                            scale=0.5, bias=rgbh[:, idx:idx + 1])
                return f

            def s2():
                for hx in order:
                    ts("dve", A[:, sl[hx]], A[:, sl[hx]], cLh[:, ci:ci + 1], cLh[:, ci:ci + 1], ALU.mult, ALU.add,
                       (Ab[hx], par_b), (Ab[hx],))

            def s3():
                for hx in order:
                    act(T[:, sl[hx]], A[:, sl[hx]], AF.Tanh, (Ab[hx],), (Tb[hx],))
                    act(B[:, sl[hx]], A[:, sl[hx]], AF.Exp, (Ab[hx],), (Bb[hx],), scale=2.0)
                    act(A[:, sl[hx]], A[:, sl[hx]], AF.Exp, (Ab[hx],), (Ab[hx],))

            def s4():
                for hx in order:
                    stt("dve", B[:, sl[hx]], B[:, sl[hx]], 1.0, T[:, sl[hx]], ALU.add, ALU.mult,
                        (Bb[hx], Tb[hx]), (Bb[hx],))

            def s6():
                for hx in order:
                    act(B[:, sl[hx]], B[:, sl[hx]], AF.Sqrt, (Bb[hx],), (Bb[hx],), scale=-1.0)

            def s7():
                for hx in order:
                    stt("dve", B[:, sl[hx]], T[:, sl[hx]], 1.0, B[:, sl[hx]], ALU.add, ALU.mult,
                        (Tb[hx], Bb[hx]), (Bb[hx],))
                    stt("dve", B[:, sl[hx]], B[:, sl[hx]], 0.5, xc[:, sl[hx]], ALU.mult, ALU.mult,
                        (Bb[hx], xc_b), (Bb[hx],))

            def s8():
                prev = None
                for hx in order:
                    if d == 0:
                        init = 0.0 if prev is None else T[:, prev * HS + HS - 1:prev * HS + HS]
                        rd = (Ab[hx], Bb[hx]) + (() if prev is None else (Tb[prev],))
                        sd.add("dve", scan_op(T[:, sl[hx]], A[:, sl[hx]], B[:, sl[hx]], init), rd, (Tb[hx],))
                    else:
                        init = 0.0 if prev is None else T[:, prev * HS:prev * HS + 1]
                        rd = (Ab[hx], Bb[hx]) + (() if prev is None else (Tb[prev],))
                        sd.add("dve", scan_op(rev(T[:, sl[hx]]), rev(A[:, sl[hx]]), rev(B[:, sl[hx]]), init),
                               rd, (Tb[hx],))
                    prev = hx
            return [gate(0, A, Ab), s2, s3, s4, gate(1, T, Tb), s6, s7, s8]

        for c in range(8):
            dma(Q_IO, xpad[:, 2:2 + S], xrT[c * 128:(c + 1) * 128, 0:S], xpad_b, (), (xpad_b,))
            dma(Q_IO, gy, gyT[c * 128:(c + 1) * 128, 0:S], gy_b, (), (gy_b,))
            ts("dve", xc, xpad[:, 0:S], convw[:, c * 4:c * 4 + 1], convb[:, c:c + 1], ALU.mult, ALU.add,
               (xpad_b, par_b), (xc_b,))
            for j in range(1, 4):
                stt("dve", xc, xpad[:, j:j + S], convw[:, c * 4 + j:c * 4 + j + 1], xc, ALU.mult, ALU.add,
                    (xpad_b, xc_b, par_b), (xc_b,))
            cp("act", xcb, xc, (xc_b,), (xcb_b,))
            st0 = dir_steps(c, 0)
            st1 = dir_steps(c, 1)
            for f0, f1 in zip(st0, st1):
                f0()
                f1()
            for hx in range(NH):
                s_ = sl[hx]
                tt("dve", T_[0][:, s_], T_[0][:, s_], T_[1][:, s_], ALU.add, (T_b[0][hx], T_b[1][hx]), (T_b[0][hx],))
                stt("dve", rec[:, s_], T_[0][:, s_], 0.5, gy[:, s_], ALU.mult, ALU.mult, (T_b[0][hx], gy_b), (rec_b,))
            dma(Q_IO, recT[c * 128:(c + 1) * 128, 0:S], rec, rec_b, (rec_b,), ())
        sd.fence()

    def phaseC(l, S, key, last):
        ar.reset()
        sd.set_pool("C")
        x = ar.alloc([8, TT], F32); x_b = Buf("C.x")
        att = ar.alloc([8, TT], BF16); att_b = Buf("C.att")
        rec = ar.alloc([8, TT], BF16); rec_b = Buf("C.rec")
        th = [ar.alloc([8, TT], BF16) for _ in range(2)]; th_b = [Buf("C.th0"), Buf("C.th1")]
        m0 = [ar.alloc([TT], F32) for _ in range(2)]; m0_b = [Buf("C.m0") for _ in range(2)]
        m1 = [ar.alloc([TT], F32) for _ in range(2)]; m1_b = [Buf("C.m1") for _ in range(2)]
        mg = ar.alloc([8, TT], BF16); mg_b = Buf("C.mg")
        sq = ar.alloc([8, TT], BF16); sq_b = Buf("C.sq")
        rstd = ar.alloc([TT], F32); rstd_b = Buf("C.rstd")
        n2 = ar.alloc([8, TT], BF16); n2_b = Buf("C.n2")
        rl = [ar.alloc([TT], F32) for _ in range(3)]; rl_b = [Buf("C.rl") for _ in range(3)]
        hh = ar.alloc([32, TT], BF16); hh_b = Buf("C.h")
        yt = [ar.alloc([D], F32) for _ in range(2)]; yt_b = [Buf("C.yt0"), Buf("C.yt1")]
        xTv = xT.rearrange("(c p) s -> p c s", p=128)
        attTv = attT.rearrange("(c p) s -> p c s", p=128)
        recTv = recT.rearrange("(c p) s -> p c s", p=128)
        thTv = thT.rearrange("(c p) s -> p c s", p=128)
        tiles = []
        for ti in range(S // TT):
            tiles += [(l, t) for t in range(14, 36)]
        ws = WStream(tiles)
        kb = [0]

        def nb():
            b = kb[0] % 8
            kb[0] += 1
            return b
        km = 0
        kr = 0
        ky = 0
        def loadsC(ti):
            t0 = ti * TT
            dma(Q_IO, att, attTv[:, :, t0:t0 + TT], att_b, (), (att_b,))
            dma(Q_IO, rec, recTv[:, :, t0:t0 + TT], rec_b, (), (rec_b,))
            dma(Q_IO, th[0], thTv[:, 0:8, t0:t0 + TT], th_b[0], (), (th_b[0],))
            dma(Q_IO, th[1], thTv[:, 8:16, t0:t0 + TT], th_b[1], (), (th_b[1],))

        loadsC(0)
        for ti in range(S // TT):
            t0 = ti * TT
            dma(Q_IO, x, xTv[:, :, t0:t0 + TT], x_b, (), (x_b,))
            for half in range(2):
                wa, wa_b = ws.next()
                wr, wr_b = ws.next()
                wav = wa.rearrange("p (c n) -> p c n", n=512)
                wrv = wr.rearrange("p (c n) -> p c n", n=512)
                for j in range(4):
                    oc = half * 4 + j
                    bA = nb()
                    for c in range(8):
                        mm(ps(bA), wav[:, c, j * 128:(j + 1) * 128], att[:, c, :], c == 0, c == 7, (att_b, wa_b), (ps_b[bA],))
                    bR = nb()
                    for c in range(8):
                        mm(ps(bR), wrv[:, c, j * 128:(j + 1) * 128], rec[:, c, :], c == 0, c == 7, (rec_b, wr_b), (ps_b[bR],))
                    i = km % 2
                    km += 1
                    stt("dve", m0[i], th[0][:, oc, :], 1.0, ps(bA), ALU.add, ALU.mult, (th_b[0], ps_b[bA]), (m0_b[i],))
                    stt("dve", m1[i], th[1][:, oc, :], 1.0, ps(bR), ALU.add, ALU.mult, (th_b[1], ps_b[bR]), (m1_b[i],))
                    tt("pool", mg[:, oc, :], m0[i], m1[i], ALU.add, (m0_b[i], m1_b[i]), (mg_b,))
                ws.done_one()
                ws.done_one()
            for half in range(2):
                wo, wo_b = ws.next()
                wov = wo.rearrange("p (c n) -> p c n", n=512)
                for j in range(4):
                    oc = half * 4 + j
                    b = nb()
                    for c in range(8):
                        mm(ps(b), wov[:, c, j * 128:(j + 1) * 128], mg[:, c, :], c == 0, c == 7, (mg_b, wo_b), (ps_b[b],))
                    stt("dve", x[:, oc, :], ps(b), 0.5, x[:, oc, :], ALU.mult, ALU.add, (ps_b[b], x_b), (x_b,))
                ws.done_one()
            act(sq, x, AF.Square, (x_b,), (sq_b,))
            b = nb()
            for c in range(8):
                mm(ps(b), onesfull, sq[:, c, :], c == 0, c == 7, (sq_b, const_b), (ps_b[b],))
            act(rstd, ps(b), AF.Ln, (ps_b[b],), (rstd_b,), bias=EPS)
            act(rstd, rstd, AF.Exp, (rstd_b,), (rstd_b,), scale=-0.5)
            for c in range(8):
                stt("dve", n2[:, c, :], x[:, c, :], g2[:, c:c + 1], rstd, ALU.mult, ALU.mult, (x_b, rstd_b, par_b), (n2_b,))
            for g in range(8):
                w1t, w1_b = ws.next()
                w1v = w1t.rearrange("p (c n) -> p c n", n=512)
                for j in range(4):
                    f = g * 4 + j
                    b = nb()
                    for c in range(8):
                        mm(ps(b), w1v[:, c, j * 128:(j + 1) * 128], n2[:, c, :], c == 0, c == 7, (n2_b, w1_b), (ps_b[b],))
                    i = kr % 3
                    kr += 1
                    act(rl[i], ps(b), AF.Relu, (ps_b[b],), (rl_b[i],))
                    tt("pool", hh[:, f, :], rl[i], rl[i], ALU.mult, (rl_b[i],), (hh_b,))
                ws.done_one()
            if ti + 1 < S // TT:
                loadsC(ti + 1)
            for oc in range(8):
                w2t, w2_b = ws.next()
                w2v = w2t.rearrange("p (k n) -> p k n", n=128)
                b = nb()
                for kf in range(32):
                    mm(ps(b), w2v[:, kf, :], hh[:, kf, :], kf == 0, kf == 31, (hh_b, w2_b), (ps_b[b],))
                tt("dve", x[:, oc, :], ps(b), x[:, oc, :], ALU.add, (ps_b[b], x_b), (x_b,))
                ws.done_one()
            if not last:
                dma(Q_IO, xTv[:, :, t0:t0 + TT], x, x_b, (x_b,), ())
            else:
                for tb in range(4):
                    i = ky % 2
                    ky += 1
                    for hv in range(2):
                        b = nb()
                        for c4 in range(4):
                            c = hv * 4 + c4
                            tr(ps(b)[:, c4 * 128:(c4 + 1) * 128], x[:, c, tb * 128:(tb + 1) * 128], ident_f,
                               (x_b, const_b), (ps_b[b],))
                        cp("act" if hv else "dve", yt[i][:, hv * 512:(hv + 1) * 512], ps(b), (ps_b[b],), (yt_b[i],))
                    dma(Q_IO, y_out[key][t0 + tb * 128:t0 + (tb + 1) * 128, :], yt[i], yt_b[i], (yt_b[i],), ())
        sd.fence()

    for key, S in (("p", SP), ("s", SS)):
        cur["key"] = key
        phase0(key, S)
        for l in range(L):
            layer_params(l)
            phaseA(l, S)
            cast_jobs = cast_layer(l + 1, lazy=True) if (key == "p" and l + 1 < L) else []
            phaseB1(l, S, cast_jobs)
            phaseB2(l, S)
            phaseC(l, S, key, l == L - 1)

    n_sems = {e: max(1, (len(sd.ops[e]) + SEM_LIMIT - 1) // SEM_LIMIT) for e in Sched.ENGS}
    for e in Sched.ENGS:
        k = 0
        for op in sd.ops[e]:
            if op.need_inc and op.dma_sem is None:
                k += 1
                op.idx = k
    import contextlib
    with contextlib.ExitStack() as st:
        eng_sems = {}
        for e in Sched.ENGS:
            cnt = sum(1 for op in sd.ops[e] if op.idx is not None)
            ns = max(1, (cnt + SEM_LIMIT - 1) // SEM_LIMIT)
            eng_sems[e] = [st.enter_context(nc.semaphore("se_%s_%d" % (e, i))) for i in range(ns)]
        for i, b in enumerate(sd.slots):
            b.sem = st.enter_context(nc.semaphore("sd_%d" % i))
        block = st.enter_context(nc.Block())

        def resolve(d):
            if d[0] == "dma":
                return d[1].sem, d[2], None
            op = d[1]
            ep = (op.idx - 1) // SEM_LIMIT
            return eng_sems[op.eng][ep], (op.idx - 1) % SEM_LIMIT + 1, op.eng

        def emit(e, h):
            waited = {}
            for op in sd.ops[e]:
                for d in op.deps:
                    if d[0] == "eng":
                        if d[1].idx is None:
                            continue
                        if d[1].eng == e and (e == "pe" or not SAME_ENG_SYNC):
                            continue
                    sem, val, _ = resolve(d)
                    key_ = id(sem)
                    if waited.get(key_, 0) >= val:
                        continue
                    waited[key_] = val
                    h.wait_ge(sem, val)
                ins = op.fn(h)
                if op.dma_sem is not None:
                    ins.then_inc(op.dma_sem.sem, 16)
                elif op.idx is not None:
                    ep = (op.idx - 1) // SEM_LIMIT
                    ins.then_inc(eng_sems[e][ep], 1)

        @block.tensor
        def _(h):
            emit("pe", h)

        @block.scalar
        def _(h):
            emit("act", h)

        @block.vector
        def _(h):
            emit("dve", h)

        @block.gpsimd
        def _(h):
            emit("pool", h)

        @block.sync
        def _(h):
            emit("sp", h)
    stats = {e: len(sd.ops[e]) for e in Sched.ENGS}
    return nc, stats


def _host_layout(inp, L, S_MAX):
    f = np.float32

    def fm(v, nch):
        return np.ascontiguousarray(np.asarray(v, f).reshape(nch, 128).T)
    out = {}
    out["g1"] = np.stack([fm(inp["norm1_g"][l], 8) for l in range(L)])
    out["g2"] = np.stack([fm(inp["norm2_g"][l], 8) for l in range(L)])
    out["bgate"] = np.stack([fm(inp["b_gate"][l], 16) for l in range(L)])
    qkg = np.zeros((L, 128, 2), f)
    for l in range(L):
        qkg[l, :, 0] = np.tile(np.asarray(inp["q_norm_g"][l], f), 2)
        qkg[l, :, 1] = np.tile(np.asarray(inp["k_norm_g"][l], f), 2)
    out["qkg"] = qkg
    out["lamv"] = np.ascontiguousarray(np.broadcast_to(np.asarray(inp["lam_vecs"], f)[:L].reshape(L, 1, 256), (L, 128, 256)))
    out["subg"] = np.ascontiguousarray(np.broadcast_to(np.asarray(inp["subln_g"], f)[:L].reshape(L, 1, 128), (L, 128, 128)))
    cw = np.asarray(inp["conv_w"], f)[:L]
    convw = np.zeros((L, 128, 8, 4), f)
    for l in range(L):
        for j in range(4):
            convw[l, :, :, j] = fm(cw[l, j], 8)
    out["convw"] = convw.reshape(L, 128, 32)
    out["convb"] = np.stack([fm(inp["conv_b"][l], 8) for l in range(L)])
    rb = np.asarray(inp["rg_b"], f)[:L]
    rgb = np.zeros((L, 128, 2, 2, 8), f)
    for l in range(L):
        for d in range(2):
            for g in range(2):
                rgb[l, :, d, g, :] = fm(rb[l, d, g], 8)
    out["rgb"] = rgb.reshape(L, 128, 32)
    rL = np.asarray(inp["rg_L"], f)[:L]
    rgL = np.zeros((L, 128, 2, 8), f)
    for l in range(L):
        for d in range(2):
            rgL[l, :, d, :] = fm(rL[l, d], 8)
    out["rgL"] = rgL.reshape(L, 128, 16)
    rw = np.asarray(inp["rg_w"], f)[:L]
    rgw = np.zeros((L, 2, 2, 8, 128, 128), f)
    for c in range(8):
        rgw[:, :, :, c, 0:64, 0:64] = rw[:, :, :, 2 * c]
        rgw[:, :, :, c, 64:128, 64:128] = rw[:, :, :, 2 * c + 1]
    out["rgw"] = rgw.reshape(L, 32, 128, 128)
    out["ident"] = np.eye(128, dtype=f)
    obd = np.zeros((128, 128), f)
    obd[0:64, 0:64] = 1.0 / 64.0
    obd[64:128, 64:128] = 1.0 / 64.0
    out["onesbd"] = obd
    out["onesfull"] = np.full((128, 128), 1.0 / 1024.0, f)
    pm = np.zeros((128, 128), f)
    cosT = np.ones((128, S_MAX), f)
    sinT = np.zeros((128, S_MAX), f)
    pos = np.arange(S_MAX, dtype=f)
    inv_freq = (np.float32(500000.0) ** (-np.arange(0, 16, 2, dtype=f) / np.float32(16))).astype(f)
    ang = (pos[:, None] * inv_freq[None, :]).astype(f)
    cs = np.cos(ang).astype(f).T
    sn = np.sin(ang).astype(f).T
    for gb in (0, 64):
        for m in range(8):
            pm[gb + m + 8, gb + m] = 1.0
            pm[gb + m, gb + m + 8] = 1.0
            cosT[gb + m] = cs[m]
            cosT[gb + m + 8] = cs[m]
            sinT[gb + m] = -sn[m]
            sinT[gb + m + 8] = sn[m]
    out["perm"] = pm
    out["cosT"] = cosT
    out["sinT"] = sinT
    return out


_CACHE = {}


def run(inputs, L=4, n_cores=8):
    xp = np.asarray(inputs["x_prompt"], np.float32)
    xs = np.asarray(inputs["x_sample"], np.float32)
    SP, SS = xp.shape[1], xs.shape[1]
    keyc = (SP, SS, L)
    if keyc not in _CACHE:
        _CACHE[keyc] = build_program(SP, SS, L)
    nc, stats = _CACHE[keyc]
    lay = _host_layout(inputs, L, max(SP, SS))
    shared = {
        "w_in": np.ascontiguousarray(np.asarray(inputs["w_in"], np.float32)[:L]),
        "wba": np.ascontiguousarray(np.asarray(inputs["w_branch_att"], np.float32)[:L]),
        "wbr": np.ascontiguousarray(np.asarray(inputs["w_branch_rec"], np.float32)[:L]),
        "wout": np.ascontiguousarray(np.asarray(inputs["w_out"], np.float32)[:L]),
        "w1": np.ascontiguousarray(np.asarray(inputs["w_ff1"], np.float32)[:L]),
        "w2": np.ascontiguousarray(np.asarray(inputs["w_ff2"], np.float32)[:L]),
    }
    shared.update(lay)
    in_maps = []
    for i in range(n_cores):
        m = dict(shared)
        m["xp"] = np.ascontiguousarray(xp[i])
        m["xs"] = np.ascontiguousarray(xs[i])
        in_maps.append(m)
    res = run_bass_kernel_spmd(nc, in_maps, core_ids=list(range(n_cores)))
    if DEBUG:
        _CACHE["dbg"] = res.results
    yp = np.stack([np.asarray(r["yp"], np.float32) for r in res.results])
    ys = np.stack([np.asarray(r["ys"], np.float32) for r in res.results])
    return yp, ys


def kernel(**inputs):
    return run(inputs, L=4, n_cores=8)
```

```python
import math
import numpy as np
import concourse.bass as bass
import concourse.mybir as mybir
from concourse.bass_utils import run_bass_kernel_spmd
from concourse.ap import AP

F32 = mybir.dt.float32
BF16 = mybir.dt.bfloat16
ALU = mybir.AluOpType
AF = mybir.ActivationFunctionType
AX = mybir.AxisListType

D = 1024
DC = 8
D_IN = 7168
D_FF = 4096
EPS = 1e-6
NT_W = 36
TT = 512
SEM_LIMIT = 30000
SAME_ENG_SYNC = True
Q_IO = "pool"
Q_W = "sp"
NW = 4
DEBUG = False


class Slot:
    __slots__ = ("cnt", "sem")

    def __init__(self):
        self.cnt = 0
        self.sem = None


class Buf:
    __slots__ = ("name", "w", "r", "slot", "glob", "nofence")

    def __init__(self, name, glob=False, nofence=False):
        self.name = name
        self.w = None
        self.r = {}
        self.slot = None
        self.glob = glob
        self.nofence = nofence


class Op:
    __slots__ = ("eng", "fn", "deps", "need_inc", "idx", "dma_sem", "ev")


class Sched:
    ENGS = ("pe", "act", "dve", "pool", "sp")

    def __init__(self):
        self.ops = {e: [] for e in self.ENGS}
        self.dirty = {}
        self.slots = []
        self.pools = {}
        self.cur_pool = None
        self.cur_idx = 0

    def set_pool(self, name):
        self.cur_pool = name
        self.cur_idx = 0

    def get_slot(self, buf):
        if buf.slot is None:
            if buf.glob or self.cur_pool is None:
                sl = Slot()
                self.slots.append(sl)
                buf.slot = sl
            else:
                pool = self.pools.setdefault(self.cur_pool, [])
                if self.cur_idx >= len(pool):
                    sl = Slot()
                    pool.append(sl)
                    self.slots.append(sl)
                buf.slot = pool[self.cur_idx]
                self.cur_idx += 1
        return buf.slot

    def add(self, eng, fn, reads=(), writes=(), dma=None, extra_deps=()):
        op = Op()
        op.eng = eng
        op.fn = fn
        op.need_inc = False
        op.idx = None
        op.dma_sem = None
        deps = list(extra_deps)
        war = []
        for b in reads:
            if b.w is not None:
                deps.append(b.w)
        for b in writes:
            if b.w is not None:
                deps.append(b.w)
            for rv in b.r.values():
                if rv[0] == "eng" and rv[1].eng == eng:
                    continue
                deps.append(rv)
        if dma is not None:
            sl = self.get_slot(dma)
            sl.cnt += 16
            ev = ("dma", sl, sl.cnt)
            op.dma_sem = sl
            if not dma.nofence:
                self.dirty[id(sl)] = ev
            rkey = ("dma", id(sl))
        else:
            ev = ("eng", op)
            rkey = ("eng", eng)
        op.ev = ev
        for d in deps:
            if d[0] == "eng":
                d[1].need_inc = True
        op.deps = deps
        for b in reads:
            b.r[rkey] = ev
        for b in writes:
            b.w = ev
            b.r = {}
        self.ops[eng].append(op)
        return op

    def fence(self):
        deps = []
        for e in self.ENGS:
            if e == "sp":
                continue
            for op in reversed(self.ops[e]):
                if op.dma_sem is None:
                    deps.append(op.ev)
                    break
        deps.extend(self.dirty.values())
        self.dirty = {}
        f = self.add("sp", lambda e: e.nop(), extra_deps=deps)
        for e in self.ENGS:
            if e != "sp":
                self.add(e, lambda en: en.nop(), extra_deps=[f.ev])
        return f


def build_program(SP, SS, L, n_layers_lam_off=0):
    nc = bass.Bass("TRN2", target_bir_lowering=False)
    S_MAX = max(SP, SS)
    sd = Sched()

    def din(name, shape, dt=F32):
        return nc.dram_tensor(name, list(shape), dt, kind="ExternalInput").ap()

    def dscr(name, shape, dt):
        if DEBUG and name != "wsc":
            return nc.dram_tensor(name, list(shape), dt, kind="ExternalOutput").ap()
        return nc.dram_tensor(name, list(shape), dt, kind="Internal").ap()

    x_in = {"p": din("xp", [SP, D]), "s": din("xs", [SS, D])}
    y_out = {"p": nc.dram_tensor("yp", [SP, D], F32, kind="ExternalOutput").ap(),
             "s": nc.dram_tensor("ys", [SS, D], F32, kind="ExternalOutput").ap()}
    w_in = din("w_in", [L, D, D_IN])
    wba = din("wba", [L, D, D])
    wbr = din("wbr", [L, D, D])
    wout = din("wout", [L, D, D])
    w1 = din("w1", [L, D, D_FF])
    w2 = din("w2", [L, D_FF, D])
    g1_d = din("g1", [L, 128, 8])
    g2_d = din("g2", [L, 128, 8])
    bgate_d = din("bgate", [L, 128, 16])
    qkg_d = din("qkg", [L, 128, 2])
    lamv_d = din("lamv", [L, 128, 256])
    subg_d = din("subg", [L, 128, 128])
    convw_d = din("convw", [L, 128, 32])
    convb_d = din("convb", [L, 128, 8])
    rgb_d = din("rgb", [L, 128, 32])
    rgL_d = din("rgL", [L, 128, 16])
    rgw_d = din("rgw", [L, 32, 128, 128])
    ident_d = din("ident", [128, 128])
    onesbd_d = din("onesbd", [128, 128])
    onesfull_d = din("onesfull", [128, 128])
    perm_d = din("perm", [128, 128])
    cos_d = din("cosT", [128, S_MAX])
    sin_d = din("sinT", [128, S_MAX])

    wsc = dscr("wsc", [L, NT_W, 128, 4096], BF16)
    xT = dscr("xT", [D, S_MAX], F32)
    qT = dscr("qT", [D, S_MAX], BF16)
    kT = dscr("kT", [D, S_MAX], BF16)
    Vs = dscr("Vs", [S_MAX, D], BF16)
    xrT = dscr("xrT", [D, S_MAX], F32)
    gyT = dscr("gyT", [D, S_MAX], BF16)
    thT = dscr("thT", [2 * D, S_MAX], BF16)
    attT = dscr("attT", [D, S_MAX], BF16)
    recT = dscr("recT", [D, S_MAX], BF16)

    ARENA_BYTES = 158 * 1024
    arena = nc.alloc_sbuf_tensor("arena", [128, ARENA_BYTES // 4], F32)
    arena_ap = arena[:] if not isinstance(arena, AP) else arena
    psum = nc.alloc_psum_tensor("psum", [128, 8, 512], F32)
    psum_ap = psum[:] if not isinstance(psum, AP) else psum

    class Arena:
        def __init__(self):
            self.off = 0

        def reset(self):
            self.off = 0

        def alloc(self, shape, dt):
            n = 1
            for s in shape:
                n *= s
            nbytes = n * (4 if dt == F32 else 2)
            nbytes = (nbytes + 31) // 32 * 32
            assert self.off + nbytes <= ARENA_BYTES, ("arena overflow", self.off, nbytes)
            a = arena_ap[:, self.off // 4:(self.off + nbytes) // 4]
            self.off += nbytes
            if dt != F32:
                a = a.bitcast(dt)
            a = a[:, 0:n]
            if len(shape) == 2:
                a = a.rearrange("p (a b) -> p a b", b=shape[1])
            elif len(shape) == 3:
                a = a.rearrange("p (a b c) -> p a b c", b=shape[1], c=shape[2])
            return a

    ar = Arena()

    def sb(name, shape, dt):
        t = nc.alloc_sbuf_tensor(name, [128] + list(shape), dt)
        return t[:] if not isinstance(t, AP) else t

    ident_f = sb("ident_f", [128], F32)
    identb = sb("identb", [128], BF16)
    onesbd = sb("onesbd_s", [128], BF16)
    onesfull = sb("onesfull_s", [128], BF16)
    perm = sb("perm_s", [128], BF16)
    const_b = Buf("consts", glob=True)
    g1 = sb("g1_s", [8], F32)
    g2 = sb("g2_s", [8], F32)
    bgh = sb("bgh_s", [16], F32)
    qkg = sb("qkg_s", [2], F32)
    lamv = sb("lamv_s", [256], F32)
    lamw = sb("lamw_s", [128], F32)
    lams = sb("lams_s", [8], F32)
    subg = sb("subg_s", [128], F32)
    convw = sb("convw_s", [32], F32)
    convb = sb("convb_s", [8], F32)
    rgbh = sb("rgbh_s", [32], F32)
    rgL = sb("rgL_s", [16], F32)
    cLh = sb("cLh_s", [16], F32)
    cL2 = sb("cL2_s", [16], F32)
    rgw = sb("rgw_s", [32, 128], BF16)
    par_b = Buf("params", glob=True)
    wring = [sb("wring%d" % i, [4096], BF16) for i in range(NW)]
    wring_b = [Buf("wring%d" % i, glob=True) for i in range(NW)]
    ps_b = [Buf("ps%d" % i) for i in range(8)]

    def ps(i):
        return psum_ap[:, i, :]

    def mm(out, lhsT, rhs, start, stop, reads, writes, skip=False):
        if skip:
            return sd.add("pe", lambda e: e.matmul(out, lhsT, rhs, start=start, stop=stop, skip_group_check=True), reads, writes)
        return sd.add("pe", lambda e: e.matmul(out, lhsT, rhs, start=start, stop=stop), reads, writes)

    def tr(out, in_, ident, reads, writes):
        return sd.add("pe", lambda e: e.transpose(out, in_, ident), reads, writes)

    def act(out, in_, func, reads, writes, scale=1.0, bias=0.0, accum=None):
        def f(e):
            if accum is not None:
                return e.activation(out=out, in_=in_, func=func, bias=bias, scale=scale, accum_out=accum)
            return e.activation(out=out, in_=in_, func=func, bias=bias, scale=scale)
        return sd.add("act", f, reads, writes)

    def stt(eng, out, in0, scalar, in1, op0, op1, reads, writes):
        return sd.add(eng, lambda e: e.scalar_tensor_tensor(out, in0, scalar, in1, op0, op1), reads, writes)

    def ts(eng, out, in0, s1, s2, op0, op1, reads, writes):
        if s2 is None:
            return sd.add(eng, lambda e: e.tensor_scalar(out, in0, s1, None, op0), reads, writes)
        return sd.add(eng, lambda e: e.tensor_scalar(out, in0, s1, s2, op0, op1), reads, writes)

    def tt(eng, out, in0, in1, op, reads, writes):
        return sd.add(eng, lambda e: e.tensor_tensor(out, in0, in1, op), reads, writes)

    def cp(eng, out, in_, reads, writes):
        if eng == "act":
            return sd.add("act", lambda e: e.copy(out, in_), reads, writes)
        return sd.add(eng, lambda e: e.tensor_copy(out, in_), reads, writes)

    def dma(q, out, in_, owner, reads=(), writes=()):
        return sd.add(q, lambda e: e.dma_start(out=out, in_=in_), reads, writes, dma=owner)

    def memset(eng, ap, val, writes):
        return sd.add(eng, lambda e: e.memset(ap, val), (), writes)

    dbg_owner = Buf("dbg", glob=True)
    cur = {"key": None}

    def dump(name, ap, reads):
        if not DEBUG or cur["key"] != "s":
            return
        shape = list(ap.shape)
        dt_ = nc.dram_tensor("dbg_" + name, shape, ap.dtype, kind="ExternalOutput").ap()
        dma(Q_IO, dt_, ap, dbg_owner, reads, ())

    wstate = {"n": 0}

    def wload(l, t):
        i = wstate["n"] % NW
        wstate["n"] += 1
        dma(Q_W, wring[i], wsc[l, t], wring_b[i], (cast_bs[l],), (wring_b[i],))
        return wring[i], wring_b[i]

    class WStream:
        def __init__(self, tiles):
            self.tiles = tiles
            self.q = []
            self.pos = 0
            for _ in range(min(NW - 1, len(tiles))):
                self._issue()

        def _issue(self):
            if self.pos < len(self.tiles):
                self.q.append(wload(*self.tiles[self.pos]))
                self.pos += 1

        def next(self):
            r = self.q.pop(0)
            return r

        def done_one(self):
            self._issue()

    dma("pool", ident_f, ident_d, const_b, (), (const_b,))
    dma("pool", identb, ident_d, const_b, (), (const_b,))
    dma("pool", onesbd, onesbd_d, const_b, (), (const_b,))
    dma("pool", onesfull, onesfull_d, const_b, (), (const_b,))
    dma("pool", perm, perm_d, const_b, (), (const_b,))
    cast_bs = [Buf("cast%d" % l, glob=True, nofence=True) for l in range(L)]

    def cast_layer(l, lazy=False):
        cast_b = cast_bs[l]

        def wview(t):
            return wsc[l, t].rearrange("p (c n) -> p c n", n=512)
        srcs = []
        win_v = w_in[l].rearrange("(c p) n -> p c n", p=128)
        for g in range(14):
            srcs.append(win_v[:, :, g * 512:(g + 1) * 512])
        a_v = wba[l].rearrange("(c p) n -> p c n", p=128)
        r_v = wbr[l].rearrange("(c p) n -> p c n", p=128)
        o_v = wout[l].rearrange("(c p) n -> p c n", p=128)
        srcs += [a_v[:, :, 0:512], r_v[:, :, 0:512], a_v[:, :, 512:1024], r_v[:, :, 512:1024],
                 o_v[:, :, 0:512], o_v[:, :, 512:1024]]
        w1_v = w1[l].rearrange("(c p) n -> p c n", p=128)
        for g in range(8):
            srcs.append(w1_v[:, :, g * 512:(g + 1) * 512])
        jobs = []
        for t, s in enumerate(srcs):
            jobs.append((lambda t=t, s=s: dma("pool", wview(t), s, cast_b, (), (cast_b,))))
        w2_v = w2[l].rearrange("(k p) n -> p k n", p=128)
        for oc in range(8):
            jobs.append((lambda oc=oc: dma("pool", wsc[l, 28 + oc].rearrange("p (k n) -> p k n", n=128),
                                           w2_v[:, :, oc * 128:(oc + 1) * 128], cast_b, (), (cast_b,))))
        if lazy:
            return jobs
        for j in jobs:
            j()
        return []

    cast_layer(0)
    sd.fence()

    def layer_params(l):
        lam_init = 0.8 - 0.6 * math.exp(-0.3 * l)
        pb = par_b
        for dst, src in ((g1, g1_d), (g2, g2_d), (bgh, bgate_d), (qkg, qkg_d), (lamv, lamv_d), (subg, subg_d),
                         (convw, convw_d), (convb, convb_d), (rgbh, rgb_d), (rgL, rgL_d)):
            dma("sp", dst, src[l], pb, (), (pb,))
        dma("pool", rgw, rgw_d[l].rearrange("t p n -> p t n"), pb, (), (pb,))
        ts("dve", bgh, bgh, 0.5, None, ALU.mult, None, (pb,), (pb,))
        ts("dve", rgbh, rgbh, 0.5, None, ALU.mult, None, (pb,), (pb,))
        ts("dve", subg, subg, 1.0 - lam_init, None, ALU.mult, None, (pb,), (pb,))
        tt("dve", lamw[:, 0:64], lamv[:, 0:64], lamv[:, 64:128], ALU.mult, (pb,), (pb,))
        tt("dve", lamw[:, 64:128], lamv[:, 128:192], lamv[:, 192:256], ALU.mult, (pb,), (pb,))
        sd.add("dve", lambda e: e.reduce_sum(lams[:, 0:1], lamw[:, 0:64], AX.X), (pb,), (pb,))
        sd.add("dve", lambda e: e.reduce_sum(lams[:, 1:2], lamw[:, 64:128], AX.X), (pb,), (pb,))
        act(lams[:, 2:4], lams[:, 0:2], AF.Exp, (pb,), (pb,))
        tt("dve", lams[:, 4:5], lams[:, 2:3], lams[:, 3:4], ALU.subtract, (pb,), (pb,))
        ts("dve", lams[:, 5:6], lams[:, 4:5], lam_init, -1.0, ALU.add, ALU.mult, (pb,), (pb,))
        act(cLh, rgL, AF.Exp, (pb,), (pb,), scale=-1.0)
        act(cLh, cLh, AF.Ln, (pb,), (pb,), bias=1.0)
        ts("dve", cLh, cLh, -4.0, None, ALU.mult, None, (pb,), (pb,))
        ts("dve", cL2, cLh, 2.0, None, ALU.mult, None, (pb,), (pb,))
        sd.fence()

    def phase0(key, S):
        ar.reset()
        sd.set_pool("P0")
        xin = [ar.alloc([4, D], F32) for _ in range(2)]
        xin_b = [Buf("xin0"), Buf("xin1")]
        xt = [ar.alloc([8, TT], F32) for _ in range(2)]
        xt_b = [Buf("xt0"), Buf("xt1")]
        xsrc = x_in[key]
        xTv = xT.rearrange("(c p) s -> p c s", p=128)
        k = 0
        for ti in range(S // TT):
            t0 = ti * TT
            i = ti % 2
            dma(Q_IO, xin[i], xsrc[t0:t0 + TT, :].rearrange("(tb p) d -> p tb d", p=128), xin_b[i], (), (xin_b[i],))
            for c in range(8):
                b = k % 8
                k += 1
                for tb in range(4):
                    tr(ps(b)[:, tb * 128:(tb + 1) * 128], xin[i][:, tb, c * 128:(c + 1) * 128], ident_f,
                       (xin_b[i], const_b), (ps_b[b],))
                cp("act" if c % 2 else "dve", xt[i][:, c, :], ps(b), (ps_b[b],), (xt_b[i],))
            dma(Q_IO, xTv[:, :, t0:t0 + TT], xt[i], xt_b[i], (xt_b[i],), ())
        sd.fence()

    def phaseA(l, S):
        ar.reset()
        sd.set_pool("A")
        xt = ar.alloc([8, TT], F32); xt_b = Buf("A.xt")
        cs = ar.alloc([2, TT], F32); cs_b = Buf("A.cs")
        sq = ar.alloc([8, TT], BF16); sq_b = Buf("A.sq")
        nT2 = [ar.alloc([8, TT], BF16) for _ in range(2)]; nT2_b = [Buf("A.nT0"), Buf("A.nT1")]
        rstd = ar.alloc([TT], F32); rstd_b = Buf("A.rstd")
        NB = 5
        sq2 = [ar.alloc([TT], BF16) for _ in range(NB)]; sq2_b = [Buf("A.sq2") for _ in range(NB)]
        r2 = [ar.alloc([TT], F32) for _ in range(NB)]; r2_b = [Buf("A.r2") for _ in range(NB)]
        qn = [ar.alloc([TT], F32) for _ in range(2)]; qn_b = [Buf("A.qn") for _ in range(2)]
        qnb = [ar.alloc([TT], BF16) for _ in range(NB)]; qnb_b = [Buf("A.qnb") for _ in range(NB)]
        t1 = [ar.alloc([TT], F32) for _ in range(NB)]; t1_b = [Buf("A.t1") for _ in range(NB)]
        tmp = [ar.alloc([TT], F32) for _ in range(NB)]; tmp_b = [Buf("A.tmp") for _ in range(NB)]
        qks = [ar.alloc([8, TT], BF16) for _ in range(2)]; qks_b = [Buf("A.qs"), Buf("A.ks")]
        vst = ar.alloc([4, D], BF16); vst_b = Buf("A.vs")
        xrs = ar.alloc([8, TT], F32); xrs_b = Buf("A.xrs")
        gys = ar.alloc([8, TT], BF16); gys_b = Buf("A.gys")
        ths = ar.alloc([16, TT], BF16); ths_b = Buf("A.ths")
        xTv = xT.rearrange("(c p) s -> p c s", p=128)
        qTv = [qT.rearrange("(c p) s -> p c s", p=128), kT.rearrange("(c p) s -> p c s", p=128)]
        xrTv = xrT.rearrange("(c p) s -> p c s", p=128)
        gyTv = gyT.rearrange("(c p) s -> p c s", p=128)
        thTv = thT.rearrange("(c p) s -> p c s", p=128)
        tiles = []
        for ti in range(S // TT):
            tiles += [(l, g) for g in range(14)]
        ws = WStream(tiles)
        kb = [0]

        def nb():
            b = kb[0] % 8
            kb[0] += 1
            return b
        kk = 0
        qk_pipe = []

        def qk_stage2(cx):
            b, i, which = cx["b"], cx["i"], cx["which"]
            b2 = nb()
            mm(ps(b2), onesbd, sq2[i], True, True, (sq2_b[i], const_b), (ps_b[b2],))
            act(r2[i], ps(b2), AF.Ln, (ps_b[b2],), (r2_b[i],), bias=EPS)
            act(r2[i], r2[i], AF.Exp, (r2_b[i],), (r2_b[i],), scale=-0.5)
            stt("dve", qnb[i], ps(b), qkg[:, which:which + 1], r2[i], ALU.mult, ALU.mult,
                (ps_b[b], r2_b[i], par_b), (qnb_b[i],))
            tt("pool", t1[i], qnb[i], cs[:, 0, :], ALU.mult, (qnb_b[i], cs_b), (t1_b[i],))
            cx["st"] = 2

        def qk_stage3(cx):
            i, which, h = cx["i"], cx["which"], cx["h"]
            b3 = nb()
            mm(ps(b3), perm, qnb[i], True, True, (qnb_b[i], const_b), (ps_b[b3],))
            tt("dve", tmp[i], ps(b3), cs[:, 1, :], ALU.mult, (ps_b[b3], cs_b), (tmp_b[i],))
            tt("dve", qks[which][:, h, :], t1[i], tmp[i], ALU.add, (t1_b[i], tmp_b[i]), (qks_b[which],))
            cx["st"] = 3

        def qk_advance(flush):
            while True:
                n = len(qk_pipe)
                if n >= 4 or (flush and n >= 1 and qk_pipe[0]["st"] == 2):
                    qk_stage3(qk_pipe.pop(0))
                    continue
                break
            for cx in qk_pipe:
                if cx["st"] == 1 and (flush or cx is not qk_pipe[-1]):
                    qk_stage2(cx)
            if flush:
                while qk_pipe:
                    cx = qk_pipe.pop(0)
                    if cx["st"] == 1:
                        qk_stage2(cx)
                    qk_stage3(cx)

        def prologue1(ti):
            t0 = ti * TT
            dma(Q_IO, xt, xTv[:, :, t0:t0 + TT], xt_b, (), (xt_b,))
            dma(Q_IO, cs[:, 0, :], cos_d[:, t0:t0 + TT], cs_b, (), (cs_b,))
            dma(Q_IO, cs[:, 1, :], sin_d[:, t0:t0 + TT], cs_b, (), (cs_b,))
            act(sq, xt, AF.Square, (xt_b,), (sq_b,))

        def prologue(ti):
            nT, nT_b = nT2[ti % 2], nT2_b[ti % 2]
            b = nb()
            for c in range(8):
                mm(ps(b), onesfull, sq[:, c, :], c == 0, c == 7, (sq_b, const_b), (ps_b[b],))
            act(rstd, ps(b), AF.Ln, (ps_b[b],), (rstd_b,), bias=EPS)
            act(rstd, rstd, AF.Exp, (rstd_b,), (rstd_b,), scale=-0.5)
            for c in range(8):
                stt("dve", nT[:, c, :], xt[:, c, :], g1[:, c:c + 1], rstd, ALU.mult, ALU.mult,
                    (xt_b, rstd_b, par_b), (nT_b,))

        prologue1(0)
        prologue(0)
        for ti in range(S // TT):
            t0 = ti * TT
            nT, nT_b = nT2[ti % 2], nT2_b[ti % 2]
            for g in range(14):
                wt, wt_b = ws.next()
                wv = wt.rearrange("p (c n) -> p c n", n=512)
                if g < 4:
                    which = g // 2
                    for j in range(4):
                        h = (g % 2) * 4 + j
                        b = nb()
                        for c in range(8):
                            mm(ps(b), wv[:, c, j * 128:(j + 1) * 128], nT[:, c, :], c == 0, c == 7,
                               (nT_b, wt_b), (ps_b[b],))
                        i = kk % NB
                        kk += 1
                        act(sq2[i], ps(b), AF.Square, (ps_b[b],), (sq2_b[i],))
                        qk_pipe.append({"b": b, "i": i, "which": which, "h": h, "st": 1})
                        qk_advance(False)
                    if g == 3:
                        qk_advance(True)
                elif g < 6:
                    half = g - 4
                    for tb in range(4):
                        b = nb()
                        for c in range(8):
                            mm(ps(b), nT[:, c, tb * 128:(tb + 1) * 128], wv[:, c, :], c == 0, c == 7,
                               (nT_b, wt_b), (ps_b[b],))
                        cp("act", vst[:, tb, half * 512:(half + 1) * 512], ps(b), (ps_b[b],), (vst_b,))
                elif g < 8:
                    for j in range(4):
                        cc = (g - 6) * 4 + j
                        b = nb()
                        for c in range(8):
                            mm(ps(b), wv[:, c, j * 128:(j + 1) * 128], nT[:, c, :], c == 0, c == 7,
                               (nT_b, wt_b), (ps_b[b],))
                        cp("act" if j % 2 else "dve", xrs[:, cc, :], ps(b), (ps_b[b],), (xrs_b,))
                elif g < 10:
                    for j in range(4):
                        cc = (g - 8) * 4 + j
                        b = nb()
                        for c in range(8):
                            mm(ps(b), wv[:, c, j * 128:(j + 1) * 128], nT[:, c, :], c == 0, c == 7,
                               (nT_b, wt_b), (ps_b[b],))
                        i = kk % NB
                        kk += 1
                        act(r2[i], ps(b), AF.Square, (ps_b[b],), (r2_b[i],))
                        ts("dve", r2[i], r2[i], 0.044715, 1.0, ALU.mult, ALU.add, (r2_b[i],), (r2_b[i],))
                        tt("dve", qn[i % 2], r2[i], ps(b), ALU.mult, (r2_b[i], ps_b[b]), (qn_b[i % 2],))
                        act(t1[i], qn[i % 2], AF.Tanh, (qn_b[i % 2],), (t1_b[i],), scale=0.7978845608028654)
                        stt("dve", gys[:, cc, :], t1[i], 1.0, ps(b), ALU.add, ALU.mult, (t1_b[i], ps_b[b]), (gys_b,))
                else:
                    for j in range(4):
                        gi = (g - 10) * 4 + j
                        b = nb()
                        for c in range(8):
                            mm(ps(b), wv[:, c, j * 128:(j + 1) * 128], nT[:, c, :], c == 0, c == 7,
                               (nT_b, wt_b), (ps_b[b],))
                        act(ths[:, gi, :], ps(b), AF.Tanh, (ps_b[b], par_b), (ths_b,), scale=0.5, bias=bgh[:, gi:gi + 1])
                ws.done_one()
                if g == 3:
                    dma(Q_IO, qTv[0][:, :, t0:t0 + TT], qks[0], qks_b[0], (qks_b[0],), ())
                    dma(Q_IO, qTv[1][:, :, t0:t0 + TT], qks[1], qks_b[1], (qks_b[1],), ())
                    if ti + 1 < S // TT:
                        prologue1(ti + 1)
                elif g == 5:
                    dma(Q_IO, Vs[t0:t0 + TT, :].rearrange("(tb p) d -> p tb d", p=128), vst, vst_b, (vst_b,), ())
                elif g == 7:
                    dma(Q_IO, xrTv[:, :, t0:t0 + TT], xrs, xrs_b, (xrs_b,), ())
                elif g == 9:
                    dma(Q_IO, gyTv[:, :, t0:t0 + TT], gys, gys_b, (gys_b,), ())
                elif g == 11:
                    if ti + 1 < S // TT:
                        prologue(ti + 1)
                elif g == 13:
                    dma(Q_IO, thTv[:, 0:8, t0:t0 + TT], ths[:, 0:8, :], ths_b, (ths_b,), ())
                    dma(Q_IO, thTv[:, 8:16, t0:t0 + TT], ths[:, 8:16, :], ths_b, (ths_b,), ())
        sd.fence()

    def phaseB1(l, S, cast_jobs=()):
        ar.reset()
        sd.set_pool("B1")
        NKB = S // 128
        NQB = S // TT
        qh = [ar.alloc([S], BF16) for _ in range(2)]
        kh = [ar.alloc([S], BF16) for _ in range(2)]
        Vh = [ar.alloc([NKB, 130], BF16) for _ in range(2)]
        qkv_b = [Buf("B1.qkv0"), Buf("B1.qkv1")]
        NE = 4
        Eb = [[ar.alloc([TT], BF16) for _ in range(2)] for _ in range(NE)]
        Eb_b = [[Buf("B1.E") for _ in range(2)] for _ in range(NE)]
        rden = ar.alloc([16], F32); rden_b = Buf("B1.rden")
        t0b = [ar.alloc([128], F32) for _ in range(2)]; t0b_b = [Buf("B1.t0") for _ in range(2)]
        attf = [ar.alloc([128], F32) for _ in range(4)]; attf_b = [Buf("B1.attf") for _ in range(4)]
        junk = ar.alloc([128], F32); junk_b = Buf("B1.junk")
        ssq = ar.alloc([8], F32); ssq_b = Buf("B1.ssq")
        attb = [ar.alloc([128], BF16) for _ in range(4)]; attb_b = [Buf("B1.attb") for _ in range(4)]
        attTs = [ar.alloc([TT], BF16) for _ in range(2)]; attTs_b = [Buf("B1.attTs0"), Buf("B1.attTs1")]
        psT = psum_ap[:, 7, :].bitcast(BF16)
        dbgO = ar.alloc([3, 512], F32); dbgO_b = Buf("dbgO")
        Oc = ar.alloc([3, 512], F32); Oc_b = [Buf("B1.Oc%d" % i_) for i_ in range(3)]
        for i in range(2):
            memset("dve", Vh[i][:, :, 128:129], 1.0, (qkv_b[i],))
        sbank = 0
        ei = 0
        nst = 0
        cast_jobs = list(cast_jobs)
        pendP2 = [None]
        pendP3 = [None]

        def make_post(h, qb, q0):
            def P1():
                for ob_ in range(3):
                    ncol = 387 if ob_ < 2 else 258
                    cp("dve", Oc[:, ob_, 0:ncol], ps(4 + ob_)[:, 0:ncol], (ps_b[4 + ob_],), (Oc_b[ob_],))
                for s_ in range(8):
                    ob = s_ // 3
                    oc = (s_ % 3) * 129
                    sd.add("dve", (lambda ob=ob, oc=oc, s_=s_: (lambda e: e.reciprocal(rden[:, s_:s_ + 1], Oc[:, ob, oc + 128:oc + 129])))(),
                           (Oc_b[ob],), (rden_b,))
                ts("dve", rden[:, 8:12], rden[:, 4:8], lams[:, 5:6], None, ALU.mult, None, (rden_b, par_b), (rden_b,))
                memset("dve", ssq[:, 0:4], 0.0, (ssq_b,))
                for j in range(4):
                    s0 = j
                    s1 = 4 + j
                    ti_ = j % 2
                    ts("dve", t0b[ti_], Oc[:, s0 // 3, (s0 % 3) * 129:(s0 % 3) * 129 + 128], rden[:, j:j + 1], None,
                       ALU.mult, None, (Oc_b[s0 // 3], rden_b), (t0b_b[ti_],))
                    stt("dve", attf[j], Oc[:, s1 // 3, (s1 % 3) * 129:(s1 % 3) * 129 + 128], rden[:, 8 + j:9 + j], t0b[ti_],
                        ALU.mult, ALU.add, (Oc_b[s1 // 3], rden_b, t0b_b[ti_]), (attf_b[j],))

            def P2():
                for j in range(4):
                    act(junk, attf[j], AF.Square, (attf_b[j],), (junk_b, ssq_b), accum=ssq[:, j:j + 1])
                act(ssq[:, 4:8], ssq[:, 0:4], AF.Ln, (ssq_b,), (ssq_b,), scale=1.0 / 128.0, bias=EPS)
                act(ssq[:, 4:8], ssq[:, 4:8], AF.Exp, (ssq_b,), (ssq_b,), scale=-0.5)

            def P3():
                ai = (h * NQB + qb) % 2
                for j in range(4):
                    stt("dve", attb[j], attf[j], ssq[:, 4 + j:5 + j], subg, ALU.mult, ALU.mult,
                        (attf_b[j], ssq_b, par_b), (attb_b[j],))
                    tr(psT[:, j * 128:(j + 1) * 128], attb[j], identb, (attb_b[j], const_b), (ps_b[7],))
                cp("dve", attTs[ai], psT[:, 0:TT], (ps_b[7],), (attTs_b[ai],))
                dma(Q_IO, attT[h * 128:(h + 1) * 128, q0:q0 + TT], attTs[ai], attTs_b[ai], (attTs_b[ai],), ())
            return P1, P2, P3

        def flush_post():
            if pendP2[0] is not None:
                pendP2[0]()
                pendP2[0] = None
            if pendP3[0] is not None:
                pendP3[0]()
                pendP3[0] = None

        for h in range(8):
            i = h % 2
            dma(Q_IO, qh[i], qT[h * 128:(h + 1) * 128, 0:S], qkv_b[i], (), (qkv_b[i],))
            dma(Q_IO, kh[i], kT[h * 128:(h + 1) * 128, 0:S], qkv_b[i], (), (qkv_b[i],))
            for k4 in range(0, NKB, 8):
                ke = min(NKB, k4 + 8)
                dma(Q_IO, Vh[i][:, k4:ke, 0:128],
                    Vs[k4 * 128:ke * 128, h * 128:(h + 1) * 128].rearrange("(kb p) e -> p kb e", p=128),
                    qkv_b[i], (), (qkv_b[i],))
            for _ in range(5):
                if cast_jobs:
                    cast_jobs.pop(0)()
            for qb in range(NQB):
                q0 = qb * TT
                fifo = []
                for kc in range(NKB + 2):
                    if kc < NKB:
                        b0 = sbank * 2
                        sbank = (sbank + 1) % 2
                        e = ei % NE
                        ei += 1
                        mm(ps(b0), kh[i][0:64, kc * 128:(kc + 1) * 128], qh[i][0:64, q0:q0 + TT], True, True,
                           (qkv_b[i],), (ps_b[b0],))
                        mm(ps(b0 + 1), kh[i][64:128, kc * 128:(kc + 1) * 128], qh[i][64:128, q0:q0 + TT], True, True,
                           (qkv_b[i],), (ps_b[b0 + 1],))
                        act(Eb[e][0], ps(b0), AF.Exp, (ps_b[b0],), (Eb_b[e][0],), scale=0.125)
                        act(Eb[e][1], ps(b0 + 1), AF.Exp, (ps_b[b0 + 1],), (Eb_b[e][1],), scale=0.125)
                        fifo.append((e, kc))
                    if kc == 2 and pendP2[0] is not None:
                        pendP2[0]()
                        pendP2[0] = None
                    if kc == min(6, NKB + 1) and pendP3[0] is not None:
                        pendP3[0]()
                        pendP3[0] = None
                    if kc >= 2:
                        e_, kc_ = fifo.pop(0)
                        for c in range(2):
                            for j in range(4):
                                s = c * 4 + j
                                ob = 4 + s // 3
                                oc = (s % 3) * 129
                                mm(ps(ob)[:, oc:oc + 129], Eb[e_][c][:, j * 128:(j + 1) * 128], Vh[i][:, kc_, 0:129],
                                   kc_ == 0 and s % 3 == 0, kc_ == NKB - 1, (Eb_b[e_][c], qkv_b[i]), (ps_b[ob],), skip=True)
                flush_post()
                P1, P2, P3 = make_post(h, qb, q0)
                P1()
                pendP2[0] = P2
                pendP3[0] = P3
        flush_post()
        for j_ in cast_jobs:
            j_()
        sd.fence()

    def phaseB2(l, S):
        ar.reset()
        sd.set_pool("B2")
        NH = 2 if S >= 1024 else 1
        HS = S // NH
        xpad = ar.alloc([S + 8], F32); xpad_b = Buf("B2.xpad")
        gy = ar.alloc([S], BF16); gy_b = Buf("B2.gy")
        xc = ar.alloc([S], F32); xc_b = Buf("B2.xc")
        xcb = ar.alloc([S], BF16); xcb_b = Buf("B2.xcb")
        A_ = [ar.alloc([S], F32) for _ in range(2)]; A_b = [[Buf("B2.A") for _ in range(NH)] for _ in range(2)]
        B_ = [ar.alloc([S], F32) for _ in range(2)]; B_b = [[Buf("B2.B") for _ in range(NH)] for _ in range(2)]
        T_ = [ar.alloc([S], F32) for _ in range(2)]; T_b = [[Buf("B2.T") for _ in range(NH)] for _ in range(2)]
        rec = ar.alloc([S], BF16); rec_b = Buf("B2.rec")
        memset("dve", xpad[:, 0:2], 0.0, (xpad_b,))
        memset("dve", xpad[:, S + 2:S + 8], 0.0, (xpad_b,))
        kbs = [0]

        def rev(a):
            base = a
            apl = [list(x) for x in base.ap]
            n = apl[-1][1]
            apl[-1] = [-1, n]
            return AP(base.tensor, base.offset + (n - 1), apl)

        def scan_op(out, a0, a1, init):
            return lambda e: e.tensor_tensor_scan(out, a0, a1, init, ALU.mult, ALU.add)
        sl = [slice(hx * HS, (hx + 1) * HS) for hx in range(NH)]

        def dir_steps(c, d):
            ci = d * 8 + c
            order = list(range(NH)) if d == 0 else list(range(NH - 1, -1, -1))
            A, B, T = A_[d], B_[d], T_[d]
            Ab, Bb, Tb = A_b[d], B_b[d], T_b[d]

            def gate(gt, dst, dstb):
                def f():
                    for blk in range(S // TT):
                        hh_ = (blk * TT) // HS
                        b = kbs[0] % 8
                        kbs[0] += 1
                        idx = (d * 2 + gt) * 8 + c
                        mm(ps(b), rgw[:, idx, :], xcb[:, blk * TT:(blk + 1) * TT], True, True, (xcb_b, par_b), (ps_b[b],))
                        act(dst[:, blk * TT:(blk + 1) * TT], ps(b), AF.Tanh, (ps_b[b], par_b), (dstb[hh_],),
                            scale=0.5, bias=rgbh[:, idx:idx + 1])
                return f

            def s2():
                pass

            def s3():
                for hx in order:
                    act(T[:, sl[hx]], A[:, sl[hx]], AF.Tanh, (Ab[hx], par_b), (Tb[hx],),
                        scale=cLh[:, ci:ci + 1], bias=cLh[:, ci:ci + 1])
                    act(B[:, sl[hx]], A[:, sl[hx]], AF.Exp, (Ab[hx], par_b), (Bb[hx],),
                        scale=cL2[:, ci:ci + 1], bias=cL2[:, ci:ci + 1])
                    act(A[:, sl[hx]], A[:, sl[hx]], AF.Exp, (Ab[hx], par_b), (Ab[hx],),
                        scale=cLh[:, ci:ci + 1], bias=cLh[:, ci:ci + 1])

            def s4():
                for hx in order:
                    stt("dve", B[:, sl[hx]], B[:, sl[hx]], 1.0, T[:, sl[hx]], ALU.add, ALU.mult,
                        (Bb[hx], Tb[hx]), (Bb[hx],))

            def s6():
                for hx in order:
                    act(B[:, sl[hx]], B[:, sl[hx]], AF.Sqrt, (Bb[hx],), (Bb[hx],), scale=-1.0)

            def s7():
                for hx in order:
                    stt("dve", B[:, sl[hx]], T[:, sl[hx]], 1.0, B[:, sl[hx]], ALU.add, ALU.mult,
                        (Tb[hx], Bb[hx]), (Bb[hx],))
                    stt("dve", B[:, sl[hx]], B[:, sl[hx]], 0.5, xc[:, sl[hx]], ALU.mult, ALU.mult,
                        (Bb[hx], xc_b), (Bb[hx],))

            def s8():
                prev = None
                for hx in order:
                    if d == 0:
                        init = 0.0 if prev is None else T[:, prev * HS + HS - 1:prev * HS + HS]
                        rd = (Ab[hx], Bb[hx]) + (() if prev is None else (Tb[prev],))
                        sd.add("dve", scan_op(T[:, sl[hx]], A[:, sl[hx]], B[:, sl[hx]], init), rd, (Tb[hx],))
                    else:
                        init = 0.0 if prev is None else T[:, prev * HS:prev * HS + 1]
                        rd = (Ab[hx], Bb[hx]) + (() if prev is None else (Tb[prev],))
                        sd.add("dve", scan_op(rev(T[:, sl[hx]]), rev(A[:, sl[hx]]), rev(B[:, sl[hx]]), init),
                               rd, (Tb[hx],))
                    prev = hx
            return [gate(0, A, Ab), s2, s3, s4, gate(1, T, Tb), s6, s7, s8]

        for c in range(8):
            dma(Q_IO, xpad[:, 2:2 + S], xrT[c * 128:(c + 1) * 128, 0:S], xpad_b, (), (xpad_b,))
            dma(Q_IO, gy, gyT[c * 128:(c + 1) * 128, 0:S], gy_b, (), (gy_b,))
            ts("dve", xc, xpad[:, 0:S], convw[:, c * 4:c * 4 + 1], convb[:, c:c + 1], ALU.mult, ALU.add,
               (xpad_b, par_b), (xc_b,))
            for j in range(1, 4):
                stt("dve", xc, xpad[:, j:j + S], convw[:, c * 4 + j:c * 4 + j + 1], xc, ALU.mult, ALU.add,
                    (xpad_b, xc_b, par_b), (xc_b,))
            cp("act", xcb, xc, (xc_b,), (xcb_b,))
            st0 = dir_steps(c, 0)
            st1 = dir_steps(c, 1)
            for f0, f1 in zip(st0, st1):
                f0()
                f1()
            for hx in range(NH):
                s_ = sl[hx]
                tt("pool", T_[0][:, s_], T_[0][:, s_], T_[1][:, s_], ALU.add, (T_b[0][hx], T_b[1][hx]), (T_b[0][hx],))
                stt("dve", rec[:, s_], T_[0][:, s_], 0.5, gy[:, s_], ALU.mult, ALU.mult, (T_b[0][hx], gy_b), (rec_b,))
            dma(Q_IO, recT[c * 128:(c + 1) * 128, 0:S], rec, rec_b, (rec_b,), ())
        sd.fence()

    def phaseC(l, S, key, last):
        ar.reset()
        sd.set_pool("C")
        x = ar.alloc([8, TT], F32); x_b = Buf("C.x")
        att = ar.alloc([8, TT], BF16); att_b = Buf("C.att")
        rec = ar.alloc([8, TT], BF16); rec_b = Buf("C.rec")
        th = [ar.alloc([8, TT], BF16) for _ in range(2)]; th_b = [Buf("C.th0"), Buf("C.th1")]
        m0 = [ar.alloc([TT], F32) for _ in range(2)]; m0_b = [Buf("C.m0") for _ in range(2)]
        m1 = [ar.alloc([TT], F32) for _ in range(2)]; m1_b = [Buf("C.m1") for _ in range(2)]
        mg = ar.alloc([8, TT], BF16); mg_b = Buf("C.mg")
        sq = ar.alloc([8, TT], BF16); sq_b = Buf("C.sq")
        rstd = ar.alloc([TT], F32); rstd_b = Buf("C.rstd")
        n2 = ar.alloc([8, TT], BF16); n2_b = Buf("C.n2")
        rl = [ar.alloc([TT], F32) for _ in range(3)]; rl_b = [Buf("C.rl") for _ in range(3)]
        hh = ar.alloc([32, TT], BF16); hh_b = Buf("C.h")
        yt = [ar.alloc([D], F32) for _ in range(2)]; yt_b = [Buf("C.yt0"), Buf("C.yt1")]
        xTv = xT.rearrange("(c p) s -> p c s", p=128)
        attTv = attT.rearrange("(c p) s -> p c s", p=128)
        recTv = recT.rearrange("(c p) s -> p c s", p=128)
        thTv = thT.rearrange("(c p) s -> p c s", p=128)
        tiles = []
        for ti in range(S // TT):
            tiles += [(l, t) for t in range(14, 36)]
        ws = WStream(tiles)
        kb = [0]

        def nb():
            b = kb[0] % 8
            kb[0] += 1
            return b
        km = 0
        kr = 0
        ky = 0
        def loadsC(ti):
            t0 = ti * TT
            dma(Q_IO, att, attTv[:, :, t0:t0 + TT], att_b, (), (att_b,))
            dma(Q_IO, rec, recTv[:, :, t0:t0 + TT], rec_b, (), (rec_b,))
            dma(Q_IO, th[0], thTv[:, 0:8, t0:t0 + TT], th_b[0], (), (th_b[0],))
            dma(Q_IO, th[1], thTv[:, 8:16, t0:t0 + TT], th_b[1], (), (th_b[1],))

        loadsC(0)
        for ti in range(S // TT):
            t0 = ti * TT
            dma(Q_IO, x, xTv[:, :, t0:t0 + TT], x_b, (), (x_b,))
            for half in range(2):
                wa, wa_b = ws.next()
                wr, wr_b = ws.next()
                wav = wa.rearrange("p (c n) -> p c n", n=512)
                wrv = wr.rearrange("p (c n) -> p c n", n=512)
                for j in range(4):
                    oc = half * 4 + j
                    bA = nb()
                    for c in range(8):
                        mm(ps(bA), wav[:, c, j * 128:(j + 1) * 128], att[:, c, :], c == 0, c == 7, (att_b, wa_b), (ps_b[bA],))
                    bR = nb()
                    for c in range(8):
                        mm(ps(bR), wrv[:, c, j * 128:(j + 1) * 128], rec[:, c, :], c == 0, c == 7, (rec_b, wr_b), (ps_b[bR],))
                    i = km % 2
                    km += 1
                    stt("dve", m0[i], th[0][:, oc, :], 1.0, ps(bA), ALU.add, ALU.mult, (th_b[0], ps_b[bA]), (m0_b[i],))
                    stt("dve", m1[i], th[1][:, oc, :], 1.0, ps(bR), ALU.add, ALU.mult, (th_b[1], ps_b[bR]), (m1_b[i],))
                    tt("pool", mg[:, oc, :], m0[i], m1[i], ALU.add, (m0_b[i], m1_b[i]), (mg_b,))
                ws.done_one()
                ws.done_one()
            for half in range(2):
                wo, wo_b = ws.next()
                wov = wo.rearrange("p (c n) -> p c n", n=512)
                for j in range(4):
                    oc = half * 4 + j
                    b = nb()
                    for c in range(8):
                        mm(ps(b), wov[:, c, j * 128:(j + 1) * 128], mg[:, c, :], c == 0, c == 7, (mg_b, wo_b), (ps_b[b],))
                    stt("dve", x[:, oc, :], ps(b), 0.5, x[:, oc, :], ALU.mult, ALU.add, (ps_b[b], x_b), (x_b,))
                ws.done_one()
            act(sq, x, AF.Square, (x_b,), (sq_b,))
            b = nb()
            for c in range(8):
                mm(ps(b), onesfull, sq[:, c, :], c == 0, c == 7, (sq_b, const_b), (ps_b[b],))
            act(rstd, ps(b), AF.Ln, (ps_b[b],), (rstd_b,), bias=EPS)
            act(rstd, rstd, AF.Exp, (rstd_b,), (rstd_b,), scale=-0.5)
            for c in range(8):
                stt("dve", n2[:, c, :], x[:, c, :], g2[:, c:c + 1], rstd, ALU.mult, ALU.mult, (x_b, rstd_b, par_b), (n2_b,))
            for g in range(8):
                w1t, w1_b = ws.next()
                w1v = w1t.rearrange("p (c n) -> p c n", n=512)
                for j in range(4):
                    f = g * 4 + j
                    b = nb()
                    for c in range(8):
                        mm(ps(b), w1v[:, c, j * 128:(j + 1) * 128], n2[:, c, :], c == 0, c == 7, (n2_b, w1_b), (ps_b[b],))
                    i = kr % 3
                    kr += 1
                    act(rl[i], ps(b), AF.Relu, (ps_b[b],), (rl_b[i],))
                    tt("pool", hh[:, f, :], rl[i], rl[i], ALU.mult, (rl_b[i],), (hh_b,))
                ws.done_one()
            if ti + 1 < S // TT:
                loadsC(ti + 1)
            for oc in range(8):
                w2t, w2_b = ws.next()
                w2v = w2t.rearrange("p (k n) -> p k n", n=128)
                b = nb()
                for kf in range(32):
                    mm(ps(b), w2v[:, kf, :], hh[:, kf, :], kf == 0, kf == 31, (hh_b, w2_b), (ps_b[b],))
                tt("dve", x[:, oc, :], ps(b), x[:, oc, :], ALU.add, (ps_b[b], x_b), (x_b,))
                ws.done_one()
            if not last:
                dma(Q_IO, xTv[:, :, t0:t0 + TT], x, x_b, (x_b,), ())
            else:
                for tb in range(4):
                    i = ky % 2
                    ky += 1
                    for hv in range(2):
                        b = nb()
                        for c4 in range(4):
                            c = hv * 4 + c4
                            tr(ps(b)[:, c4 * 128:(c4 + 1) * 128], x[:, c, tb * 128:(tb + 1) * 128], ident_f,
                               (x_b, const_b), (ps_b[b],))
                        cp("act" if hv else "dve", yt[i][:, hv * 512:(hv + 1) * 512], ps(b), (ps_b[b],), (yt_b[i],))
                    dma(Q_IO, y_out[key][t0 + tb * 128:t0 + (tb + 1) * 128, :], yt[i], yt_b[i], (yt_b[i],), ())
        sd.fence()

    for key, S in (("p", SP), ("s", SS)):
        cur["key"] = key
        phase0(key, S)
        for l in range(L):
            layer_params(l)
            phaseA(l, S)
            cast_jobs = cast_layer(l + 1, lazy=True) if (key == "p" and l + 1 < L) else []
            phaseB1(l, S, cast_jobs)
            phaseB2(l, S)
            phaseC(l, S, key, l == L - 1)

    n_sems = {e: max(1, (len(sd.ops[e]) + SEM_LIMIT - 1) // SEM_LIMIT) for e in Sched.ENGS}
    for e in Sched.ENGS:
        k = 0
        for op in sd.ops[e]:
            if op.need_inc and op.dma_sem is None:
                k += 1
                op.idx = k
    import contextlib
    with contextlib.ExitStack() as st:
        eng_sems = {}
        for e in Sched.ENGS:
            cnt = sum(1 for op in sd.ops[e] if op.idx is not None)
            ns = max(1, (cnt + SEM_LIMIT - 1) // SEM_LIMIT)
            eng_sems[e] = [st.enter_context(nc.semaphore("se_%s_%d" % (e, i))) for i in range(ns)]
        for i, b in enumerate(sd.slots):
            b.sem = st.enter_context(nc.semaphore("sd_%d" % i))
        block = st.enter_context(nc.Block())

        def resolve(d):
            if d[0] == "dma":
                return d[1].sem, d[2], None
            op = d[1]
            ep = (op.idx - 1) // SEM_LIMIT
            return eng_sems[op.eng][ep], (op.idx - 1) % SEM_LIMIT + 1, op.eng

        def emit(e, h):
            waited = {}
            for op in sd.ops[e]:
                for d in op.deps:
                    if d[0] == "eng":
                        if d[1].idx is None:
                            continue
                        if d[1].eng == e and (e == "pe" or not SAME_ENG_SYNC):
                            continue
                    sem, val, _ = resolve(d)
                    key_ = id(sem)
                    if waited.get(key_, 0) >= val:
                        continue
                    waited[key_] = val
                    h.wait_ge(sem, val)
                ins = op.fn(h)
                if op.dma_sem is not None:
                    ins.then_inc(op.dma_sem.sem, 16)
                elif op.idx is not None:
                    ep = (op.idx - 1) // SEM_LIMIT
                    ins.then_inc(eng_sems[e][ep], 1)

        @block.tensor
        def _(h):
            emit("pe", h)

        @block.scalar
        def _(h):
            emit("act", h)

        @block.vector
        def _(h):
            emit("dve", h)

        @block.gpsimd
        def _(h):
            emit("pool", h)

        @block.sync
        def _(h):
            emit("sp", h)
    stats = {e: len(sd.ops[e]) for e in Sched.ENGS}
    return nc, stats


def _host_layout(inp, L, S_MAX):
    f = np.float32

    def fm(v, nch):
        return np.ascontiguousarray(np.asarray(v, f).reshape(nch, 128).T)
    out = {}
    out["g1"] = np.stack([fm(inp["norm1_g"][l], 8) for l in range(L)])
    out["g2"] = np.stack([fm(inp["norm2_g"][l], 8) for l in range(L)])
    out["bgate"] = np.stack([fm(inp["b_gate"][l], 16) for l in range(L)])
    qkg = np.zeros((L, 128, 2), f)
    for l in range(L):
        qkg[l, :, 0] = np.tile(np.asarray(inp["q_norm_g"][l], f), 2)
        qkg[l, :, 1] = np.tile(np.asarray(inp["k_norm_g"][l], f), 2)
    out["qkg"] = qkg
    out["lamv"] = np.ascontiguousarray(np.broadcast_to(np.asarray(inp["lam_vecs"], f)[:L].reshape(L, 1, 256), (L, 128, 256)))
    out["subg"] = np.ascontiguousarray(np.broadcast_to(np.asarray(inp["subln_g"], f)[:L].reshape(L, 1, 128), (L, 128, 128)))
    cw = np.asarray(inp["conv_w"], f)[:L]
    convw = np.zeros((L, 128, 8, 4), f)
    for l in range(L):
        for j in range(4):
            convw[l, :, :, j] = fm(cw[l, j], 8)
    out["convw"] = convw.reshape(L, 128, 32)
    out["convb"] = np.stack([fm(inp["conv_b"][l], 8) for l in range(L)])
    rb = np.asarray(inp["rg_b"], f)[:L]
    rgb = np.zeros((L, 128, 2, 2, 8), f)
    for l in range(L):
        for d in range(2):
            for g in range(2):
                rgb[l, :, d, g, :] = fm(rb[l, d, g], 8)
    out["rgb"] = rgb.reshape(L, 128, 32)
    rL = np.asarray(inp["rg_L"], f)[:L]
    rgL = np.zeros((L, 128, 2, 8), f)
    for l in range(L):
        for d in range(2):
            rgL[l, :, d, :] = fm(rL[l, d], 8)
    out["rgL"] = rgL.reshape(L, 128, 16)
    rw = np.asarray(inp["rg_w"], f)[:L]
    rgw = np.zeros((L, 2, 2, 8, 128, 128), f)
    for c in range(8):
        rgw[:, :, :, c, 0:64, 0:64] = rw[:, :, :, 2 * c]
        rgw[:, :, :, c, 64:128, 64:128] = rw[:, :, :, 2 * c + 1]
    out["rgw"] = rgw.reshape(L, 32, 128, 128)
    out["ident"] = np.eye(128, dtype=f)
    obd = np.zeros((128, 128), f)
    obd[0:64, 0:64] = 1.0 / 64.0
    obd[64:128, 64:128] = 1.0 / 64.0
    out["onesbd"] = obd
    out["onesfull"] = np.full((128, 128), 1.0 / 1024.0, f)
    pm = np.zeros((128, 128), f)
    cosT = np.ones((128, S_MAX), f)
    sinT = np.zeros((128, S_MAX), f)
    pos = np.arange(S_MAX, dtype=f)
    inv_freq = (np.float32(500000.0) ** (-np.arange(0, 16, 2, dtype=f) / np.float32(16))).astype(f)
    ang = (pos[:, None] * inv_freq[None, :]).astype(f)
    cs = np.cos(ang).astype(f).T
    sn = np.sin(ang).astype(f).T
    for gb in (0, 64):
        for m in range(8):
            pm[gb + m + 8, gb + m] = 1.0
            pm[gb + m, gb + m + 8] = 1.0
            cosT[gb + m] = cs[m]
            cosT[gb + m + 8] = cs[m]
            sinT[gb + m] = -sn[m]
            sinT[gb + m + 8] = sn[m]
    out["perm"] = pm
    out["cosT"] = cosT
    out["sinT"] = sinT
    return out


_CACHE = {}


def run(inputs, L=4, n_cores=8):
    xp = np.asarray(inputs["x_prompt"], np.float32)
    xs = np.asarray(inputs["x_sample"], np.float32)
    SP, SS = xp.shape[1], xs.shape[1]
    keyc = (SP, SS, L)
    if keyc not in _CACHE:
        _CACHE[keyc] = build_program(SP, SS, L)
    nc, stats = _CACHE[keyc]
    lay = _host_layout(inputs, L, max(SP, SS))
    shared = {
        "w_in": np.ascontiguousarray(np.asarray(inputs["w_in"], np.float32)[:L]),
        "wba": np.ascontiguousarray(np.asarray(inputs["w_branch_att"], np.float32)[:L]),
        "wbr": np.ascontiguousarray(np.asarray(inputs["w_branch_rec"], np.float32)[:L]),
        "wout": np.ascontiguousarray(np.asarray(inputs["w_out"], np.float32)[:L]),
        "w1": np.ascontiguousarray(np.asarray(inputs["w_ff1"], np.float32)[:L]),
        "w2": np.ascontiguousarray(np.asarray(inputs["w_ff2"], np.float32)[:L]),
    }
    shared.update(lay)
    in_maps = []
    for i in range(n_cores):
        m = dict(shared)
        m["xp"] = np.ascontiguousarray(xp[i])
        m["xs"] = np.ascontiguousarray(xs[i])
        in_maps.append(m)
    res = run_bass_kernel_spmd(nc, in_maps, core_ids=list(range(n_cores)))
    if DEBUG:
        _CACHE["dbg"] = res.results
    yp = np.stack([np.asarray(r["yp"], np.float32) for r in res.results])
    ys = np.stack([np.asarray(r["ys"], np.float32) for r in res.results])
    return yp, ys


def kernel(**inputs):
    return run(inputs, L=4, n_cores=8)
```

```python
import math
import numpy as np
import concourse.bass as bass
import concourse.mybir as mybir
from concourse.bass_utils import run_bass_kernel_spmd
from concourse.ap import AP

F32 = mybir.dt.float32
BF16 = mybir.dt.bfloat16
ALU = mybir.AluOpType
AF = mybir.ActivationFunctionType
AX = mybir.AxisListType

D = 1024
DC = 8
D_IN = 7168
D_FF = 4096
EPS = 1e-6
NT_W = 36
TT = 512
SEM_LIMIT = 30000
SAME_ENG_SYNC = True
Q_IO = "pool"
Q_W = "sp"
NW = 4
DEBUG = False


class Slot:
    __slots__ = ("cnt", "sem")

    def __init__(self):
        self.cnt = 0
        self.sem = None


class Buf:
    __slots__ = ("name", "w", "r", "slot", "glob", "nofence")

    def __init__(self, name, glob=False, nofence=False):
        self.name = name
        self.w = None
        self.r = {}
        self.slot = None
        self.glob = glob
        self.nofence = nofence


class Op:
    __slots__ = ("eng", "fn", "deps", "need_inc", "idx", "dma_sem", "ev")


class Sched:
    ENGS = ("pe", "act", "dve", "pool", "sp")

    def __init__(self):
        self.ops = {e: [] for e in self.ENGS}
        self.dirty = {}
        self.slots = []
        self.pools = {}
        self.cur_pool = None
        self.cur_idx = 0

    def set_pool(self, name):
        self.cur_pool = name
        self.cur_idx = 0

    def get_slot(self, buf):
        if buf.slot is None:
            if buf.glob or self.cur_pool is None:
                sl = Slot()
                self.slots.append(sl)
                buf.slot = sl
            else:
                pool = self.pools.setdefault(self.cur_pool, [])
                if self.cur_idx >= len(pool):
                    sl = Slot()
                    pool.append(sl)
                    self.slots.append(sl)
                buf.slot = pool[self.cur_idx]
                self.cur_idx += 1
        return buf.slot

    def add(self, eng, fn, reads=(), writes=(), dma=None, extra_deps=()):
        op = Op()
        op.eng = eng
        op.fn = fn
        op.need_inc = False
        op.idx = None
        op.dma_sem = None
        deps = list(extra_deps)
        war = []
        for b in reads:
            if b.w is not None:
                deps.append(b.w)
        for b in writes:
            if b.w is not None:
                deps.append(b.w)
            for rv in b.r.values():
                if rv[0] == "eng" and rv[1].eng == eng:
                    continue
                deps.append(rv)
        if dma is not None:
            sl = self.get_slot(dma)
            sl.cnt += 16
            ev = ("dma", sl, sl.cnt)
            op.dma_sem = sl
            if not dma.nofence:
                self.dirty[id(sl)] = ev
            rkey = ("dma", id(sl))
        else:
            ev = ("eng", op)
            rkey = ("eng", eng)
        op.ev = ev
        for d in deps:
            if d[0] == "eng":
                d[1].need_inc = True
        op.deps = deps
        for b in reads:
            b.r[rkey] = ev
        for b in writes:
            b.w = ev
            b.r = {}
        self.ops[eng].append(op)
        return op

    def fence(self):
        deps = []
        for e in self.ENGS:
            if e == "sp":
                continue
            for op in reversed(self.ops[e]):
                if op.dma_sem is None:
                    deps.append(op.ev)
                    break
        deps.extend(self.dirty.values())
        self.dirty = {}
        f = self.add("sp", lambda e: e.nop(), extra_deps=deps)
        for e in self.ENGS:
            if e != "sp":
                self.add(e, lambda en: en.nop(), extra_deps=[f.ev])
        return f


def build_program(SP, SS, L, n_layers_lam_off=0):
    nc = bass.Bass("TRN2", target_bir_lowering=False)
    S_MAX = max(SP, SS)
    sd = Sched()

    def din(name, shape, dt=F32):
        return nc.dram_tensor(name, list(shape), dt, kind="ExternalInput").ap()

    def dscr(name, shape, dt):
        if DEBUG and name != "wsc":
            return nc.dram_tensor(name, list(shape), dt, kind="ExternalOutput").ap()
        return nc.dram_tensor(name, list(shape), dt, kind="Internal").ap()

    x_in = {"p": din("xp", [SP, D]), "s": din("xs", [SS, D])}
    y_out = {"p": nc.dram_tensor("yp", [SP, D], F32, kind="ExternalOutput").ap(),
             "s": nc.dram_tensor("ys", [SS, D], F32, kind="ExternalOutput").ap()}
    w_in = din("w_in", [L, D, D_IN])
    wba = din("wba", [L, D, D])
    wbr = din("wbr", [L, D, D])
    wout = din("wout", [L, D, D])
    w1 = din("w1", [L, D, D_FF])
    w2 = din("w2", [L, D_FF, D])
    g1_d = din("g1", [L, 128, 8])
    g2_d = din("g2", [L, 128, 8])
    bgate_d = din("bgate", [L, 128, 16])
    qkg_d = din("qkg", [L, 128, 2])
    lamv_d = din("lamv", [L, 128, 256])
    subg_d = din("subg", [L, 128, 128])
    convw_d = din("convw", [L, 128, 32])
    convb_d = din("convb", [L, 128, 8])
    rgb_d = din("rgb", [L, 128, 32])
    rgL_d = din("rgL", [L, 128, 16])
    rgw_d = din("rgw", [L, 32, 128, 128])
    ident_d = din("ident", [128, 128])
    onesbd_d = din("onesbd", [128, 128])
    onesfull_d = din("onesfull", [128, 128])
    perm_d = din("perm", [128, 128])
    cos_d = din("cosT", [128, S_MAX])
    sin_d = din("sinT", [128, S_MAX])

    wsc = dscr("wsc", [L, NT_W, 128, 4096], BF16)
    xT = dscr("xT", [D, S_MAX], F32)
    qT = dscr("qT", [D, S_MAX], BF16)
    kT = dscr("kT", [D, S_MAX], BF16)
    Vs = dscr("Vs", [S_MAX, D], BF16)
    xrT = dscr("xrT", [D, S_MAX], F32)
    gyT = dscr("gyT", [D, S_MAX], BF16)
    thT = dscr("thT", [2 * D, S_MAX], BF16)
    attT = dscr("attT", [D, S_MAX], BF16)
    recT = dscr("recT", [D, S_MAX], BF16)

    ARENA_BYTES = 158 * 1024
    arena = nc.alloc_sbuf_tensor("arena", [128, ARENA_BYTES // 4], F32)
    arena_ap = arena[:] if not isinstance(arena, AP) else arena
    psum = nc.alloc_psum_tensor("psum", [128, 8, 512], F32)
    psum_ap = psum[:] if not isinstance(psum, AP) else psum

    class Arena:
        def __init__(self):
            self.off = 0

        def reset(self):
            self.off = 0

        def alloc(self, shape, dt):
            n = 1
            for s in shape:
                n *= s
            nbytes = n * (4 if dt == F32 else 2)
            nbytes = (nbytes + 31) // 32 * 32
            assert self.off + nbytes <= ARENA_BYTES, ("arena overflow", self.off, nbytes)
            a = arena_ap[:, self.off // 4:(self.off + nbytes) // 4]
            self.off += nbytes
            if dt != F32:
                a = a.bitcast(dt)
            a = a[:, 0:n]
            if len(shape) == 2:
                a = a.rearrange("p (a b) -> p a b", b=shape[1])
            elif len(shape) == 3:
                a = a.rearrange("p (a b c) -> p a b c", b=shape[1], c=shape[2])
            return a

    ar = Arena()

    def sb(name, shape, dt):
        t = nc.alloc_sbuf_tensor(name, [128] + list(shape), dt)
        return t[:] if not isinstance(t, AP) else t

    ident_f = sb("ident_f", [128], F32)
    identb = sb("identb", [128], BF16)
    onesbd = sb("onesbd_s", [128], BF16)
    onesfull = sb("onesfull_s", [128], BF16)
    perm = sb("perm_s", [128], BF16)
    const_b = Buf("consts", glob=True)
    g1 = sb("g1_s", [8], F32)
    g2 = sb("g2_s", [8], F32)
    bgh = sb("bgh_s", [16], F32)
    qkg = sb("qkg_s", [2], F32)
    lamv = sb("lamv_s", [256], F32)
    lamw = sb("lamw_s", [128], F32)
    lams = sb("lams_s", [8], F32)
    subg = sb("subg_s", [128], F32)
    convw = sb("convw_s", [32], F32)
    convb = sb("convb_s", [8], F32)
    rgbh = sb("rgbh_s", [32], F32)
    rgL = sb("rgL_s", [16], F32)
    cLh = sb("cLh_s", [16], F32)
    cL2 = sb("cL2_s", [16], F32)
    rgw = sb("rgw_s", [32, 128], BF16)
    par_b = Buf("params", glob=True)
    wring = [sb("wring%d" % i, [4096], BF16) for i in range(NW)]
    wring_b = [Buf("wring%d" % i, glob=True) for i in range(NW)]
    ps_b = [Buf("ps%d" % i) for i in range(8)]

    def ps(i):
        return psum_ap[:, i, :]

    def mm(out, lhsT, rhs, start, stop, reads, writes, skip=False):
        if skip:
            return sd.add("pe", lambda e: e.matmul(out, lhsT, rhs, start=start, stop=stop, skip_group_check=True), reads, writes)
        return sd.add("pe", lambda e: e.matmul(out, lhsT, rhs, start=start, stop=stop), reads, writes)

    def tr(out, in_, ident, reads, writes):
        return sd.add("pe", lambda e: e.transpose(out, in_, ident), reads, writes)

    def act(out, in_, func, reads, writes, scale=1.0, bias=0.0, accum=None):
        def f(e):
            if accum is not None:
                return e.activation(out=out, in_=in_, func=func, bias=bias, scale=scale, accum_out=accum)
            return e.activation(out=out, in_=in_, func=func, bias=bias, scale=scale)
        return sd.add("act", f, reads, writes)

    def stt(eng, out, in0, scalar, in1, op0, op1, reads, writes):
        return sd.add(eng, lambda e: e.scalar_tensor_tensor(out, in0, scalar, in1, op0, op1), reads, writes)

    def ts(eng, out, in0, s1, s2, op0, op1, reads, writes):
        if s2 is None:
            return sd.add(eng, lambda e: e.tensor_scalar(out, in0, s1, None, op0), reads, writes)
        return sd.add(eng, lambda e: e.tensor_scalar(out, in0, s1, s2, op0, op1), reads, writes)

    def tt(eng, out, in0, in1, op, reads, writes):
        return sd.add(eng, lambda e: e.tensor_tensor(out, in0, in1, op), reads, writes)

    def cp(eng, out, in_, reads, writes):
        if eng == "act":
            return sd.add("act", lambda e: e.copy(out, in_), reads, writes)
        return sd.add(eng, lambda e: e.tensor_copy(out, in_), reads, writes)

    def dma(q, out, in_, owner, reads=(), writes=()):
        return sd.add(q, lambda e: e.dma_start(out=out, in_=in_), reads, writes, dma=owner)

    def memset(eng, ap, val, writes):
        return sd.add(eng, lambda e: e.memset(ap, val), (), writes)

    dbg_owner = Buf("dbg", glob=True)
    cur = {"key": None}

    def dump(name, ap, reads):
        if not DEBUG or cur["key"] != "s":
            return
        shape = list(ap.shape)
        dt_ = nc.dram_tensor("dbg_" + name, shape, ap.dtype, kind="ExternalOutput").ap()
        dma(Q_IO, dt_, ap, dbg_owner, reads, ())

    wstate = {"n": 0}

    def wload(l, t):
        i = wstate["n"] % NW
        wstate["n"] += 1
        dma(Q_W, wring[i], wsc[l, t], wring_b[i], (cast_bs[l],), (wring_b[i],))
        return wring[i], wring_b[i]

    class WStream:
        def __init__(self, tiles):
            self.tiles = tiles
            self.q = []
            self.pos = 0
            for _ in range(min(NW - 1, len(tiles))):
                self._issue()

        def _issue(self):
            if self.pos < len(self.tiles):
                self.q.append(wload(*self.tiles[self.pos]))
                self.pos += 1

        def next(self):
            r = self.q.pop(0)
            return r

        def done_one(self):
            self._issue()

    dma("pool", ident_f, ident_d, const_b, (), (const_b,))
    dma("pool", identb, ident_d, const_b, (), (const_b,))
    dma("pool", onesbd, onesbd_d, const_b, (), (const_b,))
    dma("pool", onesfull, onesfull_d, const_b, (), (const_b,))
    dma("pool", perm, perm_d, const_b, (), (const_b,))
    cast_bs = [Buf("cast%d" % l, glob=True, nofence=True) for l in range(L)]

    def cast_layer(l, lazy=False):
        cast_b = cast_bs[l]

        def wview(t):
            return wsc[l, t].rearrange("p (c n) -> p c n", n=512)
        srcs = []
        win_v = w_in[l].rearrange("(c p) n -> p c n", p=128)
        for g in range(14):
            srcs.append(win_v[:, :, g * 512:(g + 1) * 512])
        a_v = wba[l].rearrange("(c p) n -> p c n", p=128)
        r_v = wbr[l].rearrange("(c p) n -> p c n", p=128)
        o_v = wout[l].rearrange("(c p) n -> p c n", p=128)
        srcs += [a_v[:, :, 0:512], r_v[:, :, 0:512], a_v[:, :, 512:1024], r_v[:, :, 512:1024],
                 o_v[:, :, 0:512], o_v[:, :, 512:1024]]
        w1_v = w1[l].rearrange("(c p) n -> p c n", p=128)
        for g in range(8):
            srcs.append(w1_v[:, :, g * 512:(g + 1) * 512])
        jobs = []
        for t, s in enumerate(srcs):
            jobs.append((lambda t=t, s=s: dma("pool", wview(t), s, cast_b, (), (cast_b,))))
        w2_v = w2[l].rearrange("(k p) n -> p k n", p=128)
        for oc in range(8):
            jobs.append((lambda oc=oc: dma("pool", wsc[l, 28 + oc].rearrange("p (k n) -> p k n", n=128),
                                           w2_v[:, :, oc * 128:(oc + 1) * 128], cast_b, (), (cast_b,))))
        if lazy:
            return jobs
        for j in jobs:
            j()
        return []

    sd.fence()
    cast_layer(0)

    def layer_params(l):
        lam_init = 0.8 - 0.6 * math.exp(-0.3 * l)
        pb = par_b
        for dst, src in ((g1, g1_d), (g2, g2_d), (bgh, bgate_d), (qkg, qkg_d), (lamv, lamv_d), (subg, subg_d),
                         (convw, convw_d), (convb, convb_d), (rgbh, rgb_d), (rgL, rgL_d)):
            dma("sp", dst, src[l], pb, (), (pb,))
        dma("pool", rgw, rgw_d[l].rearrange("t p n -> p t n"), pb, (), (pb,))
        ts("dve", bgh, bgh, 0.5, None, ALU.mult, None, (pb,), (pb,))
        ts("dve", rgbh, rgbh, 0.5, None, ALU.mult, None, (pb,), (pb,))
        ts("dve", subg, subg, 1.0 - lam_init, None, ALU.mult, None, (pb,), (pb,))
        tt("dve", lamw[:, 0:64], lamv[:, 0:64], lamv[:, 64:128], ALU.mult, (pb,), (pb,))
        tt("dve", lamw[:, 64:128], lamv[:, 128:192], lamv[:, 192:256], ALU.mult, (pb,), (pb,))
        sd.add("dve", lambda e: e.reduce_sum(lams[:, 0:1], lamw[:, 0:64], AX.X), (pb,), (pb,))
        sd.add("dve", lambda e: e.reduce_sum(lams[:, 1:2], lamw[:, 64:128], AX.X), (pb,), (pb,))
        act(lams[:, 2:4], lams[:, 0:2], AF.Exp, (pb,), (pb,))
        tt("dve", lams[:, 4:5], lams[:, 2:3], lams[:, 3:4], ALU.subtract, (pb,), (pb,))
        ts("dve", lams[:, 5:6], lams[:, 4:5], lam_init, -1.0, ALU.add, ALU.mult, (pb,), (pb,))
        act(cLh, rgL, AF.Exp, (pb,), (pb,), scale=-1.0)
        act(cLh, cLh, AF.Ln, (pb,), (pb,), bias=1.0)
        ts("dve", cLh, cLh, -4.0, None, ALU.mult, None, (pb,), (pb,))
        ts("dve", cL2, cLh, 2.0, None, ALU.mult, None, (pb,), (pb,))
        sd.fence()

    def phase0(key, S):
        ar.reset()
        sd.set_pool("P0")
        xin = [ar.alloc([4, D], F32) for _ in range(2)]
        xin_b = [Buf("xin0"), Buf("xin1")]
        xt = [ar.alloc([8, TT], F32) for _ in range(2)]
        xt_b = [Buf("xt0"), Buf("xt1")]
        xsrc = x_in[key]
        xTv = xT.rearrange("(c p) s -> p c s", p=128)
        k = 0
        for ti in range(S // TT):
            t0 = ti * TT
            i = ti % 2
            dma(Q_IO, xin[i], xsrc[t0:t0 + TT, :].rearrange("(tb p) d -> p tb d", p=128), xin_b[i], (), (xin_b[i],))
            for c in range(8):
                b = k % 8
                k += 1
                for tb in range(4):
                    tr(ps(b)[:, tb * 128:(tb + 1) * 128], xin[i][:, tb, c * 128:(c + 1) * 128], ident_f,
                       (xin_b[i], const_b), (ps_b[b],))
                cp("act" if c % 2 else "dve", xt[i][:, c, :], ps(b), (ps_b[b],), (xt_b[i],))
            dma(Q_IO, xTv[:, :, t0:t0 + TT], xt[i], xt_b[i], (xt_b[i],), ())
        sd.fence()

    def phaseA(l, S):
        ar.reset()
        sd.set_pool("A")
        xt = ar.alloc([8, TT], F32); xt_b = Buf("A.xt")
        cs = ar.alloc([2, TT], F32); cs_b = Buf("A.cs")
        sq = ar.alloc([8, TT], BF16); sq_b = Buf("A.sq")
        nT2 = [ar.alloc([8, TT], BF16) for _ in range(2)]; nT2_b = [Buf("A.nT0"), Buf("A.nT1")]
        rstd = ar.alloc([TT], F32); rstd_b = Buf("A.rstd")
        NB = 5
        sq2 = [ar.alloc([TT], BF16) for _ in range(NB)]; sq2_b = [Buf("A.sq2") for _ in range(NB)]
        r2 = [ar.alloc([TT], F32) for _ in range(NB)]; r2_b = [Buf("A.r2") for _ in range(NB)]
        qn = [ar.alloc([TT], F32) for _ in range(2)]; qn_b = [Buf("A.qn") for _ in range(2)]
        qnb = [ar.alloc([TT], BF16) for _ in range(NB)]; qnb_b = [Buf("A.qnb") for _ in range(NB)]
        t1 = [ar.alloc([TT], F32) for _ in range(NB)]; t1_b = [Buf("A.t1") for _ in range(NB)]
        tmp = [ar.alloc([TT], F32) for _ in range(NB)]; tmp_b = [Buf("A.tmp") for _ in range(NB)]
        qks = [ar.alloc([8, TT], BF16) for _ in range(2)]; qks_b = [Buf("A.qs"), Buf("A.ks")]
        vst = ar.alloc([4, D], BF16); vst_b = Buf("A.vs")
        xrs = ar.alloc([8, TT], F32); xrs_b = Buf("A.xrs")
        gys = ar.alloc([8, TT], BF16); gys_b = Buf("A.gys")
        ths = ar.alloc([16, TT], BF16); ths_b = Buf("A.ths")
        xTv = xT.rearrange("(c p) s -> p c s", p=128)
        qTv = [qT.rearrange("(c p) s -> p c s", p=128), kT.rearrange("(c p) s -> p c s", p=128)]
        xrTv = xrT.rearrange("(c p) s -> p c s", p=128)
        gyTv = gyT.rearrange("(c p) s -> p c s", p=128)
        thTv = thT.rearrange("(c p) s -> p c s", p=128)
        tiles = []
        for ti in range(S // TT):
            tiles += [(l, g) for g in range(14)]
        ws = WStream(tiles)
        kb = [0]

        def nb():
            b = kb[0] % 8
            kb[0] += 1
            return b
        kk = 0
        qk_pipe = []

        def qk_stage2(cx):
            b, i, which = cx["b"], cx["i"], cx["which"]
            b2 = nb()
            mm(ps(b2), onesbd, sq2[i], True, True, (sq2_b[i], const_b), (ps_b[b2],))
            act(r2[i], ps(b2), AF.Ln, (ps_b[b2],), (r2_b[i],), bias=EPS)
            act(r2[i], r2[i], AF.Exp, (r2_b[i],), (r2_b[i],), scale=-0.5)
            stt("dve", qnb[i], ps(b), qkg[:, which:which + 1], r2[i], ALU.mult, ALU.mult,
                (ps_b[b], r2_b[i], par_b), (qnb_b[i],))
            tt("pool", t1[i], qnb[i], cs[:, 0, :], ALU.mult, (qnb_b[i], cs_b), (t1_b[i],))
            cx["st"] = 2

        def qk_stage3(cx):
            i, which, h = cx["i"], cx["which"], cx["h"]
            b3 = nb()
            mm(ps(b3), perm, qnb[i], True, True, (qnb_b[i], const_b), (ps_b[b3],))
            tt("dve", tmp[i], ps(b3), cs[:, 1, :], ALU.mult, (ps_b[b3], cs_b), (tmp_b[i],))
            tt("dve", qks[which][:, h, :], t1[i], tmp[i], ALU.add, (t1_b[i], tmp_b[i]), (qks_b[which],))
            cx["st"] = 3

        def qk_advance(flush):
            while True:
                n = len(qk_pipe)
                if n >= 4 or (flush and n >= 1 and qk_pipe[0]["st"] == 2):
                    qk_stage3(qk_pipe.pop(0))
                    continue
                break
            for cx in qk_pipe:
                if cx["st"] == 1 and (flush or cx is not qk_pipe[-1]):
                    qk_stage2(cx)
            if flush:
                while qk_pipe:
                    cx = qk_pipe.pop(0)
                    if cx["st"] == 1:
                        qk_stage2(cx)
                    qk_stage3(cx)

        def prologue1(ti):
            t0 = ti * TT
            dma(Q_IO, xt, xTv[:, :, t0:t0 + TT], xt_b, (), (xt_b,))
            dma(Q_IO, cs[:, 0, :], cos_d[:, t0:t0 + TT], cs_b, (), (cs_b,))
            dma(Q_IO, cs[:, 1, :], sin_d[:, t0:t0 + TT], cs_b, (), (cs_b,))
            act(sq, xt, AF.Square, (xt_b,), (sq_b,))

        def prologue(ti):
            nT, nT_b = nT2[ti % 2], nT2_b[ti % 2]
            b = nb()
            for c in range(8):
                mm(ps(b), onesfull, sq[:, c, :], c == 0, c == 7, (sq_b, const_b), (ps_b[b],))
            act(rstd, ps(b), AF.Ln, (ps_b[b],), (rstd_b,), bias=EPS)
            act(rstd, rstd, AF.Exp, (rstd_b,), (rstd_b,), scale=-0.5)
            for c in range(8):
                stt("dve", nT[:, c, :], xt[:, c, :], g1[:, c:c + 1], rstd, ALU.mult, ALU.mult,
                    (xt_b, rstd_b, par_b), (nT_b,))

        prologue1(0)
        prologue(0)
        for ti in range(S // TT):
            t0 = ti * TT
            nT, nT_b = nT2[ti % 2], nT2_b[ti % 2]
            for g in range(14):
                wt, wt_b = ws.next()
                wv = wt.rearrange("p (c n) -> p c n", n=512)
                if g < 4:
                    which = g // 2
                    for j in range(4):
                        h = (g % 2) * 4 + j
                        b = nb()
                        for c in range(8):
                            mm(ps(b), wv[:, c, j * 128:(j + 1) * 128], nT[:, c, :], c == 0, c == 7,
                               (nT_b, wt_b), (ps_b[b],))
                        i = kk % NB
                        kk += 1
                        act(sq2[i], ps(b), AF.Square, (ps_b[b],), (sq2_b[i],))
                        qk_pipe.append({"b": b, "i": i, "which": which, "h": h, "st": 1})
                        qk_advance(False)
                    if g == 3:
                        qk_advance(True)
                elif g < 6:
                    half = g - 4
                    for tb in range(4):
                        b = nb()
                        for c in range(8):
                            mm(ps(b), nT[:, c, tb * 128:(tb + 1) * 128], wv[:, c, :], c == 0, c == 7,
                               (nT_b, wt_b), (ps_b[b],))
                        cp("act", vst[:, tb, half * 512:(half + 1) * 512], ps(b), (ps_b[b],), (vst_b,))
                elif g < 8:
                    for j in range(4):
                        cc = (g - 6) * 4 + j
                        b = nb()
                        for c in range(8):
                            mm(ps(b), wv[:, c, j * 128:(j + 1) * 128], nT[:, c, :], c == 0, c == 7,
                               (nT_b, wt_b), (ps_b[b],))
                        cp("act" if j % 2 else "dve", xrs[:, cc, :], ps(b), (ps_b[b],), (xrs_b,))
                elif g < 10:
                    for j in range(4):
                        cc = (g - 8) * 4 + j
                        b = nb()
                        for c in range(8):
                            mm(ps(b), wv[:, c, j * 128:(j + 1) * 128], nT[:, c, :], c == 0, c == 7,
                               (nT_b, wt_b), (ps_b[b],))
                        i = kk % NB
                        kk += 1
                        act(r2[i], ps(b), AF.Square, (ps_b[b],), (r2_b[i],))
                        ts("dve", r2[i], r2[i], 0.044715, 1.0, ALU.mult, ALU.add, (r2_b[i],), (r2_b[i],))
                        tt("dve", qn[i % 2], r2[i], ps(b), ALU.mult, (r2_b[i], ps_b[b]), (qn_b[i % 2],))
                        act(t1[i], qn[i % 2], AF.Tanh, (qn_b[i % 2],), (t1_b[i],), scale=0.7978845608028654)
                        stt("dve", gys[:, cc, :], t1[i], 1.0, ps(b), ALU.add, ALU.mult, (t1_b[i], ps_b[b]), (gys_b,))
                else:
                    for j in range(4):
                        gi = (g - 10) * 4 + j
                        b = nb()
                        for c in range(8):
                            mm(ps(b), wv[:, c, j * 128:(j + 1) * 128], nT[:, c, :], c == 0, c == 7,
                               (nT_b, wt_b), (ps_b[b],))
                        act(ths[:, gi, :], ps(b), AF.Tanh, (ps_b[b], par_b), (ths_b,), scale=0.5, bias=bgh[:, gi:gi + 1])
                ws.done_one()
                if g == 3:
                    dma(Q_IO, qTv[0][:, :, t0:t0 + TT], qks[0], qks_b[0], (qks_b[0],), ())
                    dma(Q_IO, qTv[1][:, :, t0:t0 + TT], qks[1], qks_b[1], (qks_b[1],), ())
                    if ti + 1 < S // TT:
                        prologue1(ti + 1)
                elif g == 5:
                    dma(Q_IO, Vs[t0:t0 + TT, :].rearrange("(tb p) d -> p tb d", p=128), vst, vst_b, (vst_b,), ())
                elif g == 7:
                    dma(Q_IO, xrTv[:, :, t0:t0 + TT], xrs, xrs_b, (xrs_b,), ())
                elif g == 9:
                    dma(Q_IO, gyTv[:, :, t0:t0 + TT], gys, gys_b, (gys_b,), ())
                elif g == 11:
                    if ti + 1 < S // TT:
                        prologue(ti + 1)
                elif g == 13:
                    dma(Q_IO, thTv[:, 0:8, t0:t0 + TT], ths[:, 0:8, :], ths_b, (ths_b,), ())
                    dma(Q_IO, thTv[:, 8:16, t0:t0 + TT], ths[:, 8:16, :], ths_b, (ths_b,), ())
        sd.fence()

    def phaseB1(l, S, cast_jobs=()):
        ar.reset()
        sd.set_pool("B1")
        NKB = S // 128
        NQB = S // TT
        qh = [ar.alloc([S], BF16) for _ in range(2)]
        kh = [ar.alloc([S], BF16) for _ in range(2)]
        Vh = [ar.alloc([NKB, 130], BF16) for _ in range(2)]
        qkv_b = [Buf("B1.qkv0"), Buf("B1.qkv1")]
        NE = 4
        Eb = [[ar.alloc([TT], BF16) for _ in range(2)] for _ in range(NE)]
        Eb_b = [[Buf("B1.E") for _ in range(2)] for _ in range(NE)]
        rden = ar.alloc([16], F32); rden_b = Buf("B1.rden")
        t0b = [ar.alloc([128], F32) for _ in range(2)]; t0b_b = [Buf("B1.t0") for _ in range(2)]
        attf = [ar.alloc([128], F32) for _ in range(4)]; attf_b = [Buf("B1.attf") for _ in range(4)]
        junk = ar.alloc([128], F32); junk_b = Buf("B1.junk")
        ssq = ar.alloc([8], F32); ssq_b = Buf("B1.ssq")
        attb = [ar.alloc([128], BF16) for _ in range(4)]; attb_b = [Buf("B1.attb") for _ in range(4)]
        attTs = [ar.alloc([TT], BF16) for _ in range(2)]; attTs_b = [Buf("B1.attTs0"), Buf("B1.attTs1")]
        psT = psum_ap[:, 7, :].bitcast(BF16)
        dbgO = ar.alloc([3, 512], F32); dbgO_b = Buf("dbgO")
        Oc = ar.alloc([3, 512], F32); Oc_b = [Buf("B1.Oc%d" % i_) for i_ in range(3)]
        for i in range(2):
            memset("dve", Vh[i][:, :, 128:129], 1.0, (qkv_b[i],))
        sbank = 0
        ei = 0
        nst = 0
        cast_jobs = list(cast_jobs)
        pendP2 = [None]
        pendP3 = [None]

        def make_post(h, qb, q0):
            def P1():
                for ob_ in range(3):
                    ncol = 387 if ob_ < 2 else 258
                    cp("dve", Oc[:, ob_, 0:ncol], ps(4 + ob_)[:, 0:ncol], (ps_b[4 + ob_],), (Oc_b[ob_],))
                for s_ in range(8):
                    ob = s_ // 3
                    oc = (s_ % 3) * 129
                    sd.add("dve", (lambda ob=ob, oc=oc, s_=s_: (lambda e: e.reciprocal(rden[:, s_:s_ + 1], Oc[:, ob, oc + 128:oc + 129])))(),
                           (Oc_b[ob],), (rden_b,))
                ts("dve", rden[:, 8:12], rden[:, 4:8], lams[:, 5:6], None, ALU.mult, None, (rden_b, par_b), (rden_b,))
                memset("dve", ssq[:, 0:4], 0.0, (ssq_b,))
                for j in range(4):
                    s0 = j
                    s1 = 4 + j
                    ti_ = j % 2
                    ts("dve", t0b[ti_], Oc[:, s0 // 3, (s0 % 3) * 129:(s0 % 3) * 129 + 128], rden[:, j:j + 1], None,
                       ALU.mult, None, (Oc_b[s0 // 3], rden_b), (t0b_b[ti_],))
                    stt("dve", attf[j], Oc[:, s1 // 3, (s1 % 3) * 129:(s1 % 3) * 129 + 128], rden[:, 8 + j:9 + j], t0b[ti_],
                        ALU.mult, ALU.add, (Oc_b[s1 // 3], rden_b, t0b_b[ti_]), (attf_b[j],))

            def P2():
                for j in range(4):
                    act(junk, attf[j], AF.Square, (attf_b[j],), (junk_b, ssq_b), accum=ssq[:, j:j + 1])
                act(ssq[:, 4:8], ssq[:, 0:4], AF.Ln, (ssq_b,), (ssq_b,), scale=1.0 / 128.0, bias=EPS)
                act(ssq[:, 4:8], ssq[:, 4:8], AF.Exp, (ssq_b,), (ssq_b,), scale=-0.5)

            def P3():
                ai = (h * NQB + qb) % 2
                for j in range(4):
                    stt("dve", attb[j], attf[j], ssq[:, 4 + j:5 + j], subg, ALU.mult, ALU.mult,
                        (attf_b[j], ssq_b, par_b), (attb_b[j],))
                    tr(psT[:, j * 128:(j + 1) * 128], attb[j], identb, (attb_b[j], const_b), (ps_b[7],))
                cp("dve", attTs[ai], psT[:, 0:TT], (ps_b[7],), (attTs_b[ai],))
                dma(Q_IO, attT[h * 128:(h + 1) * 128, q0:q0 + TT], attTs[ai], attTs_b[ai], (attTs_b[ai],), ())
            return P1, P2, P3

        def flush_post():
            if pendP2[0] is not None:
                pendP2[0]()
                pendP2[0] = None
            if pendP3[0] is not None:
                pendP3[0]()
                pendP3[0] = None

        for h in range(8):
            i = h % 2
            dma(Q_IO, qh[i], qT[h * 128:(h + 1) * 128, 0:S], qkv_b[i], (), (qkv_b[i],))
            dma(Q_IO, kh[i], kT[h * 128:(h + 1) * 128, 0:S], qkv_b[i], (), (qkv_b[i],))
            for k4 in range(0, NKB, 8):
                ke = min(NKB, k4 + 8)
                dma(Q_IO, Vh[i][:, k4:ke, 0:128],
                    Vs[k4 * 128:ke * 128, h * 128:(h + 1) * 128].rearrange("(kb p) e -> p kb e", p=128),
                    qkv_b[i], (), (qkv_b[i],))
            for _ in range(5):
                if cast_jobs:
                    cast_jobs.pop(0)()
            for qb in range(NQB):
                q0 = qb * TT
                fifo = []
                for kc in range(NKB + 2):
                    if kc < NKB:
                        b0 = sbank * 2
                        sbank = (sbank + 1) % 2
                        e = ei % NE
                        ei += 1
                        mm(ps(b0), kh[i][0:64, kc * 128:(kc + 1) * 128], qh[i][0:64, q0:q0 + TT], True, True,
                           (qkv_b[i],), (ps_b[b0],))
                        mm(ps(b0 + 1), kh[i][64:128, kc * 128:(kc + 1) * 128], qh[i][64:128, q0:q0 + TT], True, True,
                           (qkv_b[i],), (ps_b[b0 + 1],))
                        act(Eb[e][0], ps(b0), AF.Exp, (ps_b[b0],), (Eb_b[e][0],), scale=0.125)
                        act(Eb[e][1], ps(b0 + 1), AF.Exp, (ps_b[b0 + 1],), (Eb_b[e][1],), scale=0.125)
                        fifo.append((e, kc))
                    if kc == 2 and pendP2[0] is not None:
                        pendP2[0]()
                        pendP2[0] = None
                    if kc == min(6, NKB + 1) and pendP3[0] is not None:
                        pendP3[0]()
                        pendP3[0] = None
                    if kc >= 2:
                        e_, kc_ = fifo.pop(0)
                        for c in range(2):
                            for j in range(4):
                                s = c * 4 + j
                                ob = 4 + s // 3
                                oc = (s % 3) * 129
                                mm(ps(ob)[:, oc:oc + 129], Eb[e_][c][:, j * 128:(j + 1) * 128], Vh[i][:, kc_, 0:129],
                                   kc_ == 0 and s % 3 == 0, kc_ == NKB - 1, (Eb_b[e_][c], qkv_b[i]), (ps_b[ob],), skip=True)
                flush_post()
                P1, P2, P3 = make_post(h, qb, q0)
                P1()
                pendP2[0] = P2
                pendP3[0] = P3
        flush_post()
        for j_ in cast_jobs:
            j_()
        sd.fence()

    def phaseB2(l, S):
        ar.reset()
        sd.set_pool("B2")
        NH = 2 if S >= 1024 else 1
        HS = S // NH
        xpad = ar.alloc([S + 8], F32); xpad_b = Buf("B2.xpad")
        gy = ar.alloc([S], BF16); gy_b = Buf("B2.gy")
        xc = ar.alloc([S], F32); xc_b = Buf("B2.xc")
        xcb = ar.alloc([S], BF16); xcb_b = Buf("B2.xcb")
        A_ = [ar.alloc([S], F32) for _ in range(2)]; A_b = [[Buf("B2.A") for _ in range(NH)] for _ in range(2)]
        B_ = [ar.alloc([S], F32) for _ in range(2)]; B_b = [[Buf("B2.B") for _ in range(NH)] for _ in range(2)]
        T_ = [ar.alloc([S], F32) for _ in range(2)]; T_b = [[Buf("B2.T") for _ in range(NH)] for _ in range(2)]
        rec = ar.alloc([S], BF16); rec_b = Buf("B2.rec")
        memset("dve", xpad[:, 0:2], 0.0, (xpad_b,))
        memset("dve", xpad[:, S + 2:S + 8], 0.0, (xpad_b,))
        kbs = [0]

        def rev(a):
            base = a
            apl = [list(x) for x in base.ap]
            n = apl[-1][1]
            apl[-1] = [-1, n]
            return AP(base.tensor, base.offset + (n - 1), apl)

        def scan_op(out, a0, a1, init):
            return lambda e: e.tensor_tensor_scan(out, a0, a1, init, ALU.mult, ALU.add)
        sl = [slice(hx * HS, (hx + 1) * HS) for hx in range(NH)]

        def dir_steps(c, d):
            ci = d * 8 + c
            order = list(range(NH)) if d == 0 else list(range(NH - 1, -1, -1))
            A, B, T = A_[d], B_[d], T_[d]
            Ab, Bb, Tb = A_b[d], B_b[d], T_b[d]

            def gate(gt, dst, dstb):
                def f():
                    for blk in range(S // TT):
                        hh_ = (blk * TT) // HS
                        b = kbs[0] % 8
                        kbs[0] += 1
                        idx = (d * 2 + gt) * 8 + c
                        mm(ps(b), rgw[:, idx, :], xcb[:, blk * TT:(blk + 1) * TT], True, True, (xcb_b, par_b), (ps_b[b],))
                        act(dst[:, blk * TT:(blk + 1) * TT], ps(b), AF.Tanh, (ps_b[b], par_b), (dstb[hh_],),
                            scale=0.5, bias=rgbh[:, idx:idx + 1])
                return f

            def s2():
                pass

            def s3():
                for hx in order:
                    act(T[:, sl[hx]], A[:, sl[hx]], AF.Tanh, (Ab[hx], par_b), (Tb[hx],),
                        scale=cLh[:, ci:ci + 1], bias=cLh[:, ci:ci + 1])
                    act(B[:, sl[hx]], A[:, sl[hx]], AF.Exp, (Ab[hx], par_b), (Bb[hx],),
                        scale=cL2[:, ci:ci + 1], bias=cL2[:, ci:ci + 1])
                    act(A[:, sl[hx]], A[:, sl[hx]], AF.Exp, (Ab[hx], par_b), (Ab[hx],),
                        scale=cLh[:, ci:ci + 1], bias=cLh[:, ci:ci + 1])

            def s4():
                for hx in order:
                    stt("dve", B[:, sl[hx]], B[:, sl[hx]], 1.0, T[:, sl[hx]], ALU.add, ALU.mult,
                        (Bb[hx], Tb[hx]), (Bb[hx],))

            def s6():
                for hx in order:
                    act(B[:, sl[hx]], B[:, sl[hx]], AF.Sqrt, (Bb[hx],), (Bb[hx],), scale=-1.0)

            def s7():
                for hx in order:
                    stt("dve", B[:, sl[hx]], T[:, sl[hx]], 1.0, B[:, sl[hx]], ALU.add, ALU.mult,
                        (Tb[hx], Bb[hx]), (Bb[hx],))
                    stt("dve", B[:, sl[hx]], B[:, sl[hx]], 0.5, xc[:, sl[hx]], ALU.mult, ALU.mult,
                        (Bb[hx], xc_b), (Bb[hx],))

            def s8():
                prev = None
                for hx in order:
                    if d == 0:
                        init = 0.0 if prev is None else T[:, prev * HS + HS - 1:prev * HS + HS]
                        rd = (Ab[hx], Bb[hx]) + (() if prev is None else (Tb[prev],))
                        sd.add("dve", scan_op(T[:, sl[hx]], A[:, sl[hx]], B[:, sl[hx]], init), rd, (Tb[hx],))
                    else:
                        init = 0.0 if prev is None else T[:, prev * HS:prev * HS + 1]
                        rd = (Ab[hx], Bb[hx]) + (() if prev is None else (Tb[prev],))
                        sd.add("dve", scan_op(rev(T[:, sl[hx]]), rev(A[:, sl[hx]]), rev(B[:, sl[hx]]), init),
                               rd, (Tb[hx],))
                    prev = hx
            return [gate(0, A, Ab), s2, s3, s4, gate(1, T, Tb), s6, s7, s8]

        dma(Q_IO, xpad[:, 2:2 + S], xrT[0:128, 0:S], xpad_b, (), (xpad_b,))
        for c in range(8):
            dma(Q_IO, gy, gyT[c * 128:(c + 1) * 128, 0:S], gy_b, (), (gy_b,))
            ts("dve", xc, xpad[:, 0:S], convw[:, c * 4:c * 4 + 1], convb[:, c:c + 1], ALU.mult, ALU.add,
               (xpad_b, par_b), (xc_b,))
            for j in range(1, 4):
                stt("dve", xc, xpad[:, j:j + S], convw[:, c * 4 + j:c * 4 + j + 1], xc, ALU.mult, ALU.add,
                    (xpad_b, xc_b, par_b), (xc_b,))
            cp("act", xcb, xc, (xc_b,), (xcb_b,))
            if c + 1 < 8:
                dma(Q_IO, xpad[:, 2:2 + S], xrT[(c + 1) * 128:(c + 2) * 128, 0:S], xpad_b, (), (xpad_b,))
            st0 = dir_steps(c, 0)
            st1 = dir_steps(c, 1)
            for f0, f1 in zip(st0, st1):
                f0()
                f1()
            for hx in range(NH):
                s_ = sl[hx]
                tt("pool", T_[0][:, s_], T_[0][:, s_], T_[1][:, s_], ALU.add, (T_b[0][hx], T_b[1][hx]), (T_b[0][hx],))
                stt("dve", rec[:, s_], T_[0][:, s_], 0.5, gy[:, s_], ALU.mult, ALU.mult, (T_b[0][hx], gy_b), (rec_b,))
            dma(Q_IO, recT[c * 128:(c + 1) * 128, 0:S], rec, rec_b, (rec_b,), ())
        sd.fence()

    def phaseC(l, S, key, last):
        ar.reset()
        sd.set_pool("C")
        x = ar.alloc([8, TT], F32); x_b = Buf("C.x")
        att = ar.alloc([8, TT], BF16); att_b = Buf("C.att")
        rec = ar.alloc([8, TT], BF16); rec_b = Buf("C.rec")
        th = [ar.alloc([8, TT], BF16) for _ in range(2)]; th_b = [Buf("C.th0"), Buf("C.th1")]
        m0 = [ar.alloc([TT], F32) for _ in range(2)]; m0_b = [Buf("C.m0") for _ in range(2)]
        m1 = [ar.alloc([TT], F32) for _ in range(2)]; m1_b = [Buf("C.m1") for _ in range(2)]
        mg = ar.alloc([8, TT], BF16); mg_b = Buf("C.mg")
        sq = ar.alloc([8, TT], BF16); sq_b = Buf("C.sq")
        rstd = ar.alloc([TT], F32); rstd_b = Buf("C.rstd")
        n2 = ar.alloc([8, TT], BF16); n2_b = Buf("C.n2")
        rl = [ar.alloc([TT], F32) for _ in range(3)]; rl_b = [Buf("C.rl") for _ in range(3)]
        hh = ar.alloc([32, TT], BF16); hh_b = Buf("C.h")
        yt = [ar.alloc([D], F32) for _ in range(2)]; yt_b = [Buf("C.yt0"), Buf("C.yt1")]
        xTv = xT.rearrange("(c p) s -> p c s", p=128)
        attTv = attT.rearrange("(c p) s -> p c s", p=128)
        recTv = recT.rearrange("(c p) s -> p c s", p=128)
        thTv = thT.rearrange("(c p) s -> p c s", p=128)
        tiles = []
        for ti in range(S // TT):
            tiles += [(l, t) for t in range(14, 36)]
        ws = WStream(tiles)
        kb = [0]

        def nb():
            b = kb[0] % 8
            kb[0] += 1
            return b
        km = 0
        kr = 0
        ky = 0
        def loadsC(ti):
            t0 = ti * TT
            dma(Q_IO, att, attTv[:, :, t0:t0 + TT], att_b, (), (att_b,))
            dma(Q_IO, rec, recTv[:, :, t0:t0 + TT], rec_b, (), (rec_b,))
            dma(Q_IO, th[0], thTv[:, 0:8, t0:t0 + TT], th_b[0], (), (th_b[0],))
            dma(Q_IO, th[1], thTv[:, 8:16, t0:t0 + TT], th_b[1], (), (th_b[1],))

        loadsC(0)
        for ti in range(S // TT):
            t0 = ti * TT
            dma(Q_IO, x, xTv[:, :, t0:t0 + TT], x_b, (), (x_b,))
            for half in range(2):
                wa, wa_b = ws.next()
                wr, wr_b = ws.next()
                wav = wa.rearrange("p (c n) -> p c n", n=512)
                wrv = wr.rearrange("p (c n) -> p c n", n=512)
                for j in range(4):
                    oc = half * 4 + j
                    bA = nb()
                    for c in range(8):
                        mm(ps(bA), wav[:, c, j * 128:(j + 1) * 128], att[:, c, :], c == 0, c == 7, (att_b, wa_b), (ps_b[bA],))
                    bR = nb()
                    for c in range(8):
                        mm(ps(bR), wrv[:, c, j * 128:(j + 1) * 128], rec[:, c, :], c == 0, c == 7, (rec_b, wr_b), (ps_b[bR],))
                    i = km % 2
                    km += 1
                    stt("dve", m0[i], th[0][:, oc, :], 1.0, ps(bA), ALU.add, ALU.mult, (th_b[0], ps_b[bA]), (m0_b[i],))
                    stt("dve", m1[i], th[1][:, oc, :], 1.0, ps(bR), ALU.add, ALU.mult, (th_b[1], ps_b[bR]), (m1_b[i],))
                    tt("pool", mg[:, oc, :], m0[i], m1[i], ALU.add, (m0_b[i], m1_b[i]), (mg_b,))
                ws.done_one()
                ws.done_one()
            for half in range(2):
                wo, wo_b = ws.next()
                wov = wo.rearrange("p (c n) -> p c n", n=512)
                for j in range(4):
                    oc = half * 4 + j
                    b = nb()
                    for c in range(8):
                        mm(ps(b), wov[:, c, j * 128:(j + 1) * 128], mg[:, c, :], c == 0, c == 7, (mg_b, wo_b), (ps_b[b],))
                    stt("dve", x[:, oc, :], ps(b), 0.5, x[:, oc, :], ALU.mult, ALU.add, (ps_b[b], x_b), (x_b,))
                ws.done_one()
            act(sq, x, AF.Square, (x_b,), (sq_b,))
            b = nb()
            for c in range(8):
                mm(ps(b), onesfull, sq[:, c, :], c == 0, c == 7, (sq_b, const_b), (ps_b[b],))
            act(rstd, ps(b), AF.Ln, (ps_b[b],), (rstd_b,), bias=EPS)
            act(rstd, rstd, AF.Exp, (rstd_b,), (rstd_b,), scale=-0.5)
            for c in range(8):
                stt("dve", n2[:, c, :], x[:, c, :], g2[:, c:c + 1], rstd, ALU.mult, ALU.mult, (x_b, rstd_b, par_b), (n2_b,))
            for g in range(8):
                w1t, w1_b = ws.next()
                w1v = w1t.rearrange("p (c n) -> p c n", n=512)
                for j in range(4):
                    f = g * 4 + j
                    b = nb()
                    for c in range(8):
                        mm(ps(b), w1v[:, c, j * 128:(j + 1) * 128], n2[:, c, :], c == 0, c == 7, (n2_b, w1_b), (ps_b[b],))
                    i = kr % 3
                    kr += 1
                    act(rl[i], ps(b), AF.Relu, (ps_b[b],), (rl_b[i],))
                    tt("pool", hh[:, f, :], rl[i], rl[i], ALU.mult, (rl_b[i],), (hh_b,))
                ws.done_one()
            if ti + 1 < S // TT:
                loadsC(ti + 1)
            for oc in range(8):
                w2t, w2_b = ws.next()
                w2v = w2t.rearrange("p (k n) -> p k n", n=128)
                b = nb()
                for kf in range(32):
                    mm(ps(b), w2v[:, kf, :], hh[:, kf, :], kf == 0, kf == 31, (hh_b, w2_b), (ps_b[b],))
                tt("dve", x[:, oc, :], ps(b), x[:, oc, :], ALU.add, (ps_b[b], x_b), (x_b,))
                ws.done_one()
            if not last:
                dma(Q_IO, xTv[:, :, t0:t0 + TT], x, x_b, (x_b,), ())
            else:
                for tb in range(4):
                    i = ky % 2
                    ky += 1
                    for hv in range(2):
                        b = nb()
                        for c4 in range(4):
                            c = hv * 4 + c4
                            tr(ps(b)[:, c4 * 128:(c4 + 1) * 128], x[:, c, tb * 128:(tb + 1) * 128], ident_f,
                               (x_b, const_b), (ps_b[b],))
                        cp("act" if hv else "dve", yt[i][:, hv * 512:(hv + 1) * 512], ps(b), (ps_b[b],), (yt_b[i],))
                    dma(Q_IO, y_out[key][t0 + tb * 128:t0 + (tb + 1) * 128, :], yt[i], yt_b[i], (yt_b[i],), ())
        sd.fence()

    for key, S in (("p", SP), ("s", SS)):
        cur["key"] = key
        phase0(key, S)
        for l in range(L):
            layer_params(l)
            phaseA(l, S)
            cast_jobs = cast_layer(l + 1, lazy=True) if (key == "p" and l + 1 < L) else []
            phaseB1(l, S, cast_jobs)
            phaseB2(l, S)
            phaseC(l, S, key, l == L - 1)

    n_sems = {e: max(1, (len(sd.ops[e]) + SEM_LIMIT - 1) // SEM_LIMIT) for e in Sched.ENGS}
    for e in Sched.ENGS:
        k = 0
        for op in sd.ops[e]:
            if op.need_inc and op.dma_sem is None:
                k += 1
                op.idx = k
    import contextlib
    with contextlib.ExitStack() as st:
        eng_sems = {}
        for e in Sched.ENGS:
            cnt = sum(1 for op in sd.ops[e] if op.idx is not None)
            ns = max(1, (cnt + SEM_LIMIT - 1) // SEM_LIMIT)
            eng_sems[e] = [st.enter_context(nc.semaphore("se_%s_%d" % (e, i))) for i in range(ns)]
        for i, b in enumerate(sd.slots):
            b.sem = st.enter_context(nc.semaphore("sd_%d" % i))
        block = st.enter_context(nc.Block())

        def resolve(d):
            if d[0] == "dma":
                return d[1].sem, d[2], None
            op = d[1]
            ep = (op.idx - 1) // SEM_LIMIT
            return eng_sems[op.eng][ep], (op.idx - 1) % SEM_LIMIT + 1, op.eng

        def emit(e, h):
            waited = {}
            for op in sd.ops[e]:
                for d in op.deps:
                    if d[0] == "eng":
                        if d[1].idx is None:
                            continue
                        if d[1].eng == e and (e == "pe" or not SAME_ENG_SYNC):
                            continue
                    sem, val, _ = resolve(d)
                    key_ = id(sem)
                    if waited.get(key_, 0) >= val:
                        continue
                    waited[key_] = val
                    h.wait_ge(sem, val)
                ins = op.fn(h)
                if op.dma_sem is not None:
                    ins.then_inc(op.dma_sem.sem, 16)
                elif op.idx is not None:
                    ep = (op.idx - 1) // SEM_LIMIT
                    ins.then_inc(eng_sems[e][ep], 1)

        @block.tensor
        def _(h):
            emit("pe", h)

        @block.scalar
        def _(h):
            emit("act", h)

        @block.vector
        def _(h):
            emit("dve", h)

        @block.gpsimd
        def _(h):
            emit("pool", h)

        @block.sync
        def _(h):
            emit("sp", h)
    stats = {e: len(sd.ops[e]) for e in Sched.ENGS}
    return nc, stats


def _host_layout(inp, L, S_MAX):
    f = np.float32

    def fm(v, nch):
        return np.ascontiguousarray(np.asarray(v, f).reshape(nch, 128).T)
    out = {}
    out["g1"] = np.stack([fm(inp["norm1_g"][l], 8) for l in range(L)])
    out["g2"] = np.stack([fm(inp["norm2_g"][l], 8) for l in range(L)])
    out["bgate"] = np.stack([fm(inp["b_gate"][l], 16) for l in range(L)])
    qkg = np.zeros((L, 128, 2), f)
    for l in range(L):
        qkg[l, :, 0] = np.tile(np.asarray(inp["q_norm_g"][l], f), 2)
        qkg[l, :, 1] = np.tile(np.asarray(inp["k_norm_g"][l], f), 2)
    out["qkg"] = qkg
    out["lamv"] = np.ascontiguousarray(np.broadcast_to(np.asarray(inp["lam_vecs"], f)[:L].reshape(L, 1, 256), (L, 128, 256)))
    out["subg"] = np.ascontiguousarray(np.broadcast_to(np.asarray(inp["subln_g"], f)[:L].reshape(L, 1, 128), (L, 128, 128)))
    cw = np.asarray(inp["conv_w"], f)[:L]
    convw = np.zeros((L, 128, 8, 4), f)
    for l in range(L):
        for j in range(4):
            convw[l, :, :, j] = fm(cw[l, j], 8)
    out["convw"] = convw.reshape(L, 128, 32)
    out["convb"] = np.stack([fm(inp["conv_b"][l], 8) for l in range(L)])
    rb = np.asarray(inp["rg_b"], f)[:L]
    rgb = np.zeros((L, 128, 2, 2, 8), f)
    for l in range(L):
        for d in range(2):
            for g in range(2):
                rgb[l, :, d, g, :] = fm(rb[l, d, g], 8)
    out["rgb"] = rgb.reshape(L, 128, 32)
    rL = np.asarray(inp["rg_L"], f)[:L]
    rgL = np.zeros((L, 128, 2, 8), f)
    for l in range(L):
        for d in range(2):
            rgL[l, :, d, :] = fm(rL[l, d], 8)
    out["rgL"] = rgL.reshape(L, 128, 16)
    rw = np.asarray(inp["rg_w"], f)[:L]
    rgw = np.zeros((L, 2, 2, 8, 128, 128), f)
    for c in range(8):
        rgw[:, :, :, c, 0:64, 0:64] = rw[:, :, :, 2 * c]
        rgw[:, :, :, c, 64:128, 64:128] = rw[:, :, :, 2 * c + 1]
    out["rgw"] = rgw.reshape(L, 32, 128, 128)
    out["ident"] = np.eye(128, dtype=f)
    obd = np.zeros((128, 128), f)
    obd[0:64, 0:64] = 1.0 / 64.0
    obd[64:128, 64:128] = 1.0 / 64.0
    out["onesbd"] = obd
    out["onesfull"] = np.full((128, 128), 1.0 / 1024.0, f)
    pm = np.zeros((128, 128), f)
    cosT = np.ones((128, S_MAX), f)
    sinT = np.zeros((128, S_MAX), f)
    pos = np.arange(S_MAX, dtype=f)
    inv_freq = (np.float32(500000.0) ** (-np.arange(0, 16, 2, dtype=f) / np.float32(16))).astype(f)
    ang = (pos[:, None] * inv_freq[None, :]).astype(f)
    cs = np.cos(ang).astype(f).T
    sn = np.sin(ang).astype(f).T
    for gb in (0, 64):
        for m in range(8):
            pm[gb + m + 8, gb + m] = 1.0
            pm[gb + m, gb + m + 8] = 1.0
            cosT[gb + m] = cs[m]
            cosT[gb + m + 8] = cs[m]
            sinT[gb + m] = -sn[m]
            sinT[gb + m + 8] = sn[m]
    out["perm"] = pm
    out["cosT"] = cosT
    out["sinT"] = sinT
    return out


_CACHE = {}


def run(inputs, L=4, n_cores=8):
    xp = np.asarray(inputs["x_prompt"], np.float32)
    xs = np.asarray(inputs["x_sample"], np.float32)
    SP, SS = xp.shape[1], xs.shape[1]
    keyc = (SP, SS, L)
    if keyc not in _CACHE:
        _CACHE[keyc] = build_program(SP, SS, L)
    nc, stats = _CACHE[keyc]
    lay = _host_layout(inputs, L, max(SP, SS))
    shared = {
        "w_in": np.ascontiguousarray(np.asarray(inputs["w_in"], np.float32)[:L]),
        "wba": np.ascontiguousarray(np.asarray(inputs["w_branch_att"], np.float32)[:L]),
        "wbr": np.ascontiguousarray(np.asarray(inputs["w_branch_rec"], np.float32)[:L]),
        "wout": np.ascontiguousarray(np.asarray(inputs["w_out"], np.float32)[:L]),
        "w1": np.ascontiguousarray(np.asarray(inputs["w_ff1"], np.float32)[:L]),
        "w2": np.ascontiguousarray(np.asarray(inputs["w_ff2"], np.float32)[:L]),
    }
    shared.update(lay)
    in_maps = []
    for i in range(n_cores):
        m = dict(shared)
        m["xp"] = np.ascontiguousarray(xp[i])
        m["xs"] = np.ascontiguousarray(xs[i])
        in_maps.append(m)
    res = run_bass_kernel_spmd(nc, in_maps, core_ids=list(range(n_cores)))
    if DEBUG:
        _CACHE["dbg"] = res.results
    yp = np.stack([np.asarray(r["yp"], np.float32) for r in res.results])
    ys = np.stack([np.asarray(r["ys"], np.float32) for r in res.results])
    return yp, ys


def kernel(**inputs):
    return run(inputs, L=4, n_cores=8)
```

```python
import math
import numpy as np
import concourse.bass as bass
import concourse.mybir as mybir
from concourse.bass_utils import run_bass_kernel_spmd
from concourse.ap import AP

F32 = mybir.dt.float32
BF16 = mybir.dt.bfloat16
ALU = mybir.AluOpType
AF = mybir.ActivationFunctionType
AX = mybir.AxisListType

D = 1024
DC = 8
D_IN = 7168
D_FF = 4096
EPS = 1e-6
NT_W = 36
TT = 512
SEM_LIMIT = 30000
SAME_ENG_SYNC = True
Q_IO = "pool"
Q_W = "sp"
NW = 4
DEBUG = False


class Slot:
    __slots__ = ("cnt", "sem")

    def __init__(self):
        self.cnt = 0
        self.sem = None


class Buf:
    __slots__ = ("name", "w", "r", "slot", "glob", "nofence")

    def __init__(self, name, glob=False, nofence=False):
        self.name = name
        self.w = None
        self.r = {}
        self.slot = None
        self.glob = glob
        self.nofence = nofence


class Op:
    __slots__ = ("eng", "fn", "deps", "need_inc", "idx", "dma_sem", "ev")


class Sched:
    ENGS = ("pe", "act", "dve", "pool", "sp")

    def __init__(self):
        self.ops = {e: [] for e in self.ENGS}
        self.dirty = {}
        self.slots = []
        self.pools = {}
        self.cur_pool = None
        self.cur_idx = 0

    def set_pool(self, name):
        self.cur_pool = name
        self.cur_idx = 0

    def get_slot(self, buf):
        if buf.slot is None:
            if buf.glob or self.cur_pool is None:
                sl = Slot()
                self.slots.append(sl)
                buf.slot = sl
            else:
                pool = self.pools.setdefault(self.cur_pool, [])
                if self.cur_idx >= len(pool):
                    sl = Slot()
                    pool.append(sl)
                    self.slots.append(sl)
                buf.slot = pool[self.cur_idx]
                self.cur_idx += 1
        return buf.slot

    def add(self, eng, fn, reads=(), writes=(), dma=None, extra_deps=()):
        op = Op()
        op.eng = eng
        op.fn = fn
        op.need_inc = False
        op.idx = None
        op.dma_sem = None
        deps = list(extra_deps)
        war = []
        for b in reads:
            if b.w is not None:
                deps.append(b.w)
        for b in writes:
            if b.w is not None:
                deps.append(b.w)
            for rv in b.r.values():
                if rv[0] == "eng" and rv[1].eng == eng:
                    continue
                deps.append(rv)
        if dma is not None:
            sl = self.get_slot(dma)
            sl.cnt += 16
            ev = ("dma", sl, sl.cnt)
            op.dma_sem = sl
            if not dma.nofence:
                self.dirty[id(sl)] = ev
            rkey = ("dma", id(sl))
        else:
            ev = ("eng", op)
            rkey = ("eng", eng)
        op.ev = ev
        for d in deps:
            if d[0] == "eng":
                d[1].need_inc = True
        op.deps = deps
        for b in reads:
            b.r[rkey] = ev
        for b in writes:
            b.w = ev
            b.r = {}
        self.ops[eng].append(op)
        return op

    def fence(self):
        deps = []
        for e in self.ENGS:
            if e == "sp":
                continue
            for op in reversed(self.ops[e]):
                if op.dma_sem is None:
                    deps.append(op.ev)
                    break
        deps.extend(self.dirty.values())
        self.dirty = {}
        f = self.add("sp", lambda e: e.nop(), extra_deps=deps)
        for e in self.ENGS:
            if e != "sp":
                self.add(e, lambda en: en.nop(), extra_deps=[f.ev])
        return f


def build_program(SP, SS, L, n_layers_lam_off=0):
    nc = bass.Bass("TRN2", target_bir_lowering=False)
    S_MAX = max(SP, SS)
    sd = Sched()

    def din(name, shape, dt=F32):
        return nc.dram_tensor(name, list(shape), dt, kind="ExternalInput").ap()

    def dscr(name, shape, dt):
        if DEBUG and name != "wsc":
            return nc.dram_tensor(name, list(shape), dt, kind="ExternalOutput").ap()
        return nc.dram_tensor(name, list(shape), dt, kind="Internal").ap()

    x_in = {"p": din("xp", [SP, D]), "s": din("xs", [SS, D])}
    y_out = {"p": nc.dram_tensor("yp", [SP, D], F32, kind="ExternalOutput").ap(),
             "s": nc.dram_tensor("ys", [SS, D], F32, kind="ExternalOutput").ap()}
    w_in = din("w_in", [L, D, D_IN])
    wba = din("wba", [L, D, D])
    wbr = din("wbr", [L, D, D])
    wout = din("wout", [L, D, D])
    w1 = din("w1", [L, D, D_FF])
    w2 = din("w2", [L, D_FF, D])
    g1_d = din("g1", [L, 128, 8])
    g2_d = din("g2", [L, 128, 8])
    bgate_d = din("bgate", [L, 128, 16])
    qkg_d = din("qkg", [L, 128, 2])
    lamv_d = din("lamv", [L, 128, 256])
    subg_d = din("subg", [L, 128, 128])
    convw_d = din("convw", [L, 128, 32])
    convb_d = din("convb", [L, 128, 8])
    rgb_d = din("rgb", [L, 128, 32])
    rgL_d = din("rgL", [L, 128, 16])
    rgw_d = din("rgw", [L, 32, 128, 128])
    ident_d = din("ident", [128, 128])
    onesbd_d = din("onesbd", [128, 128])
    onesfull_d = din("onesfull", [128, 128])
    perm_d = din("perm", [128, 128])
    cos_d = din("cosT", [128, S_MAX])
    sin_d = din("sinT", [128, S_MAX])

    wsc = dscr("wsc", [L, NT_W, 128, 4096], BF16)
    xT = dscr("xT", [D, S_MAX], F32)
    qT = dscr("qT", [D, S_MAX], BF16)
    kT = dscr("kT", [D, S_MAX], BF16)
    Vs = dscr("Vs", [S_MAX, D], BF16)
    xrT = dscr("xrT", [D, S_MAX], F32)
    gyT = dscr("gyT", [D, S_MAX], BF16)
    thT = dscr("thT", [2 * D, S_MAX], BF16)
    attT = dscr("attT", [D, S_MAX], BF16)
    recT = dscr("recT", [D, S_MAX], BF16)

    ARENA_BYTES = 158 * 1024
    arena = nc.alloc_sbuf_tensor("arena", [128, ARENA_BYTES // 4], F32)
    arena_ap = arena[:] if not isinstance(arena, AP) else arena
    psum = nc.alloc_psum_tensor("psum", [128, 8, 512], F32)
    psum_ap = psum[:] if not isinstance(psum, AP) else psum

    class Arena:
        def __init__(self):
            self.off = 0

        def reset(self):
            self.off = 0

        def alloc(self, shape, dt):
            n = 1
            for s in shape:
                n *= s
            nbytes = n * (4 if dt == F32 else 2)
            nbytes = (nbytes + 31) // 32 * 32
            assert self.off + nbytes <= ARENA_BYTES, ("arena overflow", self.off, nbytes)
            a = arena_ap[:, self.off // 4:(self.off + nbytes) // 4]
            self.off += nbytes
            if dt != F32:
                a = a.bitcast(dt)
            a = a[:, 0:n]
            if len(shape) == 2:
                a = a.rearrange("p (a b) -> p a b", b=shape[1])
            elif len(shape) == 3:
                a = a.rearrange("p (a b c) -> p a b c", b=shape[1], c=shape[2])
            return a

    ar = Arena()

    def sb(name, shape, dt):
        t = nc.alloc_sbuf_tensor(name, [128] + list(shape), dt)
        return t[:] if not isinstance(t, AP) else t

    ident_f = sb("ident_f", [128], F32)
    identb = sb("identb", [128], BF16)
    onesbd = sb("onesbd_s", [128], BF16)
    onesfull = sb("onesfull_s", [128], BF16)
    perm = sb("perm_s", [128], BF16)
    const_b = Buf("consts", glob=True)
    g1 = sb("g1_s", [8], F32)
    g2 = sb("g2_s", [8], F32)
    bgh = sb("bgh_s", [16], F32)
    qkg = sb("qkg_s", [2], F32)
    lamv = sb("lamv_s", [256], F32)
    lamw = sb("lamw_s", [128], F32)
    lams = sb("lams_s", [8], F32)
    subg = sb("subg_s", [128], F32)
    convw = sb("convw_s", [32], F32)
    convb = sb("convb_s", [8], F32)
    rgbh = sb("rgbh_s", [32], F32)
    rgL = sb("rgL_s", [16], F32)
    cLh = sb("cLh_s", [16], F32)
    cL2 = sb("cL2_s", [16], F32)
    rgw = sb("rgw_s", [32, 128], BF16)
    par_b = Buf("params", glob=True)
    rgw_b = Buf("rgw", glob=True)
    wring = [sb("wring%d" % i, [4096], BF16) for i in range(NW)]
    wring_b = [Buf("wring%d" % i, glob=True) for i in range(NW)]
    ps_b = [Buf("ps%d" % i) for i in range(8)]

    def ps(i):
        return psum_ap[:, i, :]

    def mm(out, lhsT, rhs, start, stop, reads, writes, skip=False):
        if skip:
            return sd.add("pe", lambda e: e.matmul(out, lhsT, rhs, start=start, stop=stop, skip_group_check=True), reads, writes)
        return sd.add("pe", lambda e: e.matmul(out, lhsT, rhs, start=start, stop=stop), reads, writes)

    def tr(out, in_, ident, reads, writes):
        return sd.add("pe", lambda e: e.transpose(out, in_, ident), reads, writes)

    def act(out, in_, func, reads, writes, scale=1.0, bias=0.0, accum=None):
        def f(e):
            if accum is not None:
                return e.activation(out=out, in_=in_, func=func, bias=bias, scale=scale, accum_out=accum)
            return e.activation(out=out, in_=in_, func=func, bias=bias, scale=scale)
        return sd.add("act", f, reads, writes)

    def stt(eng, out, in0, scalar, in1, op0, op1, reads, writes):
        return sd.add(eng, lambda e: e.scalar_tensor_tensor(out, in0, scalar, in1, op0, op1), reads, writes)

    def ts(eng, out, in0, s1, s2, op0, op1, reads, writes):
        if s2 is None:
            return sd.add(eng, lambda e: e.tensor_scalar(out, in0, s1, None, op0), reads, writes)
        return sd.add(eng, lambda e: e.tensor_scalar(out, in0, s1, s2, op0, op1), reads, writes)

    def tt(eng, out, in0, in1, op, reads, writes):
        return sd.add(eng, lambda e: e.tensor_tensor(out, in0, in1, op), reads, writes)

    def cp(eng, out, in_, reads, writes):
        if eng == "act":
            return sd.add("act", lambda e: e.copy(out, in_), reads, writes)
        return sd.add(eng, lambda e: e.tensor_copy(out, in_), reads, writes)

    def dma(q, out, in_, owner, reads=(), writes=()):
        return sd.add(q, lambda e: e.dma_start(out=out, in_=in_), reads, writes, dma=owner)

    def memset(eng, ap, val, writes):
        return sd.add(eng, lambda e: e.memset(ap, val), (), writes)

    dbg_owner = Buf("dbg", glob=True)
    cur = {"key": None}

    def dump(name, ap, reads):
        if not DEBUG or cur["key"] != "s":
            return
        shape = list(ap.shape)
        dt_ = nc.dram_tensor("dbg_" + name, shape, ap.dtype, kind="ExternalOutput").ap()
        dma(Q_IO, dt_, ap, dbg_owner, reads, ())

    wstate = {"n": 0}

    def wload(l, t):
        i = wstate["n"] % NW
        wstate["n"] += 1
        dma(Q_W, wring[i], wsc[l, t], wring_b[i], (cast_bs[l],), (wring_b[i],))
        return wring[i], wring_b[i]

    class WStream:
        def __init__(self, tiles):
            self.tiles = tiles
            self.q = []
            self.pos = 0
            for _ in range(min(NW - 1, len(tiles))):
                self._issue()

        def _issue(self):
            if self.pos < len(self.tiles):
                self.q.append(wload(*self.tiles[self.pos]))
                self.pos += 1

        def next(self):
            r = self.q.pop(0)
            return r

        def done_one(self):
            self._issue()

    dma("pool", ident_f, ident_d, const_b, (), (const_b,))
    dma("pool", identb, ident_d, const_b, (), (const_b,))
    dma("pool", onesbd, onesbd_d, const_b, (), (const_b,))
    dma("pool", onesfull, onesfull_d, const_b, (), (const_b,))
    dma("pool", perm, perm_d, const_b, (), (const_b,))
    cast_bs = [Buf("cast%d" % l, glob=True, nofence=True) for l in range(L)]

    def cast_layer(l, lazy=False):
        cast_b = cast_bs[l]

        def wview(t):
            return wsc[l, t].rearrange("p (c n) -> p c n", n=512)
        srcs = []
        win_v = w_in[l].rearrange("(c p) n -> p c n", p=128)
        for g in range(14):
            srcs.append(win_v[:, :, g * 512:(g + 1) * 512])
        a_v = wba[l].rearrange("(c p) n -> p c n", p=128)
        r_v = wbr[l].rearrange("(c p) n -> p c n", p=128)
        o_v = wout[l].rearrange("(c p) n -> p c n", p=128)
        srcs += [a_v[:, :, 0:512], r_v[:, :, 0:512], a_v[:, :, 512:1024], r_v[:, :, 512:1024],
                 o_v[:, :, 0:512], o_v[:, :, 512:1024]]
        w1_v = w1[l].rearrange("(c p) n -> p c n", p=128)
        for g in range(8):
            srcs.append(w1_v[:, :, g * 512:(g + 1) * 512])
        jobs = []
        for t, s in enumerate(srcs):
            jobs.append((lambda t=t, s=s: dma("pool", wview(t), s, cast_b, (), (cast_b,))))
        w2_v = w2[l].rearrange("(k p) n -> p k n", p=128)
        for oc in range(8):
            jobs.append((lambda oc=oc: dma("pool", wsc[l, 28 + oc].rearrange("p (k n) -> p k n", n=128),
                                           w2_v[:, :, oc * 128:(oc + 1) * 128], cast_b, (), (cast_b,))))
        if lazy:
            return jobs
        for j in jobs:
            j()
        return []

    sd.fence()
    cast_layer(0)

    def layer_params(l):
        lam_init = 0.8 - 0.6 * math.exp(-0.3 * l)
        pb = par_b
        for dst, src in ((g1, g1_d), (g2, g2_d), (bgh, bgate_d), (qkg, qkg_d), (lamv, lamv_d), (subg, subg_d),
                         (convw, convw_d), (convb, convb_d), (rgbh, rgb_d), (rgL, rgL_d)):
            dma("sp", dst, src[l], pb, (), (pb,))
        dma("pool", rgw, rgw_d[l].rearrange("t p n -> p t n"), rgw_b, (), (rgw_b,))
        ts("dve", bgh, bgh, 0.5, None, ALU.mult, None, (pb,), (pb,))
        ts("dve", rgbh, rgbh, 0.5, None, ALU.mult, None, (pb,), (pb,))
        ts("dve", subg, subg, 1.0 - lam_init, None, ALU.mult, None, (pb,), (pb,))
        tt("dve", lamw[:, 0:64], lamv[:, 0:64], lamv[:, 64:128], ALU.mult, (pb,), (pb,))
        tt("dve", lamw[:, 64:128], lamv[:, 128:192], lamv[:, 192:256], ALU.mult, (pb,), (pb,))
        sd.add("dve", lambda e: e.reduce_sum(lams[:, 0:1], lamw[:, 0:64], AX.X), (pb,), (pb,))
        sd.add("dve", lambda e: e.reduce_sum(lams[:, 1:2], lamw[:, 64:128], AX.X), (pb,), (pb,))
        act(lams[:, 2:4], lams[:, 0:2], AF.Exp, (pb,), (pb,))
        tt("dve", lams[:, 4:5], lams[:, 2:3], lams[:, 3:4], ALU.subtract, (pb,), (pb,))
        ts("dve", lams[:, 5:6], lams[:, 4:5], lam_init, -1.0, ALU.add, ALU.mult, (pb,), (pb,))
        act(cLh, rgL, AF.Exp, (pb,), (pb,), scale=-1.0)
        act(cLh, cLh, AF.Ln, (pb,), (pb,), bias=1.0)
        ts("dve", cLh, cLh, -4.0, None, ALU.mult, None, (pb,), (pb,))
        ts("dve", cL2, cLh, 2.0, None, ALU.mult, None, (pb,), (pb,))
        sd.fence()

    def phase0(key, S):
        ar.reset()
        sd.set_pool("P0")
        xin = [ar.alloc([4, D], F32) for _ in range(2)]
        xin_b = [Buf("xin0"), Buf("xin1")]
        xt = [ar.alloc([8, TT], F32) for _ in range(2)]
        xt_b = [Buf("xt0"), Buf("xt1")]
        xsrc = x_in[key]
        xTv = xT.rearrange("(c p) s -> p c s", p=128)
        k = 0
        for ti in range(S // TT):
            t0 = ti * TT
            i = ti % 2
            dma(Q_IO, xin[i], xsrc[t0:t0 + TT, :].rearrange("(tb p) d -> p tb d", p=128), xin_b[i], (), (xin_b[i],))
            for c in range(8):
                b = k % 8
                k += 1
                for tb in range(4):
                    tr(ps(b)[:, tb * 128:(tb + 1) * 128], xin[i][:, tb, c * 128:(c + 1) * 128], ident_f,
                       (xin_b[i], const_b), (ps_b[b],))
                cp("act" if c % 2 else "dve", xt[i][:, c, :], ps(b), (ps_b[b],), (xt_b[i],))
            dma(Q_IO, xTv[:, :, t0:t0 + TT], xt[i], xt_b[i], (xt_b[i],), ())
        sd.fence()

    def phaseA(l, S):
        ar.reset()
        sd.set_pool("A")
        xt = ar.alloc([8, TT], F32); xt_b = Buf("A.xt")
        cs = ar.alloc([2, TT], F32); cs_b = Buf("A.cs")
        sq = ar.alloc([8, TT], BF16); sq_b = Buf("A.sq")
        nT2 = [ar.alloc([8, TT], BF16) for _ in range(2)]; nT2_b = [Buf("A.nT0"), Buf("A.nT1")]
        rstd = ar.alloc([TT], F32); rstd_b = Buf("A.rstd")
        NB = 5
        sq2 = [ar.alloc([TT], BF16) for _ in range(NB)]; sq2_b = [Buf("A.sq2") for _ in range(NB)]
        r2 = [ar.alloc([TT], F32) for _ in range(NB)]; r2_b = [Buf("A.r2") for _ in range(NB)]
        qn = [ar.alloc([TT], F32) for _ in range(2)]; qn_b = [Buf("A.qn") for _ in range(2)]
        qnb = [ar.alloc([TT], BF16) for _ in range(NB)]; qnb_b = [Buf("A.qnb") for _ in range(NB)]
        t1 = [ar.alloc([TT], F32) for _ in range(NB)]; t1_b = [Buf("A.t1") for _ in range(NB)]
        tmp = [ar.alloc([TT], F32) for _ in range(NB)]; tmp_b = [Buf("A.tmp") for _ in range(NB)]
        qks = [ar.alloc([8, TT], BF16) for _ in range(2)]; qks_b = [Buf("A.qs"), Buf("A.ks")]
        vst = ar.alloc([4, D], BF16); vst_b = Buf("A.vs")
        xrs = ar.alloc([8, TT], F32); xrs_b = Buf("A.xrs")
        gys = ar.alloc([8, TT], BF16); gys_b = Buf("A.gys")
        ths = ar.alloc([16, TT], BF16); ths_b = Buf("A.ths")
        xTv = xT.rearrange("(c p) s -> p c s", p=128)
        qTv = [qT.rearrange("(c p) s -> p c s", p=128), kT.rearrange("(c p) s -> p c s", p=128)]
        xrTv = xrT.rearrange("(c p) s -> p c s", p=128)
        gyTv = gyT.rearrange("(c p) s -> p c s", p=128)
        thTv = thT.rearrange("(c p) s -> p c s", p=128)
        tiles = []
        for ti in range(S // TT):
            tiles += [(l, g) for g in range(14)]
        ws = WStream(tiles)
        kb = [0]

        def nb():
            b = kb[0] % 8
            kb[0] += 1
            return b
        kk = 0
        qk_pipe = []

        def qk_stage2(cx):
            b, i, which = cx["b"], cx["i"], cx["which"]
            b2 = nb()
            mm(ps(b2), onesbd, sq2[i], True, True, (sq2_b[i], const_b), (ps_b[b2],))
            act(r2[i], ps(b2), AF.Ln, (ps_b[b2],), (r2_b[i],), bias=EPS)
            act(r2[i], r2[i], AF.Exp, (r2_b[i],), (r2_b[i],), scale=-0.5)
            stt("dve", qnb[i], ps(b), qkg[:, which:which + 1], r2[i], ALU.mult, ALU.mult,
                (ps_b[b], r2_b[i], par_b), (qnb_b[i],))
            tt("pool", t1[i], qnb[i], cs[:, 0, :], ALU.mult, (qnb_b[i], cs_b), (t1_b[i],))
            cx["st"] = 2

        def qk_stage3(cx):
            i, which, h = cx["i"], cx["which"], cx["h"]
            b3 = nb()
            mm(ps(b3), perm, qnb[i], True, True, (qnb_b[i], const_b), (ps_b[b3],))
            tt("dve", tmp[i], ps(b3), cs[:, 1, :], ALU.mult, (ps_b[b3], cs_b), (tmp_b[i],))
            tt("dve", qks[which][:, h, :], t1[i], tmp[i], ALU.add, (t1_b[i], tmp_b[i]), (qks_b[which],))
            cx["st"] = 3

        def qk_advance(flush):
            while True:
                n = len(qk_pipe)
                if n >= 4 or (flush and n >= 1 and qk_pipe[0]["st"] == 2):
                    qk_stage3(qk_pipe.pop(0))
                    continue
                break
            for cx in qk_pipe:
                if cx["st"] == 1 and (flush or cx is not qk_pipe[-1]):
                    qk_stage2(cx)
            if flush:
                while qk_pipe:
                    cx = qk_pipe.pop(0)
                    if cx["st"] == 1:
                        qk_stage2(cx)
                    qk_stage3(cx)

        def prologue1(ti):
            t0 = ti * TT
            dma(Q_IO, xt, xTv[:, :, t0:t0 + TT], xt_b, (), (xt_b,))
            dma(Q_IO, cs[:, 0, :], cos_d[:, t0:t0 + TT], cs_b, (), (cs_b,))
            dma(Q_IO, cs[:, 1, :], sin_d[:, t0:t0 + TT], cs_b, (), (cs_b,))
            act(sq, xt, AF.Square, (xt_b,), (sq_b,))

        def prologue(ti):
            nT, nT_b = nT2[ti % 2], nT2_b[ti % 2]
            b = nb()
            for c in range(8):
                mm(ps(b), onesfull, sq[:, c, :], c == 0, c == 7, (sq_b, const_b), (ps_b[b],))
            act(rstd, ps(b), AF.Ln, (ps_b[b],), (rstd_b,), bias=EPS)
            act(rstd, rstd, AF.Exp, (rstd_b,), (rstd_b,), scale=-0.5)
            for c in range(8):
                stt("dve", nT[:, c, :], xt[:, c, :], g1[:, c:c + 1], rstd, ALU.mult, ALU.mult,
                    (xt_b, rstd_b, par_b), (nT_b,))

        prologue1(0)
        prologue(0)
        for ti in range(S // TT):
            t0 = ti * TT
            nT, nT_b = nT2[ti % 2], nT2_b[ti % 2]
            for g in range(14):
                wt, wt_b = ws.next()
                wv = wt.rearrange("p (c n) -> p c n", n=512)
                if g < 4:
                    which = g // 2
                    for j in range(4):
                        h = (g % 2) * 4 + j
                        b = nb()
                        for c in range(8):
                            mm(ps(b), wv[:, c, j * 128:(j + 1) * 128], nT[:, c, :], c == 0, c == 7,
                               (nT_b, wt_b), (ps_b[b],))
                        i = kk % NB
                        kk += 1
                        act(sq2[i], ps(b), AF.Square, (ps_b[b],), (sq2_b[i],))
                        qk_pipe.append({"b": b, "i": i, "which": which, "h": h, "st": 1})
                        qk_advance(False)
                    if g == 3:
                        qk_advance(True)
                elif g < 6:
                    half = g - 4
                    for tb in range(4):
                        b = nb()
                        for c in range(8):
                            mm(ps(b), nT[:, c, tb * 128:(tb + 1) * 128], wv[:, c, :], c == 0, c == 7,
                               (nT_b, wt_b), (ps_b[b],))
                        cp("act", vst[:, tb, half * 512:(half + 1) * 512], ps(b), (ps_b[b],), (vst_b,))
                elif g < 8:
                    for j in range(4):
                        cc = (g - 6) * 4 + j
                        b = nb()
                        for c in range(8):
                            mm(ps(b), wv[:, c, j * 128:(j + 1) * 128], nT[:, c, :], c == 0, c == 7,
                               (nT_b, wt_b), (ps_b[b],))
                        cp("act" if j % 2 else "dve", xrs[:, cc, :], ps(b), (ps_b[b],), (xrs_b,))
                elif g < 10:
                    for j in range(4):
                        cc = (g - 8) * 4 + j
                        b = nb()
                        for c in range(8):
                            mm(ps(b), wv[:, c, j * 128:(j + 1) * 128], nT[:, c, :], c == 0, c == 7,
                               (nT_b, wt_b), (ps_b[b],))
                        i = kk % NB
                        kk += 1
                        act(r2[i], ps(b), AF.Square, (ps_b[b],), (r2_b[i],))
                        ts("dve", r2[i], r2[i], 0.044715, 1.0, ALU.mult, ALU.add, (r2_b[i],), (r2_b[i],))
                        tt("dve", qn[i % 2], r2[i], ps(b), ALU.mult, (r2_b[i], ps_b[b]), (qn_b[i % 2],))
                        act(t1[i], qn[i % 2], AF.Tanh, (qn_b[i % 2],), (t1_b[i],), scale=0.7978845608028654)
                        stt("dve", gys[:, cc, :], t1[i], 1.0, ps(b), ALU.add, ALU.mult, (t1_b[i], ps_b[b]), (gys_b,))
                else:
                    for j in range(4):
                        gi = (g - 10) * 4 + j
                        b = nb()
                        for c in range(8):
                            mm(ps(b), wv[:, c, j * 128:(j + 1) * 128], nT[:, c, :], c == 0, c == 7,
                               (nT_b, wt_b), (ps_b[b],))
                        act(ths[:, gi, :], ps(b), AF.Tanh, (ps_b[b], par_b), (ths_b,), scale=0.5, bias=bgh[:, gi:gi + 1])
                ws.done_one()
                if g == 3:
                    dma(Q_IO, qTv[0][:, :, t0:t0 + TT], qks[0], qks_b[0], (qks_b[0],), ())
                    dma(Q_IO, qTv[1][:, :, t0:t0 + TT], qks[1], qks_b[1], (qks_b[1],), ())
                    if ti + 1 < S // TT:
                        prologue1(ti + 1)
                elif g == 5:
                    dma(Q_IO, Vs[t0:t0 + TT, :].rearrange("(tb p) d -> p tb d", p=128), vst, vst_b, (vst_b,), ())
                elif g == 7:
                    dma(Q_IO, xrTv[:, :, t0:t0 + TT], xrs, xrs_b, (xrs_b,), ())
                elif g == 9:
                    dma(Q_IO, gyTv[:, :, t0:t0 + TT], gys, gys_b, (gys_b,), ())
                elif g == 11:
                    if ti + 1 < S // TT:
                        prologue(ti + 1)
                elif g == 13:
                    dma(Q_IO, thTv[:, 0:8, t0:t0 + TT], ths[:, 0:8, :], ths_b, (ths_b,), ())
                    dma(Q_IO, thTv[:, 8:16, t0:t0 + TT], ths[:, 8:16, :], ths_b, (ths_b,), ())
        sd.fence()

    def phaseB1(l, S, cast_jobs=()):
        ar.reset()
        sd.set_pool("B1")
        NKB = S // 128
        NQB = S // TT
        qh = [ar.alloc([S], BF16) for _ in range(2)]
        kh = [ar.alloc([S], BF16) for _ in range(2)]
        Vh = [ar.alloc([NKB, 130], BF16) for _ in range(2)]
        qkv_b = [Buf("B1.qkv0"), Buf("B1.qkv1")]
        NE = 4
        Eb = [[ar.alloc([TT], BF16) for _ in range(2)] for _ in range(NE)]
        Eb_b = [[Buf("B1.E") for _ in range(2)] for _ in range(NE)]
        rden = ar.alloc([16], F32); rden_b = Buf("B1.rden")
        t0b = [ar.alloc([128], F32) for _ in range(2)]; t0b_b = [Buf("B1.t0") for _ in range(2)]
        attf = [ar.alloc([128], F32) for _ in range(4)]; attf_b = [Buf("B1.attf") for _ in range(4)]
        junk = ar.alloc([128], F32); junk_b = Buf("B1.junk")
        ssq = ar.alloc([8], F32); ssq_b = Buf("B1.ssq")
        attb = [ar.alloc([128], BF16) for _ in range(4)]; attb_b = [Buf("B1.attb") for _ in range(4)]
        attTs = [ar.alloc([TT], BF16) for _ in range(2)]; attTs_b = [Buf("B1.attTs0"), Buf("B1.attTs1")]
        psT = psum_ap[:, 7, :].bitcast(BF16)
        dbgO = ar.alloc([3, 512], F32); dbgO_b = Buf("dbgO")
        Oc = ar.alloc([3, 512], F32); Oc_b = [Buf("B1.Oc%d" % i_) for i_ in range(3)]
        for i in range(2):
            memset("dve", Vh[i][:, :, 128:129], 1.0, (qkv_b[i],))
        sbank = 0
        ei = 0
        nst = 0
        cast_jobs = list(cast_jobs)
        pendP2 = [None]
        pendP3 = [None]

        def make_post(h, qb, q0):
            def P1():
                for ob_ in range(3):
                    ncol = 387 if ob_ < 2 else 258
                    cp("dve", Oc[:, ob_, 0:ncol], ps(4 + ob_)[:, 0:ncol], (ps_b[4 + ob_],), (Oc_b[ob_],))
                for s_ in range(8):
                    ob = s_ // 3
                    oc = (s_ % 3) * 129
                    sd.add("dve", (lambda ob=ob, oc=oc, s_=s_: (lambda e: e.reciprocal(rden[:, s_:s_ + 1], Oc[:, ob, oc + 128:oc + 129])))(),
                           (Oc_b[ob],), (rden_b,))
                ts("dve", rden[:, 8:12], rden[:, 4:8], lams[:, 5:6], None, ALU.mult, None, (rden_b, par_b), (rden_b,))
                memset("dve", ssq[:, 0:4], 0.0, (ssq_b,))
                for j in range(4):
                    s0 = j
                    s1 = 4 + j
                    ti_ = j % 2
                    ts("dve", t0b[ti_], Oc[:, s0 // 3, (s0 % 3) * 129:(s0 % 3) * 129 + 128], rden[:, j:j + 1], None,
                       ALU.mult, None, (Oc_b[s0 // 3], rden_b), (t0b_b[ti_],))
                    stt("dve", attf[j], Oc[:, s1 // 3, (s1 % 3) * 129:(s1 % 3) * 129 + 128], rden[:, 8 + j:9 + j], t0b[ti_],
                        ALU.mult, ALU.add, (Oc_b[s1 // 3], rden_b, t0b_b[ti_]), (attf_b[j],))

            def P2():
                for j in range(4):
                    act(junk, attf[j], AF.Square, (attf_b[j],), (junk_b, ssq_b), accum=ssq[:, j:j + 1])
                act(ssq[:, 4:8], ssq[:, 0:4], AF.Ln, (ssq_b,), (ssq_b,), scale=1.0 / 128.0, bias=EPS)
                act(ssq[:, 4:8], ssq[:, 4:8], AF.Exp, (ssq_b,), (ssq_b,), scale=-0.5)

            def P3():
                ai = (h * NQB + qb) % 2
                for j in range(4):
                    stt("dve", attb[j], attf[j], ssq[:, 4 + j:5 + j], subg, ALU.mult, ALU.mult,
                        (attf_b[j], ssq_b, par_b), (attb_b[j],))
                    tr(psT[:, j * 128:(j + 1) * 128], attb[j], identb, (attb_b[j], const_b), (ps_b[7],))
                cp("dve", attTs[ai], psT[:, 0:TT], (ps_b[7],), (attTs_b[ai],))
                dma(Q_IO, attT[h * 128:(h + 1) * 128, q0:q0 + TT], attTs[ai], attTs_b[ai], (attTs_b[ai],), ())
            return P1, P2, P3

        def flush_post():
            if pendP2[0] is not None:
                pendP2[0]()
                pendP2[0] = None
            if pendP3[0] is not None:
                pendP3[0]()
                pendP3[0] = None

        for h in range(8):
            i = h % 2
            dma(Q_IO, qh[i], qT[h * 128:(h + 1) * 128, 0:S], qkv_b[i], (), (qkv_b[i],))
            dma(Q_IO, kh[i], kT[h * 128:(h + 1) * 128, 0:S], qkv_b[i], (), (qkv_b[i],))
            for k4 in range(0, NKB, 8):
                ke = min(NKB, k4 + 8)
                dma(Q_IO, Vh[i][:, k4:ke, 0:128],
                    Vs[k4 * 128:ke * 128, h * 128:(h + 1) * 128].rearrange("(kb p) e -> p kb e", p=128),
                    qkv_b[i], (), (qkv_b[i],))
            for _ in range(5):
                if cast_jobs:
                    cast_jobs.pop(0)()
            for qb in range(NQB):
                q0 = qb * TT
                fifo = []
                for kc in range(NKB + 2):
                    if kc < NKB:
                        b0 = sbank * 2
                        sbank = (sbank + 1) % 2
                        e = ei % NE
                        ei += 1
                        mm(ps(b0), kh[i][0:64, kc * 128:(kc + 1) * 128], qh[i][0:64, q0:q0 + TT], True, True,
                           (qkv_b[i],), (ps_b[b0],))
                        mm(ps(b0 + 1), kh[i][64:128, kc * 128:(kc + 1) * 128], qh[i][64:128, q0:q0 + TT], True, True,
                           (qkv_b[i],), (ps_b[b0 + 1],))
                        act(Eb[e][0], ps(b0), AF.Exp, (ps_b[b0],), (Eb_b[e][0],), scale=0.125)
                        act(Eb[e][1], ps(b0 + 1), AF.Exp, (ps_b[b0 + 1],), (Eb_b[e][1],), scale=0.125)
                        fifo.append((e, kc))
                    if kc == 2 and pendP2[0] is not None:
                        pendP2[0]()
                        pendP2[0] = None
                    if kc == min(6, NKB + 1) and pendP3[0] is not None:
                        pendP3[0]()
                        pendP3[0] = None
                    if kc >= 2:
                        e_, kc_ = fifo.pop(0)
                        for c in range(2):
                            for j in range(4):
                                s = c * 4 + j
                                ob = 4 + s // 3
                                oc = (s % 3) * 129
                                mm(ps(ob)[:, oc:oc + 129], Eb[e_][c][:, j * 128:(j + 1) * 128], Vh[i][:, kc_, 0:129],
                                   kc_ == 0 and s % 3 == 0, kc_ == NKB - 1, (Eb_b[e_][c], qkv_b[i]), (ps_b[ob],), skip=True)
                flush_post()
                P1, P2, P3 = make_post(h, qb, q0)
                P1()
                pendP2[0] = P2
                pendP3[0] = P3
        flush_post()
        for j_ in cast_jobs:
            j_()
        sd.fence()

    def phaseB2(l, S):
        ar.reset()
        sd.set_pool("B2")
        NH = 2 if S >= 1024 else 1
        HS = S // NH
        xpad = ar.alloc([S + 8], F32); xpad_b = Buf("B2.xpad")
        gy = ar.alloc([S], BF16); gy_b = Buf("B2.gy")
        xc = ar.alloc([S], F32); xc_b = Buf("B2.xc")
        xcb = ar.alloc([S], BF16); xcb_b = Buf("B2.xcb")
        A_ = [ar.alloc([S], F32) for _ in range(2)]; A_b = [[Buf("B2.A") for _ in range(NH)] for _ in range(2)]
        B_ = [ar.alloc([S], F32) for _ in range(2)]; B_b = [[Buf("B2.B") for _ in range(NH)] for _ in range(2)]
        T_ = [ar.alloc([S], F32) for _ in range(2)]; T_b = [[Buf("B2.T") for _ in range(NH)] for _ in range(2)]
        rec = ar.alloc([S], BF16); rec_b = Buf("B2.rec")
        memset("dve", xpad[:, 0:2], 0.0, (xpad_b,))
        memset("dve", xpad[:, S + 2:S + 8], 0.0, (xpad_b,))
        kbs = [0]

        def rev(a):
            base = a
            apl = [list(x) for x in base.ap]
            n = apl[-1][1]
            apl[-1] = [-1, n]
            return AP(base.tensor, base.offset + (n - 1), apl)

        def scan_op(out, a0, a1, init):
            return lambda e: e.tensor_tensor_scan(out, a0, a1, init, ALU.mult, ALU.add)
        sl = [slice(hx * HS, (hx + 1) * HS) for hx in range(NH)]

        def dir_steps(c, d):
            ci = d * 8 + c
            order = list(range(NH)) if d == 0 else list(range(NH - 1, -1, -1))
            A, B, T = A_[d], B_[d], T_[d]
            Ab, Bb, Tb = A_b[d], B_b[d], T_b[d]

            def gate(gt, dst, dstb):
                def f():
                    for blk in range(S // TT):
                        hh_ = (blk * TT) // HS
                        b = kbs[0] % 8
                        kbs[0] += 1
                        idx = (d * 2 + gt) * 8 + c
                        mm(ps(b), rgw[:, idx, :], xcb[:, blk * TT:(blk + 1) * TT], True, True, (xcb_b, rgw_b), (ps_b[b],))
                        act(dst[:, blk * TT:(blk + 1) * TT], ps(b), AF.Tanh, (ps_b[b], par_b), (dstb[hh_],),
                            scale=0.5, bias=rgbh[:, idx:idx + 1])
                return f

            def s2():
                pass

            def s3():
                for hx in order:
                    act(T[:, sl[hx]], A[:, sl[hx]], AF.Tanh, (Ab[hx], par_b), (Tb[hx],),
                        scale=cLh[:, ci:ci + 1], bias=cLh[:, ci:ci + 1])
                    act(B[:, sl[hx]], A[:, sl[hx]], AF.Exp, (Ab[hx], par_b), (Bb[hx],),
                        scale=cL2[:, ci:ci + 1], bias=cL2[:, ci:ci + 1])
                    act(A[:, sl[hx]], A[:, sl[hx]], AF.Exp, (Ab[hx], par_b), (Ab[hx],),
                        scale=cLh[:, ci:ci + 1], bias=cLh[:, ci:ci + 1])

            def s4():
                for hx in order:
                    stt("dve", B[:, sl[hx]], B[:, sl[hx]], 1.0, T[:, sl[hx]], ALU.add, ALU.mult,
                        (Bb[hx], Tb[hx]), (Bb[hx],))

            def s6():
                for hx in order:
                    act(B[:, sl[hx]], B[:, sl[hx]], AF.Sqrt, (Bb[hx],), (Bb[hx],), scale=-1.0)

            def s7():
                for hx in order:
                    stt("dve", B[:, sl[hx]], T[:, sl[hx]], 1.0, B[:, sl[hx]], ALU.add, ALU.mult,
                        (Tb[hx], Bb[hx]), (Bb[hx],))
                    stt("dve", B[:, sl[hx]], B[:, sl[hx]], 0.5, xc[:, sl[hx]], ALU.mult, ALU.mult,
                        (Bb[hx], xc_b), (Bb[hx],))

            def s8():
                prev = None
                for hx in order:
                    if d == 0:
                        init = 0.0 if prev is None else T[:, prev * HS + HS - 1:prev * HS + HS]
                        rd = (Ab[hx], Bb[hx]) + (() if prev is None else (Tb[prev],))
                        sd.add("dve", scan_op(T[:, sl[hx]], A[:, sl[hx]], B[:, sl[hx]], init), rd, (Tb[hx],))
                    else:
                        init = 0.0 if prev is None else T[:, prev * HS:prev * HS + 1]
                        rd = (Ab[hx], Bb[hx]) + (() if prev is None else (Tb[prev],))
                        sd.add("dve", scan_op(rev(T[:, sl[hx]]), rev(A[:, sl[hx]]), rev(B[:, sl[hx]]), init),
                               rd, (Tb[hx],))
                    prev = hx
            return [gate(0, A, Ab), s2, s3, s4, gate(1, T, Tb), s6, s7, s8]

        dma(Q_IO, xpad[:, 2:2 + S], xrT[0:128, 0:S], xpad_b, (), (xpad_b,))
        for c in range(8):
            dma(Q_IO, gy, gyT[c * 128:(c + 1) * 128, 0:S], gy_b, (), (gy_b,))
            ts("dve", xc, xpad[:, 0:S], convw[:, c * 4:c * 4 + 1], convb[:, c:c + 1], ALU.mult, ALU.add,
               (xpad_b, par_b), (xc_b,))
            for j in range(1, 4):
                stt("dve", xc, xpad[:, j:j + S], convw[:, c * 4 + j:c * 4 + j + 1], xc, ALU.mult, ALU.add,
                    (xpad_b, xc_b, par_b), (xc_b,))
            cp("act", xcb, xc, (xc_b,), (xcb_b,))
            if c + 1 < 8:
                dma(Q_IO, xpad[:, 2:2 + S], xrT[(c + 1) * 128:(c + 2) * 128, 0:S], xpad_b, (), (xpad_b,))
            st0 = dir_steps(c, 0)
            st1 = dir_steps(c, 1)
            for f0, f1 in zip(st0, st1):
                f0()
                f1()
            for hx in range(NH):
                s_ = sl[hx]
                tt("pool", T_[0][:, s_], T_[0][:, s_], T_[1][:, s_], ALU.add, (T_b[0][hx], T_b[1][hx]), (T_b[0][hx],))
                stt("dve", rec[:, s_], T_[0][:, s_], 0.5, gy[:, s_], ALU.mult, ALU.mult, (T_b[0][hx], gy_b), (rec_b,))
            dma(Q_IO, recT[c * 128:(c + 1) * 128, 0:S], rec, rec_b, (rec_b,), ())
        sd.fence()

    def phaseC(l, S, key, last):
        ar.reset()
        sd.set_pool("C")
        x = ar.alloc([8, TT], F32); x_b = Buf("C.x")
        att = ar.alloc([8, TT], BF16); att_b = Buf("C.att")
        rec = ar.alloc([8, TT], BF16); rec_b = Buf("C.rec")
        th = [ar.alloc([8, TT], BF16) for _ in range(2)]; th_b = [Buf("C.th0"), Buf("C.th1")]
        m0 = [ar.alloc([TT], F32) for _ in range(2)]; m0_b = [Buf("C.m0") for _ in range(2)]
        m1 = [ar.alloc([TT], F32) for _ in range(2)]; m1_b = [Buf("C.m1") for _ in range(2)]
        mg = ar.alloc([8, TT], BF16); mg_b = Buf("C.mg")
        sq = ar.alloc([8, TT], BF16); sq_b = Buf("C.sq")
        rstd = ar.alloc([TT], F32); rstd_b = Buf("C.rstd")
        n2 = ar.alloc([8, TT], BF16); n2_b = Buf("C.n2")
        rl = [ar.alloc([TT], F32) for _ in range(3)]; rl_b = [Buf("C.rl") for _ in range(3)]
        hh = ar.alloc([32, TT], BF16); hh_b = Buf("C.h")
        yt = [ar.alloc([D], F32) for _ in range(2)]; yt_b = [Buf("C.yt0"), Buf("C.yt1")]
        xTv = xT.rearrange("(c p) s -> p c s", p=128)
        attTv = attT.rearrange("(c p) s -> p c s", p=128)
        recTv = recT.rearrange("(c p) s -> p c s", p=128)
        thTv = thT.rearrange("(c p) s -> p c s", p=128)
        tiles = []
        for ti in range(S // TT):
            tiles += [(l, t) for t in range(14, 36)]
        ws = WStream(tiles)
        kb = [0]

        def nb():
            b = kb[0] % 8
            kb[0] += 1
            return b
        km = 0
        kr = 0
        ky = 0
        def loadsC(ti):
            t0 = ti * TT
            dma(Q_IO, att, attTv[:, :, t0:t0 + TT], att_b, (), (att_b,))
            dma(Q_IO, rec, recTv[:, :, t0:t0 + TT], rec_b, (), (rec_b,))
            dma(Q_IO, th[0], thTv[:, 0:8, t0:t0 + TT], th_b[0], (), (th_b[0],))
            dma(Q_IO, th[1], thTv[:, 8:16, t0:t0 + TT], th_b[1], (), (th_b[1],))

        loadsC(0)
        for ti in range(S // TT):
            t0 = ti * TT
            dma(Q_IO, x, xTv[:, :, t0:t0 + TT], x_b, (), (x_b,))
            for half in range(2):
                wa, wa_b = ws.next()
                wr, wr_b = ws.next()
                wav = wa.rearrange("p (c n) -> p c n", n=512)
                wrv = wr.rearrange("p (c n) -> p c n", n=512)
                for j in range(4):
                    oc = half * 4 + j
                    bA = nb()
                    for c in range(8):
                        mm(ps(bA), wav[:, c, j * 128:(j + 1) * 128], att[:, c, :], c == 0, c == 7, (att_b, wa_b), (ps_b[bA],))
                    bR = nb()
                    for c in range(8):
                        mm(ps(bR), wrv[:, c, j * 128:(j + 1) * 128], rec[:, c, :], c == 0, c == 7, (rec_b, wr_b), (ps_b[bR],))
                    i = km % 2
                    km += 1
                    stt("dve", m0[i], th[0][:, oc, :], 1.0, ps(bA), ALU.add, ALU.mult, (th_b[0], ps_b[bA]), (m0_b[i],))
                    stt("dve", m1[i], th[1][:, oc, :], 1.0, ps(bR), ALU.add, ALU.mult, (th_b[1], ps_b[bR]), (m1_b[i],))
                    tt("pool", mg[:, oc, :], m0[i], m1[i], ALU.add, (m0_b[i], m1_b[i]), (mg_b,))
                ws.done_one()
                ws.done_one()
            for half in range(2):
                wo, wo_b = ws.next()
                wov = wo.rearrange("p (c n) -> p c n", n=512)
                for j in range(4):
                    oc = half * 4 + j
                    b = nb()
                    for c in range(8):
                        mm(ps(b), wov[:, c, j * 128:(j + 1) * 128], mg[:, c, :], c == 0, c == 7, (mg_b, wo_b), (ps_b[b],))
                    stt("dve", x[:, oc, :], ps(b), 0.5, x[:, oc, :], ALU.mult, ALU.add, (ps_b[b], x_b), (x_b,))
                ws.done_one()
            act(sq, x, AF.Square, (x_b,), (sq_b,))
            b = nb()
            for c in range(8):
                mm(ps(b), onesfull, sq[:, c, :], c == 0, c == 7, (sq_b, const_b), (ps_b[b],))
            act(rstd, ps(b), AF.Ln, (ps_b[b],), (rstd_b,), bias=EPS)
            act(rstd, rstd, AF.Exp, (rstd_b,), (rstd_b,), scale=-0.5)
            for c in range(8):
                stt("dve", n2[:, c, :], x[:, c, :], g2[:, c:c + 1], rstd, ALU.mult, ALU.mult, (x_b, rstd_b, par_b), (n2_b,))
            for g in range(8):
                w1t, w1_b = ws.next()
                w1v = w1t.rearrange("p (c n) -> p c n", n=512)
                for j in range(4):
                    f = g * 4 + j
                    b = nb()
                    for c in range(8):
                        mm(ps(b), w1v[:, c, j * 128:(j + 1) * 128], n2[:, c, :], c == 0, c == 7, (n2_b, w1_b), (ps_b[b],))
                    i = kr % 3
                    kr += 1
                    act(rl[i], ps(b), AF.Relu, (ps_b[b],), (rl_b[i],))
                    tt("pool", hh[:, f, :], rl[i], rl[i], ALU.mult, (rl_b[i],), (hh_b,))
                ws.done_one()
            if ti + 1 < S // TT:
                loadsC(ti + 1)
            for oc in range(8):
                w2t, w2_b = ws.next()
                w2v = w2t.rearrange("p (k n) -> p k n", n=128)
                b = nb()
                for kf in range(32):
                    mm(ps(b), w2v[:, kf, :], hh[:, kf, :], kf == 0, kf == 31, (hh_b, w2_b), (ps_b[b],))
                tt("dve", x[:, oc, :], ps(b), x[:, oc, :], ALU.add, (ps_b[b], x_b), (x_b,))
                ws.done_one()
            if not last:
                dma(Q_IO, xTv[:, :, t0:t0 + TT], x, x_b, (x_b,), ())
            else:
                for tb in range(4):
                    i = ky % 2
                    ky += 1
                    for hv in range(2):
                        b = nb()
                        for c4 in range(4):
                            c = hv * 4 + c4
                            tr(ps(b)[:, c4 * 128:(c4 + 1) * 128], x[:, c, tb * 128:(tb + 1) * 128], ident_f,
                               (x_b, const_b), (ps_b[b],))
                        cp("act" if hv else "dve", yt[i][:, hv * 512:(hv + 1) * 512], ps(b), (ps_b[b],), (yt_b[i],))
                    dma(Q_IO, y_out[key][t0 + tb * 128:t0 + (tb + 1) * 128, :], yt[i], yt_b[i], (yt_b[i],), ())
        sd.fence()

    for key, S in (("p", SP), ("s", SS)):
        cur["key"] = key
        phase0(key, S)
        for l in range(L):
            layer_params(l)
            phaseA(l, S)
            cast_jobs = cast_layer(l + 1, lazy=True) if (key == "p" and l + 1 < L) else []
            phaseB1(l, S, cast_jobs)
            phaseB2(l, S)
            phaseC(l, S, key, l == L - 1)

    n_sems = {e: max(1, (len(sd.ops[e]) + SEM_LIMIT - 1) // SEM_LIMIT) for e in Sched.ENGS}
    for e in Sched.ENGS:
        k = 0
        for op in sd.ops[e]:
            if op.need_inc and op.dma_sem is None:
                k += 1
                op.idx = k
    import contextlib
    with contextlib.ExitStack() as st:
        eng_sems = {}
        for e in Sched.ENGS:
            cnt = sum(1 for op in sd.ops[e] if op.idx is not None)
            ns = max(1, (cnt + SEM_LIMIT - 1) // SEM_LIMIT)
            eng_sems[e] = [st.enter_context(nc.semaphore("se_%s_%d" % (e, i))) for i in range(ns)]
        for i, b in enumerate(sd.slots):
            b.sem = st.enter_context(nc.semaphore("sd_%d" % i))
        block = st.enter_context(nc.Block())

        def resolve(d):
            if d[0] == "dma":
                return d[1].sem, d[2], None
            op = d[1]
            ep = (op.idx - 1) // SEM_LIMIT
            return eng_sems[op.eng][ep], (op.idx - 1) % SEM_LIMIT + 1, op.eng

        def emit(e, h):
            waited = {}
            for op in sd.ops[e]:
                for d in op.deps:
                    if d[0] == "eng":
                        if d[1].idx is None:
                            continue
                        if d[1].eng == e and (e == "pe" or not SAME_ENG_SYNC):
                            continue
                    sem, val, _ = resolve(d)
                    key_ = id(sem)
                    if waited.get(key_, 0) >= val:
                        continue
                    waited[key_] = val
                    h.wait_ge(sem, val)
                ins = op.fn(h)
                if op.dma_sem is not None:
                    ins.then_inc(op.dma_sem.sem, 16)
                elif op.idx is not None:
                    ep = (op.idx - 1) // SEM_LIMIT
                    ins.then_inc(eng_sems[e][ep], 1)

        @block.tensor
        def _(h):
            emit("pe", h)

        @block.scalar
        def _(h):
            emit("act", h)

        @block.vector
        def _(h):
            emit("dve", h)

        @block.gpsimd
        def _(h):
            emit("pool", h)

        @block.sync
        def _(h):
            emit("sp", h)
    stats = {e: len(sd.ops[e]) for e in Sched.ENGS}
    return nc, stats


def _host_layout(inp, L, S_MAX):
    f = np.float32

    def fm(v, nch):
        return np.ascontiguousarray(np.asarray(v, f).reshape(nch, 128).T)
    out = {}
    out["g1"] = np.stack([fm(inp["norm1_g"][l], 8) for l in range(L)])
    out["g2"] = np.stack([fm(inp["norm2_g"][l], 8) for l in range(L)])
    out["bgate"] = np.stack([fm(inp["b_gate"][l], 16) for l in range(L)])
    qkg = np.zeros((L, 128, 2), f)
    for l in range(L):
        qkg[l, :, 0] = np.tile(np.asarray(inp["q_norm_g"][l], f), 2)
        qkg[l, :, 1] = np.tile(np.asarray(inp["k_norm_g"][l], f), 2)
    out["qkg"] = qkg
    out["lamv"] = np.ascontiguousarray(np.broadcast_to(np.asarray(inp["lam_vecs"], f)[:L].reshape(L, 1, 256), (L, 128, 256)))
    out["subg"] = np.ascontiguousarray(np.broadcast_to(np.asarray(inp["subln_g"], f)[:L].reshape(L, 1, 128), (L, 128, 128)))
    cw = np.asarray(inp["conv_w"], f)[:L]
    convw = np.zeros((L, 128, 8, 4), f)
    for l in range(L):
        for j in range(4):
            convw[l, :, :, j] = fm(cw[l, j], 8)
    out["convw"] = convw.reshape(L, 128, 32)
    out["convb"] = np.stack([fm(inp["conv_b"][l], 8) for l in range(L)])
    rb = np.asarray(inp["rg_b"], f)[:L]
    rgb = np.zeros((L, 128, 2, 2, 8), f)
    for l in range(L):
        for d in range(2):
            for g in range(2):
                rgb[l, :, d, g, :] = fm(rb[l, d, g], 8)
    out["rgb"] = rgb.reshape(L, 128, 32)
    rL = np.asarray(inp["rg_L"], f)[:L]
    rgL = np.zeros((L, 128, 2, 8), f)
    for l in range(L):
        for d in range(2):
            rgL[l, :, d, :] = fm(rL[l, d], 8)
    out["rgL"] = rgL.reshape(L, 128, 16)
    rw = np.asarray(inp["rg_w"], f)[:L]
    rgw = np.zeros((L, 2, 2, 8, 128, 128), f)
    for c in range(8):
        rgw[:, :, :, c, 0:64, 0:64] = rw[:, :, :, 2 * c]
        rgw[:, :, :, c, 64:128, 64:128] = rw[:, :, :, 2 * c + 1]
    out["rgw"] = rgw.reshape(L, 32, 128, 128)
    out["ident"] = np.eye(128, dtype=f)
    obd = np.zeros((128, 128), f)
    obd[0:64, 0:64] = 1.0 / 64.0
    obd[64:128, 64:128] = 1.0 / 64.0
    out["onesbd"] = obd
    out["onesfull"] = np.full((128, 128), 1.0 / 1024.0, f)
    pm = np.zeros((128, 128), f)
    cosT = np.ones((128, S_MAX), f)
    sinT = np.zeros((128, S_MAX), f)
    pos = np.arange(S_MAX, dtype=f)
    inv_freq = (np.float32(500000.0) ** (-np.arange(0, 16, 2, dtype=f) / np.float32(16))).astype(f)
    ang = (pos[:, None] * inv_freq[None, :]).astype(f)
    cs = np.cos(ang).astype(f).T
    sn = np.sin(ang).astype(f).T
    for gb in (0, 64):
        for m in range(8):
            pm[gb + m + 8, gb + m] = 1.0
            pm[gb + m, gb + m + 8] = 1.0
            cosT[gb + m] = cs[m]
            cosT[gb + m + 8] = cs[m]
            sinT[gb + m] = -sn[m]
            sinT[gb + m + 8] = sn[m]
    out["perm"] = pm
    out["cosT"] = cosT
    out["sinT"] = sinT
    return out


_CACHE = {}


def run(inputs, L=4, n_cores=8):
    xp = np.asarray(inputs["x_prompt"], np.float32)
    xs = np.asarray(inputs["x_sample"], np.float32)
    SP, SS = xp.shape[1], xs.shape[1]
    keyc = (SP, SS, L)
    if keyc not in _CACHE:
        _CACHE[keyc] = build_program(SP, SS, L)
    nc, stats = _CACHE[keyc]
    lay = _host_layout(inputs, L, max(SP, SS))
    shared = {
        "w_in": np.ascontiguousarray(np.asarray(inputs["w_in"], np.float32)[:L]),
        "wba": np.ascontiguousarray(np.asarray(inputs["w_branch_att"], np.float32)[:L]),
        "wbr": np.ascontiguousarray(np.asarray(inputs["w_branch_rec"], np.float32)[:L]),
        "wout": np.ascontiguousarray(np.asarray(inputs["w_out"], np.float32)[:L]),
        "w1": np.ascontiguousarray(np.asarray(inputs["w_ff1"], np.float32)[:L]),
        "w2": np.ascontiguousarray(np.asarray(inputs["w_ff2"], np.float32)[:L]),
    }
    shared.update(lay)
    in_maps = []
    for i in range(n_cores):
        m = dict(shared)
        m["xp"] = np.ascontiguousarray(xp[i])
        m["xs"] = np.ascontiguousarray(xs[i])
        in_maps.append(m)
    res = run_bass_kernel_spmd(nc, in_maps, core_ids=list(range(n_cores)))
    if DEBUG:
        _CACHE["dbg"] = res.results
    yp = np.stack([np.asarray(r["yp"], np.float32) for r in res.results])
    ys = np.stack([np.asarray(r["ys"], np.float32) for r in res.results])
    return yp, ys


def kernel(**inputs):
    return run(inputs, L=4, n_cores=8)
```
